# Optimizing a Trainium2 kernel written in Bass

```python
import math
import jax, jax.numpy as jnp
from jax import lax
import numpy as np

D_MODEL = 1024
BATCH = 8
SEQ = 4096
DEPTH = 1

D_MIX = D_MODEL
ATT_HEADS = 8
ATT_HEAD_DIM = 64
ATT_WIDTH = ATT_HEADS * ATT_HEAD_DIM
DILATED_GROUPS = ((128, 1), (512, 4), (2048, 16))
ATT_BLOCK = 128
MLSTM_HEADS = 4
MLSTM_HEAD_DIM = 128
MLSTM_WIDTH = MLSTM_HEADS * MLSTM_HEAD_DIM
MLSTM_CHUNK = 64
MLSTM_CONV = 4
D_FF = 2816
FFN_CONV = 3
EPS = 1e-6
PROJ_WIDTH = 3 * ATT_WIDTH + 4 * MLSTM_WIDTH + 2 * MLSTM_HEADS
SPLITS = (ATT_WIDTH, 2 * ATT_WIDTH, 3 * ATT_WIDTH,
          3 * ATT_WIDTH + MLSTM_WIDTH, 3 * ATT_WIDTH + 2 * MLSTM_WIDTH,
          3 * ATT_WIDTH + 3 * MLSTM_WIDTH, 3 * ATT_WIDTH + 4 * MLSTM_WIDTH,
          3 * ATT_WIDTH + 4 * MLSTM_WIDTH + MLSTM_HEADS)

kernel_name = "hymba_dilated_mlstm_convffn"


def _rmsnorm(x, w):
    xf = x.astype(jnp.float32)
    y = xf * lax.rsqrt(jnp.mean(xf * xf, axis=-1, keepdims=True) + EPS)
    return (y * w.astype(jnp.float32)).astype(x.dtype)


def _causal_dwconv(x, w, b):
    K = w.shape[0]
    S = x.shape[1]
    xp = jnp.pad(x, ((0, 0), (K - 1, 0), (0, 0)))
    y = b
    for j in range(K):
        y = y + xp[:, j:j + S, :] * w[j]
    return y


def _banded_attention(q, k, v, span):
    N, L, H, Dh = q.shape
    nb = -(-L // ATT_BLOCK)
    Lp = nb * ATT_BLOCK
    pad = ((0, 0), (0, Lp - L), (0, 0), (0, 0))

    def blocks(t):
        return jnp.pad(t.astype(jnp.float32), pad).reshape(N, nb, ATT_BLOCK, H, Dh)

    qb, kb, vb = blocks(q), blocks(k), blocks(v)

    def with_prev(t):
        prev = jnp.pad(t, ((0, 0), (1, 0), (0, 0), (0, 0), (0, 0)))[:, :-1]
        return jnp.concatenate([prev, t], axis=2)

    kk, vv = with_prev(kb), with_prev(vb)
    s = jnp.einsum('nbqhd,nbkhd->nbhqk', qb, kk) * (Dh ** -0.5)
    qi = jnp.arange(ATT_BLOCK)[:, None]
    kj = jnp.arange(2 * ATT_BLOCK)[None, :]
    dist = ATT_BLOCK + qi - kj
    key_pos = (jnp.arange(nb)[:, None, None] - 1) * ATT_BLOCK + kj[None]
    mask = (dist >= 0) & (dist <= span) & (key_pos >= 0)
    s = jnp.where(mask[None, :, None], s, -jnp.inf)
    m = jnp.max(s, axis=-1, keepdims=True)
    p = jnp.exp(s - m)
    den = jnp.sum(p, axis=-1, keepdims=True)
    o = jnp.einsum('nbhqk,nbkhd->nbqhd', p / den, vv).reshape(N, Lp, H, Dh)[:, :L]
    lse = (m + jnp.log(den))[..., 0].transpose(0, 1, 3, 2).reshape(N, Lp, H)[:, :L]
    return o, lse


def _dilated_attention(q, k, v):
    B, S, H, Dh = q.shape
    outs, lses = [], []
    for window, dil in DILATED_GROUPS:
        L = S // dil

        def to_sub(t):
            return t.reshape(B, L, dil, H, Dh).transpose(0, 2, 1, 3, 4).reshape(B * dil, L, H, Dh)

        o, lse = _banded_attention(to_sub(q), to_sub(k), to_sub(v), window // dil)
        outs.append(o.reshape(B, dil, L, H, Dh).transpose(0, 2, 1, 3, 4).reshape(B, S, H, Dh))
        lses.append(lse.reshape(B, dil, L, H).transpose(0, 2, 1, 3).reshape(B, S, H))
    wts = jax.nn.softmax(jnp.stack(lses, axis=0), axis=0)
    return jnp.sum(wts[..., None] * jnp.stack(outs, axis=0), axis=0)


def _mlstm_chunkwise(q, k, v, log_i, log_f):
    B, S, H, D = q.shape
    Lc = MLSTM_CHUNK
    nc = S // Lc

    def chunks(t):
        return t.reshape(B, nc, Lc, H, D).transpose(0, 3, 1, 2, 4)

    def chunks_g(t):
        return t.reshape(B, nc, Lc, H).transpose(0, 3, 1, 2)

    qc, kc, vc = chunks(q), chunks(k), chunks(v)
    li, lf = chunks_g(log_i), chunks_g(log_f)
    bcum = jnp.cumsum(lf, axis=-1)
    g = bcum[..., -1]
    a = g[..., None] - bcum + li

    def step(carry, xs):
        C, n, m = carry
        k_j, v_j, a_j, g_j = xs
        m_new = jnp.maximum(g_j + m, jnp.max(a_j, axis=-1))
        decay = jnp.exp(g_j + m - m_new)
        w = jnp.exp(a_j - m_new[..., None])
        C_new = decay[..., None, None] * C + jnp.einsum('bhs,bhsv,bhsk->bhvk', w, v_j, k_j)
        n_new = decay[..., None] * n + jnp.einsum('bhs,bhsk->bhk', w, k_j)
        return (C_new, n_new, m_new), (C, n, m)

    init = (jnp.zeros((B, H, D, D), jnp.float32), jnp.zeros((B, H, D), jnp.float32),
            jnp.zeros((B, H), jnp.float32))
    xs = (kc.transpose(2, 0, 1, 3, 4), vc.transpose(2, 0, 1, 3, 4),
          a.transpose(2, 0, 1, 3), g.transpose(2, 0, 1))
    _, (C_prev, n_prev, m_prev) = lax.scan(step, init, xs)
    C_prev = C_prev.transpose(1, 2, 0, 3, 4)
    n_prev = n_prev.transpose(1, 2, 0, 3)
    m_prev = m_prev.transpose(1, 2, 0)

    causal = jnp.tril(jnp.ones((Lc, Lc), dtype=bool))
    dmat = jnp.where(causal, bcum[..., :, None] - bcum[..., None, :] + li[..., None, :], -jnp.inf)
    inter_log = bcum + m_prev[..., None]
    m_t = jnp.maximum(inter_log, jnp.max(dmat, axis=-1))
    scores = jnp.einsum('bhctd,bhcsd->bhcts', qc, kc) * jnp.exp(dmat - m_t[..., None])
    inter = jnp.exp(inter_log - m_t)
    num = (jnp.einsum('bhcts,bhcsd->bhctd', scores, vc)
           + inter[..., None] * jnp.einsum('bhcvk,bhctk->bhctv', C_prev, qc))
    den = jnp.sum(scores, axis=-1) + inter * jnp.einsum('bhck,bhctk->bhct', n_prev, qc)
    h = num / jnp.maximum(jnp.abs(den), jnp.exp(-m_t))[..., None]
    return h.transpose(0, 2, 3, 1, 4).reshape(B, S, H, D)


def _hybrid_mixer(h, w_in, mlstm_conv_w, mlstm_conv_b, mlstm_i_bias, mlstm_f_bias,
                  att_out_norm_w, mlstm_out_norm_w, w_out):
    B, S, _ = h.shape
    proj = h @ w_in
    aq, ak, av, mq, mk, mv, mo, mig, mfg = jnp.split(proj, SPLITS, axis=-1)

    def att_heads(t):
        return t.reshape(B, S, ATT_HEADS, ATT_HEAD_DIM)

    att = _dilated_attention(att_heads(aq), att_heads(ak), att_heads(av))
    att = _rmsnorm(att, att_out_norm_w.reshape(ATT_HEADS, ATT_HEAD_DIM)).reshape(B, S, ATT_WIDTH)

    mqk = jax.nn.silu(_causal_dwconv(jnp.concatenate([mq, mk], axis=-1), mlstm_conv_w, mlstm_conv_b))
    mq, mk = jnp.split(mqk, 2, axis=-1)

    def ml_heads(t):
        return t.astype(jnp.float32).reshape(B, S, MLSTM_HEADS, MLSTM_HEAD_DIM)

    log_i = (mig + mlstm_i_bias).astype(jnp.float32)
    log_f = jax.nn.log_sigmoid((mfg + mlstm_f_bias).astype(jnp.float32))
    hm = _mlstm_chunkwise(ml_heads(mq), ml_heads(mk) * (MLSTM_HEAD_DIM ** -0.5), ml_heads(mv),
                          log_i, log_f)
    hm = _rmsnorm(hm, mlstm_out_norm_w.reshape(MLSTM_HEADS, MLSTM_HEAD_DIM)).reshape(B, S, MLSTM_WIDTH)
    hm = jax.nn.sigmoid(mo.astype(jnp.float32)) * hm

    y = jnp.concatenate([att, hm], axis=-1).astype(h.dtype)
    return y @ w_out


def _conv_ffn(h, w_ffn_up, ffn_conv_w, ffn_conv_b, w_ffn_down):
    u = _causal_dwconv(h @ w_ffn_up, ffn_conv_w, ffn_conv_b)
    gate, val = jnp.split(u, 2, axis=-1)
    return (jax.nn.silu(gate) * val) @ w_ffn_down


def setup_inputs(seed: int = 0) -> dict:
    key = jax.random.key(seed)
    ks = jax.random.split(key, 17)
    f32 = jnp.float32

    def nrm(k, shape, scale):
        return jax.random.normal(k, shape, f32) * scale

    def gain(k, shape):
        return 1.0 + 0.02 * jax.random.normal(k, shape, f32)

    f_bias = (jnp.linspace(3.0, 6.0, MLSTM_HEADS, dtype=f32)[None, :]
              + 0.1 * jax.random.normal(ks[5], (DEPTH, MLSTM_HEADS), f32))
    return {
        "x": jax.random.normal(ks[0], (BATCH, SEQ, D_MODEL), f32),
        "w_in": nrm(ks[1], (DEPTH, D_MODEL, PROJ_WIDTH), D_MODEL ** -0.5),
        "mlstm_conv_w": nrm(ks[2], (DEPTH, MLSTM_CONV, 2 * MLSTM_WIDTH), MLSTM_CONV ** -0.5),
        "mlstm_conv_b": nrm(ks[3], (DEPTH, 2 * MLSTM_WIDTH), 0.01),
        "mlstm_i_bias": nrm(ks[4], (DEPTH, MLSTM_HEADS), 0.1),
        "mlstm_f_bias": f_bias,
        "att_out_norm_w": gain(ks[6], (DEPTH, ATT_WIDTH)),
        "mlstm_out_norm_w": gain(ks[7], (DEPTH, MLSTM_WIDTH)),
        "w_out": nrm(ks[8], (DEPTH, D_MIX, D_MODEL), D_MIX ** -0.5),
        "mixer_norm_w": gain(ks[9], (DEPTH, D_MODEL)),
        "ffn_norm_w": gain(ks[10], (DEPTH, D_MODEL)),
        "w_ffn_up": nrm(ks[11], (DEPTH, D_MODEL, 2 * D_FF), D_MODEL ** -0.5),
        "ffn_conv_w": nrm(ks[12], (DEPTH, FFN_CONV, 2 * D_FF), FFN_CONV ** -0.5),
        "ffn_conv_b": nrm(ks[13], (DEPTH, 2 * D_FF), 0.01),
        "w_ffn_down": nrm(ks[14], (DEPTH, D_FF, D_MODEL), D_FF ** -0.5),
        "final_norm_w": gain(ks[15], (D_MODEL,)),
    }


def reference(x, w_in, mlstm_conv_w, mlstm_conv_b, mlstm_i_bias, mlstm_f_bias,
              att_out_norm_w, mlstm_out_norm_w, w_out, mixer_norm_w, ffn_norm_w,
              w_ffn_up, ffn_conv_w, ffn_conv_b, w_ffn_down, final_norm_w):
    h = x
    for layer in range(DEPTH):
        h = h + _hybrid_mixer(_rmsnorm(h, mixer_norm_w[layer]), w_in[layer],
                              mlstm_conv_w[layer], mlstm_conv_b[layer],
                              mlstm_i_bias[layer], mlstm_f_bias[layer],
                              att_out_norm_w[layer], mlstm_out_norm_w[layer], w_out[layer])
        h = h + _conv_ffn(_rmsnorm(h, ffn_norm_w[layer]), w_ffn_up[layer],
                          ffn_conv_w[layer], ffn_conv_b[layer], w_ffn_down[layer])
    return _rmsnorm(h, final_norm_w)
```

```python
import numpy as np
from collections import defaultdict
from contextlib import ExitStack
import concourse.bass as bass
import concourse.mybir as mybir
from concourse.bass_utils import run_bass_kernel_spmd

F32 = mybir.dt.float32
BF16 = mybir.dt.bfloat16
AF = mybir.ActivationFunctionType
ALU = mybir.AluOpType
AX = mybir.AxisListType

SEQ = 4096
D = 1024
PROJ = 3592
DFF = 2816
NCH = 22
EPS = 1e-6
GROUPS = (1, 4, 16)


class Res:
    __slots__ = ("w", "r", "excl")

    def __init__(self, excl=False):
        self.w = None
        self.r = {}
        self.excl = excl


def PRes():
    return Res(excl=True)


class Sched:
    CE = ("pe", "act", "dve", "pool")

    def __init__(self, nc, es, nq=8):
        self.nc = nc
        self.eng = {"pe": nc.tensor, "act": nc.scalar, "dve": nc.vector, "pool": nc.gpsimd, "sp": nc.sync}
        self.sems = {}
        self.count = {}
        for e in self.CE:
            self.sems[e] = es.enter_context(nc.semaphore("s_" + e))
            self.count[e] = 0
        self.nq = nq
        self.rr = {}
        for q in ("sp", "act", "pool"):
            self.rr[q] = 0
            for i in range(nq):
                n = f"d_{q}{i}"
                self.sems[n] = es.enter_context(nc.semaphore(n))
                self.count[n] = 0
        self.seen = {e: defaultdict(int) for e in self.eng}
        self.ninst = defaultdict(int)
        self.dead = False

    def need(self, E, tok):
        if tok is None:
            return
        s, v = tok
        if s.startswith("d_"):
            v = self.count[s]
        elif s == E and E == "pe":
            return
        if self.seen[E][s] < v:
            self.eng[E].wait_ge(self.sems[s], v)
            self.seen[E][s] = v
            self.ninst[E] += 1

    def _pre(self, E, reads, writes):
        for r in reads:
            self.need(E, r.w)
            if r.excl:
                for e2, t in r.r.items():
                    if e2 != E:
                        self.need(E, t)
        for w in writes:
            self.need(E, w.w)
            for e2, t in w.r.items():
                if e2 != E:
                    self.need(E, t)

    def op(self, E, fn, reads=(), writes=()):
        if self.dead:
            return None
        self._pre(E, reads, writes)
        inst = fn(self.eng[E])
        self.count[E] += 1
        inst.then_inc(self.sems[E], 1)
        self.ninst[E] += 1
        tok = (E, self.count[E])
        for r in reads:
            r.r[E] = tok
        for w in writes:
            w.w = tok
            w.r = {}
        return inst

    def dma(self, q, out, in_, reads=(), writes=(), **kw):
        if self.dead:
            return None
        self._pre(q, reads, writes)
        inst = self.eng[q].dma_start(out=out, in_=in_, **kw)
        n = f"d_{q}{self.rr[q] % self.nq}"
        self.rr[q] += 1
        self.count[n] += 16
        inst.then_inc(self.sems[n], 16)
        self.ninst[q] += 1
        tok = (n, self.count[n])
        for r in reads:
            r.r[n] = tok
        for w in writes:
            w.w = tok
            w.r = {}
        return inst

    def barrier(self):
        for E in self.eng:
            for s in self.sems:
                if self.count[s] > 0 and s != E:
                    self.need(E, (s, self.count[s]))


def build_nc(debug=False, stop_after=None):
    nc = bass.Bass("TRN2", target_bir_lowering=False)
    din = lambda n, s: nc.dram_tensor(n, s, F32, kind="ExternalInput").ap()
    x = din("x", [SEQ, D])
    w_in = din("w_in", [D, PROJ])
    mlstm_conv_w = din("mlstm_conv_w", [4, 1024])
    mlstm_conv_b = din("mlstm_conv_b", [1024])
    mlstm_i_bias = din("mlstm_i_bias", [4])
    mlstm_f_bias = din("mlstm_f_bias", [4])
    att_out_norm_w = din("att_out_norm_w", [512])
    mlstm_out_norm_w = din("mlstm_out_norm_w", [512])
    w_out = din("w_out", [D, D])
    mixer_norm_w = din("mixer_norm_w", [D])
    ffn_norm_w = din("ffn_norm_w", [D])
    w_ffn_up = din("w_ffn_up", [D, 2 * DFF])
    ffn_conv_w = din("ffn_conv_w", [3, 2 * DFF])
    ffn_conv_b = din("ffn_conv_b", [2 * DFF])
    w_ffn_down = din("w_ffn_down", [DFF, D])
    final_norm_w = din("final_norm_w", [D])
    out = nc.dram_tensor("out", [SEQ, D], F32, kind="ExternalOutput").ap()
    ybuf = nc.dram_tensor("ybuf", [D, SEQ], BF16, kind=("ExternalOutput" if debug else "Internal")).ap()
    r_ybuf = [Res() for _ in range(8)]
    dbg = {}

    def dout(name, shape, dt=F32):
        dbg[name] = nc.dram_tensor(name, shape, dt, kind="ExternalOutput").ap()
        return dbg[name]

    with ExitStack() as es:
        S = Sched(nc, es)

        def stop(tag):
            if stop_after == tag:
                S.barrier()
                S.dead = True
        sbt = lambda st, n, s, d: st.enter_context(nc.sbuf_tensor(n, s, d))
        pst = lambda st, n, s, d: st.enter_context(nc.psum_tensor(n, s, d))

        ident_f = sbt(es, "ident_f", [128, 128], F32); r_idf = Res()
        ident_b = sbt(es, "ident_b", [128, 128], BF16); r_idb = Res()
        ones_f = sbt(es, "ones_f", [128, 128], F32); r_ones = Res()
        zero_f = sbt(es, "zero_f", [128, 256], F32); r_zero = Res()
        sel_b = sbt(es, "sel_b", [128, 2, 128], BF16); r_sel = Res()
        maskf = sbt(es, "maskf", [128, 256], F32); r_maskf = Res()
        maskb = sbt(es, "maskb", [128, 256], BF16); r_maskb = Res()
        mask01 = sbt(es, "mask01", [128, 128], F32); r_m01 = Res()
        S.op("pool", lambda e: e.memset(ones_f[:], 1.0), writes=[r_ones])
        S.op("pool", lambda e: e.memset(zero_f[:], 0.0), writes=[r_zero])
        S.op("pool", lambda e: e.affine_select(out=ident_f[:], in_=ones_f[:], pattern=[[-1, 128]], compare_op=ALU.is_equal,
                                               fill=0.0, base=0, channel_multiplier=1), reads=[r_ones], writes=[r_idf])
        S.op("dve", lambda e: e.tensor_copy(out=ident_b[:], in_=ident_f[:]), reads=[r_idf], writes=[r_idb])
        S.op("dve", lambda e: e.memset(sel_b[:], 0.0), writes=[r_sel])
        S.op("dve", lambda e: e.memset(sel_b[0:64, 0, :], 1.0), writes=[r_sel])
        S.op("dve", lambda e: e.memset(sel_b[64:128, 1, :], 1.0), writes=[r_sel])
        S.op("pool", lambda e: e.affine_select(out=maskf[:, 0:128], in_=zero_f[:, 0:128], pattern=[[1, 128]], compare_op=ALU.is_ge,
                                               fill=-30000.0, base=0, channel_multiplier=-1), reads=[r_zero], writes=[r_maskf])
        S.op("pool", lambda e: e.affine_select(out=maskf[:, 128:256], in_=zero_f[:, 128:256], pattern=[[-1, 128]], compare_op=ALU.is_ge,
                                               fill=-30000.0, base=0, channel_multiplier=1), reads=[r_zero], writes=[r_maskf])
        S.op("dve", lambda e: e.tensor_copy(out=maskb[:], in_=maskf[:]), reads=[r_maskf], writes=[r_maskb])
        S.op("pool", lambda e: e.affine_select(out=mask01[:], in_=ones_f[:], pattern=[[1, 128]], compare_op=ALU.is_ge,
                                               fill=0.0, base=0, channel_multiplier=-1), reads=[r_ones], writes=[r_m01])

        colA = sbt(es, "colA", [128, 64], F32); r_colA = Res()
        colB = sbt(es, "colB", [128, 88], F32); r_colB = Res()
        colC = sbt(es, "colC", [128, 88], F32); r_colC = Res()
        FW = sbt(es, "FW", [128, D], F32); r_FW = Res()
        MNW = sbt(es, "MNW", [128, 512], F32); r_MNW = Res()
        gb = sbt(es, "gb", [4, 2], F32); r_gb = Res()
        with ExitStack() as p0:
            rowA = sbt(p0, "rowA", [64, 128], F32); r_rowA = Res()
            rowB = sbt(p0, "rowB", [88, 128], F32); r_rowB = Res()
            rowC = sbt(p0, "rowC", [88, 128], F32); r_rowC = Res()
            pcol = pst(p0, "pcol", [128, 512], F32); r_pcol = PRes()
            S.op("dve", lambda e: e.memset(rowA[:], 0.0), writes=[r_rowA])
            S.dma("sp", rowA[0:8, :], mixer_norm_w.rearrange("(c p) -> c p", p=128), writes=[r_rowA])
            S.dma("sp", rowA[8:16, :], ffn_norm_w.rearrange("(c p) -> c p", p=128), writes=[r_rowA])
            S.dma("sp", rowA[16:48, :], mlstm_conv_w.rearrange("j (c p) -> (j c) p", p=128), writes=[r_rowA])
            S.dma("sp", rowA[48:56, :], mlstm_conv_b.rearrange("(c p) -> c p", p=128), writes=[r_rowA])
            S.dma("sp", rowA[56:64, 0:64], att_out_norm_w.rearrange("(h p) -> h p", p=64), writes=[r_rowA])
            S.dma("sp", rowB[:, :], ffn_conv_w[0:2, :].rearrange("j (c p) -> (j c) p", p=128), writes=[r_rowB])
            S.dma("sp", rowC[0:44, :], ffn_conv_w[2:3, :].rearrange("j (c p) -> (j c) p", p=128), writes=[r_rowC])
            S.dma("sp", rowC[44:88, :], ffn_conv_b.rearrange("(c p) -> c p", p=128), writes=[r_rowC])
            S.dma("sp", FW[:], final_norm_w.partition_broadcast(128), writes=[r_FW])
            S.dma("sp", MNW[:], mlstm_out_norm_w.partition_broadcast(128), writes=[r_MNW])
            with nc.allow_non_contiguous_dma(reason="tiny gate bias"):
                S.dma("sp", gb[:, 0:1], mlstm_i_bias.rearrange("(p o) -> p o", o=1), writes=[r_gb])
                S.dma("sp", gb[:, 1:2], mlstm_f_bias.rearrange("(p o) -> p o", o=1), writes=[r_gb])
            S.op("pe", lambda e: e.transpose(out=pcol[:, 0:64], in_=rowA[:], identity=ident_f[0:64, 0:64]), reads=[r_rowA, r_idf], writes=[r_pcol])
            S.op("pe", lambda e: e.transpose(out=pcol[:, 64:152], in_=rowB[:], identity=ident_f[0:88, 0:88]), reads=[r_rowB, r_idf], writes=[r_pcol])
            S.op("pe", lambda e: e.transpose(out=pcol[:, 152:240], in_=rowC[:], identity=ident_f[0:88, 0:88]), reads=[r_rowC, r_idf], writes=[r_pcol])
            S.op("dve", lambda e: e.tensor_copy(out=colA[:], in_=pcol[:, 0:64]), reads=[r_pcol], writes=[r_colA])
            S.op("dve", lambda e: e.tensor_copy(out=colB[:], in_=pcol[:, 64:152]), reads=[r_pcol], writes=[r_colB])
            S.op("dve", lambda e: e.tensor_copy(out=colC[:], in_=pcol[:, 152:240]), reads=[r_pcol], writes=[r_colC])
            S.barrier()
        mnw = lambda c: colA[:, c:c + 1]
        fnw = lambda c: colA[:, 8 + c:9 + c]
        mcw = lambda j, c: colA[:, 16 + j * 8 + c:17 + j * 8 + c]
        mcb = lambda c: colA[:, 48 + c:49 + c]
        aow = lambda h: colA[0:64, 56 + h:57 + h]

        def fcw(j, ch):
            if j < 2:
                return colB[:, j * 44 + ch:j * 44 + ch + 1]
            return colC[:, ch:ch + 1]
        fcb = lambda ch: colC[:, 44 + ch:45 + ch]

        if debug:
            o_colA = dout("o_colA", [128, 64]); o_colB = dout("o_colB", [128, 88]); o_colC = dout("o_colC", [128, 88])
            S.dma("sp", o_colA[:, :], colA[:], reads=[r_colA]); S.dma("sp", o_colB[:, :], colB[:], reads=[r_colB]); S.dma("sp", o_colC[:, :], colC[:], reads=[r_colC])
            S.barrier()
        with ExitStack() as p12:
          if stop_after != "p0":
                xnT = sbt(p12, "xnT", [128, 8, SEQ], BF16)
                r_xnT = [Res() for _ in range(8)]

                with ExitStack() as ph:
                    xt = [sbt(ph, f"p1x{i}", [128, 4, D], F32) for i in range(2)]; r_xt = [Res(), Res()]
                    xb = [sbt(ph, f"p1xb{i}", [128, 4, D], BF16) for i in range(2)]; r_xb = [Res(), Res()]
                    junk = sbt(ph, "p1junk", [128, D], BF16); r_junk = Res()
                    ss = [sbt(ph, f"p1ss{i}", [128, 4], F32) for i in range(2)]; r_ss = [Res(), Res()]
                    rs = [sbt(ph, f"p1rs{i}", [128, 4], F32) for i in range(2)]; r_rs = [Res(), Res()]
                    pT_ = [pst(ph, f"p1pT{i}", [128, 1024], BF16) for i in range(4)]; r_pT = [PRes() for _ in range(4)]
                    pT = [t[:, 0:512] for t in pT_]
                    for tt in range(8):
                        b = tt % 2
                        S.dma("sp", xt[b][:], x[tt * 512:(tt + 1) * 512, :].rearrange("(s p) d -> p s d", p=128), writes=[r_xt[b]])
                        for s in range(4):
                            S.op("act", lambda e: e.activation(out=junk[:], in_=xt[b][:, s, :], func=AF.Square, scale=1.0 / 32.0,
                                                               accum_out=ss[b][:, s:s + 1]), reads=[r_xt[b]], writes=[r_junk, r_ss[b]])
                        S.op("act", lambda e: e.activation(out=rs[b][:], in_=ss[b][:], func=AF.Ln, bias=EPS, scale=1.0), reads=[r_ss[b]], writes=[r_rs[b]])
                        S.op("act", lambda e: e.activation(out=rs[b][:], in_=rs[b][:], func=AF.Exp, scale=-0.5), reads=[r_rs[b]], writes=[r_rs[b]])
                        for s in range(4):
                            eng = "dve" if s % 2 == 0 else "pool"
                            S.op(eng, lambda e: e.tensor_scalar(out=xb[b][:, s, :], in0=xt[b][:, s, :], scalar1=rs[b][:, s:s + 1], scalar2=None,
                                                                op0=ALU.mult), reads=[r_xt[b], r_rs[b]], writes=[r_xb[b]])
                        for c in range(8):
                            k = (tt * 8 + c) % 4
                            for s in range(4):
                                S.op("pe", lambda e: e.transpose(out=pT[k][:, s * 128:(s + 1) * 128], in_=xb[b][:, s, c * 128:(c + 1) * 128],
                                                                 identity=ident_b[:]), reads=[r_xb[b], r_idb], writes=[r_pT[k]])
                            if c % 2 == 0:
                                S.op("dve", lambda e: e.tensor_scalar(out=xnT[:, c, tt * 512:(tt + 1) * 512], in0=pT[k], scalar1=mnw(c), scalar2=None,
                                                                      op0=ALU.mult), reads=[r_pT[k], r_colA], writes=[r_xnT[tt]])
                            else:
                                S.op("act", lambda e: e.activation(out=xnT[:, c, tt * 512:(tt + 1) * 512], in_=pT[k], func=AF.Copy, scale=mnw(c)),
                                     reads=[r_pT[k], r_colA], writes=[r_xnT[tt]])
                    S.barrier()
                if debug:
                    o_xnT = dout("o_xnT", [D, SEQ], BF16)
                    for c in range(8):
                        S.dma("sp", o_xnT[c * 128:(c + 1) * 128, :], xnT[:, c, :], reads=r_xnT)
                    S.barrier()

                if stop_after not in ("p1", "mlg", "mlh0", "mlh1", "mlh2", "mlh"):
                    with ExitStack() as ph:
                        wqkv = [sbt(ph, f"wqkv{i}", [128, 8, 3, 128], BF16) for i in range(2)]; r_wqkv = [Res(), Res()]
                        QT = sbt(ph, "QT", [128, SEQ], BF16); r_QT = Res()
                        KT = sbt(ph, "KT", [128, SEQ], BF16); r_KT = Res()
                        VT = sbt(ph, "VT", [128, SEQ], BF16); r_VT = Res()
                        sq = sbt(ph, "sq", [128, SEQ], BF16); r_sq = Res()
                        mx = sbt(ph, "mx", [128, 4, 8], F32); r_mx = Res()
                        st = sbt(ph, "st", [128, 4], F32); r_st = Res()
                        nbias = sbt(ph, "nbias", [128, 2], F32); r_nb = Res()
                        Vaug = sbt(ph, "Vaug", [128, 32, 2, 128], BF16); r_Vaug = Res()
                        acc = [sbt(ph, f"acc{i}", [128, SEQ], F32) for i in range(2)]; r_acc = [Res(), Res()]
                        PT = [sbt(ph, f"PT{i}", [128, 256], BF16) for i in range(4)]; r_PT = [Res() for _ in range(4)]
                        e_rden = sbt(ph, "e_rden", [64, 512], F32); r_erden = Res()
                        e_o = sbt(ph, "e_o", [64, 512], F32); r_eo = Res()
                        e_sq = sbt(ph, "e_sq", [64, 512], F32); r_esq = Res()
                        e_rs = sbt(ph, "e_rs", [64, 512], F32); r_ers = Res()
                        yT = [sbt(ph, f"yTa{i}", [64, SEQ], BF16) for i in range(2)]; r_yT = [Res(), Res()]
                        pJ = [pst(ph, f"aJ{i}", [128, 512], F32) for i in range(2)]; r_pJ = [PRes(), PRes()]
                        pS = [pst(ph, f"aS{i}", [128, 512], F32) for i in range(2)]; r_pS = [PRes(), PRes()]
                        pO = [pst(ph, f"aO{i}", [128, 512], F32) for i in range(2)]; r_pO = [PRes(), PRes()]
                        pV_ = [pst(ph, f"aV{i}", [128, 1024], BF16) for i in range(2)]; r_pV = [PRes(), PRes()]
                        pV = [t[:, 0:512] for t in pV_]
                        S.op("pool", lambda e: e.memset(Vaug[:, :, :, 64:128], 1.0), writes=[r_Vaug])
                        nj = 0
                        for pr in range(4):
                            wb = wqkv[pr % 2]; rwb = r_wqkv[pr % 2]
                            for j in range(3):
                                S.dma("pool", wb[:, :, j, :], w_in[:, j * 512 + pr * 128:j * 512 + (pr + 1) * 128].rearrange("(c p) m -> p c m", p=128),
                                      writes=[rwb])
                            for j, (dst, rdst) in enumerate(((QT, r_QT), (KT, r_KT), (VT, r_VT))):
                                for tt in range(8):
                                    k = nj % 2; nj += 1
                                    for c in range(8):
                                        S.op("pe", lambda e: e.matmul(pJ[k][:], lhsT=wb[:, c, j, :], rhs=xnT[:, c, tt * 512:(tt + 1) * 512],
                                                                      start=(c == 0), stop=(c == 7)), reads=[rwb, r_xnT[tt]], writes=[r_pJ[k]])
                                    if nj % 2 == 0:
                                        S.op("dve", lambda e: e.tensor_copy(out=dst[:, tt * 512:(tt + 1) * 512], in_=pJ[k][:]), reads=[r_pJ[k]], writes=[rdst])
                                    else:
                                        S.op("act", lambda e: e.activation(out=dst[:, tt * 512:(tt + 1) * 512], in_=pJ[k][:], func=AF.Copy),
                                             reads=[r_pJ[k]], writes=[rdst])
                            for qi, (src, rsrc) in enumerate(((QT, r_QT), (KT, r_KT))):
                                S.op("pool", lambda e: e.tensor_tensor(out=sq[:], in0=src[:], in1=src[:], op=ALU.mult), reads=[rsrc], writes=[r_sq])
                                for hh in range(2):
                                    for tt in range(8):
                                        k = nj % 2; nj += 1
                                        S.op("pe", lambda e: e.matmul(pJ[k][:], lhsT=sel_b[:, hh, :], rhs=sq[:, tt * 512:(tt + 1) * 512], start=True, stop=True),
                                             reads=[r_sel, r_sq], writes=[r_pJ[k]])
                                        S.op("dve", lambda e: e.tensor_reduce(out=mx[:, qi * 2 + hh, tt:tt + 1], in_=pJ[k][:], axis=AX.X, op=ALU.max),
                                             reads=[r_pJ[k]], writes=[r_mx])
                            S.op("dve", lambda e: e.tensor_reduce(out=st[:], in_=mx[:], axis=AX.X, op=ALU.max), reads=[r_mx], writes=[r_st])
                            S.op("dve", lambda e: e.tensor_tensor(out=nbias[:], in0=st[:, 0:2], in1=st[:, 2:4], op=ALU.mult), reads=[r_st], writes=[r_nb])
                            S.op("act", lambda e: e.activation(out=nbias[:], in_=nbias[:], func=AF.Ln), reads=[r_nb], writes=[r_nb])
                            S.op("act", lambda e: e.activation(out=nbias[:], in_=nbias[:], func=AF.Exp, scale=0.5), reads=[r_nb], writes=[r_nb])
                            S.op("dve", lambda e: e.tensor_scalar(out=nbias[:], in0=nbias[:], scalar1=-0.125 * 1.02, scalar2=None, op0=ALU.mult),
                                 reads=[r_nb], writes=[r_nb])
                            nV = 0; nS = 0; nO = 0
                            for gi, d in enumerate(GROUPS):
                                nb_ = 32 // d
                                for kt0 in range(0, 32, 4):
                                    k = nV % 2; nV += 1
                                    for sl in range(4):
                                        kt = kt0 + sl
                                        r_, b_ = kt // nb_, kt % nb_
                                        t0 = 128 * b_ * d + r_
                                        S.op("pe", lambda e: e.transpose(out=pV[k][:, sl * 128:(sl + 1) * 128], in_=VT[:, t0:t0 + 127 * d + 1:d],
                                                                         identity=ident_b[:]), reads=[r_VT, r_idb], writes=[r_pV[k]])
                                    eng = "dve" if (nV % 2 == 0) else "act"
                                    src = pV[k].rearrange("p (s h e) -> p s h e", s=4, h=2)
                                    if eng == "dve":
                                        S.op("dve", lambda e: e.tensor_copy(out=Vaug[:, kt0:kt0 + 4, :, 0:64], in_=src), reads=[r_pV[k]], writes=[r_Vaug])
                                    else:
                                        S.op("act", lambda e: e.activation(out=Vaug[:, kt0:kt0 + 4, :, 0:64], in_=src, func=AF.Copy),
                                             reads=[r_pV[k]], writes=[r_Vaug])
                                for hh in range(2):
                                    hs = slice(hh * 64, (hh + 1) * 64)
                                    ah = acc[hh]; rah = r_acc[hh]
                                    slot = 0
                                    prevPT = None
                                    for r_ in range(d):
                                        for b_ in range(nb_):
                                            kt = r_ * nb_ + b_
                                            N = 256 if b_ + 1 < nb_ else 128
                                            t0 = 128 * b_ * d + r_
                                            ks = slice(t0, t0 + 127 * d + 1, d)
                                            qs = slice(t0, t0 + (N - 1) * d + 1, d)
                                            ks_ = nS % 2; pi = nS % 4; nS += 1
                                            S.op("pe", lambda e: e.matmul(pS[ks_][:, 0:N], lhsT=KT[hs, ks], rhs=QT[hs, qs], start=True, stop=False),
                                                 reads=[r_KT, r_QT], writes=[r_pS[ks_]])
                                            S.op("pe", lambda e: e.matmul(pS[ks_][:, 0:N], lhsT=ident_b[:], rhs=maskb[:, 0:N], start=False, stop=True),
                                                 reads=[r_idb, r_maskb], writes=[r_pS[ks_]])
                                            S.op("act", lambda e: e.activation(out=PT[pi][:, 0:N], in_=pS[ks_][:, 0:N], func=AF.Exp, scale=0.125,
                                                                               bias=nbias[:, hh:hh + 1]), reads=[r_pS[ks_], r_nb], writes=[r_PT[pi]])
                                            ko = nO % 2
                                            osl = pO[ko][:, slot * 128:(slot + 1) * 128]
                                            if b_ > 0:
                                                S.op("pe", lambda e: e.matmul(osl, lhsT=Vaug[:, kt - 1, hh, :], rhs=PT[prevPT][:, 128:256], start=True, stop=False),
                                                     reads=[r_Vaug, r_PT[prevPT]], writes=[r_pO[ko]])
                                            S.op("pe", lambda e: e.matmul(osl, lhsT=Vaug[:, kt, hh, :], rhs=PT[pi][:, 0:128], start=(b_ == 0), stop=True),
                                                 reads=[r_Vaug, r_PT[pi]], writes=[r_pO[ko]])
                                            prevPT = pi
                                            slot += 1
                                            if slot == 4:
                                                slot = 0
                                                nO += 1
                                                if d == 16:
                                                    dst = ah[:].rearrange("p (i dd) -> p dd i", dd=16)[:, r_ - 1:r_ + 1, :]
                                                    src = pO[ko][:].rearrange("p (a i) -> p a i", a=2)
                                                else:
                                                    b0 = b_ - 3
                                                    ts = 128 * b0 * d + r_
                                                    dst = ah[:, ts:ts + 511 * d + 1:d]
                                                    src = pO[ko][:]
                                                if gi == 0:
                                                    S.op("dve", lambda e: e.tensor_copy(out=dst, in_=src), reads=[r_pO[ko]], writes=[rah])
                                                else:
                                                    S.op("dve", lambda e: e.tensor_tensor(out=dst, in0=dst, in1=src, op=ALU.add), reads=[r_pO[ko], rah], writes=[rah])
                            for hh in range(2):
                                h = pr * 2 + hh
                                ah = acc[hh]; rah = r_acc[hh]
                                yb = yT[h % 2]; ryb = r_yT[h % 2]
                                for tt in range(8):
                                    cs = slice(tt * 512, (tt + 1) * 512)
                                    k = nj % 2; nj += 1
                                    S.op("dve", lambda e: e.reciprocal(out=e_rden[:], in_=ah[64:128, cs]), reads=[rah], writes=[r_erden])
                                    S.op("dve", lambda e: e.tensor_tensor(out=e_o[:], in0=ah[0:64, cs], in1=e_rden[:], op=ALU.mult), reads=[rah, r_erden], writes=[r_eo])
                                    S.op("act", lambda e: e.activation(out=e_sq[:], in_=e_o[:], func=AF.Square), reads=[r_eo], writes=[r_esq])
                                    S.op("pe", lambda e: e.matmul(pJ[k][0:64, :], lhsT=ones_f[0:64, 0:64], rhs=e_sq[:], start=True, stop=True),
                                         reads=[r_ones, r_esq], writes=[r_pJ[k]])
                                    S.op("act", lambda e: e.activation(out=e_rs[:], in_=pJ[k][0:64, :], func=AF.Ln, bias=EPS, scale=1.0 / 64.0),
                                         reads=[r_pJ[k]], writes=[r_ers])
                                    S.op("act", lambda e: e.activation(out=e_rs[:], in_=e_rs[:], func=AF.Exp, scale=-0.5), reads=[r_ers], writes=[r_ers])
                                    S.op("dve", lambda e: e.scalar_tensor_tensor(out=yb[:, cs], in0=e_o[:], scalar=aow(h), in1=e_rs[:], op0=ALU.mult, op1=ALU.mult),
                                         reads=[r_eo, r_ers, r_colA], writes=[ryb])
                                S.dma("sp", ybuf[h * 64:(h + 1) * 64, :], yb[:], reads=[ryb], writes=[r_ybuf[h // 2]])
                        S.barrier()

                if stop_after not in ("p1", "att"):
                    with ExitStack() as ph:
                        wexpT = sbt(ph, "wexpT", [128, 128], F32); r_wexpT = Res()
                        floorT = sbt(ph, "floorT", [128, 128], F32); r_floorT = Res()
                        decB = sbt(ph, "decB", [128, 128], F32); r_decB = Res()

                        with ExitStack() as pg:
                            wg = sbt(pg, "wg", [128, 8, 8], BF16); r_wg = Res()
                            pG = pst(pg, "mG", [128, 512], F32); r_pG = PRes()
                            gA = sbt(pg, "gA", [4, SEQ], F32); r_gA = Res()
                            gB = sbt(pg, "gB", [4, SEQ], F32); r_gB = Res()
                            gC = sbt(pg, "gC", [4, SEQ], F32); r_gC = Res()
                            gD = sbt(pg, "gD", [4, SEQ], F32); r_gD = Res()
                            sm = sbt(pg, "sm", [4, 12, 32], F32); r_sm = Res()
                            nfb = sbt(pg, "nfb", [4, 1], F32); r_nfb = Res()
                            dmask = sbt(pg, "dmask", [4, 128], F32); r_dmask = Res()
                            decD = sbt(pg, "decD", [4, 128], F32); r_decD = Res()
                            wgf = sbt(pg, "wgf", [128, 8, 8], F32); r_wgf = Res()
                            with nc.allow_non_contiguous_dma(reason="gate weight columns (32B runs)"):
                                S.dma("sp", wgf[:], w_in[:, 3584:3592].rearrange("(c p) m -> p c m", p=128), writes=[r_wgf])
                            S.op("dve", lambda e: e.tensor_copy(out=wg[:], in_=wgf[:]), reads=[r_wgf], writes=[r_wg])
                            S.op("dve", lambda e: e.memset(gC[:], 1.0), writes=[r_gC])
                            S.op("dve", lambda e: e.memset(gC[:].rearrange("p (j t) -> p j t", t=128)[:, :, 0:1], 0.0), writes=[r_gC])
                            S.op("dve", lambda e: e.tensor_scalar(out=nfb[:], in0=gb[:, 1:2], scalar1=-1.0, scalar2=None, op0=ALU.mult), reads=[r_gb], writes=[r_nfb])
                            S.op("pool", lambda e: e.affine_select(out=dmask[:].rearrange("p (h j) -> p h j", h=4), in_=ones_f[0:4, :].rearrange("p (h j) -> p h j", h=4),
                                                                   pattern=[[-1, 4], [0, 32]], compare_op=ALU.is_equal, fill=0.0, base=0, channel_multiplier=1),
                                 reads=[r_ones], writes=[r_dmask])
                            for tt in range(8):
                                cs = slice(tt * 512, (tt + 1) * 512)
                                for c in range(8):
                                    S.op("pe", lambda e: e.matmul(pG[0:4, :], lhsT=wg[:, c, 0:4], rhs=xnT[:, c, cs], start=(c == 0), stop=(c == 7)),
                                         reads=[r_wg, r_xnT[tt]], writes=[r_pG])
                                S.op("act", lambda e: e.activation(out=gA[:, cs], in_=pG[0:4, :], func=AF.Identity, bias=gb[:, 0:1], scale=1.0),
                                     reads=[r_pG, r_gb], writes=[r_gA])
                                for c in range(8):
                                    S.op("pe", lambda e: e.matmul(pG[0:4, :], lhsT=wg[:, c, 4:8], rhs=xnT[:, c, cs], start=(c == 0), stop=(c == 7)),
                                         reads=[r_wg, r_xnT[tt]], writes=[r_pG])
                                S.op("act", lambda e: e.activation(out=gB[:, cs], in_=pG[0:4, :], func=AF.Exp, bias=nfb[:, 0:1], scale=-1.0),
                                     reads=[r_pG, r_nfb], writes=[r_gB])
                            S.op("act", lambda e: e.activation(out=gB[:], in_=gB[:], func=AF.Ln, bias=1.0, scale=1.0), reads=[r_gB], writes=[r_gB])
                            S.op("dve", lambda e: e.tensor_scalar(out=gD[:], in0=gB[:], scalar1=-1.0, scalar2=None, op0=ALU.mult), reads=[r_gB], writes=[r_gD])
                            S.op("dve", lambda e: e.tensor_tensor_scan(out=gB[:], data0=gC[:], data1=gD[:], initial=0.0, op0=ALU.mult, op1=ALU.add),
                                 reads=[r_gC, r_gD], writes=[r_gB])
                            S.op("dve", lambda e: e.tensor_tensor(out=gA[:], in0=gA[:], in1=gB[:], op=ALU.subtract), reads=[r_gA, r_gB], writes=[r_gA])
                            rmax, gch, ginc, gx, rho, mun, mu, u_, dec_, tmp_ = [sm[:, i, :] for i in range(10)]
                            S.op("dve", lambda e: e.tensor_reduce(out=rmax, in_=gA[:].rearrange("p (j t) -> p j t", t=128), axis=AX.X, op=ALU.max),
                                 reads=[r_gA], writes=[r_sm])
                            S.op("dve", lambda e: e.tensor_copy(out=gch, in_=gB[:].rearrange("p (j t) -> p j t", t=128)[:, :, 127]), reads=[r_gB], writes=[r_sm])
                            S.op("dve", lambda e: e.memset(tmp_, 1.0), writes=[r_sm])
                            S.op("dve", lambda e: e.tensor_tensor_scan(out=ginc, data0=tmp_, data1=gch, initial=0.0, op0=ALU.mult, op1=ALU.add), reads=[r_sm], writes=[r_sm])
                            S.op("dve", lambda e: e.tensor_tensor(out=gx, in0=ginc, in1=gch, op=ALU.subtract), reads=[r_sm], writes=[r_sm])
                            S.op("dve", lambda e: e.tensor_tensor(out=rho, in0=rmax, in1=gx, op=ALU.subtract), reads=[r_sm], writes=[r_sm])
                            S.op("dve", lambda e: e.tensor_tensor_scan(out=mun, data0=rho, data1=rho, initial=0.0, op0=ALU.max, op1=ALU.max), reads=[r_sm], writes=[r_sm])
                            S.op("dve", lambda e: e.memset(mu, 0.0), writes=[r_sm])
                            S.op("dve", lambda e: e.tensor_copy(out=sm[:, 6, 1:32], in_=sm[:, 5, 0:31]), reads=[r_sm], writes=[r_sm])
                            S.op("dve", lambda e: e.tensor_tensor(out=u_, in0=gx, in1=mun, op=ALU.add), reads=[r_sm], writes=[r_sm])
                            S.op("dve", lambda e: e.tensor_tensor(out=dec_, in0=mu, in1=mun, op=ALU.subtract), reads=[r_sm], writes=[r_sm])
                            S.op("act", lambda e: e.activation(out=dec_, in_=dec_, func=AF.Exp), reads=[r_sm], writes=[r_sm])
                            ub = sm[:, 7, :].unsqueeze(2).to_broadcast([4, 32, 128])
                            S.op("dve", lambda e: e.tensor_tensor(out=gA[:].rearrange("p (j t) -> p j t", t=128), in0=gA[:].rearrange("p (j t) -> p j t", t=128),
                                                                  in1=ub, op=ALU.subtract), reads=[r_gA, r_sm], writes=[r_gA])
                            S.op("dve", lambda e: e.tensor_scalar(out=gA[:], in0=gA[:], scalar1=-0.5 * float(np.log(128.0)), scalar2=None, op0=ALU.add),
                                 reads=[r_gA], writes=[r_gA])
                            S.op("act", lambda e: e.activation(out=gA[:], in_=gA[:], func=AF.Exp), reads=[r_gA], writes=[r_gA])
                            S.op("dve", lambda e: e.tensor_tensor(out=gB[:].rearrange("p (j t) -> p j t", t=128), in0=gB[:].rearrange("p (j t) -> p j t", t=128),
                                                                  in1=ub, op=ALU.add), reads=[r_gB, r_sm], writes=[r_gB])
                            S.op("act", lambda e: e.activation(out=gB[:], in_=gB[:], func=AF.Exp, scale=-1.0), reads=[r_gB], writes=[r_gB])
                            for src, rsrc, dstT, rdstT in ((gA, r_gA, wexpT, r_wexpT), (gB, r_gB, floorT, r_floorT)):
                                for j in range(32):
                                    S.op("pe", lambda e: e.matmul(pG[:, j * 4:(j + 1) * 4], lhsT=src[0:4, j * 128:(j + 1) * 128], rhs=ident_f[0:4, 0:4], start=True, stop=True),
                                         reads=[rsrc, r_idf], writes=[r_pG])
                                S.op("dve", lambda e: e.tensor_copy(out=dstT[:].rearrange("p (h j) -> p j h", h=4), in_=pG[:, 0:128].rearrange("p (j h) -> p j h", h=4)),
                                     reads=[r_pG], writes=[rdstT])
                            S.op("dve", lambda e: e.tensor_tensor(out=decD[:].rearrange("p (h j) -> p h j", h=4), in0=dmask[:].rearrange("p (h j) -> p h j", h=4),
                                                                  in1=sm[:, 8, :].unsqueeze(1).to_broadcast([4, 4, 32]), op=ALU.mult), reads=[r_dmask, r_sm], writes=[r_decD])
                            S.op("pe", lambda e: e.matmul(pG[:, 0:128], lhsT=ones_f[0:4, :], rhs=decD[:], start=True, stop=True), reads=[r_ones, r_decD], writes=[r_pG])
                            S.op("dve", lambda e: e.tensor_copy(out=decB[:], in_=pG[:, 0:128]), reads=[r_pG], writes=[r_decB])
                            if debug:
                                o_wexp = dout("o_wexp", [4, SEQ]); o_floor = dout("o_floor", [4, SEQ]); o_sm = dout("o_sm", [4, 12 * 32])
                                o_wexpT = dout("o_wexpT", [128, 128]); o_decB = dout("o_decB", [128, 128])
                                S.dma("sp", o_wexp[:, :], gA[:], reads=[r_gA]); S.dma("sp", o_floor[:, :], gB[:], reads=[r_gB])
                                S.dma("sp", o_sm[:, :], sm[:].rearrange("p a b -> p (a b)"), reads=[r_sm])
                                S.dma("sp", o_wexpT[:, :], wexpT[:], reads=[r_wexpT]); S.dma("sp", o_decB[:, :], decB[:], reads=[r_decB])
                            S.barrier()

                        stop("mlg")
                        wqk = [sbt(ph, f"wqk{i}", [128, 8, 2, 128], BF16) for i in range(2)]; r_wqk = [Res(), Res()]
                        wvo = [sbt(ph, f"wvo{i}", [128, 8, 2, 128], BF16) for i in range(2)]; r_wvo = [Res(), Res()]
                        xpre = sbt(ph, "xpre", [128, 3 + SEQ], F32); r_xpre = Res()
                        cv = sbt(ph, "cv", [128, SEQ], F32); r_cv = Res()
                        qkT = sbt(ph, "qkT", [128, 2, SEQ], BF16); r_qkT = [Res(), Res()]
                        Vm = sbt(ph, "Vm", [128, 32, 132], BF16); r_Vm = Res()
                        SG = sbt(ph, "SG", [128, 32, 128], F32); r_SG = Res()
                        sgt = sbt(ph, "sgt", [128, 2, 128], F32); r_sgt = Res()
                        Kw = sbt(ph, "Kw", [128, 32, 128], BF16); r_Kw = Res()
                        yTm = [sbt(ph, f"yTm{i}", [128, SEQ], BF16) for i in range(2)]; r_yTm = [Res(), Res()]
                        Cst = sbt(ph, "Cst", [128, 129], F32); r_C = Res()
                        Cbf = sbt(ph, "Cbf", [128, 129], BF16); r_Cb = Res()
                        PTm = [sbt(ph, f"PTm{i}", [128, 128], BF16) for i in range(2)]; r_PTm = [Res(), Res()]
                        ytl = [sbt(ph, f"ytl{i}", [128, 128], BF16) for i in range(2)]; r_ytl = [Res(), Res()]
                        sc = [sbt(ph, f"msc{i}", [128, 8], F32) for i in range(2)]; r_sc = [Res(), Res()]
                        junkm = sbt(ph, "junkm", [128, 128], F32); r_junkm = Res()
                        pP = [pst(ph, f"mP{i}", [128, 512], F32) for i in range(2)]; r_pP = [PRes(), PRes()]
                        pST_ = [pst(ph, f"mST{i}", [128, 512], F32) for i in range(2)]; r_pST = [PRes(), PRes()]
                        pST = [t[:, 0:128] for t in pST_]
                        pA_ = [pst(ph, f"mA{i}", [128, 512], F32) for i in range(2)]; r_pA = [PRes(), PRes()]
                        pA = [t[:, 0:129] for t in pA_]
                        pC_ = pst(ph, "mC", [128, 512], F32); r_pC = PRes()
                        pC = pC_[:, 0:129]
                        pTb_ = pst(ph, "mTb", [128, 1024], BF16); r_pTb = [PRes()]
                        pTb = [pTb_[:, 0:512]]
                        S.op("pool", lambda e: e.memset(Vm[:, :, 128:129], 1.0), writes=[r_Vm])
                        S.op("pool", lambda e: e.memset(xpre[:, 0:3], 0.0), writes=[r_xpre])
                        npj = 0
                        for h in range(4):
                            wq_ = wqk[h % 2]; rwq = r_wqk[h % 2]
                            wv_ = wvo[h % 2]; rwv = r_wvo[h % 2]
                            for j in range(2):
                                c0 = 1536 + j * 512 + h * 128
                                S.dma("pool", wq_[:, :, j, :], w_in[:, c0:c0 + 128].rearrange("(c p) m -> p c m", p=128), writes=[rwq])
                                c1 = 2560 + j * 512 + h * 128
                                S.dma("pool", wv_[:, :, j, :], w_in[:, c1:c1 + 128].rearrange("(c p) m -> p c m", p=128), writes=[rwv])
                            for j in range(2):
                                ch = j * 4 + h
                                for tt in range(8):
                                    k = npj % 2; npj += 1
                                    for c in range(8):
                                        S.op("pe", lambda e: e.matmul(pP[k][:], lhsT=wq_[:, c, j, :], rhs=xnT[:, c, tt * 512:(tt + 1) * 512], start=(c == 0), stop=(c == 7)),
                                             reads=[rwq, r_xnT[tt]], writes=[r_pP[k]])
                                    S.op("act", lambda e: e.activation(out=xpre[:, 3 + tt * 512:3 + (tt + 1) * 512], in_=pP[k][:], func=AF.Copy),
                                         reads=[r_pP[k]], writes=[r_xpre])
                                S.op("dve", lambda e: e.tensor_scalar(out=cv[:], in0=xpre[:, 3:3 + SEQ], scalar1=mcw(3, ch), scalar2=mcb(ch), op0=ALU.mult, op1=ALU.add),
                                     reads=[r_xpre, r_colA], writes=[r_cv])
                                for tap in range(3):
                                    S.op("dve", lambda e: e.scalar_tensor_tensor(out=cv[:], in0=xpre[:, tap:tap + SEQ], scalar=mcw(tap, ch), in1=cv[:], op0=ALU.mult, op1=ALU.add),
                                         reads=[r_xpre, r_cv, r_colA], writes=[r_cv])
                                S.op("act", lambda e: e.activation(out=qkT[:, j, :], in_=cv[:], func=AF.Silu), reads=[r_cv], writes=[r_qkT[j]])
                            stop("mlh0")
                            for jt in range(0, 32, 2):
                                k = npj % 2; npj += 1
                                for a in range(2):
                                    tsl = slice((jt + a) * 128, (jt + a + 1) * 128)
                                    for c in range(8):
                                        S.op("pe", lambda e: e.matmul(pP[k][:, a * 256:(a + 1) * 256], lhsT=xnT[:, c, tsl], rhs=wv_[:, c, :, :].rearrange("p a b -> p (a b)"),
                                                                      start=(c == 0), stop=(c == 7)), reads=[rwv, r_xnT[(jt + a) // 4]], writes=[r_pP[k]])
                                pv = pP[k][:].rearrange("p (a j e) -> p a j e", a=2, j=2)
                                S.op("dve", lambda e: e.tensor_copy(out=Vm[:, jt:jt + 2, 0:128], in_=pv[:, :, 0, :]), reads=[r_pP[k]], writes=[r_Vm])
                                S.op("act", lambda e: e.activation(out=sgt[:], in_=pv[:, :, 1, :], func=AF.Sigmoid), reads=[r_pP[k]], writes=[r_sgt])
                                S.op("dve", lambda e: e.tensor_tensor(out=SG[:, jt:jt + 2, :], in0=sgt[:], in1=MNW[:, h * 128:(h + 1) * 128].unsqueeze(1).to_broadcast([128, 2, 128]),
                                                                       op=ALU.mult), reads=[r_sgt, r_MNW], writes=[r_SG])
                            stop("mlh1")
                            for j0 in range(0, 32, 4):
                                for a in range(4):
                                    j = j0 + a
                                    S.op("pe", lambda e: e.transpose(out=pTb[0][:, a * 128:(a + 1) * 128], in_=qkT[:, 1, j * 128:(j + 1) * 128], identity=ident_b[:]),
                                         reads=[r_qkT[1], r_idb], writes=[r_pTb[0]])
                                for a in range(4):
                                    j = j0 + a
                                    col = h * 32 + j
                                    if a % 2 == 0:
                                        S.op("dve", lambda e: e.tensor_scalar(out=Kw[:, j, :], in0=pTb[0][:, a * 128:(a + 1) * 128], scalar1=wexpT[:, col:col + 1], scalar2=None,
                                                                              op0=ALU.mult), reads=[r_pTb[0], r_wexpT], writes=[r_Kw])
                                    else:
                                        S.op("act", lambda e: e.activation(out=Kw[:, j, :], in_=pTb[0][:, a * 128:(a + 1) * 128], func=AF.Copy, scale=wexpT[:, col:col + 1]),
                                             reads=[r_pTb[0], r_wexpT], writes=[r_Kw])
                            stop("mlh2")
                            ym = yTm[h % 2]; rym = r_yTm[h % 2]
                            for j in range(32):
                                col = h * 32 + j
                                tsl = slice(j * 128, (j + 1) * 128)
                                kk = j % 2
                                if j > 0:
                                    S.op("dve", lambda e: e.tensor_scalar(out=Cst[:], in0=Cst[:], scalar1=decB[:, col:col + 1], scalar2=None, op0=ALU.mult),
                                         reads=[r_C, r_decB], writes=[r_C])
                                    S.op("act", lambda e: e.activation(out=Cbf[:], in_=Cst[:], func=AF.Copy), reads=[r_C], writes=[r_Cb])
                                S.op("pe", lambda e: e.matmul(pST[kk], lhsT=qkT[:, 1, tsl], rhs=qkT[:, 0, tsl], start=True, stop=True),
                                     reads=[r_qkT[0], r_qkT[1]], writes=[r_pST[kk]])
                                S.op("dve", lambda e: e.scalar_tensor_tensor(out=PTm[kk][:], in0=pST[kk], scalar=wexpT[:, col:col + 1], in1=mask01[:], op0=ALU.mult, op1=ALU.mult),
                                     reads=[r_pST[kk], r_wexpT, r_m01], writes=[r_PTm[kk]])
                                S.op("pe", lambda e: e.matmul(pA[kk], lhsT=PTm[kk][:], rhs=Vm[:, j, 0:129], start=True, stop=(j == 0)), reads=[r_PTm[kk], r_Vm], writes=[r_pA[kk]])
                                if j > 0:
                                    S.op("pe", lambda e: e.matmul(pA[kk], lhsT=qkT[:, 0, tsl], rhs=Cbf[:], start=False, stop=True), reads=[r_qkT[0], r_Cb], writes=[r_pA[kk]])
                                S.op("pe", lambda e: e.matmul(pC, lhsT=Kw[:, j, :], rhs=Vm[:, j, 0:129], start=True, stop=True), reads=[r_Kw, r_Vm], writes=[r_pC])
                                if j == 0:
                                    S.op("dve", lambda e: e.tensor_copy(out=Cst[:], in_=pC), reads=[r_pC], writes=[r_C])
                                else:
                                    S.op("dve", lambda e: e.tensor_tensor(out=Cst[:], in0=Cst[:], in1=pC, op=ALU.add), reads=[r_C, r_pC], writes=[r_C])
                                s_ = sc[kk]; rs_ = r_sc[kk]
                                S.op("act", lambda e: e.activation(out=s_[:, 0:1], in_=pA[kk][:, 128:129], func=AF.Abs), reads=[r_pA[kk]], writes=[rs_])
                                S.op("dve", lambda e: e.tensor_tensor(out=s_[:, 1:2], in0=s_[:, 0:1], in1=floorT[:, col:col + 1], op=ALU.max), reads=[rs_, r_floorT], writes=[rs_])
                                S.op("dve", lambda e: e.reciprocal(out=s_[:, 2:3], in_=s_[:, 1:2]), reads=[rs_], writes=[rs_])
                                S.op("act", lambda e: e.activation(out=junkm[:], in_=pA[kk][:, 0:128], func=AF.Square, scale=s_[:, 2:3], accum_out=s_[:, 3:4]),
                                     reads=[r_pA[kk], rs_], writes=[r_junkm, rs_])
                                S.op("act", lambda e: e.activation(out=s_[:, 4:5], in_=s_[:, 3:4], func=AF.Ln, bias=EPS, scale=1.0 / 128.0), reads=[rs_], writes=[rs_])
                                S.op("act", lambda e: e.activation(out=s_[:, 4:5], in_=s_[:, 4:5], func=AF.Exp, scale=-0.5), reads=[rs_], writes=[rs_])
                                S.op("dve", lambda e: e.tensor_tensor(out=s_[:, 5:6], in0=s_[:, 4:5], in1=s_[:, 2:3], op=ALU.mult), reads=[rs_], writes=[rs_])
                                S.op("dve", lambda e: e.scalar_tensor_tensor(out=ytl[kk][:], in0=pA[kk][:, 0:128], scalar=s_[:, 5:6], in1=SG[:, j, :], op0=ALU.mult, op1=ALU.mult),
                                     reads=[r_pA[kk], rs_, r_SG], writes=[r_ytl[kk]])
                                a = j % 4
                                S.op("pe", lambda e: e.transpose(out=pTb[0][:, a * 128:(a + 1) * 128], in_=ytl[kk][:], identity=ident_b[:]), reads=[r_ytl[kk], r_idb], writes=[r_pTb[0]])
                                if a == 3:
                                    S.op("act", lambda e: e.activation(out=ym[:, (j - 3) * 128:(j + 1) * 128], in_=pTb[0], func=AF.Copy), reads=[r_pTb[0]], writes=[rym])
                            S.dma("sp", ybuf[512 + h * 128:512 + (h + 1) * 128, :], ym[:], reads=[rym], writes=[r_ybuf[4 + h]])
                        S.barrier()
        S.barrier()

        if stop_after is None:
            TT = 256
            NTT = SEQ // TT
            NSUB = TT // 128
            with ExitStack() as ph:
                wout = sbt(ph, "wout", [128, 8, D], BF16); r_wout = Res()
                wup = sbt(ph, "wup", [128, 8, 2 * DFF], BF16); r_wup = [Res() for _ in range(8)]
                wdn = sbt(ph, "wdn", [128, NCH, D], BF16); r_wdn = Res()
                yt_ = [sbt(ph, f"f_y{i}", [128, 8, TT], BF16) for i in range(2)]; r_yt = [Res(), Res()]
                h1 = [sbt(ph, f"f_h{i}", [128, NSUB, D], F32) for i in range(1)]; r_h1 = [Res()]
                hb = sbt(ph, "f_hb", [128, NSUB, D], BF16); r_hb = Res()
                hnT = sbt(ph, "f_hnT", [128, 8, TT], BF16); r_hnT = Res()
                Rb = [sbt(ph, f"f_R{i}", [128, 2 + TT], F32) for i in range(4)]; r_Rb = [Res() for _ in range(4)]
                cvb = [sbt(ph, f"f_cv{i}", [128, TT], F32) for i in range(4)]; r_cvb = [Res() for _ in range(4)]
                sgb = [sbt(ph, f"f_sg{i}", [128, TT], F32) for i in range(2)]; r_sgb = [Res(), Res()]
                gT = sbt(ph, "f_gT", [128, NCH, TT], BF16); r_gT = Res()
                halo = sbt(ph, "f_halo", [128, 2 * NCH, 2], F32); r_halo = Res()
                fss = sbt(ph, "f_ss", [128, 8], F32); r_fss = Res()
                fjunk = sbt(ph, "f_junk", [128, D], BF16); r_fjunk = Res()
                pH = [pst(ph, f"fH{i}", [128, D], F32) for i in range(1)]; r_pH = [PRes()]
                pU = [pst(ph, f"fU{i}", [128, 512], F32) for i in range(4)]; r_pU = [PRes() for _ in range(4)]
                pT3_ = [pst(ph, f"fT{i}", [128, 1024], BF16) for i in range(2)]; r_pT3 = [PRes(), PRes()]
                pT3 = [t[:, 0:512] for t in pT3_]
                for c in range(8):
                    S.dma("pool", wout[:, c, :], w_out[c * 128:(c + 1) * 128, :], writes=[r_wout])
                for c in range(8):
                    S.dma("pool", wup[:, c, :], w_ffn_up[c * 128:(c + 1) * 128, :], writes=[r_wup[c]])
                for m in range(NCH):
                    S.dma("pool", wdn[:, m, :], w_ffn_down[m * 128:(m + 1) * 128, :], writes=[r_wdn])
                S.op("dve", lambda e: e.memset(halo[:], 0.0), writes=[r_halo])
                nU = 0; nR = 0; nT3 = 0
                for tt in range(NTT):
                    yb = yt_[tt % 2]; ryb = r_yt[tt % 2]
                    hh_ = h1[0]; rhh = r_h1[0]
                    S.dma("sp", yb[:], ybuf[:, tt * TT:(tt + 1) * TT].rearrange("(c p) t -> p c t", p=128), reads=r_ybuf, writes=[ryb])
                    S.dma("sp", hh_[:], x[tt * TT:(tt + 1) * TT, :].rearrange("(s p) d -> p s d", p=128), writes=[rhh])
                    for s in range(NSUB):
                        for hf in range(2):
                            for c in range(8):
                                S.op("pe", lambda e: e.matmul(pH[0][:, hf * 512:(hf + 1) * 512], lhsT=yb[:, c, s * 128:(s + 1) * 128], rhs=wout[:, c, hf * 512:(hf + 1) * 512],
                                                              start=(c == 0), stop=(c == 7)), reads=[ryb, r_wout], writes=[r_pH[0]])
                        S.op("dve", lambda e: e.tensor_tensor(out=hh_[:, s, :], in0=hh_[:, s, :], in1=pH[0][:], op=ALU.add), reads=[rhh, r_pH[0]], writes=[rhh])
                        S.op("act", lambda e: e.activation(out=fjunk[:], in_=hh_[:, s, :], func=AF.Square, scale=1.0 / 32.0, accum_out=fss[:, s:s + 1]),
                             reads=[rhh], writes=[r_fjunk, r_fss])
                    S.op("act", lambda e: e.activation(out=fss[:, 2:2 + NSUB], in_=fss[:, 0:NSUB], func=AF.Ln, bias=EPS, scale=1.0), reads=[r_fss], writes=[r_fss])
                    S.op("act", lambda e: e.activation(out=fss[:, 2:2 + NSUB], in_=fss[:, 2:2 + NSUB], func=AF.Exp, scale=-0.5), reads=[r_fss], writes=[r_fss])
                    for s in range(NSUB):
                        eng = "dve" if s % 2 == 0 else "pool"
                        S.op(eng, lambda e: e.tensor_scalar(out=hb[:, s, :], in0=hh_[:, s, :], scalar1=fss[:, 2 + s:3 + s], scalar2=None, op0=ALU.mult),
                             reads=[rhh, r_fss], writes=[r_hb])
                    for c0 in range(0, 8, 2):
                        k = nT3 % 2; nT3 += 1
                        for a in range(2):
                            for s in range(NSUB):
                                S.op("pe", lambda e: e.transpose(out=pT3[k][:, a * 256 + s * 128:a * 256 + (s + 1) * 128], in_=hb[:, s, (c0 + a) * 128:(c0 + a + 1) * 128],
                                                                 identity=ident_b[:]), reads=[r_hb, r_idb], writes=[r_pT3[k]])
                        for a in range(2):
                            c = c0 + a
                            if a == 0:
                                S.op("dve", lambda e: e.tensor_scalar(out=hnT[:, c, :], in0=pT3[k][:, a * 256:(a + 1) * 256], scalar1=fnw(c), scalar2=None, op0=ALU.mult),
                                     reads=[r_pT3[k], r_colA], writes=[r_hnT])
                            else:
                                S.op("act", lambda e: e.activation(out=hnT[:, c, :], in_=pT3[k][:, a * 256:(a + 1) * 256], func=AF.Copy, scale=fnw(c)),
                                     reads=[r_pT3[k], r_colA], writes=[r_hnT])
                    for m in range(NCH):
                        kU = nU % 4; nU += 1
                        res_cv = []
                        for gv in range(2):
                            ch = gv * NCH + m
                            col0 = gv * DFF + m * 128
                            psl = pU[kU][:, gv * 256:(gv + 1) * 256]
                            for c in range(8):
                                S.op("pe", lambda e: e.matmul(psl, lhsT=wup[:, c, col0:col0 + 128], rhs=hnT[:, c, :], start=(c == 0), stop=(c == 7)),
                                     reads=[r_wup[c], r_hnT], writes=[r_pU[kU]])
                        for gv in range(2):
                            ch = gv * NCH + m
                            psl = pU[kU][:, gv * 256:(gv + 1) * 256]
                            kR = nR % 4; nR += 1
                            R_ = Rb[kR]; rR = r_Rb[kR]; cv_ = cvb[kR]; rcv = r_cvb[kR]
                            S.op("act", lambda e: e.activation(out=R_[:, 2:2 + TT], in_=psl, func=AF.Copy), reads=[r_pU[kU]], writes=[rR])
                            S.op("pool", lambda e: e.tensor_copy(out=R_[:, 0:2], in_=halo[:, ch, :]), reads=[r_halo], writes=[rR])
                            S.op("act", lambda e: e.activation(out=cv_[:], in_=psl, func=AF.Identity, scale=fcw(2, ch), bias=fcb(ch)), reads=[r_pU[kU], r_colB, r_colC], writes=[rcv])
                            S.op("pool", lambda e: e.tensor_copy(out=halo[:, ch, :], in_=R_[:, TT:TT + 2]), reads=[rR], writes=[r_halo])
                            S.op("dve", lambda e: e.scalar_tensor_tensor(out=cv_[:], in0=R_[:, 1:1 + TT], scalar=fcw(1, ch), in1=cv_[:], op0=ALU.mult, op1=ALU.add),
                                 reads=[rR, rcv, r_colB], writes=[rcv])
                            S.op("dve", lambda e: e.scalar_tensor_tensor(out=cv_[:], in0=R_[:, 0:TT], scalar=fcw(0, ch), in1=cv_[:], op0=ALU.mult, op1=ALU.add),
                                 reads=[rR, rcv, r_colB], writes=[rcv])
                            res_cv.append((cv_, rcv))
                        sg_ = sgb[m % 2]; rsg = r_sgb[m % 2]
                        S.op("act", lambda e: e.activation(out=sg_[:], in_=res_cv[0][0][:], func=AF.Silu), reads=[res_cv[0][1]], writes=[rsg])
                        S.op("pool", lambda e: e.tensor_tensor(out=gT[:, m, :], in0=sg_[:], in1=res_cv[1][0][:], op=ALU.mult), reads=[rsg, res_cv[1][1]], writes=[r_gT])
                    for s in range(NSUB):
                        for hf in range(2):
                            for m in range(NCH):
                                S.op("pe", lambda e: e.matmul(pH[0][:, hf * 512:(hf + 1) * 512], lhsT=gT[:, m, s * 128:(s + 1) * 128], rhs=wdn[:, m, hf * 512:(hf + 1) * 512],
                                                              start=(m == 0), stop=(m == NCH - 1)), reads=[r_gT, r_wdn], writes=[r_pH[0]])
                        S.op("dve", lambda e: e.tensor_tensor(out=hh_[:, s, :], in0=hh_[:, s, :], in1=pH[0][:], op=ALU.add), reads=[rhh, r_pH[0]], writes=[rhh])
                        S.op("act", lambda e: e.activation(out=fjunk[:], in_=hh_[:, s, :], func=AF.Square, scale=1.0 / 32.0, accum_out=fss[:, 4 + s:5 + s]),
                             reads=[rhh], writes=[r_fjunk, r_fss])
                    S.op("act", lambda e: e.activation(out=fss[:, 6:6 + NSUB], in_=fss[:, 4:4 + NSUB], func=AF.Ln, bias=EPS, scale=1.0), reads=[r_fss], writes=[r_fss])
                    S.op("act", lambda e: e.activation(out=fss[:, 6:6 + NSUB], in_=fss[:, 6:6 + NSUB], func=AF.Exp, scale=-0.5), reads=[r_fss], writes=[r_fss])
                    for s in range(NSUB):
                        S.op("dve", lambda e: e.scalar_tensor_tensor(out=hh_[:, s, :], in0=hh_[:, s, :], scalar=fss[:, 6 + s:7 + s], in1=FW[:], op0=ALU.mult, op1=ALU.mult),
                             reads=[rhh, r_fss, r_FW], writes=[rhh])
                    S.dma("sp", out[tt * TT:(tt + 1) * TT, :].rearrange("(s p) d -> p s d", p=128), hh_[:], reads=[rhh])
                S.barrier()
        S.barrier()
        nc._ninst = dict(S.ninst)
    return nc, dbg


_NC_CACHE = {}


def _squeeze(a):
    return np.ascontiguousarray(np.asarray(a, dtype=np.float32))


def kernel(x, w_in, mlstm_conv_w, mlstm_conv_b, mlstm_i_bias, mlstm_f_bias, att_out_norm_w, mlstm_out_norm_w,
           w_out, mixer_norm_w, ffn_norm_w, w_ffn_up, ffn_conv_w, ffn_conv_b, w_ffn_down, final_norm_w):
    n = 8
    if "nc" not in _NC_CACHE:
        _NC_CACHE["nc"] = build_nc()[0]
    nc = _NC_CACHE["nc"]
    shared = {
        "w_in": _squeeze(w_in[0]), "mlstm_conv_w": _squeeze(mlstm_conv_w[0]), "mlstm_conv_b": _squeeze(mlstm_conv_b[0]),
        "mlstm_i_bias": _squeeze(mlstm_i_bias[0]), "mlstm_f_bias": _squeeze(mlstm_f_bias[0]),
        "att_out_norm_w": _squeeze(att_out_norm_w[0]), "mlstm_out_norm_w": _squeeze(mlstm_out_norm_w[0]),
        "w_out": _squeeze(w_out[0]), "mixer_norm_w": _squeeze(mixer_norm_w[0]), "ffn_norm_w": _squeeze(ffn_norm_w[0]),
        "w_ffn_up": _squeeze(w_ffn_up[0]), "ffn_conv_w": _squeeze(ffn_conv_w[0]), "ffn_conv_b": _squeeze(ffn_conv_b[0]),
        "w_ffn_down": _squeeze(w_ffn_down[0]), "final_norm_w": _squeeze(final_norm_w),
    }
    xs = np.asarray(x, dtype=np.float32)
    in_maps = [dict(shared, x=np.ascontiguousarray(xs[i])) for i in range(n)]
    res = run_bass_kernel_spmd(nc, in_maps, core_ids=list(range(n)))
    return np.stack([np.asarray(r["out"], dtype=np.float32) for r in res.results], axis=0)
```

```python
import numpy as np
from collections import defaultdict
from contextlib import ExitStack
import concourse.bass as bass
import concourse.mybir as mybir
from concourse.bass_utils import run_bass_kernel_spmd

F32 = mybir.dt.float32
BF16 = mybir.dt.bfloat16
AF = mybir.ActivationFunctionType
ALU = mybir.AluOpType
AX = mybir.AxisListType

SEQ = 4096
D = 1024
PROJ = 3592
DFF = 2816
NCH = 22
EPS = 1e-6
GROUPS = (1, 4, 16)


class Res:
    __slots__ = ("ws", "r", "excl")

    def __init__(self, excl=False):
        self.ws = {}
        self.r = {}
        self.excl = excl


def PRes():
    return Res(excl=True)


class Sched:
    CE = ("pe", "act", "dve", "pool")

    def __init__(self, nc, es, nq=8):
        self.nc = nc
        self.eng = {"pe": nc.tensor, "act": nc.scalar, "dve": nc.vector, "pool": nc.gpsimd, "sp": nc.sync}
        self.sems = {}
        self.count = {}
        for e in self.CE:
            self.sems[e] = es.enter_context(nc.semaphore("s_" + e))
            self.count[e] = 0
        self.nq = nq
        self.rr = {}
        for q in ("sp", "act", "pool"):
            self.rr[q] = 0
            for i in range(nq):
                n = f"d_{q}{i}"
                self.sems[n] = es.enter_context(nc.semaphore(n))
                self.count[n] = 0
        self.seen = {e: defaultdict(int) for e in self.eng}
        self.ninst = defaultdict(int)
        self.dead = False

    def need(self, E, tok):
        if tok is None:
            return
        s, v = tok
        if s.startswith("d_"):
            v = self.count[s]
        elif s == E and E == "pe":
            return
        if self.seen[E][s] < v:
            self.eng[E].wait_ge(self.sems[s], v)
            self.seen[E][s] = v
            self.ninst[E] += 1

    def _pre(self, E, reads, writes, pwrites):
        for r in reads:
            for t in r.ws.values():
                self.need(E, t)
            if r.excl:
                for e2, t in r.r.items():
                    if e2 != E:
                        self.need(E, t)
        for w in writes:
            for t in w.ws.values():
                self.need(E, t)
            for e2, t in w.r.items():
                if e2 != E:
                    self.need(E, t)
        for w in pwrites:
            for e2, t in w.r.items():
                if e2 != E:
                    self.need(E, t)
            if w.excl:
                for e2, t in w.ws.items():
                    if e2 != E:
                        self.need(E, t)

    def _post(self, key, tok, reads, writes, pwrites):
        for r in reads:
            r.r[key] = tok
        for w in writes:
            w.ws = {key: tok}
            w.r = {}
        for w in pwrites:
            w.ws[key] = tok

    def op(self, E, fn, reads=(), writes=(), pwrites=()):
        if self.dead:
            return None
        self._pre(E, reads, writes, pwrites)
        inst = fn(self.eng[E])
        self.count[E] += 1
        inst.then_inc(self.sems[E], 1)
        self.ninst[E] += 1
        self._post(E, (E, self.count[E]), reads, writes, pwrites)
        return inst

    def dma(self, q, out, in_, reads=(), writes=(), pwrites=(), **kw):
        if self.dead:
            return None
        self._pre(q, reads, writes, pwrites)
        inst = self.eng[q].dma_start(out=out, in_=in_, **kw)
        n = f"d_{q}{self.rr[q] % self.nq}"
        self.rr[q] += 1
        self.count[n] += 16
        inst.then_inc(self.sems[n], 16)
        self.ninst[q] += 1
        self._post(n, (n, self.count[n]), reads, writes, pwrites)
        return inst

    def barrier(self):
        for E in self.eng:
            for s in self.sems:
                if self.count[s] > 0 and s != E:
                    self.need(E, (s, self.count[s]))


def build_nc(debug=False, stop_after=None):
    nc = bass.Bass("TRN2", target_bir_lowering=False)
    din = lambda n, s: nc.dram_tensor(n, s, F32, kind="ExternalInput").ap()
    x = din("x", [SEQ, D])
    w_in = din("w_in", [D, PROJ])
    mlstm_conv_w = din("mlstm_conv_w", [4, 1024])
    mlstm_conv_b = din("mlstm_conv_b", [1024])
    mlstm_i_bias = din("mlstm_i_bias", [4])
    mlstm_f_bias = din("mlstm_f_bias", [4])
    att_out_norm_w = din("att_out_norm_w", [512])
    mlstm_out_norm_w = din("mlstm_out_norm_w", [512])
    w_out = din("w_out", [D, D])
    mixer_norm_w = din("mixer_norm_w", [D])
    ffn_norm_w = din("ffn_norm_w", [D])
    w_ffn_up = din("w_ffn_up", [D, 2 * DFF])
    ffn_conv_w = din("ffn_conv_w", [3, 2 * DFF])
    ffn_conv_b = din("ffn_conv_b", [2 * DFF])
    w_ffn_down = din("w_ffn_down", [DFF, D])
    final_norm_w = din("final_norm_w", [D])
    out = nc.dram_tensor("out", [SEQ, D], F32, kind="ExternalOutput").ap()
    ybuf = nc.dram_tensor("ybuf", [D, SEQ], BF16, kind=("ExternalOutput" if debug else "Internal")).ap()
    r_ybuf = [Res() for _ in range(8)]
    dbg = {}

    def dout(name, shape, dt=F32):
        dbg[name] = nc.dram_tensor(name, shape, dt, kind="ExternalOutput").ap()
        return dbg[name]

    with ExitStack() as es:
        S = Sched(nc, es)

        def stop(tag):
            if stop_after == tag:
                S.barrier()
                S.dead = True
        sbt = lambda st, n, s, d: st.enter_context(nc.sbuf_tensor(n, s, d))
        pst = lambda st, n, s, d: st.enter_context(nc.psum_tensor(n, s, d))

        ident_b = sbt(es, "ident_b", [128, 128], BF16); r_idb = Res()
        colA = sbt(es, "colA", [128, 64], F32); r_colA = Res()
        colB = sbt(es, "colB", [128, 88], F32); r_colB = Res()
        colC = sbt(es, "colC", [128, 88], F32); r_colC = Res()
        FW = sbt(es, "FW", [128, D], F32); r_FW = Res()
        c2 = ExitStack()
        ident_f = sbt(c2, "ident_f", [128, 128], F32); r_idf = Res()
        ones_f = sbt(c2, "ones_f", [128, 128], F32); r_ones = Res()
        zero_f = sbt(c2, "zero_f", [128, 256], F32); r_zero = Res()
        sel_b = sbt(c2, "sel_b", [128, 2, 128], BF16); r_sel = Res()
        maskf = sbt(c2, "maskf", [128, 256], F32); r_maskf = Res()
        maskb = sbt(c2, "maskb", [128, 256], BF16); r_maskb = Res()
        mask01 = sbt(c2, "mask01", [128, 128], F32); r_m01 = Res()
        MNW = sbt(c2, "MNW", [128, 512], F32); r_MNW = Res()
        gb = sbt(c2, "gb", [4, 2], F32); r_gb = Res()
        S.op("pool", lambda e: e.memset(ones_f[:], 1.0), writes=[r_ones])
        S.op("pool", lambda e: e.memset(zero_f[:], 0.0), writes=[r_zero])
        S.op("pool", lambda e: e.affine_select(out=ident_f[:], in_=ones_f[:], pattern=[[-1, 128]], compare_op=ALU.is_equal,
                                               fill=0.0, base=0, channel_multiplier=1), reads=[r_ones], writes=[r_idf])
        S.op("dve", lambda e: e.tensor_copy(out=ident_b[:], in_=ident_f[:]), reads=[r_idf], writes=[r_idb])
        S.op("dve", lambda e: e.memset(sel_b[:], 0.0), writes=[r_sel])
        S.op("dve", lambda e: e.memset(sel_b[0:64, 0, :], 1.0), writes=[r_sel])
        S.op("dve", lambda e: e.memset(sel_b[64:128, 1, :], 1.0), writes=[r_sel])
        S.op("pool", lambda e: e.affine_select(out=maskf[:, 0:128], in_=zero_f[:, 0:128], pattern=[[1, 128]], compare_op=ALU.is_ge,
                                               fill=-30000.0, base=0, channel_multiplier=-1), reads=[r_zero], writes=[r_maskf])
        S.op("pool", lambda e: e.affine_select(out=maskf[:, 128:256], in_=zero_f[:, 128:256], pattern=[[-1, 128]], compare_op=ALU.is_ge,
                                               fill=-30000.0, base=0, channel_multiplier=1), reads=[r_zero], writes=[r_maskf])
        S.op("dve", lambda e: e.tensor_copy(out=maskb[:], in_=maskf[:]), reads=[r_maskf], writes=[r_maskb])
        S.op("pool", lambda e: e.affine_select(out=mask01[:], in_=ones_f[:], pattern=[[1, 128]], compare_op=ALU.is_ge,
                                               fill=0.0, base=0, channel_multiplier=-1), reads=[r_ones], writes=[r_m01])

        with ExitStack() as p0:
            rowA = sbt(p0, "rowA", [64, 128], F32); r_rowA = Res()
            rowB = sbt(p0, "rowB", [88, 128], F32); r_rowB = Res()
            rowC = sbt(p0, "rowC", [88, 128], F32); r_rowC = Res()
            pcol = pst(p0, "pcol", [128, 512], F32); r_pcol = PRes()
            S.op("dve", lambda e: e.memset(rowA[:], 0.0), writes=[r_rowA])
            S.dma("sp", rowA[0:8, :], mixer_norm_w.rearrange("(c p) -> c p", p=128), writes=[r_rowA])
            S.dma("sp", rowA[8:16, :], ffn_norm_w.rearrange("(c p) -> c p", p=128), writes=[r_rowA])
            S.dma("sp", rowA[16:48, :], mlstm_conv_w.rearrange("j (c p) -> (j c) p", p=128), writes=[r_rowA])
            S.dma("sp", rowA[48:56, :], mlstm_conv_b.rearrange("(c p) -> c p", p=128), writes=[r_rowA])
            S.dma("sp", rowA[56:64, 0:64], att_out_norm_w.rearrange("(h p) -> h p", p=64), writes=[r_rowA])
            S.dma("sp", rowB[:, :], ffn_conv_w[0:2, :].rearrange("j (c p) -> (j c) p", p=128), writes=[r_rowB])
            S.dma("sp", rowC[0:44, :], ffn_conv_w[2:3, :].rearrange("j (c p) -> (j c) p", p=128), writes=[r_rowC])
            S.dma("sp", rowC[44:88, :], ffn_conv_b.rearrange("(c p) -> c p", p=128), writes=[r_rowC])
            S.dma("sp", FW[:], final_norm_w.partition_broadcast(128), writes=[r_FW])
            S.dma("sp", MNW[:], mlstm_out_norm_w.partition_broadcast(128), writes=[r_MNW])
            with nc.allow_non_contiguous_dma(reason="tiny gate bias"):
                S.dma("sp", gb[:, 0:1], mlstm_i_bias.rearrange("(p o) -> p o", o=1), writes=[r_gb])
                S.dma("sp", gb[:, 1:2], mlstm_f_bias.rearrange("(p o) -> p o", o=1), writes=[r_gb])
            S.op("pe", lambda e: e.transpose(out=pcol[:, 0:64], in_=rowA[:], identity=ident_f[0:64, 0:64]), reads=[r_rowA, r_idf], writes=[r_pcol])
            S.op("pe", lambda e: e.transpose(out=pcol[:, 64:152], in_=rowB[:], identity=ident_f[0:88, 0:88]), reads=[r_rowB, r_idf], writes=[r_pcol])
            S.op("pe", lambda e: e.transpose(out=pcol[:, 152:240], in_=rowC[:], identity=ident_f[0:88, 0:88]), reads=[r_rowC, r_idf], writes=[r_pcol])
            S.op("dve", lambda e: e.tensor_copy(out=colA[:], in_=pcol[:, 0:64]), reads=[r_pcol], writes=[r_colA])
            S.op("dve", lambda e: e.tensor_copy(out=colB[:], in_=pcol[:, 64:152]), reads=[r_pcol], writes=[r_colB])
            S.op("dve", lambda e: e.tensor_copy(out=colC[:], in_=pcol[:, 152:240]), reads=[r_pcol], writes=[r_colC])
            S.barrier()
        mnw = lambda c: colA[:, c:c + 1]
        fnw = lambda c: colA[:, 8 + c:9 + c]
        mcw = lambda j, c: colA[:, 16 + j * 8 + c:17 + j * 8 + c]
        mcb = lambda c: colA[:, 48 + c:49 + c]
        aow = lambda h: colA[0:64, 56 + h:57 + h]

        def fcw(j, ch):
            if j < 2:
                return colB[:, j * 44 + ch:j * 44 + ch + 1]
            return colC[:, ch:ch + 1]
        fcb = lambda ch: colC[:, 44 + ch:45 + ch]

        if debug:
            o_colA = dout("o_colA", [128, 64]); o_colB = dout("o_colB", [128, 88]); o_colC = dout("o_colC", [128, 88])
            S.dma("sp", o_colA[:, :], colA[:], reads=[r_colA]); S.dma("sp", o_colB[:, :], colB[:], reads=[r_colB]); S.dma("sp", o_colC[:, :], colC[:], reads=[r_colC])
            S.barrier()
        with ExitStack() as p12:
          if stop_after != "p0":
                xnT = sbt(p12, "xnT", [128, 8, SEQ], BF16)
                r_xnT = [Res() for _ in range(8)]

                with ExitStack() as ph:
                    xt = [sbt(ph, f"p1x{i}", [128, 4, D], F32) for i in range(2)]; r_xt = [Res(), Res()]
                    xb = [sbt(ph, f"p1xb{i}", [128, 4, D], BF16) for i in range(2)]; r_xb = [Res(), Res()]
                    junk = sbt(ph, "p1junk", [128, D], BF16); r_junk = Res()
                    ss = [sbt(ph, f"p1ss{i}", [128, 4], F32) for i in range(2)]; r_ss = [Res(), Res()]
                    rs = [sbt(ph, f"p1rs{i}", [128, 4], F32) for i in range(2)]; r_rs = [Res(), Res()]
                    pT_ = [pst(ph, f"p1pT{i}", [128, 1024], BF16) for i in range(4)]; r_pT = [PRes() for _ in range(4)]
                    pT = [t[:, 0:512] for t in pT_]
                    for tt in range(8):
                        b = tt % 2
                        S.dma("sp", xt[b][:], x[tt * 512:(tt + 1) * 512, :].rearrange("(s p) d -> p s d", p=128), writes=[r_xt[b]])
                        for s in range(4):
                            S.op("act", lambda e: e.activation(out=junk[:], in_=xt[b][:, s, :], func=AF.Square, scale=1.0 / 32.0,
                                                               accum_out=ss[b][:, s:s + 1]), reads=[r_xt[b]], pwrites=[r_ss[b]])
                        S.op("act", lambda e: e.activation(out=rs[b][:], in_=ss[b][:], func=AF.Ln, bias=EPS, scale=1.0), reads=[r_ss[b]], writes=[r_rs[b]])
                        S.op("act", lambda e: e.activation(out=rs[b][:], in_=rs[b][:], func=AF.Exp, scale=-0.5), reads=[r_rs[b]], writes=[r_rs[b]])
                        for s in range(4):
                            eng = "dve"
                            S.op(eng, lambda e: e.tensor_scalar(out=xb[b][:, s, :], in0=xt[b][:, s, :], scalar1=rs[b][:, s:s + 1], scalar2=None,
                                                                op0=ALU.mult), reads=[r_xt[b], r_rs[b]], pwrites=[r_xb[b]])
                        for c in range(8):
                            k = (tt * 8 + c) % 4
                            for s in range(4):
                                S.op("pe", lambda e: e.transpose(out=pT[k][:, s * 128:(s + 1) * 128], in_=xb[b][:, s, c * 128:(c + 1) * 128],
                                                                 identity=ident_b[:]), reads=[r_xb[b], r_idb], writes=[r_pT[k]])
                            if c % 2 == 0:
                                S.op("dve", lambda e: e.tensor_scalar(out=xnT[:, c, tt * 512:(tt + 1) * 512], in0=pT[k], scalar1=mnw(c), scalar2=None,
                                                                      op0=ALU.mult), reads=[r_pT[k], r_colA], pwrites=[r_xnT[tt]])
                            else:
                                S.op("act", lambda e: e.activation(out=xnT[:, c, tt * 512:(tt + 1) * 512], in_=pT[k], func=AF.Copy, scale=mnw(c)),
                                     reads=[r_pT[k], r_colA], pwrites=[r_xnT[tt]])
                    S.barrier()
                if debug:
                    o_xnT = dout("o_xnT", [D, SEQ], BF16)
                    for c in range(8):
                        S.dma("sp", o_xnT[c * 128:(c + 1) * 128, :], xnT[:, c, :], reads=r_xnT)
                    S.barrier()

                if stop_after not in ("p1", "mlg", "mlh0", "mlh1", "mlh2", "mlh"):
                    with ExitStack() as ph:
                        wqkv = [sbt(ph, f"wqkv{i}", [128, 8, 3, 128], BF16) for i in range(2)]; r_wqkv = [Res(), Res()]
                        QT = sbt(ph, "QT", [128, SEQ], BF16); r_QT = Res()
                        KT = sbt(ph, "KT", [128, SEQ], BF16); r_KT = Res()
                        VT = sbt(ph, "VT", [128, SEQ], BF16); r_VT = Res()
                        sq = sbt(ph, "sq", [128, SEQ], BF16); r_sq = Res()
                        mx = sbt(ph, "mx", [128, 4, 8], F32); r_mx = Res()
                        st = sbt(ph, "st", [128, 4], F32); r_st = Res()
                        nbias = sbt(ph, "nbias", [128, 2], F32); r_nb = Res()
                        Vaug = sbt(ph, "Vaug", [128, 32, 2, 128], BF16); r_Vaug = Res()
                        acc = [sbt(ph, f"acc{i}", [128, SEQ], F32) for i in range(2)]; r_acc = [Res(), Res()]
                        PT = [sbt(ph, f"PT{i}", [128, 256], BF16) for i in range(4)]; r_PT = [Res() for _ in range(4)]
                        e_rden = sbt(ph, "e_rden", [64, 512], F32); r_erden = Res()
                        e_o = sbt(ph, "e_o", [64, 512], F32); r_eo = Res()
                        e_sq = sbt(ph, "e_sq", [64, 512], F32); r_esq = Res()
                        e_rs = sbt(ph, "e_rs", [64, 512], F32); r_ers = Res()
                        yT = [sbt(ph, f"yTa{i}", [64, SEQ], BF16) for i in range(2)]; r_yT = [Res(), Res()]
                        pJ = [pst(ph, f"aJ{i}", [128, 512], F32) for i in range(2)]; r_pJ = [PRes(), PRes()]
                        pS = [pst(ph, f"aS{i}", [128, 512], F32) for i in range(2)]; r_pS = [PRes(), PRes()]
                        pO = [pst(ph, f"aO{i}", [128, 512], F32) for i in range(2)]; r_pO = [PRes(), PRes()]
                        pV_ = [pst(ph, f"aV{i}", [128, 1024], BF16) for i in range(2)]; r_pV = [PRes(), PRes()]
                        pV = [t[:, 0:512] for t in pV_]
                        S.op("pool", lambda e: e.memset(Vaug[:, :, :, 64:128], 1.0), writes=[r_Vaug])
                        nj = 0
                        for pr in range(4):
                            wb = wqkv[pr % 2]; rwb = r_wqkv[pr % 2]
                            for j in range(3):
                                S.dma("pool", wb[:, :, j, :], w_in[:, j * 512 + pr * 128:j * 512 + (pr + 1) * 128].rearrange("(c p) m -> p c m", p=128),
                                      writes=[rwb])
                            for j, (dst, rdst) in enumerate(((QT, r_QT), (KT, r_KT), (VT, r_VT))):
                                for tt in range(8):
                                    k = nj % 2; nj += 1
                                    for c in range(8):
                                        S.op("pe", lambda e: e.matmul(pJ[k][:], lhsT=wb[:, c, j, :], rhs=xnT[:, c, tt * 512:(tt + 1) * 512],
                                                                      start=(c == 0), stop=(c == 7)), reads=[rwb, r_xnT[tt]], writes=[r_pJ[k]])
                                    if nj % 2 == 0:
                                        S.op("dve", lambda e: e.tensor_copy(out=dst[:, tt * 512:(tt + 1) * 512], in_=pJ[k][:]), reads=[r_pJ[k]], pwrites=[rdst])
                                    else:
                                        S.op("act", lambda e: e.activation(out=dst[:, tt * 512:(tt + 1) * 512], in_=pJ[k][:], func=AF.Copy),
                                             reads=[r_pJ[k]], pwrites=[rdst])
                            for qi, (src, rsrc) in enumerate(((QT, r_QT), (KT, r_KT))):
                                S.op("dve", lambda e: e.tensor_tensor(out=sq[:], in0=src[:], in1=src[:], op=ALU.mult), reads=[rsrc], writes=[r_sq])
                                for hh in range(2):
                                    for tt in range(8):
                                        k = nj % 2; nj += 1
                                        S.op("pe", lambda e: e.matmul(pJ[k][:], lhsT=sel_b[:, hh, :], rhs=sq[:, tt * 512:(tt + 1) * 512], start=True, stop=True),
                                             reads=[r_sel, r_sq], writes=[r_pJ[k]])
                                        S.op("dve", lambda e: e.tensor_reduce(out=mx[:, qi * 2 + hh, tt:tt + 1], in_=pJ[k][:], axis=AX.X, op=ALU.max),
                                             reads=[r_pJ[k]], pwrites=[r_mx])
                            S.op("dve", lambda e: e.tensor_reduce(out=st[:], in_=mx[:], axis=AX.X, op=ALU.max), reads=[r_mx], writes=[r_st])
                            S.op("dve", lambda e: e.tensor_tensor(out=nbias[:], in0=st[:, 0:2], in1=st[:, 2:4], op=ALU.mult), reads=[r_st], writes=[r_nb])
                            S.op("act", lambda e: e.activation(out=nbias[:], in_=nbias[:], func=AF.Ln), reads=[r_nb], writes=[r_nb])
                            S.op("act", lambda e: e.activation(out=nbias[:], in_=nbias[:], func=AF.Exp, scale=0.5), reads=[r_nb], writes=[r_nb])
                            S.op("dve", lambda e: e.tensor_scalar(out=nbias[:], in0=nbias[:], scalar1=-0.125 * 1.02, scalar2=None, op0=ALU.mult),
                                 reads=[r_nb], writes=[r_nb])
                            nV = 0; nS = 0; nO = 0
                            for gi, d in enumerate(GROUPS):
                                nb_ = 32 // d
                                for kt0 in range(0, 32, 4):
                                    k = nV % 2; nV += 1
                                    for sl in range(4):
                                        kt = kt0 + sl
                                        r_, b_ = kt // nb_, kt % nb_
                                        t0 = 128 * b_ * d + r_
                                        S.op("pe", lambda e: e.transpose(out=pV[k][:, sl * 128:(sl + 1) * 128], in_=VT[:, t0:t0 + 127 * d + 1:d],
                                                                         identity=ident_b[:]), reads=[r_VT, r_idb], writes=[r_pV[k]])
                                    eng = "dve" if (nV % 2 == 0) else "act"
                                    src = pV[k].rearrange("p (s h e) -> p s h e", s=4, h=2)
                                    if eng == "dve":
                                        S.op("dve", lambda e: e.tensor_copy(out=Vaug[:, kt0:kt0 + 4, :, 0:64], in_=src), reads=[r_pV[k]], pwrites=[r_Vaug])
                                    else:
                                        S.op("act", lambda e: e.activation(out=Vaug[:, kt0:kt0 + 4, :, 0:64], in_=src, func=AF.Copy),
                                             reads=[r_pV[k]], pwrites=[r_Vaug])
                                for hh in range(2):
                                    hs = slice(hh * 64, (hh + 1) * 64)
                                    ah = acc[hh]; rah = r_acc[hh]
                                    slot = 0
                                    prevPT = None
                                    for r_ in range(d):
                                        for b_ in range(nb_):
                                            kt = r_ * nb_ + b_
                                            N = 256 if b_ + 1 < nb_ else 128
                                            t0 = 128 * b_ * d + r_
                                            ks = slice(t0, t0 + 127 * d + 1, d)
                                            qs = slice(t0, t0 + (N - 1) * d + 1, d)
                                            ks_ = nS % 2; pi = nS % 4; nS += 1
                                            S.op("pe", lambda e: e.matmul(pS[ks_][:, 0:N], lhsT=KT[hs, ks], rhs=QT[hs, qs], start=True, stop=False),
                                                 reads=[r_KT, r_QT], writes=[r_pS[ks_]])
                                            S.op("pe", lambda e: e.matmul(pS[ks_][:, 0:N], lhsT=ident_b[:], rhs=maskb[:, 0:N], start=False, stop=True),
                                                 reads=[r_idb, r_maskb], writes=[r_pS[ks_]])
                                            S.op("act", lambda e: e.activation(out=PT[pi][:, 0:N], in_=pS[ks_][:, 0:N], func=AF.Exp, scale=0.125,
                                                                               bias=nbias[:, hh:hh + 1]), reads=[r_pS[ks_], r_nb], writes=[r_PT[pi]])
                                            ko = nO % 2
                                            osl = pO[ko][:, slot * 128:(slot + 1) * 128]
                                            if b_ > 0:
                                                S.op("pe", lambda e: e.matmul(osl, lhsT=Vaug[:, kt - 1, hh, :], rhs=PT[prevPT][:, 128:256], start=True, stop=False),
                                                     reads=[r_Vaug, r_PT[prevPT]], writes=[r_pO[ko]])
                                            S.op("pe", lambda e: e.matmul(osl, lhsT=Vaug[:, kt, hh, :], rhs=PT[pi][:, 0:128], start=(b_ == 0), stop=True),
                                                 reads=[r_Vaug, r_PT[pi]], writes=[r_pO[ko]])
                                            prevPT = pi
                                            slot += 1
                                            if slot == 4:
                                                slot = 0
                                                nO += 1
                                                if d == 16:
                                                    dst = ah[:].rearrange("p (i dd) -> p dd i", dd=16)[:, r_ - 1:r_ + 1, :]
                                                    src = pO[ko][:].rearrange("p (a i) -> p a i", a=2)
                                                else:
                                                    b0 = b_ - 3
                                                    ts = 128 * b0 * d + r_
                                                    dst = ah[:, ts:ts + 511 * d + 1:d]
                                                    src = pO[ko][:]
                                                if gi == 0:
                                                    S.op("dve", lambda e: e.tensor_copy(out=dst, in_=src), reads=[r_pO[ko]], pwrites=[rah])
                                                else:
                                                    S.op("dve", lambda e: e.tensor_tensor(out=dst, in0=dst, in1=src, op=ALU.add), reads=[r_pO[ko], rah], pwrites=[rah])
                            for hh in range(2):
                                h = pr * 2 + hh
                                ah = acc[hh]; rah = r_acc[hh]
                                yb = yT[h % 2]; ryb = r_yT[h % 2]
                                for tt in range(8):
                                    cs = slice(tt * 512, (tt + 1) * 512)
                                    k = nj % 2; nj += 1
                                    S.op("dve", lambda e: e.reciprocal(out=e_rden[:], in_=ah[64:128, cs]), reads=[rah], writes=[r_erden])
                                    S.op("dve", lambda e: e.tensor_tensor(out=e_o[:], in0=ah[0:64, cs], in1=e_rden[:], op=ALU.mult), reads=[rah, r_erden], writes=[r_eo])
                                    S.op("act", lambda e: e.activation(out=e_sq[:], in_=e_o[:], func=AF.Square), reads=[r_eo], writes=[r_esq])
                                    S.op("pe", lambda e: e.matmul(pJ[k][0:64, :], lhsT=ones_f[0:64, 0:64], rhs=e_sq[:], start=True, stop=True),
                                         reads=[r_ones, r_esq], writes=[r_pJ[k]])
                                    S.op("act", lambda e: e.activation(out=e_rs[:], in_=pJ[k][0:64, :], func=AF.Ln, bias=EPS, scale=1.0 / 64.0),
                                         reads=[r_pJ[k]], writes=[r_ers])
                                    S.op("act", lambda e: e.activation(out=e_rs[:], in_=e_rs[:], func=AF.Exp, scale=-0.5), reads=[r_ers], writes=[r_ers])
                                    S.op("dve", lambda e: e.scalar_tensor_tensor(out=yb[:, cs], in0=e_o[:], scalar=aow(h), in1=e_rs[:], op0=ALU.mult, op1=ALU.mult),
                                         reads=[r_eo, r_ers, r_colA], pwrites=[ryb])
                                S.dma("sp", ybuf[h * 64:(h + 1) * 64, :], yb[:], reads=[ryb], writes=[r_ybuf[h // 2]])
                        S.barrier()

                if stop_after not in ("p1", "att"):
                    with ExitStack() as ph:
                        wexpT = sbt(ph, "wexpT", [128, 128], F32); r_wexpT = Res()
                        floorT = sbt(ph, "floorT", [128, 128], F32); r_floorT = Res()
                        decB = sbt(ph, "decB", [128, 128], F32); r_decB = Res()

                        with ExitStack() as pg:
                            wg = sbt(pg, "wg", [128, 8, 8], BF16); r_wg = Res()
                            pG = pst(pg, "mG", [128, 512], F32); r_pG = PRes()
                            gA = sbt(pg, "gA", [4, SEQ], F32); r_gA = Res()
                            gB = sbt(pg, "gB", [4, SEQ], F32); r_gB = Res()
                            gC = sbt(pg, "gC", [4, SEQ], F32); r_gC = Res()
                            gD = sbt(pg, "gD", [4, SEQ], F32); r_gD = Res()
                            sm = sbt(pg, "sm", [4, 12, 32], F32); r_sm = Res()
                            nfb = sbt(pg, "nfb", [4, 1], F32); r_nfb = Res()
                            dmask = sbt(pg, "dmask", [4, 128], F32); r_dmask = Res()
                            decD = sbt(pg, "decD", [4, 128], F32); r_decD = Res()
                            wgf = sbt(pg, "wgf", [128, 8, 8], F32); r_wgf = Res()
                            with nc.allow_non_contiguous_dma(reason="gate weight columns (32B runs)"):
                                S.dma("sp", wgf[:], w_in[:, 3584:3592].rearrange("(c p) m -> p c m", p=128), writes=[r_wgf])
                            S.op("dve", lambda e: e.tensor_copy(out=wg[:], in_=wgf[:]), reads=[r_wgf], writes=[r_wg])
                            S.op("dve", lambda e: e.memset(gC[:], 1.0), writes=[r_gC])
                            S.op("dve", lambda e: e.memset(gC[:].rearrange("p (j t) -> p j t", t=128)[:, :, 0:1], 0.0), writes=[r_gC])
                            S.op("dve", lambda e: e.tensor_scalar(out=nfb[:], in0=gb[:, 1:2], scalar1=-1.0, scalar2=None, op0=ALU.mult), reads=[r_gb], writes=[r_nfb])
                            S.op("pool", lambda e: e.affine_select(out=dmask[:].rearrange("p (h j) -> p h j", h=4), in_=ones_f[0:4, :].rearrange("p (h j) -> p h j", h=4),
                                                                   pattern=[[-1, 4], [0, 32]], compare_op=ALU.is_equal, fill=0.0, base=0, channel_multiplier=1),
                                 reads=[r_ones], writes=[r_dmask])
                            for tt in range(8):
                                cs = slice(tt * 512, (tt + 1) * 512)
                                for c in range(8):
                                    S.op("pe", lambda e: e.matmul(pG[0:4, :], lhsT=wg[:, c, 0:4], rhs=xnT[:, c, cs], start=(c == 0), stop=(c == 7)),
                                         reads=[r_wg, r_xnT[tt]], writes=[r_pG])
                                S.op("act", lambda e: e.activation(out=gA[:, cs], in_=pG[0:4, :], func=AF.Identity, bias=gb[:, 0:1], scale=1.0),
                                     reads=[r_pG, r_gb], pwrites=[r_gA])
                                for c in range(8):
                                    S.op("pe", lambda e: e.matmul(pG[0:4, :], lhsT=wg[:, c, 4:8], rhs=xnT[:, c, cs], start=(c == 0), stop=(c == 7)),
                                         reads=[r_wg, r_xnT[tt]], writes=[r_pG])
                                S.op("act", lambda e: e.activation(out=gB[:, cs], in_=pG[0:4, :], func=AF.Exp, bias=nfb[:, 0:1], scale=-1.0),
                                     reads=[r_pG, r_nfb], pwrites=[r_gB])
                            S.op("act", lambda e: e.activation(out=gB[:], in_=gB[:], func=AF.Ln, bias=1.0, scale=1.0), reads=[r_gB], writes=[r_gB])
                            S.op("dve", lambda e: e.tensor_scalar(out=gD[:], in0=gB[:], scalar1=-1.0, scalar2=None, op0=ALU.mult), reads=[r_gB], writes=[r_gD])
                            S.op("dve", lambda e: e.tensor_tensor_scan(out=gB[:], data0=gC[:], data1=gD[:], initial=0.0, op0=ALU.mult, op1=ALU.add),
                                 reads=[r_gC, r_gD], writes=[r_gB])
                            S.op("dve", lambda e: e.tensor_tensor(out=gA[:], in0=gA[:], in1=gB[:], op=ALU.subtract), reads=[r_gA, r_gB], writes=[r_gA])
                            rmax, gch, ginc, gx, rho, mun, mu, u_, dec_, tmp_ = [sm[:, i, :] for i in range(10)]
                            S.op("dve", lambda e: e.tensor_reduce(out=rmax, in_=gA[:].rearrange("p (j t) -> p j t", t=128), axis=AX.X, op=ALU.max),
                                 reads=[r_gA], writes=[r_sm])
                            S.op("dve", lambda e: e.tensor_copy(out=gch, in_=gB[:].rearrange("p (j t) -> p j t", t=128)[:, :, 127]), reads=[r_gB], writes=[r_sm])
                            S.op("dve", lambda e: e.memset(tmp_, 1.0), writes=[r_sm])
                            S.op("dve", lambda e: e.tensor_tensor_scan(out=ginc, data0=tmp_, data1=gch, initial=0.0, op0=ALU.mult, op1=ALU.add), reads=[r_sm], writes=[r_sm])
                            S.op("dve", lambda e: e.tensor_tensor(out=gx, in0=ginc, in1=gch, op=ALU.subtract), reads=[r_sm], writes=[r_sm])
                            S.op("dve", lambda e: e.tensor_tensor(out=rho, in0=rmax, in1=gx, op=ALU.subtract), reads=[r_sm], writes=[r_sm])
                            S.op("dve", lambda e: e.tensor_tensor_scan(out=mun, data0=rho, data1=rho, initial=0.0, op0=ALU.max, op1=ALU.max), reads=[r_sm], writes=[r_sm])
                            S.op("dve", lambda e: e.memset(mu, 0.0), writes=[r_sm])
                            S.op("dve", lambda e: e.tensor_copy(out=sm[:, 6, 1:32], in_=sm[:, 5, 0:31]), reads=[r_sm], writes=[r_sm])
                            S.op("dve", lambda e: e.tensor_tensor(out=u_, in0=gx, in1=mun, op=ALU.add), reads=[r_sm], writes=[r_sm])
                            S.op("dve", lambda e: e.tensor_tensor(out=dec_, in0=mu, in1=mun, op=ALU.subtract), reads=[r_sm], writes=[r_sm])
                            S.op("act", lambda e: e.activation(out=dec_, in_=dec_, func=AF.Exp), reads=[r_sm], writes=[r_sm])
                            ub = sm[:, 7, :].unsqueeze(2).to_broadcast([4, 32, 128])
                            S.op("dve", lambda e: e.tensor_tensor(out=gA[:].rearrange("p (j t) -> p j t", t=128), in0=gA[:].rearrange("p (j t) -> p j t", t=128),
                                                                  in1=ub, op=ALU.subtract), reads=[r_gA, r_sm], writes=[r_gA])
                            S.op("dve", lambda e: e.tensor_scalar(out=gA[:], in0=gA[:], scalar1=-0.5 * float(np.log(128.0)), scalar2=None, op0=ALU.add),
                                 reads=[r_gA], writes=[r_gA])
                            S.op("act", lambda e: e.activation(out=gA[:], in_=gA[:], func=AF.Exp), reads=[r_gA], writes=[r_gA])
                            S.op("dve", lambda e: e.tensor_tensor(out=gB[:].rearrange("p (j t) -> p j t", t=128), in0=gB[:].rearrange("p (j t) -> p j t", t=128),
                                                                  in1=ub, op=ALU.add), reads=[r_gB, r_sm], writes=[r_gB])
                            S.op("act", lambda e: e.activation(out=gB[:], in_=gB[:], func=AF.Exp, scale=-1.0), reads=[r_gB], writes=[r_gB])
                            for src, rsrc, dstT, rdstT in ((gA, r_gA, wexpT, r_wexpT), (gB, r_gB, floorT, r_floorT)):
                                for j in range(32):
                                    S.op("pe", lambda e: e.matmul(pG[:, j * 4:(j + 1) * 4], lhsT=src[0:4, j * 128:(j + 1) * 128], rhs=ident_f[0:4, 0:4], start=True, stop=True),
                                         reads=[rsrc, r_idf], writes=[r_pG])
                                S.op("dve", lambda e: e.tensor_copy(out=dstT[:].rearrange("p (h j) -> p j h", h=4), in_=pG[:, 0:128].rearrange("p (j h) -> p j h", h=4)),
                                     reads=[r_pG], writes=[rdstT])
                            S.op("dve", lambda e: e.tensor_tensor(out=decD[:].rearrange("p (h j) -> p h j", h=4), in0=dmask[:].rearrange("p (h j) -> p h j", h=4),
                                                                  in1=sm[:, 8, :].unsqueeze(1).to_broadcast([4, 4, 32]), op=ALU.mult), reads=[r_dmask, r_sm], writes=[r_decD])
                            S.op("pe", lambda e: e.matmul(pG[:, 0:128], lhsT=ones_f[0:4, :], rhs=decD[:], start=True, stop=True), reads=[r_ones, r_decD], writes=[r_pG])
                            S.op("dve", lambda e: e.tensor_copy(out=decB[:], in_=pG[:, 0:128]), reads=[r_pG], writes=[r_decB])
                            if debug:
                                o_wexp = dout("o_wexp", [4, SEQ]); o_floor = dout("o_floor", [4, SEQ]); o_sm = dout("o_sm", [4, 12 * 32])
                                o_wexpT = dout("o_wexpT", [128, 128]); o_decB = dout("o_decB", [128, 128])
                                S.dma("sp", o_wexp[:, :], gA[:], reads=[r_gA]); S.dma("sp", o_floor[:, :], gB[:], reads=[r_gB])
                                S.dma("sp", o_sm[:, :], sm[:].rearrange("p a b -> p (a b)"), reads=[r_sm])
                                S.dma("sp", o_wexpT[:, :], wexpT[:], reads=[r_wexpT]); S.dma("sp", o_decB[:, :], decB[:], reads=[r_decB])
                            S.barrier()

                        stop("mlg")
                        wqk = [sbt(ph, f"wqk{i}", [128, 8, 2, 128], BF16) for i in range(2)]; r_wqk = [Res(), Res()]
                        wvo = [sbt(ph, f"wvo{i}", [128, 8, 2, 128], BF16) for i in range(2)]; r_wvo = [Res(), Res()]
                        xpre = sbt(ph, "xpre", [128, 3 + SEQ], F32); r_xpre = Res()
                        cv = sbt(ph, "cv", [128, SEQ], F32); r_cv = Res()
                        qkT = sbt(ph, "qkT", [128, 2, SEQ], BF16); r_qkT = [Res(), Res()]
                        Vm = sbt(ph, "Vm", [128, 32, 132], BF16); r_Vm = Res()
                        SG = sbt(ph, "SG", [128, 32, 128], F32); r_SG = Res()
                        sgt = sbt(ph, "sgt", [128, 2, 128], F32); r_sgt = Res()
                        Kw = sbt(ph, "Kw", [128, 32, 128], BF16); r_Kw = Res()
                        yTm = [sbt(ph, f"yTm{i}", [128, SEQ], BF16) for i in range(2)]; r_yTm = [Res(), Res()]
                        Cst = sbt(ph, "Cst", [128, 129], F32); r_C = Res()
                        Cbf = sbt(ph, "Cbf", [128, 129], BF16); r_Cb = Res()
                        PTm = [sbt(ph, f"PTm{i}", [128, 128], BF16) for i in range(2)]; r_PTm = [Res(), Res()]
                        ytl = [sbt(ph, f"ytl{i}", [128, 128], BF16) for i in range(2)]; r_ytl = [Res(), Res()]
                        sc = [sbt(ph, f"msc{i}", [128, 8], F32) for i in range(2)]; r_sc = [Res(), Res()]
                        junkm = sbt(ph, "junkm", [128, 128], F32); r_junkm = Res()
                        pP = [pst(ph, f"mP{i}", [128, 512], F32) for i in range(2)]; r_pP = [PRes(), PRes()]
                        pST_ = [pst(ph, f"mST{i}", [128, 512], F32) for i in range(2)]; r_pST = [PRes(), PRes()]
                        pST = [t[:, 0:128] for t in pST_]
                        pA_ = [pst(ph, f"mA{i}", [128, 512], F32) for i in range(2)]; r_pA = [PRes(), PRes()]
                        pA = [t[:, 0:129] for t in pA_]
                        pC_ = pst(ph, "mC", [128, 512], F32); r_pC = PRes()
                        pC = pC_[:, 0:129]
                        pTb_ = pst(ph, "mTb", [128, 1024], BF16); r_pTb = [PRes()]
                        pTb = [pTb_[:, 0:512]]
                        S.op("pool", lambda e: e.memset(Vm[:, :, 128:129], 1.0), writes=[r_Vm])
                        S.op("pool", lambda e: e.memset(xpre[:, 0:3], 0.0), writes=[r_xpre])
                        npj = 0
                        for h in range(4):
                            wq_ = wqk[h % 2]; rwq = r_wqk[h % 2]
                            wv_ = wvo[h % 2]; rwv = r_wvo[h % 2]
                            for j in range(2):
                                c0 = 1536 + j * 512 + h * 128
                                S.dma("pool", wq_[:, :, j, :], w_in[:, c0:c0 + 128].rearrange("(c p) m -> p c m", p=128), writes=[rwq])
                                c1 = 2560 + j * 512 + h * 128
                                S.dma("pool", wv_[:, :, j, :], w_in[:, c1:c1 + 128].rearrange("(c p) m -> p c m", p=128), writes=[rwv])
                            for j in range(2):
                                ch = j * 4 + h
                                for tt in range(8):
                                    k = npj % 2; npj += 1
                                    for c in range(8):
                                        S.op("pe", lambda e: e.matmul(pP[k][:], lhsT=wq_[:, c, j, :], rhs=xnT[:, c, tt * 512:(tt + 1) * 512], start=(c == 0), stop=(c == 7)),
                                             reads=[rwq, r_xnT[tt]], writes=[r_pP[k]])
                                    S.op("act", lambda e: e.activation(out=xpre[:, 3 + tt * 512:3 + (tt + 1) * 512], in_=pP[k][:], func=AF.Copy),
                                         reads=[r_pP[k]], pwrites=[r_xpre])
                                S.op("dve", lambda e: e.tensor_scalar(out=cv[:], in0=xpre[:, 3:3 + SEQ], scalar1=mcw(3, ch), scalar2=mcb(ch), op0=ALU.mult, op1=ALU.add),
                                     reads=[r_xpre, r_colA], writes=[r_cv])
                                for tap in range(3):
                                    S.op("dve", lambda e: e.scalar_tensor_tensor(out=cv[:], in0=xpre[:, tap:tap + SEQ], scalar=mcw(tap, ch), in1=cv[:], op0=ALU.mult, op1=ALU.add),
                                         reads=[r_xpre, r_cv, r_colA], writes=[r_cv])
                                S.op("act", lambda e: e.activation(out=qkT[:, j, :], in_=cv[:], func=AF.Silu), reads=[r_cv], writes=[r_qkT[j]])
                            stop("mlh0")
                            for jt in range(0, 32, 2):
                                k = npj % 2; npj += 1
                                for a in range(2):
                                    tsl = slice((jt + a) * 128, (jt + a + 1) * 128)
                                    for c in range(8):
                                        S.op("pe", lambda e: e.matmul(pP[k][:, a * 256:(a + 1) * 256], lhsT=xnT[:, c, tsl], rhs=wv_[:, c, :, :].rearrange("p a b -> p (a b)"),
                                                                      start=(c == 0), stop=(c == 7)), reads=[rwv, r_xnT[(jt + a) // 4]], writes=[r_pP[k]])
                                pv = pP[k][:].rearrange("p (a j e) -> p a j e", a=2, j=2)
                                S.op("dve", lambda e: e.tensor_copy(out=Vm[:, jt:jt + 2, 0:128], in_=pv[:, :, 0, :]), reads=[r_pP[k]], pwrites=[r_Vm])
                                S.op("act", lambda e: e.activation(out=sgt[:], in_=pv[:, :, 1, :], func=AF.Sigmoid), reads=[r_pP[k]], writes=[r_sgt])
                                S.op("dve", lambda e: e.tensor_tensor(out=SG[:, jt:jt + 2, :], in0=sgt[:], in1=MNW[:, h * 128:(h + 1) * 128].unsqueeze(1).to_broadcast([128, 2, 128]),
                                                                       op=ALU.mult), reads=[r_sgt, r_MNW], pwrites=[r_SG])
                            stop("mlh1")
                            for j0 in range(0, 32, 4):
                                for a in range(4):
                                    j = j0 + a
                                    S.op("pe", lambda e: e.transpose(out=pTb[0][:, a * 128:(a + 1) * 128], in_=qkT[:, 1, j * 128:(j + 1) * 128], identity=ident_b[:]),
                                         reads=[r_qkT[1], r_idb], writes=[r_pTb[0]])
                                for a in range(4):
                                    j = j0 + a
                                    col = h * 32 + j
                                    if a % 2 == 0:
                                        S.op("dve", lambda e: e.tensor_scalar(out=Kw[:, j, :], in0=pTb[0][:, a * 128:(a + 1) * 128], scalar1=wexpT[:, col:col + 1], scalar2=None,
                                                                              op0=ALU.mult), reads=[r_pTb[0], r_wexpT], pwrites=[r_Kw])
                                    else:
                                        S.op("act", lambda e: e.activation(out=Kw[:, j, :], in_=pTb[0][:, a * 128:(a + 1) * 128], func=AF.Copy, scale=wexpT[:, col:col + 1]),
                                             reads=[r_pTb[0], r_wexpT], pwrites=[r_Kw])
                            stop("mlh2")
                            ym = yTm[h % 2]; rym = r_yTm[h % 2]
                            for j in range(32):
                                col = h * 32 + j
                                tsl = slice(j * 128, (j + 1) * 128)
                                kk = j % 2
                                if j > 0:
                                    S.op("dve", lambda e: e.tensor_scalar(out=Cst[:], in0=Cst[:], scalar1=decB[:, col:col + 1], scalar2=None, op0=ALU.mult),
                                         reads=[r_C, r_decB], writes=[r_C])
                                    S.op("act", lambda e: e.activation(out=Cbf[:], in_=Cst[:], func=AF.Copy), reads=[r_C], writes=[r_Cb])
                                S.op("pe", lambda e: e.matmul(pST[kk], lhsT=qkT[:, 1, tsl], rhs=qkT[:, 0, tsl], start=True, stop=True),
                                     reads=[r_qkT[0], r_qkT[1]], writes=[r_pST[kk]])
                                S.op("dve", lambda e: e.scalar_tensor_tensor(out=PTm[kk][:], in0=pST[kk], scalar=wexpT[:, col:col + 1], in1=mask01[:], op0=ALU.mult, op1=ALU.mult),
                                     reads=[r_pST[kk], r_wexpT, r_m01], writes=[r_PTm[kk]])
                                S.op("pe", lambda e: e.matmul(pA[kk], lhsT=PTm[kk][:], rhs=Vm[:, j, 0:129], start=True, stop=(j == 0)), reads=[r_PTm[kk], r_Vm], writes=[r_pA[kk]])
                                if j > 0:
                                    S.op("pe", lambda e: e.matmul(pA[kk], lhsT=qkT[:, 0, tsl], rhs=Cbf[:], start=False, stop=True), reads=[r_qkT[0], r_Cb], writes=[r_pA[kk]])
                                S.op("pe", lambda e: e.matmul(pC, lhsT=Kw[:, j, :], rhs=Vm[:, j, 0:129], start=True, stop=True), reads=[r_Kw, r_Vm], writes=[r_pC])
                                if j == 0:
                                    S.op("dve", lambda e: e.tensor_copy(out=Cst[:], in_=pC), reads=[r_pC], writes=[r_C])
                                else:
                                    S.op("dve", lambda e: e.tensor_tensor(out=Cst[:], in0=Cst[:], in1=pC, op=ALU.add), reads=[r_C, r_pC], writes=[r_C])
                                s_ = sc[kk]; rs_ = r_sc[kk]
                                S.op("act", lambda e: e.activation(out=s_[:, 0:1], in_=pA[kk][:, 128:129], func=AF.Abs), reads=[r_pA[kk]], writes=[rs_])
                                S.op("dve", lambda e: e.tensor_tensor(out=s_[:, 1:2], in0=s_[:, 0:1], in1=floorT[:, col:col + 1], op=ALU.max), reads=[rs_, r_floorT], writes=[rs_])
                                S.op("dve", lambda e: e.reciprocal(out=s_[:, 2:3], in_=s_[:, 1:2]), reads=[rs_], writes=[rs_])
                                S.op("act", lambda e: e.activation(out=junkm[:], in_=pA[kk][:, 0:128], func=AF.Square, scale=s_[:, 2:3], accum_out=s_[:, 3:4]),
                                     reads=[r_pA[kk], rs_], writes=[rs_])
                                S.op("act", lambda e: e.activation(out=s_[:, 4:5], in_=s_[:, 3:4], func=AF.Ln, bias=EPS, scale=1.0 / 128.0), reads=[rs_], writes=[rs_])
                                S.op("act", lambda e: e.activation(out=s_[:, 4:5], in_=s_[:, 4:5], func=AF.Exp, scale=-0.5), reads=[rs_], writes=[rs_])
                                S.op("dve", lambda e: e.tensor_tensor(out=s_[:, 5:6], in0=s_[:, 4:5], in1=s_[:, 2:3], op=ALU.mult), reads=[rs_], writes=[rs_])
                                S.op("dve", lambda e: e.scalar_tensor_tensor(out=ytl[kk][:], in0=pA[kk][:, 0:128], scalar=s_[:, 5:6], in1=SG[:, j, :], op0=ALU.mult, op1=ALU.mult),
                                     reads=[r_pA[kk], rs_, r_SG], writes=[r_ytl[kk]])
                                a = j % 4
                                S.op("pe", lambda e: e.transpose(out=pTb[0][:, a * 128:(a + 1) * 128], in_=ytl[kk][:], identity=ident_b[:]), reads=[r_ytl[kk], r_idb], writes=[r_pTb[0]])
                                if a == 3:
                                    S.op("act", lambda e: e.activation(out=ym[:, (j - 3) * 128:(j + 1) * 128], in_=pTb[0], func=AF.Copy), reads=[r_pTb[0]], pwrites=[rym])
                            S.dma("sp", ybuf[512 + h * 128:512 + (h + 1) * 128, :], ym[:], reads=[rym], writes=[r_ybuf[4 + h]])
                        S.barrier()
        S.barrier()

        c2.close()
        if stop_after is None:
            TT = 256
            NTT = SEQ // TT
            NSUB = TT // 128
            NR = 6
            with ExitStack() as ph:
                wout = sbt(ph, "wout", [128, 8, D], BF16); r_wout = Res()
                wup = sbt(ph, "wup", [128, 8, 2 * DFF], BF16); r_wup = [Res() for _ in range(8)]
                wdn = sbt(ph, "wdn", [128, NCH, D], BF16); r_wdn = Res()
                ytb = sbt(ph, "f_y", [128, 8, TT], BF16); r_ytb = Res()
                h1 = [sbt(ph, f"f_h{i}", [128, NSUB, D], F32) for i in range(2)]; r_h1 = [Res(), Res()]
                hb = sbt(ph, "f_hb", [128, NSUB, D], BF16); r_hb = Res()
                hnT = sbt(ph, "f_hnT", [128, 8, TT], BF16); r_hnT = Res()
                Rb = [sbt(ph, f"f_R{i}", [128, 2 + TT], F32) for i in range(NR)]; r_Rb = [Res() for _ in range(NR)]
                cvb = [sbt(ph, f"f_cv{i}", [128, TT], F32) for i in range(NR)]; r_cvb = [Res() for _ in range(NR)]
                gT = sbt(ph, "f_gT", [128, NCH, TT], BF16); r_gT = Res()
                halo = sbt(ph, "f_halo", [128, 2 * NCH, 2], F32); r_halo = [Res() for _ in range(2 * NCH)]
                fss = [sbt(ph, f"f_ss{i}", [128, 8], F32) for i in range(2)]; r_fss = [Res(), Res()]
                fjunk = sbt(ph, "f_junk", [128, D], BF16)
                pH = [pst(ph, f"fH{i}", [128, 512], F32) for i in range(2)]; r_pH = [PRes(), PRes()]
                pU = [pst(ph, f"fU{i}", [128, 512], F32) for i in range(4)]; r_pU = [PRes() for _ in range(4)]
                pT3_ = [pst(ph, f"fT{i}", [128, 1024], BF16) for i in range(2)]; r_pT3 = [PRes(), PRes()]
                pT3 = [t[:, 0:512] for t in pT3_]
                for c in range(8):
                    S.dma("pool", wout[:, c, :], w_out[c * 128:(c + 1) * 128, :], pwrites=[r_wout])
                for c in range(8):
                    S.dma("pool", wup[:, c, :], w_ffn_up[c * 128:(c + 1) * 128, :], writes=[r_wup[c]])
                for m in range(NCH):
                    S.dma("pool", wdn[:, m, :], w_ffn_down[m * 128:(m + 1) * 128, :], pwrites=[r_wdn])
                S.op("dve", lambda e: e.memset(halo[:], 0.0), writes=r_halo)
                cnt = {"H": 0, "U": 0, "R": 0, "T": 0}

                def load_tile(tt):
                    S.dma("sp", ytb[:], ybuf[:, tt * TT:(tt + 1) * TT].rearrange("(c p) t -> p c t", p=128), reads=r_ybuf, writes=[r_ytb])
                    S.dma("sp", h1[tt % 2][:], x[tt * TT:(tt + 1) * TT, :].rearrange("(s p) d -> p s d", p=128), writes=[r_h1[tt % 2]])

                def outproj(tt):
                    hh_ = h1[tt % 2]; rhh = r_h1[tt % 2]; fs = fss[tt % 2]; rfs = r_fss[tt % 2]
                    for s in range(NSUB):
                        for hf in range(2):
                            k = cnt["H"] % 2; cnt["H"] += 1
                            cs = slice(hf * 512, (hf + 1) * 512)
                            for c in range(8):
                                S.op("pe", lambda e: e.matmul(pH[k][:], lhsT=ytb[:, c, s * 128:(s + 1) * 128], rhs=wout[:, c, cs], start=(c == 0), stop=(c == 7)),
                                     reads=[r_ytb, r_wout], writes=[r_pH[k]])
                            S.op("dve", lambda e: e.tensor_tensor(out=hh_[:, s, cs], in0=hh_[:, s, cs], in1=pH[k][:], op=ALU.add), reads=[rhh, r_pH[k]], pwrites=[rhh])
                        S.op("act", lambda e: e.activation(out=fjunk[:], in_=hh_[:, s, :], func=AF.Square, scale=1.0 / 32.0, accum_out=fs[:, s:s + 1]),
                             reads=[rhh], pwrites=[rfs])
                    S.op("act", lambda e: e.activation(out=fs[:, 2:2 + NSUB], in_=fs[:, 0:NSUB], func=AF.Ln, bias=EPS, scale=1.0), reads=[rfs], pwrites=[rfs])
                    S.op("act", lambda e: e.activation(out=fs[:, 2:2 + NSUB], in_=fs[:, 2:2 + NSUB], func=AF.Exp, scale=-0.5), reads=[rfs], pwrites=[rfs])
                    for s in range(NSUB):
                        S.op("dve", lambda e: e.tensor_scalar(out=hb[:, s, :], in0=hh_[:, s, :], scalar1=fs[:, 2 + s:3 + s], scalar2=None, op0=ALU.mult),
                             reads=[rhh, rfs], pwrites=[r_hb])

                def transposes(tt):
                    for c0 in range(0, 8, 2):
                        k = cnt["T"] % 2; cnt["T"] += 1
                        for a in range(2):
                            for s in range(NSUB):
                                S.op("pe", lambda e: e.transpose(out=pT3[k][:, a * 256 + s * 128:a * 256 + (s + 1) * 128], in_=hb[:, s, (c0 + a) * 128:(c0 + a + 1) * 128],
                                                                 identity=ident_b[:]), reads=[r_hb, r_idb], writes=[r_pT3[k]])
                        S.op("dve", lambda e: e.tensor_scalar(out=hnT[:, c0, :], in0=pT3[k][:, 0:256], scalar1=fnw(c0), scalar2=None, op0=ALU.mult),
                             reads=[r_pT3[k], r_colA], pwrites=[r_hnT])
                        S.op("act", lambda e: e.activation(out=hnT[:, c0 + 1, :], in_=pT3[k][:, 256:512], func=AF.Copy, scale=fnw(c0 + 1)),
                             reads=[r_pT3[k], r_colA], pwrites=[r_hnT])

                def up(tt):
                    for m in range(NCH):
                        kU = cnt["U"] % 4; cnt["U"] += 1
                        for gv in range(2):
                            col0 = gv * DFF + m * 128
                            psl = pU[kU][:, gv * 256:(gv + 1) * 256]
                            for c in range(8):
                                S.op("pe", lambda e: e.matmul(psl, lhsT=wup[:, c, col0:col0 + 128], rhs=hnT[:, c, :], start=(c == 0), stop=(c == 7)),
                                     reads=[r_wup[c], r_hnT], writes=[r_pU[kU]])
                        bufs = []
                        for gv in range(2):
                            ch = gv * NCH + m
                            psl = pU[kU][:, gv * 256:(gv + 1) * 256]
                            kR = cnt["R"] % NR; cnt["R"] += 1
                            R_ = Rb[kR]; rR = r_Rb[kR]; cv_ = cvb[kR]; rcv = r_cvb[kR]
                            S.op("act", lambda e: e.activation(out=R_[:, 2:2 + TT], in_=psl, func=AF.Copy), reads=[r_pU[kU]], writes=[rR])
                            S.op("act", lambda e: e.activation(out=cv_[:], in_=psl, func=AF.Identity, scale=fcw(2, ch), bias=fcb(ch)), reads=[r_pU[kU], r_colB, r_colC], writes=[rcv])
                            S.op("pool", lambda e: e.tensor_copy(out=R_[:, 0:2], in_=halo[:, ch, :]), reads=[r_halo[ch]], pwrites=[rR])
                            S.op("pool", lambda e: e.tensor_copy(out=halo[:, ch, :], in_=R_[:, TT:TT + 2]), reads=[rR], writes=[r_halo[ch]])
                            S.op("dve", lambda e: e.scalar_tensor_tensor(out=cv_[:], in0=R_[:, 1:1 + TT], scalar=fcw(1, ch), in1=cv_[:], op0=ALU.mult, op1=ALU.add),
                                 reads=[rR, rcv, r_colB], writes=[rcv])
                            S.op("dve", lambda e: e.scalar_tensor_tensor(out=cv_[:], in0=R_[:, 0:TT], scalar=fcw(0, ch), in1=cv_[:], op0=ALU.mult, op1=ALU.add),
                                 reads=[rR, rcv, r_colB], writes=[rcv])
                            bufs.append((cv_, rcv))
                        (cg, rcg), (cvv, rcvv) = bufs
                        S.op("act", lambda e: e.activation(out=cg[:], in_=cg[:], func=AF.Silu), reads=[rcg], writes=[rcg])
                        S.op("pool", lambda e: e.tensor_tensor(out=gT[:, m, :], in0=cg[:], in1=cvv[:], op=ALU.mult), reads=[rcg, rcvv], pwrites=[r_gT])

                def down(tt, s):
                    hh_ = h1[tt % 2]; rhh = r_h1[tt % 2]; fs = fss[tt % 2]; rfs = r_fss[tt % 2]
                    for hf in range(2):
                        k = cnt["H"] % 2; cnt["H"] += 1
                        cs = slice(hf * 512, (hf + 1) * 512)
                        for m in range(NCH):
                            S.op("pe", lambda e: e.matmul(pH[k][:], lhsT=gT[:, m, s * 128:(s + 1) * 128], rhs=wdn[:, m, cs], start=(m == 0), stop=(m == NCH - 1)),
                                 reads=[r_gT, r_wdn], writes=[r_pH[k]])
                        S.op("dve", lambda e: e.tensor_tensor(out=hh_[:, s, cs], in0=hh_[:, s, cs], in1=pH[k][:], op=ALU.add), reads=[rhh, r_pH[k]], pwrites=[rhh])
                    S.op("act", lambda e: e.activation(out=fjunk[:], in_=hh_[:, s, :], func=AF.Square, scale=1.0 / 32.0, accum_out=fs[:, 4 + s:5 + s]),
                         reads=[rhh], pwrites=[rfs])

                def final(tt):
                    hh_ = h1[tt % 2]; rhh = r_h1[tt % 2]; fs = fss[tt % 2]; rfs = r_fss[tt % 2]
                    S.op("act", lambda e: e.activation(out=fs[:, 6:6 + NSUB], in_=fs[:, 4:4 + NSUB], func=AF.Ln, bias=EPS, scale=1.0), reads=[rfs], pwrites=[rfs])
                    S.op("act", lambda e: e.activation(out=fs[:, 6:6 + NSUB], in_=fs[:, 6:6 + NSUB], func=AF.Exp, scale=-0.5), reads=[rfs], pwrites=[rfs])
                    for s in range(NSUB):
                        S.op("dve", lambda e: e.scalar_tensor_tensor(out=hh_[:, s, :], in0=hh_[:, s, :], scalar=fs[:, 6 + s:7 + s], in1=FW[:], op0=ALU.mult, op1=ALU.mult),
                             reads=[rhh, rfs, r_FW], pwrites=[rhh])
                    S.dma("sp", out[tt * TT:(tt + 1) * TT, :].rearrange("(s p) d -> p s d", p=128), hh_[:], reads=[rhh])

                load_tile(0)
                outproj(0)
                transposes(0)
                for tt in range(NTT):
                    up(tt)
                    if tt + 1 < NTT:
                        load_tile(tt + 1)
                        outproj(tt + 1)
                    down(tt, 0)
                    if tt + 1 < NTT:
                        transposes(tt + 1)
                    down(tt, 1)
                    final(tt)
                S.barrier()
        S.barrier()
        nc._ninst = dict(S.ninst)
    return nc, dbg


_NC_CACHE = {}


def _squeeze(a):
    return np.ascontiguousarray(np.asarray(a, dtype=np.float32))


def kernel(x, w_in, mlstm_conv_w, mlstm_conv_b, mlstm_i_bias, mlstm_f_bias, att_out_norm_w, mlstm_out_norm_w,
           w_out, mixer_norm_w, ffn_norm_w, w_ffn_up, ffn_conv_w, ffn_conv_b, w_ffn_down, final_norm_w):
    n = 8
    if "nc" not in _NC_CACHE:
        _NC_CACHE["nc"] = build_nc()[0]
    nc = _NC_CACHE["nc"]
    shared = {
        "w_in": _squeeze(w_in[0]), "mlstm_conv_w": _squeeze(mlstm_conv_w[0]), "mlstm_conv_b": _squeeze(mlstm_conv_b[0]),
        "mlstm_i_bias": _squeeze(mlstm_i_bias[0]), "mlstm_f_bias": _squeeze(mlstm_f_bias[0]),
        "att_out_norm_w": _squeeze(att_out_norm_w[0]), "mlstm_out_norm_w": _squeeze(mlstm_out_norm_w[0]),
        "w_out": _squeeze(w_out[0]), "mixer_norm_w": _squeeze(mixer_norm_w[0]), "ffn_norm_w": _squeeze(ffn_norm_w[0]),
        "w_ffn_up": _squeeze(w_ffn_up[0]), "ffn_conv_w": _squeeze(ffn_conv_w[0]), "ffn_conv_b": _squeeze(ffn_conv_b[0]),
        "w_ffn_down": _squeeze(w_ffn_down[0]), "final_norm_w": _squeeze(final_norm_w),
    }
    xs = np.asarray(x, dtype=np.float32)
    in_maps = [dict(shared, x=np.ascontiguousarray(xs[i])) for i in range(n)]
    res = run_bass_kernel_spmd(nc, in_maps, core_ids=list(range(n)))
    return np.stack([np.asarray(r["out"], dtype=np.float32) for r in res.results], axis=0)
```

```python
import numpy as np
from collections import defaultdict
from contextlib import ExitStack
import concourse.bass as bass
import concourse.mybir as mybir
from concourse.bass_utils import run_bass_kernel_spmd

F32 = mybir.dt.float32
BF16 = mybir.dt.bfloat16
AF = mybir.ActivationFunctionType
ALU = mybir.AluOpType
AX = mybir.AxisListType

SEQ = 4096
D = 1024
PROJ = 3592
DFF = 2816
NCH = 22
EPS = 1e-6
GROUPS = (1, 4, 16)


class Res:
    __slots__ = ("ws", "r", "excl")

    def __init__(self, excl=False):
        self.ws = {}
        self.r = {}
        self.excl = excl


def PRes():
    return Res(excl=True)


class Sched:
    CE = ("pe", "act", "dve", "pool")

    def __init__(self, nc, es, nq=8):
        self.nc = nc
        self.eng = {"pe": nc.tensor, "act": nc.scalar, "dve": nc.vector, "pool": nc.gpsimd, "sp": nc.sync}
        self.sems = {}
        self.count = {}
        for e in self.CE:
            self.sems[e] = es.enter_context(nc.semaphore("s_" + e))
            self.count[e] = 0
        self.nq = nq
        self.rr = {}
        for q in ("sp", "act", "pool"):
            self.rr[q] = 0
            for i in range(nq):
                n = f"d_{q}{i}"
                self.sems[n] = es.enter_context(nc.semaphore(n))
                self.count[n] = 0
        self.seen = {e: defaultdict(int) for e in self.eng}
        self.ninst = defaultdict(int)
        self.dead = False

    def need(self, E, tok):
        if tok is None:
            return
        s, v = tok
        if s.startswith("d_"):
            v = self.count[s]
        elif s == E and E == "pe":
            return
        if self.seen[E][s] < v:
            self.eng[E].wait_ge(self.sems[s], v)
            self.seen[E][s] = v
            self.ninst[E] += 1

    def _pre(self, E, reads, writes, pwrites):
        for r in reads:
            for t in r.ws.values():
                self.need(E, t)
            if r.excl:
                for e2, t in r.r.items():
                    if e2 != E:
                        self.need(E, t)
        for w in writes:
            for t in w.ws.values():
                self.need(E, t)
            for e2, t in w.r.items():
                if e2 != E:
                    self.need(E, t)
        for w in pwrites:
            for e2, t in w.r.items():
                if e2 != E:
                    self.need(E, t)
            if w.excl:
                for e2, t in w.ws.items():
                    if e2 != E:
                        self.need(E, t)

    def _post(self, key, tok, reads, writes, pwrites):
        for r in reads:
            r.r[key] = tok
        for w in writes:
            w.ws = {key: tok}
            w.r = {}
        for w in pwrites:
            w.ws[key] = tok

    def op(self, E, fn, reads=(), writes=(), pwrites=()):
        if self.dead:
            return None
        self._pre(E, reads, writes, pwrites)
        inst = fn(self.eng[E])
        self.count[E] += 1
        inst.then_inc(self.sems[E], 1)
        self.ninst[E] += 1
        self._post(E, (E, self.count[E]), reads, writes, pwrites)
        return inst

    def dma(self, q, out, in_, reads=(), writes=(), pwrites=(), **kw):
        if self.dead:
            return None
        self._pre(q, reads, writes, pwrites)
        inst = self.eng[q].dma_start(out=out, in_=in_, **kw)
        n = f"d_{q}{self.rr[q] % self.nq}"
        self.rr[q] += 1
        self.count[n] += 16
        inst.then_inc(self.sems[n], 16)
        self.ninst[q] += 1
        self._post(n, (n, self.count[n]), reads, writes, pwrites)
        return inst

    def barrier(self):
        for E in self.eng:
            for s in self.sems:
                if self.count[s] > 0 and s != E:
                    self.need(E, (s, self.count[s]))


def build_nc(debug=False, stop_after=None):
    nc = bass.Bass("TRN2", target_bir_lowering=False)
    din = lambda n, s: nc.dram_tensor(n, s, F32, kind="ExternalInput").ap()
    x = din("x", [SEQ, D])
    w_in = din("w_in", [D, PROJ])
    mlstm_conv_w = din("mlstm_conv_w", [4, 1024])
    mlstm_conv_b = din("mlstm_conv_b", [1024])
    mlstm_i_bias = din("mlstm_i_bias", [4])
    mlstm_f_bias = din("mlstm_f_bias", [4])
    att_out_norm_w = din("att_out_norm_w", [512])
    mlstm_out_norm_w = din("mlstm_out_norm_w", [512])
    w_out = din("w_out", [D, D])
    mixer_norm_w = din("mixer_norm_w", [D])
    ffn_norm_w = din("ffn_norm_w", [D])
    w_ffn_up = din("w_ffn_up", [D, 2 * DFF])
    ffn_conv_w = din("ffn_conv_w", [3, 2 * DFF])
    ffn_conv_b = din("ffn_conv_b", [2 * DFF])
    w_ffn_down = din("w_ffn_down", [DFF, D])
    final_norm_w = din("final_norm_w", [D])
    out = nc.dram_tensor("out", [SEQ, D], F32, kind="ExternalOutput").ap()
    ybuf = nc.dram_tensor("ybuf", [D, SEQ], BF16, kind=("ExternalOutput" if debug else "Internal")).ap()
    r_ybuf = [Res() for _ in range(8)]
    dbg = {}

    def dout(name, shape, dt=F32):
        dbg[name] = nc.dram_tensor(name, shape, dt, kind="ExternalOutput").ap()
        return dbg[name]

    with ExitStack() as es:
        S = Sched(nc, es)

        def stop(tag):
            if stop_after == tag:
                S.barrier()
                S.dead = True
        sbt = lambda st, n, s, d: st.enter_context(nc.sbuf_tensor(n, s, d))
        pst = lambda st, n, s, d: st.enter_context(nc.psum_tensor(n, s, d))

        ident_b = sbt(es, "ident_b", [128, 128], BF16); r_idb = Res()
        colA = sbt(es, "colA", [128, 64], F32); r_colA = Res()
        colB = sbt(es, "colB", [128, 88], F32); r_colB = Res()
        colC = sbt(es, "colC", [128, 88], F32); r_colC = Res()
        FW = sbt(es, "FW", [128, D], F32); r_FW = Res()
        wdn = sbt(es, "wdn", [128, NCH, D], BF16); r_wdn = Res()
        wdn_f32 = wdn[:].rearrange("p m d -> p (m d)").bitcast(F32)
        c2 = ExitStack()
        ident_f = sbt(c2, "ident_f", [128, 128], F32); r_idf = Res()
        ones_f = sbt(c2, "ones_f", [128, 128], F32); r_ones = Res()
        zero_f = sbt(c2, "zero_f", [128, 256], F32); r_zero = Res()
        sel_b = sbt(c2, "sel_b", [128, 2, 128], BF16); r_sel = Res()
        maskf = sbt(c2, "maskf", [128, 256], F32); r_maskf = Res()
        maskb = sbt(c2, "maskb", [128, 256], BF16); r_maskb = Res()
        mask01 = sbt(c2, "mask01", [128, 128], F32); r_m01 = Res()
        MNW = sbt(c2, "MNW", [128, 512], F32); r_MNW = Res()
        gb = sbt(c2, "gb", [4, 2], F32); r_gb = Res()
        S.op("pool", lambda e: e.memset(ones_f[:], 1.0), writes=[r_ones])
        S.op("pool", lambda e: e.memset(zero_f[:], 0.0), writes=[r_zero])
        S.op("pool", lambda e: e.affine_select(out=ident_f[:], in_=ones_f[:], pattern=[[-1, 128]], compare_op=ALU.is_equal,
                                               fill=0.0, base=0, channel_multiplier=1), reads=[r_ones], writes=[r_idf])
        S.op("dve", lambda e: e.tensor_copy(out=ident_b[:], in_=ident_f[:]), reads=[r_idf], writes=[r_idb])
        S.op("dve", lambda e: e.memset(sel_b[:], 0.0), writes=[r_sel])
        S.op("dve", lambda e: e.memset(sel_b[0:64, 0, :], 1.0), writes=[r_sel])
        S.op("dve", lambda e: e.memset(sel_b[64:128, 1, :], 1.0), writes=[r_sel])
        S.op("pool", lambda e: e.affine_select(out=maskf[:, 0:128], in_=zero_f[:, 0:128], pattern=[[1, 128]], compare_op=ALU.is_ge,
                                               fill=-30000.0, base=0, channel_multiplier=-1), reads=[r_zero], writes=[r_maskf])
        S.op("pool", lambda e: e.affine_select(out=maskf[:, 128:256], in_=zero_f[:, 128:256], pattern=[[-1, 128]], compare_op=ALU.is_ge,
                                               fill=-30000.0, base=0, channel_multiplier=1), reads=[r_zero], writes=[r_maskf])
        S.op("dve", lambda e: e.tensor_copy(out=maskb[:], in_=maskf[:]), reads=[r_maskf], writes=[r_maskb])
        S.op("pool", lambda e: e.affine_select(out=mask01[:], in_=ones_f[:], pattern=[[1, 128]], compare_op=ALU.is_ge,
                                               fill=0.0, base=0, channel_multiplier=-1), reads=[r_ones], writes=[r_m01])

        with ExitStack() as p0:
            rowA = sbt(p0, "rowA", [64, 128], F32); r_rowA = Res()
            rowB = sbt(p0, "rowB", [88, 128], F32); r_rowB = Res()
            rowC = sbt(p0, "rowC", [88, 128], F32); r_rowC = Res()
            pcol = pst(p0, "pcol", [128, 512], F32); r_pcol = PRes()
            S.op("dve", lambda e: e.memset(rowA[:], 0.0), writes=[r_rowA])
            S.dma("sp", rowA[0:8, :], mixer_norm_w.rearrange("(c p) -> c p", p=128), writes=[r_rowA])
            S.dma("sp", rowA[8:16, :], ffn_norm_w.rearrange("(c p) -> c p", p=128), writes=[r_rowA])
            S.dma("sp", rowA[16:48, :], mlstm_conv_w.rearrange("j (c p) -> (j c) p", p=128), writes=[r_rowA])
            S.dma("sp", rowA[48:56, :], mlstm_conv_b.rearrange("(c p) -> c p", p=128), writes=[r_rowA])
            S.dma("sp", rowA[56:64, 0:64], att_out_norm_w.rearrange("(h p) -> h p", p=64), writes=[r_rowA])
            S.dma("sp", rowB[:, :], ffn_conv_w[0:2, :].rearrange("j (c p) -> (j c) p", p=128), writes=[r_rowB])
            S.dma("sp", rowC[0:44, :], ffn_conv_w[2:3, :].rearrange("j (c p) -> (j c) p", p=128), writes=[r_rowC])
            S.dma("sp", rowC[44:88, :], ffn_conv_b.rearrange("(c p) -> c p", p=128), writes=[r_rowC])
            S.dma("sp", FW[:], final_norm_w.partition_broadcast(128), writes=[r_FW])
            S.dma("sp", MNW[:], mlstm_out_norm_w.partition_broadcast(128), writes=[r_MNW])
            with nc.allow_non_contiguous_dma(reason="tiny gate bias"):
                S.dma("sp", gb[:, 0:1], mlstm_i_bias.rearrange("(p o) -> p o", o=1), writes=[r_gb])
                S.dma("sp", gb[:, 1:2], mlstm_f_bias.rearrange("(p o) -> p o", o=1), writes=[r_gb])
            S.op("pe", lambda e: e.transpose(out=pcol[:, 0:64], in_=rowA[:], identity=ident_f[0:64, 0:64]), reads=[r_rowA, r_idf], writes=[r_pcol])
            S.op("pe", lambda e: e.transpose(out=pcol[:, 64:152], in_=rowB[:], identity=ident_f[0:88, 0:88]), reads=[r_rowB, r_idf], writes=[r_pcol])
            S.op("pe", lambda e: e.transpose(out=pcol[:, 152:240], in_=rowC[:], identity=ident_f[0:88, 0:88]), reads=[r_rowC, r_idf], writes=[r_pcol])
            S.op("dve", lambda e: e.tensor_copy(out=colA[:], in_=pcol[:, 0:64]), reads=[r_pcol], writes=[r_colA])
            S.op("dve", lambda e: e.tensor_copy(out=colB[:], in_=pcol[:, 64:152]), reads=[r_pcol], writes=[r_colB])
            S.op("dve", lambda e: e.tensor_copy(out=colC[:], in_=pcol[:, 152:240]), reads=[r_pcol], writes=[r_colC])
            S.barrier()
        mnw = lambda c: colA[:, c:c + 1]
        fnw = lambda c: colA[:, 8 + c:9 + c]
        mcw = lambda j, c: colA[:, 16 + j * 8 + c:17 + j * 8 + c]
        mcb = lambda c: colA[:, 48 + c:49 + c]
        aow = lambda h: colA[0:64, 56 + h:57 + h]

        def fcw(j, ch):
            if j < 2:
                return colB[:, j * 44 + ch:j * 44 + ch + 1]
            return colC[:, ch:ch + 1]
        fcb = lambda ch: colC[:, 44 + ch:45 + ch]

        if debug:
            o_colA = dout("o_colA", [128, 64]); o_colB = dout("o_colB", [128, 88]); o_colC = dout("o_colC", [128, 88])
            S.dma("sp", o_colA[:, :], colA[:], reads=[r_colA]); S.dma("sp", o_colB[:, :], colB[:], reads=[r_colB]); S.dma("sp", o_colC[:, :], colC[:], reads=[r_colC])
            S.barrier()
        with ExitStack() as p12:
          if stop_after != "p0":
                xnT = sbt(p12, "xnT", [128, 8, SEQ], BF16)
                r_xnT = [Res() for _ in range(8)]

                with ExitStack() as ph:
                    xt = [sbt(ph, f"p1x{i}", [128, 4, D], F32) for i in range(2)]; r_xt = [Res(), Res()]
                    xb = [sbt(ph, f"p1xb{i}", [128, 4, D], BF16) for i in range(2)]; r_xb = [Res(), Res()]
                    junk = sbt(ph, "p1junk", [128, D], BF16); r_junk = Res()
                    ss = [sbt(ph, f"p1ss{i}", [128, 4], F32) for i in range(2)]; r_ss = [Res(), Res()]
                    rs = [sbt(ph, f"p1rs{i}", [128, 4], F32) for i in range(2)]; r_rs = [Res(), Res()]
                    pT_ = [pst(ph, f"p1pT{i}", [128, 1024], BF16) for i in range(4)]; r_pT = [PRes() for _ in range(4)]
                    pT = [t[:, 0:512] for t in pT_]
                    for tt in range(8):
                        b = tt % 2
                        S.dma("sp", xt[b][:], x[tt * 512:(tt + 1) * 512, :].rearrange("(s p) d -> p s d", p=128), writes=[r_xt[b]])
                        for s in range(4):
                            S.op("act", lambda e: e.activation(out=junk[:], in_=xt[b][:, s, :], func=AF.Square, scale=1.0 / 32.0,
                                                               accum_out=ss[b][:, s:s + 1]), reads=[r_xt[b]], pwrites=[r_ss[b]])
                        S.op("act", lambda e: e.activation(out=rs[b][:], in_=ss[b][:], func=AF.Ln, bias=EPS, scale=1.0), reads=[r_ss[b]], writes=[r_rs[b]])
                        S.op("act", lambda e: e.activation(out=rs[b][:], in_=rs[b][:], func=AF.Exp, scale=-0.5), reads=[r_rs[b]], writes=[r_rs[b]])
                        for s in range(4):
                            eng = "dve"
                            S.op(eng, lambda e: e.tensor_scalar(out=xb[b][:, s, :], in0=xt[b][:, s, :], scalar1=rs[b][:, s:s + 1], scalar2=None,
                                                                op0=ALU.mult), reads=[r_xt[b], r_rs[b]], pwrites=[r_xb[b]])
                        for c in range(8):
                            k = (tt * 8 + c) % 4
                            for s in range(4):
                                S.op("pe", lambda e: e.transpose(out=pT[k][:, s * 128:(s + 1) * 128], in_=xb[b][:, s, c * 128:(c + 1) * 128],
                                                                 identity=ident_b[:]), reads=[r_xb[b], r_idb], writes=[r_pT[k]])
                            if c % 2 == 0:
                                S.op("dve", lambda e: e.tensor_scalar(out=xnT[:, c, tt * 512:(tt + 1) * 512], in0=pT[k], scalar1=mnw(c), scalar2=None,
                                                                      op0=ALU.mult), reads=[r_pT[k], r_colA], pwrites=[r_xnT[tt]])
                            else:
                                S.op("act", lambda e: e.activation(out=xnT[:, c, tt * 512:(tt + 1) * 512], in_=pT[k], func=AF.Copy, scale=mnw(c)),
                                     reads=[r_pT[k], r_colA], pwrites=[r_xnT[tt]])
                    S.barrier()
                if debug:
                    o_xnT = dout("o_xnT", [D, SEQ], BF16)
                    for c in range(8):
                        S.dma("sp", o_xnT[c * 128:(c + 1) * 128, :], xnT[:, c, :], reads=r_xnT)
                    S.barrier()

                if stop_after not in ("p1", "mlg", "mlh0", "mlh1", "mlh2", "mlh"):
                    with ExitStack() as ph:
                        wqkv = [sbt(ph, f"wqkv{i}", [128, 8, 3, 128], BF16) for i in range(2)]; r_wqkv = [Res(), Res()]
                        QT = sbt(ph, "QT", [128, SEQ], BF16); r_QT = Res()
                        KT = sbt(ph, "KT", [128, SEQ], BF16); r_KT = Res()
                        VT = sbt(ph, "VT", [128, SEQ], BF16); r_VT = Res()
                        sq = sbt(ph, "sq", [128, SEQ], BF16); r_sq = Res()
                        mx = sbt(ph, "mx", [128, 4, 8], F32); r_mx = Res()
                        st = sbt(ph, "st", [128, 4], F32); r_st = Res()
                        nbias = sbt(ph, "nbias", [128, 2], F32); r_nb = Res()
                        mask2 = sbt(ph, "mask2", [128, 2, 256], BF16); r_mask2 = Res()
                        Vaug = sbt(ph, "Vaug", [128, 32, 2, 128], BF16); r_Vaug = Res()
                        acc = [wdn_f32[:, 0:SEQ], wdn_f32[:, SEQ:2 * SEQ]]; r_acc = [Res(), Res()]
                        NPT = 4
                        PT = [sbt(ph, f"PT{i}", [128, 512], BF16) for i in range(NPT)]; r_PT = [Res() for _ in range(NPT)]
                        e_n2 = sbt(ph, "e_n2", [64, 1024], BF16); r_en2 = Res()
                        e_d2 = wdn_f32[0:64, 2 * SEQ:2 * SEQ + 1024]; r_ed2 = Res()
                        e_rs = wdn_f32[0:64, 2 * SEQ + 1024:2 * SEQ + 2048]; r_ers = Res()
                        ones_b = sbt(ph, "ones_b", [64, 64], BF16); r_onesb = Res()
                        yT = [sbt(ph, f"yTa{i}", [64, SEQ], BF16) for i in range(2)]; r_yT = [Res(), Res()]
                        pJ = [pst(ph, f"aJ{i}", [128, 512], F32) for i in range(2)]; r_pJ = [PRes(), PRes()]
                        pS = [pst(ph, f"aS{i}", [128, 512], F32) for i in range(2)]; r_pS = [PRes(), PRes()]
                        pO = [pst(ph, f"aO{i}", [128, 512], F32) for i in range(2)]; r_pO = [PRes(), PRes()]
                        pV_ = [pst(ph, f"aV{i}", [128, 1024], BF16) for i in range(2)]; r_pV = [PRes(), PRes()]
                        pV = [t[:, 0:512] for t in pV_]
                        S.op("pool", lambda e: e.memset(Vaug[:, :, :, 64:128], 1.0), writes=[r_Vaug])
                        S.op("dve", lambda e: e.tensor_copy(out=ones_b[:], in_=ones_f[0:64, 0:64]), reads=[r_ones], writes=[r_onesb])
                        for a in range(2):
                            S.op("dve", lambda e: e.tensor_scalar(out=mask2[:, a, :], in0=maskf[:], scalar1=0.0, scalar2=None, op0=ALU.is_equal),
                                 reads=[r_maskf], pwrites=[r_mask2])
                        nj = 0
                        cnt_a = {"V": 0, "S": 0, "O": 0}
                        for pr in range(4):
                            wb = wqkv[pr % 2]; rwb = r_wqkv[pr % 2]
                            for j in range(3):
                                S.dma("pool", wb[:, :, j, :], w_in[:, j * 512 + pr * 128:j * 512 + (pr + 1) * 128].rearrange("(c p) m -> p c m", p=128),
                                      pwrites=[rwb])
                            for j, (dst, rdst) in enumerate(((QT, r_QT), (KT, r_KT), (VT, r_VT))):
                                for tt in range(8):
                                    k = nj % 2; nj += 1
                                    for c in range(8):
                                        S.op("pe", lambda e: e.matmul(pJ[k][:], lhsT=wb[:, c, j, :], rhs=xnT[:, c, tt * 512:(tt + 1) * 512],
                                                                      start=(c == 0), stop=(c == 7)), reads=[rwb, r_xnT[tt]], writes=[r_pJ[k]])
                                    if nj % 2 == 0:
                                        S.op("dve", lambda e: e.tensor_copy(out=dst[:, tt * 512:(tt + 1) * 512], in_=pJ[k][:]), reads=[r_pJ[k]], pwrites=[rdst])
                                    else:
                                        S.op("act", lambda e: e.activation(out=dst[:, tt * 512:(tt + 1) * 512], in_=pJ[k][:], func=AF.Copy),
                                             reads=[r_pJ[k]], pwrites=[rdst])
                            for qi, (src, rsrc) in enumerate(((QT, r_QT), (KT, r_KT))):
                                S.op("dve", lambda e: e.tensor_tensor(out=sq[:], in0=src[:], in1=src[:], op=ALU.mult), reads=[rsrc], writes=[r_sq])
                                for hh in range(2):
                                    for tt in range(8):
                                        k = nj % 2; nj += 1
                                        S.op("pe", lambda e: e.matmul(pJ[k][:], lhsT=sel_b[:, hh, :], rhs=sq[:, tt * 512:(tt + 1) * 512], start=True, stop=True),
                                             reads=[r_sel, r_sq], writes=[r_pJ[k]])
                                        S.op("dve", lambda e: e.tensor_reduce(out=mx[:, qi * 2 + hh, tt:tt + 1], in_=pJ[k][:], axis=AX.X, op=ALU.max),
                                             reads=[r_pJ[k]], pwrites=[r_mx])
                            S.op("dve", lambda e: e.tensor_reduce(out=st[:], in_=mx[:], axis=AX.X, op=ALU.max), reads=[r_mx], writes=[r_st])
                            S.op("dve", lambda e: e.tensor_tensor(out=nbias[:], in0=st[:, 0:2], in1=st[:, 2:4], op=ALU.mult), reads=[r_st], writes=[r_nb])
                            S.op("act", lambda e: e.activation(out=nbias[:], in_=nbias[:], func=AF.Ln), reads=[r_nb], writes=[r_nb])
                            S.op("act", lambda e: e.activation(out=nbias[:], in_=nbias[:], func=AF.Exp, scale=0.5), reads=[r_nb], writes=[r_nb])
                            S.op("dve", lambda e: e.tensor_scalar(out=nbias[:], in0=nbias[:], scalar1=-0.125 * 1.02, scalar2=None, op0=ALU.mult),
                                 reads=[r_nb], writes=[r_nb])
                            for gi, d in enumerate(GROUPS):
                                nb_ = 32 // d
                                for kt0 in range(0, 32, 4):
                                    k = cnt_a["V"] % 2; cnt_a["V"] += 1
                                    for sl in range(4):
                                        kt = kt0 + sl
                                        r_, b_ = kt // nb_, kt % nb_
                                        t0 = 128 * b_ * d + r_
                                        S.op("pe", lambda e: e.transpose(out=pV[k][:, sl * 128:(sl + 1) * 128], in_=VT[:, t0:t0 + 127 * d + 1:d],
                                                                         identity=ident_b[:]), reads=[r_VT, r_idb], writes=[r_pV[k]])
                                    src = pV[k].rearrange("p (s h e) -> p s h e", s=4, h=2)
                                    if cnt_a["V"] % 2 == 0:
                                        S.op("dve", lambda e: e.tensor_copy(out=Vaug[:, kt0:kt0 + 4, :, 0:64], in_=src), reads=[r_pV[k]], pwrites=[r_Vaug])
                                    else:
                                        S.op("act", lambda e: e.activation(out=Vaug[:, kt0:kt0 + 4, :, 0:64], in_=src, func=AF.Copy),
                                             reads=[r_pV[k]], pwrites=[r_Vaug])
                                units = [(hh, r_, b0) for hh in range(2) for r_ in range(d) for b0 in range(0, nb_, 2)]

                                def emit_S(u):
                                    hh, r_, b0 = u
                                    hs = slice(hh * 64, (hh + 1) * 64)
                                    bank = cnt_a["S"] % 2; pi = cnt_a["S"] % NPT; cnt_a["S"] += 1
                                    for jj in range(2):
                                        b_ = b0 + jj
                                        N = 256 if b_ + 1 < nb_ else 128
                                        t0 = 128 * b_ * d + r_
                                        ks = slice(t0, t0 + 127 * d + 1, d)
                                        qs = slice(t0, t0 + (N - 1) * d + 1, d)
                                        S.op("pe", lambda e: e.matmul(pS[bank][:, jj * 256:jj * 256 + N], lhsT=KT[hs, ks], rhs=QT[hs, qs], start=True, stop=True),
                                             reads=[r_KT, r_QT], writes=[r_pS[bank]])
                                    return bank, pi

                                def emit_E(u, info):
                                    hh, r_, b0 = u
                                    bank, pi = info
                                    S.op("act", lambda e: e.activation(out=PT[pi][:], in_=pS[bank][:], func=AF.Exp, scale=0.125, bias=nbias[:, hh:hh + 1]),
                                         reads=[r_pS[bank], r_nb], writes=[r_PT[pi]])
                                    S.op("dve", lambda e: e.tensor_tensor(out=PT[pi][:], in0=PT[pi][:], in1=mask2[:].rearrange("p a n -> p (a n)"), op=ALU.mult),
                                         reads=[r_PT[pi], r_mask2], writes=[r_PT[pi]])

                                def emit_PV(u, info, pinfo, slot0):
                                    hh, r_, b0 = u
                                    bank, pi = info
                                    ko = cnt_a["O"] % 2
                                    for jj in range(2):
                                        b_ = b0 + jj
                                        kt = r_ * nb_ + b_
                                        osl = pO[ko][:, (slot0 + jj) * 128:(slot0 + jj + 1) * 128]
                                        if b_ > 0:
                                            if jj == 0:
                                                ppi = pinfo[1]
                                                prhs = PT[ppi][:, 384:512]; rprev = r_PT[ppi]
                                            else:
                                                prhs = PT[pi][:, 128:256]; rprev = r_PT[pi]
                                            S.op("pe", lambda e: e.matmul(osl, lhsT=Vaug[:, kt - 1, hh, :], rhs=prhs, start=True, stop=False),
                                                 reads=[r_Vaug, rprev], writes=[r_pO[ko]])
                                        S.op("pe", lambda e: e.matmul(osl, lhsT=Vaug[:, kt, hh, :], rhs=PT[pi][:, jj * 256:jj * 256 + 128], start=(b_ == 0), stop=True),
                                             reads=[r_Vaug, r_PT[pi]], writes=[r_pO[ko]])

                                def emit_acc(u, ko):
                                    hh, r_, b0 = u
                                    ah = acc[hh]; rah = r_acc[hh]
                                    b_ = b0 + 1
                                    if d == 16:
                                        dst = ah.rearrange("p (i dd) -> p dd i", dd=16)[:, r_ - 1:r_ + 1, :]
                                        src = pO[ko][:].rearrange("p (a i) -> p a i", a=2)
                                    else:
                                        bs = b_ - 3
                                        ts = 128 * bs * d + r_
                                        dst = ah[:, ts:ts + 511 * d + 1:d]
                                        src = pO[ko][:]
                                    if gi == 0:
                                        S.op("dve", lambda e: e.tensor_copy(out=dst, in_=src), reads=[r_pO[ko]], pwrites=[rah])
                                    else:
                                        S.op("dve", lambda e: e.tensor_tensor(out=dst, in0=dst, in1=src, op=ALU.add), reads=[r_pO[ko], rah], pwrites=[rah])

                                infos = {}
                                infos[0] = emit_S(units[0])
                                slot = 0
                                for ui, u in enumerate(units):
                                    if ui + 1 < len(units):
                                        infos[ui + 1] = emit_S(units[ui + 1])
                                    emit_E(u, infos[ui])
                                    emit_PV(u, infos[ui], infos.get(ui - 1), slot)
                                    slot += 2
                                    if slot == 4:
                                        slot = 0
                                        emit_acc(u, cnt_a["O"] % 2)
                                        cnt_a["O"] += 1
                            for hh in range(2):
                                h = pr * 2 + hh
                                ah = acc[hh]; rah = r_acc[hh]
                                yb = yT[h % 2]; ryb = r_yT[h % 2]
                                for t4 in range(4):
                                    cs = slice(t4 * 1024, (t4 + 1) * 1024)
                                    S.op("act", lambda e: e.activation(out=e_n2[:], in_=ah[0:64, cs], func=AF.Square), reads=[rah], writes=[r_en2])
                                    S.op("act", lambda e: e.activation(out=e_d2, in_=ah[64:128, cs], func=AF.Square, scale=float(np.sqrt(EPS))), reads=[rah], writes=[r_ed2])
                                    for a in range(2):
                                        S.op("pe", lambda e: e.matmul(pJ[a][0:64, :], lhsT=ones_b[:], rhs=e_n2[:, a * 512:(a + 1) * 512], start=True, stop=True),
                                             reads=[r_onesb, r_en2], writes=[r_pJ[a]])
                                        S.op("dve", lambda e: e.scalar_tensor_tensor(out=e_rs[:, a * 512:(a + 1) * 512], in0=pJ[a][0:64, :], scalar=1.0 / 64.0,
                                                                                     in1=e_d2[:, a * 512:(a + 1) * 512], op0=ALU.mult, op1=ALU.add),
                                             reads=[r_pJ[a], r_ed2], pwrites=[r_ers])
                                    S.op("act", lambda e: e.activation(out=e_rs, in_=e_rs, func=AF.Ln), reads=[r_ers], writes=[r_ers])
                                    S.op("act", lambda e: e.activation(out=e_rs, in_=e_rs, func=AF.Exp, scale=-0.5), reads=[r_ers], writes=[r_ers])
                                    S.op("dve", lambda e: e.scalar_tensor_tensor(out=yb[:, cs], in0=ah[0:64, cs], scalar=aow(h), in1=e_rs, op0=ALU.mult, op1=ALU.mult),
                                         reads=[rah, r_ers, r_colA], pwrites=[ryb])
                                S.dma("sp", ybuf[h * 64:(h + 1) * 64, :], yb[:], reads=[ryb], writes=[r_ybuf[h // 2]])
                        S.barrier()

                if stop_after not in ("p1", "att"):
                    with ExitStack() as ph:
                        wexpT = sbt(ph, "wexpT", [128, 128], F32); r_wexpT = Res()
                        floorT = sbt(ph, "floorT", [128, 128], F32); r_floorT = Res()
                        decB = sbt(ph, "decB", [128, 128], F32); r_decB = Res()

                        with ExitStack() as pg:
                            wg = sbt(pg, "wg", [128, 8, 8], BF16); r_wg = Res()
                            pG = pst(pg, "mG", [128, 512], F32); r_pG = PRes()
                            gA = sbt(pg, "gA", [4, SEQ], F32); r_gA = Res()
                            gB = sbt(pg, "gB", [4, SEQ], F32); r_gB = Res()
                            gC = sbt(pg, "gC", [4, SEQ], F32); r_gC = Res()
                            gD = sbt(pg, "gD", [4, SEQ], F32); r_gD = Res()
                            sm = sbt(pg, "sm", [4, 12, 32], F32); r_sm = Res()
                            nfb = sbt(pg, "nfb", [4, 1], F32); r_nfb = Res()
                            dmask = sbt(pg, "dmask", [4, 128], F32); r_dmask = Res()
                            decD = sbt(pg, "decD", [4, 128], F32); r_decD = Res()
                            wgf = sbt(pg, "wgf", [128, 8, 8], F32); r_wgf = Res()
                            with nc.allow_non_contiguous_dma(reason="gate weight columns (32B runs)"):
                                S.dma("sp", wgf[:], w_in[:, 3584:3592].rearrange("(c p) m -> p c m", p=128), writes=[r_wgf])
                            S.op("dve", lambda e: e.tensor_copy(out=wg[:], in_=wgf[:]), reads=[r_wgf], writes=[r_wg])
                            S.op("dve", lambda e: e.memset(gC[:], 1.0), writes=[r_gC])
                            S.op("dve", lambda e: e.memset(gC[:].rearrange("p (j t) -> p j t", t=128)[:, :, 0:1], 0.0), writes=[r_gC])
                            S.op("dve", lambda e: e.tensor_scalar(out=nfb[:], in0=gb[:, 1:2], scalar1=-1.0, scalar2=None, op0=ALU.mult), reads=[r_gb], writes=[r_nfb])
                            S.op("pool", lambda e: e.affine_select(out=dmask[:].rearrange("p (h j) -> p h j", h=4), in_=ones_f[0:4, :].rearrange("p (h j) -> p h j", h=4),
                                                                   pattern=[[-1, 4], [0, 32]], compare_op=ALU.is_equal, fill=0.0, base=0, channel_multiplier=1),
                                 reads=[r_ones], writes=[r_dmask])
                            for tt in range(8):
                                cs = slice(tt * 512, (tt + 1) * 512)
                                for c in range(8):
                                    S.op("pe", lambda e: e.matmul(pG[0:4, :], lhsT=wg[:, c, 0:4], rhs=xnT[:, c, cs], start=(c == 0), stop=(c == 7)),
                                         reads=[r_wg, r_xnT[tt]], writes=[r_pG])
                                S.op("act", lambda e: e.activation(out=gA[:, cs], in_=pG[0:4, :], func=AF.Identity, bias=gb[:, 0:1], scale=1.0),
                                     reads=[r_pG, r_gb], pwrites=[r_gA])
                                for c in range(8):
                                    S.op("pe", lambda e: e.matmul(pG[0:4, :], lhsT=wg[:, c, 4:8], rhs=xnT[:, c, cs], start=(c == 0), stop=(c == 7)),
                                         reads=[r_wg, r_xnT[tt]], writes=[r_pG])
                                S.op("act", lambda e: e.activation(out=gB[:, cs], in_=pG[0:4, :], func=AF.Exp, bias=nfb[:, 0:1], scale=-1.0),
                                     reads=[r_pG, r_nfb], pwrites=[r_gB])
                            S.op("act", lambda e: e.activation(out=gB[:], in_=gB[:], func=AF.Ln, bias=1.0, scale=1.0), reads=[r_gB], writes=[r_gB])
                            S.op("dve", lambda e: e.tensor_scalar(out=gD[:], in0=gB[:], scalar1=-1.0, scalar2=None, op0=ALU.mult), reads=[r_gB], writes=[r_gD])
                            S.op("dve", lambda e: e.tensor_tensor_scan(out=gB[:], data0=gC[:], data1=gD[:], initial=0.0, op0=ALU.mult, op1=ALU.add),
                                 reads=[r_gC, r_gD], writes=[r_gB])
                            S.op("dve", lambda e: e.tensor_tensor(out=gA[:], in0=gA[:], in1=gB[:], op=ALU.subtract), reads=[r_gA, r_gB], writes=[r_gA])
                            rmax, gch, ginc, gx, rho, mun, mu, u_, dec_, tmp_ = [sm[:, i, :] for i in range(10)]
                            S.op("dve", lambda e: e.tensor_reduce(out=rmax, in_=gA[:].rearrange("p (j t) -> p j t", t=128), axis=AX.X, op=ALU.max),
                                 reads=[r_gA], writes=[r_sm])
                            S.op("dve", lambda e: e.tensor_copy(out=gch, in_=gB[:].rearrange("p (j t) -> p j t", t=128)[:, :, 127]), reads=[r_gB], writes=[r_sm])
                            S.op("dve", lambda e: e.memset(tmp_, 1.0), writes=[r_sm])
                            S.op("dve", lambda e: e.tensor_tensor_scan(out=ginc, data0=tmp_, data1=gch, initial=0.0, op0=ALU.mult, op1=ALU.add), reads=[r_sm], writes=[r_sm])
                            S.op("dve", lambda e: e.tensor_tensor(out=gx, in0=ginc, in1=gch, op=ALU.subtract), reads=[r_sm], writes=[r_sm])
                            S.op("dve", lambda e: e.tensor_tensor(out=rho, in0=rmax, in1=gx, op=ALU.subtract), reads=[r_sm], writes=[r_sm])
                            S.op("dve", lambda e: e.tensor_tensor_scan(out=mun, data0=rho, data1=rho, initial=0.0, op0=ALU.max, op1=ALU.max), reads=[r_sm], writes=[r_sm])
                            S.op("dve", lambda e: e.memset(mu, 0.0), writes=[r_sm])
                            S.op("dve", lambda e: e.tensor_copy(out=sm[:, 6, 1:32], in_=sm[:, 5, 0:31]), reads=[r_sm], writes=[r_sm])
                            S.op("dve", lambda e: e.tensor_tensor(out=u_, in0=gx, in1=mun, op=ALU.add), reads=[r_sm], writes=[r_sm])
                            S.op("dve", lambda e: e.tensor_tensor(out=dec_, in0=mu, in1=mun, op=ALU.subtract), reads=[r_sm], writes=[r_sm])
                            S.op("act", lambda e: e.activation(out=dec_, in_=dec_, func=AF.Exp), reads=[r_sm], writes=[r_sm])
                            ub = sm[:, 7, :].unsqueeze(2).to_broadcast([4, 32, 128])
                            S.op("dve", lambda e: e.tensor_tensor(out=gA[:].rearrange("p (j t) -> p j t", t=128), in0=gA[:].rearrange("p (j t) -> p j t", t=128),
                                                                  in1=ub, op=ALU.subtract), reads=[r_gA, r_sm], writes=[r_gA])
                            S.op("dve", lambda e: e.tensor_scalar(out=gA[:], in0=gA[:], scalar1=-0.5 * float(np.log(128.0)), scalar2=None, op0=ALU.add),
                                 reads=[r_gA], writes=[r_gA])
                            S.op("act", lambda e: e.activation(out=gA[:], in_=gA[:], func=AF.Exp), reads=[r_gA], writes=[r_gA])
                            S.op("dve", lambda e: e.tensor_tensor(out=gB[:].rearrange("p (j t) -> p j t", t=128), in0=gB[:].rearrange("p (j t) -> p j t", t=128),
                                                                  in1=ub, op=ALU.add), reads=[r_gB, r_sm], writes=[r_gB])
                            S.op("act", lambda e: e.activation(out=gB[:], in_=gB[:], func=AF.Exp, scale=-1.0), reads=[r_gB], writes=[r_gB])
                            for src, rsrc, dstT, rdstT in ((gA, r_gA, wexpT, r_wexpT), (gB, r_gB, floorT, r_floorT)):
                                for j in range(32):
                                    S.op("pe", lambda e: e.matmul(pG[:, j * 4:(j + 1) * 4], lhsT=src[0:4, j * 128:(j + 1) * 128], rhs=ident_f[0:4, 0:4], start=True, stop=True),
                                         reads=[rsrc, r_idf], writes=[r_pG])
                                S.op("dve", lambda e: e.tensor_copy(out=dstT[:].rearrange("p (h j) -> p j h", h=4), in_=pG[:, 0:128].rearrange("p (j h) -> p j h", h=4)),
                                     reads=[r_pG], writes=[rdstT])
                            S.op("dve", lambda e: e.tensor_tensor(out=decD[:].rearrange("p (h j) -> p h j", h=4), in0=dmask[:].rearrange("p (h j) -> p h j", h=4),
                                                                  in1=sm[:, 8, :].unsqueeze(1).to_broadcast([4, 4, 32]), op=ALU.mult), reads=[r_dmask, r_sm], writes=[r_decD])
                            S.op("pe", lambda e: e.matmul(pG[:, 0:128], lhsT=ones_f[0:4, :], rhs=decD[:], start=True, stop=True), reads=[r_ones, r_decD], writes=[r_pG])
                            S.op("dve", lambda e: e.tensor_copy(out=decB[:], in_=pG[:, 0:128]), reads=[r_pG], writes=[r_decB])
                            if debug:
                                o_wexp = dout("o_wexp", [4, SEQ]); o_floor = dout("o_floor", [4, SEQ]); o_sm = dout("o_sm", [4, 12 * 32])
                                o_wexpT = dout("o_wexpT", [128, 128]); o_decB = dout("o_decB", [128, 128])
                                S.dma("sp", o_wexp[:, :], gA[:], reads=[r_gA]); S.dma("sp", o_floor[:, :], gB[:], reads=[r_gB])
                                S.dma("sp", o_sm[:, :], sm[:].rearrange("p a b -> p (a b)"), reads=[r_sm])
                                S.dma("sp", o_wexpT[:, :], wexpT[:], reads=[r_wexpT]); S.dma("sp", o_decB[:, :], decB[:], reads=[r_decB])
                            S.barrier()

                        stop("mlg")
                        WD = sbt(ph, "WD", [128, 128], F32); r_WD = Res()
                        fl2 = sbt(ph, "fl2", [128, 128], F32); r_fl2 = Res()
                        S.op("dve", lambda e: e.tensor_tensor(out=WD[:, 0:127], in0=wexpT[:, 0:127], in1=decB[:, 1:128], op=ALU.mult), reads=[r_wexpT, r_decB], writes=[r_WD])
                        S.op("dve", lambda e: e.tensor_tensor(out=fl2[:], in0=floorT[:], in1=floorT[:], op=ALU.mult), reads=[r_floorT], writes=[r_fl2])
                        HS = SEQ // 2
                        wqk = [sbt(ph, f"wqk{i}", [128, 8, 2, 128], BF16) for i in range(2)]; r_wqk = [Res(), Res()]
                        wvo = [sbt(ph, f"wvo{i}", [128, 8, 2, 128], BF16) for i in range(2)]; r_wvo = [Res(), Res()]
                        xpre = sbt(ph, "xpre", [128, 3 + HS], F32); r_xpre = Res(); r_xh = Res()
                        cv = sbt(ph, "cv", [128, HS], F32); r_cv = Res()
                        qkT = sbt(ph, "qkT", [128, 2, SEQ], BF16); r_qkT = [Res(), Res()]
                        Vm = sbt(ph, "Vm", [128, 32, 132], BF16); r_Vm = Res()
                        SG = sbt(ph, "SG", [128, 32, 128], BF16); r_SG = Res()
                        sgt = sbt(ph, "sgt", [128, 2, 128], F32); r_sgt = Res()
                        Kw = sbt(ph, "Kw", [128, 32, 128], BF16); r_Kw = Res()
                        yTm = sbt(ph, "yTm", [128, SEQ], BF16); r_yTm = Res()
                        Cst = sbt(ph, "Cst", [128, 129], F32); r_C = Res()
                        Cbf = sbt(ph, "Cbf", [128, 132], BF16); r_Cb = Res()
                        PTm = [sbt(ph, f"PTm{i}", [128, 128], BF16) for i in range(2)]; r_PTm = [Res(), Res()]
                        ytl = [sbt(ph, f"ytl{i}", [128, 128], BF16) for i in range(2)]; r_ytl = [Res(), Res()]
                        NSC = 4
                        sc = [sbt(ph, f"msc{i}", [128, 8], F32) for i in range(NSC)]; r_sc = [[Res() for _ in range(4)] for _ in range(NSC)]
                        junkm = [sbt(ph, f"junkm{i}", [128, 128], BF16) for i in range(2)]; r_junkm = [Res(), Res()]
                        pP = [pst(ph, f"mP{i}", [128, 512], F32) for i in range(2)]; r_pP = [PRes(), PRes()]
                        pST_ = [pst(ph, f"mST{i}", [128, 512], F32) for i in range(2)]; r_pST = [PRes(), PRes()]
                        pST = [t[:, 0:128] for t in pST_]
                        pA_ = [pst(ph, f"mA{i}", [128, 512], F32) for i in range(2)]
                        pA = [pA_[0][:, 0:129], pA_[1][:, 0:129], pP[0][:, 0:129]]; r_pA = [PRes(), PRes(), r_pP[0]]
                        pC_ = pst(ph, "mC", [128, 512], F32); r_pC = PRes()
                        pC = pC_[:, 0:129]
                        pTb_ = pst(ph, "mTb", [128, 1024], BF16); r_pTb = [PRes()]
                        pTb = [pTb_[:, 0:512]]
                        for m in range(NCH):
                            S.dma("pool", wdn[:, m, :], w_ffn_down[m * 128:(m + 1) * 128, :], pwrites=[r_wdn])
                        S.op("pool", lambda e: e.memset(Vm[:, :, 128:129], 1.0), writes=[r_Vm])
                        npj = 0
                        for h in range(4):
                            wq_ = wqk[h % 2]; rwq = r_wqk[h % 2]
                            wv_ = wvo[h % 2]; rwv = r_wvo[h % 2]
                            for j in range(2):
                                c0 = 1536 + j * 512 + h * 128
                                S.dma("pool", wq_[:, :, j, :], w_in[:, c0:c0 + 128].rearrange("(c p) m -> p c m", p=128), pwrites=[rwq])
                                c1 = 2560 + j * 512 + h * 128
                                S.dma("pool", wv_[:, :, j, :], w_in[:, c1:c1 + 128].rearrange("(c p) m -> p c m", p=128), pwrites=[rwv])
                            for j in range(2):
                                ch = j * 4 + h
                                for half in range(2):
                                    if half == 0:
                                        S.op("dve", lambda e: e.memset(xpre[:, 0:3], 0.0), writes=[r_xh])
                                    else:
                                        S.op("dve", lambda e: e.tensor_copy(out=xpre[:, 0:3], in_=xpre[:, HS:HS + 3]), reads=[r_xpre], writes=[r_xh])
                                    for t4 in range(4):
                                        tt = half * 4 + t4
                                        k = npj % 2; npj += 1
                                        for c in range(8):
                                            S.op("pe", lambda e: e.matmul(pP[k][:], lhsT=wq_[:, c, j, :], rhs=xnT[:, c, tt * 512:(tt + 1) * 512], start=(c == 0), stop=(c == 7)),
                                                 reads=[rwq, r_xnT[tt]], writes=[r_pP[k]])
                                        S.op("act", lambda e: e.activation(out=xpre[:, 3 + t4 * 512:3 + (t4 + 1) * 512], in_=pP[k][:], func=AF.Copy),
                                             reads=[r_pP[k]], writes=([r_xpre] if t4 == 0 else []), pwrites=([] if t4 == 0 else [r_xpre]))
                                    S.op("dve", lambda e: e.tensor_scalar(out=cv[:], in0=xpre[:, 3:3 + HS], scalar1=mcw(3, ch), scalar2=mcb(ch), op0=ALU.mult, op1=ALU.add),
                                         reads=[r_xpre, r_xh, r_colA], writes=[r_cv])
                                    for tap in range(3):
                                        S.op("dve", lambda e: e.scalar_tensor_tensor(out=cv[:], in0=xpre[:, tap:tap + HS], scalar=mcw(tap, ch), in1=cv[:], op0=ALU.mult, op1=ALU.add),
                                             reads=[r_xpre, r_xh, r_cv, r_colA], writes=[r_cv])
                                    S.op("act", lambda e: e.activation(out=qkT[:, j, half * HS:(half + 1) * HS], in_=cv[:], func=AF.Silu), reads=[r_cv], pwrites=[r_qkT[j]])
                            for jt in range(0, 32, 2):
                                k = npj % 2; npj += 1
                                for a in range(2):
                                    tsl = slice((jt + a) * 128, (jt + a + 1) * 128)
                                    for c in range(8):
                                        S.op("pe", lambda e: e.matmul(pP[k][:, a * 256:(a + 1) * 256], lhsT=xnT[:, c, tsl], rhs=wv_[:, c, :, :].rearrange("p a b -> p (a b)"),
                                                                      start=(c == 0), stop=(c == 7)), reads=[rwv, r_xnT[(jt + a) // 4]], writes=[r_pP[k]])
                                pv = pP[k][:].rearrange("p (a j e) -> p a j e", a=2, j=2)
                                S.op("dve", lambda e: e.tensor_copy(out=Vm[:, jt:jt + 2, 0:128], in_=pv[:, :, 0, :]), reads=[r_pP[k]], pwrites=[r_Vm])
                                S.op("act", lambda e: e.activation(out=sgt[:], in_=pv[:, :, 1, :], func=AF.Sigmoid), reads=[r_pP[k]], writes=[r_sgt])
                                S.op("dve", lambda e: e.tensor_tensor(out=SG[:, jt:jt + 2, :], in0=sgt[:], in1=MNW[:, h * 128:(h + 1) * 128].unsqueeze(1).to_broadcast([128, 2, 128]),
                                                                      op=ALU.mult), reads=[r_sgt, r_MNW], pwrites=[r_SG])
                            for j0 in range(0, 32, 4):
                                for a in range(4):
                                    j = j0 + a
                                    S.op("pe", lambda e: e.transpose(out=pTb[0][:, a * 128:(a + 1) * 128], in_=qkT[:, 1, j * 128:(j + 1) * 128], identity=ident_b[:]),
                                         reads=[r_qkT[1], r_idb], writes=[r_pTb[0]])
                                for a in range(4):
                                    j = j0 + a
                                    col = h * 32 + j
                                    if j == 31:
                                        continue
                                    if a % 2 == 0:
                                        S.op("dve", lambda e: e.tensor_scalar(out=Kw[:, j, :], in0=pTb[0][:, a * 128:(a + 1) * 128], scalar1=WD[:, col:col + 1], scalar2=None,
                                                                              op0=ALU.mult), reads=[r_pTb[0], r_WD], pwrites=[r_Kw])
                                    else:
                                        S.op("act", lambda e: e.activation(out=Kw[:, j, :], in_=pTb[0][:, a * 128:(a + 1) * 128], func=AF.Copy, scale=WD[:, col:col + 1]),
                                             reads=[r_pTb[0], r_WD], pwrites=[r_Kw])
                            NJ = 32
                            colh = lambda j: h * 32 + j

                            def st_ST(j):
                                kk = j % 2; tsl = slice(j * 128, (j + 1) * 128)
                                S.op("pe", lambda e: e.matmul(pST[kk], lhsT=qkT[:, 1, tsl], rhs=qkT[:, 0, tsl], start=True, stop=True),
                                     reads=[r_qkT[0], r_qkT[1]], writes=[r_pST[kk]])

                            def st_PTm(j):
                                kk = j % 2; col = colh(j)
                                S.op("dve", lambda e: e.scalar_tensor_tensor(out=PTm[kk][:], in0=pST[kk], scalar=wexpT[:, col:col + 1], in1=mask01[:], op0=ALU.mult, op1=ALU.mult),
                                     reads=[r_pST[kk], r_wexpT, r_m01], writes=[r_PTm[kk]])

                            def st_pA(j):
                                kk = j % 2; ka = j % 3; tsl = slice(j * 128, (j + 1) * 128)
                                S.op("pe", lambda e: e.matmul(pA[ka], lhsT=PTm[kk][:], rhs=Vm[:, j, 0:129], start=True, stop=(j == 0)), reads=[r_PTm[kk], r_Vm], writes=[r_pA[ka]])
                                if j > 0:
                                    S.op("pe", lambda e: e.matmul(pA[ka], lhsT=qkT[:, 0, tsl], rhs=Cbf[:, 0:129], start=False, stop=True), reads=[r_qkT[0], r_Cb], writes=[r_pA[ka]])

                            def st_pC(j):
                                S.op("pe", lambda e: e.matmul(pC, lhsT=Kw[:, j, :], rhs=Vm[:, j, 0:129], start=True, stop=True), reads=[r_Kw, r_Vm], writes=[r_pC])

                            def st_state(j):
                                col = colh(j)
                                if j == 0:
                                    S.op("dve", lambda e: e.tensor_copy(out=Cst[:], in_=pC), reads=[r_pC], writes=[r_C])
                                else:
                                    S.op("dve", lambda e: e.scalar_tensor_tensor(out=Cst[:], in0=Cst[:], scalar=decB[:, col + 1:col + 2], in1=pC, op0=ALU.mult, op1=ALU.add),
                                         reads=[r_C, r_decB, r_pC], writes=[r_C])

                            def st_Cbf(j):
                                S.op("act", lambda e: e.activation(out=Cbf[:, 0:129], in_=Cst[:], func=AF.Copy), reads=[r_C], writes=[r_Cb])

                            def st_sq(j):
                                ka = j % 3; s_ = sc[j % NSC]; rs_ = r_sc[j % NSC]
                                S.op("act", lambda e: e.activation(out=s_[:, 0:1], in_=pA[ka][:, 128:129], func=AF.Square), reads=[r_pA[ka]], writes=[rs_[0]])
                                S.op("act", lambda e: e.activation(out=junkm[j % 2][:], in_=pA[ka][:, 0:128], func=AF.Square, accum_out=s_[:, 1:2]),
                                     reads=[r_pA[ka]], writes=[r_junkm[j % 2]], pwrites=[rs_[0]])

                            def st_tv(j):
                                col = colh(j); s_ = sc[j % NSC]; rs_ = r_sc[j % NSC]
                                S.op("dve", lambda e: e.tensor_scalar(out=s_[:, 2:3], in0=s_[:, 0:1], scalar1=fl2[:, col:col + 1], scalar2=EPS, op0=ALU.max, op1=ALU.mult),
                                     reads=[rs_[0], r_fl2], writes=[rs_[1]])
                                S.op("dve", lambda e: e.scalar_tensor_tensor(out=s_[:, 3:4], in0=s_[:, 1:2], scalar=1.0 / 128.0, in1=s_[:, 2:3], op0=ALU.mult, op1=ALU.add),
                                     reads=[rs_[0], rs_[1]], writes=[rs_[2]])

                            def st_lnexp(j):
                                s_ = sc[j % NSC]; rs_ = r_sc[j % NSC]
                                S.op("act", lambda e: e.activation(out=s_[:, 4:5], in_=s_[:, 3:4], func=AF.Ln), reads=[rs_[2]], writes=[rs_[3]])
                                S.op("act", lambda e: e.activation(out=s_[:, 5:6], in_=s_[:, 4:5], func=AF.Exp, scale=-0.5), reads=[rs_[3]], writes=[rs_[3]])

                            def st_ytl(j):
                                ka = j % 3; s_ = sc[j % NSC]; rs_ = r_sc[j % NSC]
                                S.op("dve", lambda e: e.scalar_tensor_tensor(out=ytl[j % 2][:], in0=pA[ka][:, 0:128], scalar=s_[:, 5:6], in1=SG[:, j, :], op0=ALU.mult, op1=ALU.mult),
                                     reads=[r_pA[ka], rs_[3], r_SG], writes=[r_ytl[j % 2]])

                            def st_T(j):
                                a = j % 4
                                S.op("pe", lambda e: e.transpose(out=pTb[0][:, a * 128:(a + 1) * 128], in_=ytl[j % 2][:], identity=ident_b[:]), reads=[r_ytl[j % 2], r_idb], writes=[r_pTb[0]])

                            def st_ym(j):
                                if j % 4 == 3:
                                    S.op("act", lambda e: e.activation(out=yTm[:, (j - 3) * 128:(j + 1) * 128], in_=pTb[0], func=AF.Copy), reads=[r_pTb[0]], pwrites=[r_yTm])

                            ok = lambda j: 0 <= j < NJ
                            st_ST(0); st_PTm(0)
                            for it in range(NJ + 5):
                                if ok(it + 1): st_ST(it + 1)
                                if ok(it): st_pA(it)
                                if ok(it) and it < NJ - 1: st_pC(it)
                                if ok(it - 3): st_T(it - 3)
                                if ok(it + 1): st_PTm(it + 1)
                                if ok(it): st_sq(it)
                                if ok(it) and it < NJ - 1:
                                    st_state(it)
                                    st_Cbf(it + 1)
                                if ok(it - 1): st_tv(it - 1)
                                if ok(it - 1): st_lnexp(it - 1)
                                if ok(it - 2): st_ytl(it - 2)
                                if ok(it - 3): st_ym(it - 3)
                            S.dma("sp", ybuf[512 + h * 128:512 + (h + 1) * 128, :], yTm[:], reads=[r_yTm], writes=[r_ybuf[4 + h]])
                        S.barrier()
        S.barrier()

        c2.close()
        if stop_after is None:
            TT = 256
            NTT = SEQ // TT
            NSUB = TT // 128
            NR = 6
            with ExitStack() as ph:
                wout = sbt(ph, "wout", [128, 8, D], BF16); r_wout = Res()
                wup = sbt(ph, "wup", [128, 8, 2 * DFF], BF16); r_wup = [Res() for _ in range(8)]
                ytb = sbt(ph, "f_y", [128, 8, TT], BF16); r_ytb = Res()
                h1 = [sbt(ph, f"f_h{i}", [128, NSUB, D], F32) for i in range(2)]; r_h1 = [Res(), Res()]
                hb = sbt(ph, "f_hb", [128, NSUB, D], BF16); r_hb = Res()
                hnT = sbt(ph, "f_hnT", [128, 8, TT], BF16); r_hnT = Res()
                Rb = [sbt(ph, f"f_R{i}", [128, 2 + TT], F32) for i in range(NR)]; r_Rb = [Res() for _ in range(NR)]
                cvb = [sbt(ph, f"f_cv{i}", [128, TT], F32) for i in range(NR)]; r_cvb = [Res() for _ in range(NR)]
                gT = sbt(ph, "f_gT", [128, NCH, TT], BF16); r_gT = Res()
                halo = sbt(ph, "f_halo", [128, 2 * NCH, 2], F32); r_halo = [Res() for _ in range(2 * NCH)]
                fss = [sbt(ph, f"f_ss{i}", [128, 8], F32) for i in range(2)]; r_fss = [Res(), Res()]
                fjunk = sbt(ph, "f_junk", [128, D], BF16)
                pH = [pst(ph, f"fH{i}", [128, 512], F32) for i in range(2)]; r_pH = [PRes(), PRes()]
                pU = [pst(ph, f"fU{i}", [128, 512], F32) for i in range(4)]; r_pU = [PRes() for _ in range(4)]
                pT3_ = [pst(ph, f"fT{i}", [128, 1024], BF16) for i in range(2)]; r_pT3 = [PRes(), PRes()]
                pT3 = [t[:, 0:512] for t in pT3_]
                for c in range(8):
                    S.dma("pool", wout[:, c, :], w_out[c * 128:(c + 1) * 128, :], pwrites=[r_wout])
                for c in range(8):
                    S.dma("pool", wup[:, c, :], w_ffn_up[c * 128:(c + 1) * 128, :], writes=[r_wup[c]])
                S.op("dve", lambda e: e.memset(halo[:], 0.0), writes=r_halo)
                cnt = {"H": 0, "U": 0, "R": 0, "T": 0}

                def load_tile(tt):
                    S.dma("sp", ytb[:], ybuf[:, tt * TT:(tt + 1) * TT].rearrange("(c p) t -> p c t", p=128), reads=r_ybuf, writes=[r_ytb])
                    S.dma("sp", h1[tt % 2][:], x[tt * TT:(tt + 1) * TT, :].rearrange("(s p) d -> p s d", p=128), writes=[r_h1[tt % 2]])

                def outproj(tt):
                    hh_ = h1[tt % 2]; rhh = r_h1[tt % 2]; fs = fss[tt % 2]; rfs = r_fss[tt % 2]
                    for s in range(NSUB):
                        for hf in range(2):
                            k = cnt["H"] % 2; cnt["H"] += 1
                            cs = slice(hf * 512, (hf + 1) * 512)
                            for c in range(8):
                                S.op("pe", lambda e: e.matmul(pH[k][:], lhsT=ytb[:, c, s * 128:(s + 1) * 128], rhs=wout[:, c, cs], start=(c == 0), stop=(c == 7)),
                                     reads=[r_ytb, r_wout], writes=[r_pH[k]])
                            S.op("dve", lambda e: e.tensor_tensor(out=hh_[:, s, cs], in0=hh_[:, s, cs], in1=pH[k][:], op=ALU.add), reads=[rhh, r_pH[k]], pwrites=[rhh])
                        S.op("act", lambda e: e.activation(out=fjunk[:], in_=hh_[:, s, :], func=AF.Square, scale=1.0 / 32.0, accum_out=fs[:, s:s + 1]),
                             reads=[rhh], pwrites=[rfs])
                    S.op("act", lambda e: e.activation(out=fs[:, 2:2 + NSUB], in_=fs[:, 0:NSUB], func=AF.Ln, bias=EPS, scale=1.0), reads=[rfs], pwrites=[rfs])
                    S.op("act", lambda e: e.activation(out=fs[:, 2:2 + NSUB], in_=fs[:, 2:2 + NSUB], func=AF.Exp, scale=-0.5), reads=[rfs], pwrites=[rfs])
                    for s in range(NSUB):
                        S.op("dve", lambda e: e.tensor_scalar(out=hb[:, s, :], in0=hh_[:, s, :], scalar1=fs[:, 2 + s:3 + s], scalar2=None, op0=ALU.mult),
                             reads=[rhh, rfs], pwrites=[r_hb])

                def transposes(tt):
                    for c0 in range(0, 8, 2):
                        k = cnt["T"] % 2; cnt["T"] += 1
                        for a in range(2):
                            for s in range(NSUB):
                                S.op("pe", lambda e: e.transpose(out=pT3[k][:, a * 256 + s * 128:a * 256 + (s + 1) * 128], in_=hb[:, s, (c0 + a) * 128:(c0 + a + 1) * 128],
                                                                 identity=ident_b[:]), reads=[r_hb, r_idb], writes=[r_pT3[k]])
                        S.op("dve", lambda e: e.tensor_scalar(out=hnT[:, c0, :], in0=pT3[k][:, 0:256], scalar1=fnw(c0), scalar2=None, op0=ALU.mult),
                             reads=[r_pT3[k], r_colA], pwrites=[r_hnT])
                        S.op("act", lambda e: e.activation(out=hnT[:, c0 + 1, :], in_=pT3[k][:, 256:512], func=AF.Copy, scale=fnw(c0 + 1)),
                             reads=[r_pT3[k], r_colA], pwrites=[r_hnT])

                def up(tt):
                    for m in range(NCH):
                        kU = cnt["U"] % 4; cnt["U"] += 1
                        for gv in range(2):
                            col0 = gv * DFF + m * 128
                            psl = pU[kU][:, gv * 256:(gv + 1) * 256]
                            for c in range(8):
                                S.op("pe", lambda e: e.matmul(psl, lhsT=wup[:, c, col0:col0 + 128], rhs=hnT[:, c, :], start=(c == 0), stop=(c == 7)),
                                     reads=[r_wup[c], r_hnT], writes=[r_pU[kU]])
                        bufs = []
                        for gv in range(2):
                            ch = gv * NCH + m
                            psl = pU[kU][:, gv * 256:(gv + 1) * 256]
                            kR = cnt["R"] % NR; cnt["R"] += 1
                            R_ = Rb[kR]; rR = r_Rb[kR]; cv_ = cvb[kR]; rcv = r_cvb[kR]
                            S.op("act", lambda e: e.activation(out=R_[:, 2:2 + TT], in_=psl, func=AF.Copy), reads=[r_pU[kU]], writes=[rR])
                            S.op("act", lambda e: e.activation(out=cv_[:], in_=psl, func=AF.Identity, scale=fcw(2, ch), bias=fcb(ch)), reads=[r_pU[kU], r_colB, r_colC], writes=[rcv])
                            S.op("pool", lambda e: e.tensor_copy(out=R_[:, 0:2], in_=halo[:, ch, :]), reads=[r_halo[ch]], pwrites=[rR])
                            S.op("pool", lambda e: e.tensor_copy(out=halo[:, ch, :], in_=R_[:, TT:TT + 2]), reads=[rR], writes=[r_halo[ch]])
                            S.op("dve", lambda e: e.scalar_tensor_tensor(out=cv_[:], in0=R_[:, 1:1 + TT], scalar=fcw(1, ch), in1=cv_[:], op0=ALU.mult, op1=ALU.add),
                                 reads=[rR, rcv, r_colB], writes=[rcv])
                            S.op("dve", lambda e: e.scalar_tensor_tensor(out=cv_[:], in0=R_[:, 0:TT], scalar=fcw(0, ch), in1=cv_[:], op0=ALU.mult, op1=ALU.add),
                                 reads=[rR, rcv, r_colB], writes=[rcv])
                            bufs.append((cv_, rcv))
                        (cg, rcg), (cvv, rcvv) = bufs
                        S.op("act", lambda e: e.activation(out=cg[:], in_=cg[:], func=AF.Silu), reads=[rcg], writes=[rcg])
                        S.op("pool", lambda e: e.tensor_tensor(out=gT[:, m, :], in0=cg[:], in1=cvv[:], op=ALU.mult), reads=[rcg, rcvv], pwrites=[r_gT])

                def down(tt, s):
                    hh_ = h1[tt % 2]; rhh = r_h1[tt % 2]; fs = fss[tt % 2]; rfs = r_fss[tt % 2]
                    for hf in range(2):
                        k = cnt["H"] % 2; cnt["H"] += 1
                        cs = slice(hf * 512, (hf + 1) * 512)
                        for m in range(NCH):
                            S.op("pe", lambda e: e.matmul(pH[k][:], lhsT=gT[:, m, s * 128:(s + 1) * 128], rhs=wdn[:, m, cs], start=(m == 0), stop=(m == NCH - 1)),
                                 reads=[r_gT, r_wdn], writes=[r_pH[k]])
                        S.op("dve", lambda e: e.tensor_tensor(out=hh_[:, s, cs], in0=hh_[:, s, cs], in1=pH[k][:], op=ALU.add), reads=[rhh, r_pH[k]], pwrites=[rhh])
                    S.op("act", lambda e: e.activation(out=fjunk[:], in_=hh_[:, s, :], func=AF.Square, scale=1.0 / 32.0, accum_out=fs[:, 4 + s:5 + s]),
                         reads=[rhh], pwrites=[rfs])

                def final(tt):
                    hh_ = h1[tt % 2]; rhh = r_h1[tt % 2]; fs = fss[tt % 2]; rfs = r_fss[tt % 2]
                    S.op("act", lambda e: e.activation(out=fs[:, 6:6 + NSUB], in_=fs[:, 4:4 + NSUB], func=AF.Ln, bias=EPS, scale=1.0), reads=[rfs], pwrites=[rfs])
                    S.op("act", lambda e: e.activation(out=fs[:, 6:6 + NSUB], in_=fs[:, 6:6 + NSUB], func=AF.Exp, scale=-0.5), reads=[rfs], pwrites=[rfs])
                    for s in range(NSUB):
                        S.op("dve", lambda e: e.scalar_tensor_tensor(out=hh_[:, s, :], in0=hh_[:, s, :], scalar=fs[:, 6 + s:7 + s], in1=FW[:], op0=ALU.mult, op1=ALU.mult),
                             reads=[rhh, rfs, r_FW], pwrites=[rhh])
                    S.dma("sp", out[tt * TT:(tt + 1) * TT, :].rearrange("(s p) d -> p s d", p=128), hh_[:], reads=[rhh])

                load_tile(0)
                outproj(0)
                transposes(0)
                for tt in range(NTT):
                    up(tt)
                    if tt + 1 < NTT:
                        load_tile(tt + 1)
                        outproj(tt + 1)
                    down(tt, 0)
                    if tt + 1 < NTT:
                        transposes(tt + 1)
                    down(tt, 1)
                    final(tt)
                S.barrier()
        S.barrier()
        nc._ninst = dict(S.ninst)
    return nc, dbg


_NC_CACHE = {}


def _squeeze(a):
    return np.ascontiguousarray(np.asarray(a, dtype=np.float32))


def kernel(x, w_in, mlstm_conv_w, mlstm_conv_b, mlstm_i_bias, mlstm_f_bias, att_out_norm_w, mlstm_out_norm_w,
           w_out, mixer_norm_w, ffn_norm_w, w_ffn_up, ffn_conv_w, ffn_conv_b, w_ffn_down, final_norm_w):
    n = 8
    if "nc" not in _NC_CACHE:
        _NC_CACHE["nc"] = build_nc()[0]
    nc = _NC_CACHE["nc"]
    shared = {
        "w_in": _squeeze(w_in[0]), "mlstm_conv_w": _squeeze(mlstm_conv_w[0]), "mlstm_conv_b": _squeeze(mlstm_conv_b[0]),
        "mlstm_i_bias": _squeeze(mlstm_i_bias[0]), "mlstm_f_bias": _squeeze(mlstm_f_bias[0]),
        "att_out_norm_w": _squeeze(att_out_norm_w[0]), "mlstm_out_norm_w": _squeeze(mlstm_out_norm_w[0]),
        "w_out": _squeeze(w_out[0]), "mixer_norm_w": _squeeze(mixer_norm_w[0]), "ffn_norm_w": _squeeze(ffn_norm_w[0]),
        "w_ffn_up": _squeeze(w_ffn_up[0]), "ffn_conv_w": _squeeze(ffn_conv_w[0]), "ffn_conv_b": _squeeze(ffn_conv_b[0]),
        "w_ffn_down": _squeeze(w_ffn_down[0]), "final_norm_w": _squeeze(final_norm_w),
    }
    xs = np.asarray(x, dtype=np.float32)
    in_maps = [dict(shared, x=np.ascontiguousarray(xs[i])) for i in range(n)]
    res = run_bass_kernel_spmd(nc, in_maps, core_ids=list(range(n)))
    return np.stack([np.asarray(r["out"], dtype=np.float32) for r in res.results], axis=0)
```

```python
import numpy as np
from collections import defaultdict
from contextlib import ExitStack
import concourse.bass as bass
import concourse.mybir as mybir
from concourse.bass_utils import run_bass_kernel_spmd

F32 = mybir.dt.float32
BF16 = mybir.dt.bfloat16
AF = mybir.ActivationFunctionType
ALU = mybir.AluOpType
AX = mybir.AxisListType

SEQ = 4096
D = 1024
PROJ = 3592
DFF = 2816
NCH = 22
EPS = 1e-6
GROUPS = (1, 4, 16)


class Res:
    __slots__ = ("ws", "r", "excl")

    def __init__(self, excl=False):
        self.ws = {}
        self.r = {}
        self.excl = excl


def PRes():
    return Res(excl=True)


class Sched:
    CE = ("pe", "act", "dve", "pool")

    def __init__(self, nc, es, nq=8):
        self.nc = nc
        self.eng = {"pe": nc.tensor, "act": nc.scalar, "dve": nc.vector, "pool": nc.gpsimd, "sp": nc.sync}
        self.sems = {}
        self.count = {}
        for e in self.CE:
            self.sems[e] = es.enter_context(nc.semaphore("s_" + e))
            self.count[e] = 0
        self.nq = nq
        self.rr = {}
        for q in ("sp", "act", "pool"):
            self.rr[q] = 0
            for i in range(nq):
                n = f"d_{q}{i}"
                self.sems[n] = es.enter_context(nc.semaphore(n))
                self.count[n] = 0
        self.seen = {e: defaultdict(int) for e in self.eng}
        self.ninst = defaultdict(int)
        self.dead = False

    def need(self, E, tok):
        if tok is None:
            return
        s, v = tok
        if s.startswith("d_"):
            v = self.count[s]
        elif s == E and E == "pe":
            return
        if self.seen[E][s] < v:
            self.eng[E].wait_ge(self.sems[s], v)
            self.seen[E][s] = v
            self.ninst[E] += 1

    def _pre(self, E, reads, writes, pwrites):
        for r in reads:
            for t in r.ws.values():
                self.need(E, t)
            if r.excl:
                for e2, t in r.r.items():
                    if e2 != E:
                        self.need(E, t)
        for w in writes:
            for t in w.ws.values():
                self.need(E, t)
            for e2, t in w.r.items():
                self.need(E, t)
        for w in pwrites:
            for e2, t in w.r.items():
                self.need(E, t)
            if w.excl:
                for e2, t in w.ws.items():
                    if e2 != E:
                        self.need(E, t)

    def _post(self, key, tok, reads, writes, pwrites):
        for r in reads:
            r.r[key] = tok
        for w in writes:
            w.ws = {key: tok}
            w.r = {}
        for w in pwrites:
            w.ws[key] = tok

    def op(self, E, fn, reads=(), writes=(), pwrites=()):
        if self.dead:
            return None
        self._pre(E, reads, writes, pwrites)
        inst = fn(self.eng[E])
        self.count[E] += 1
        inst.then_inc(self.sems[E], 1)
        self.ninst[E] += 1
        self._post(E, (E, self.count[E]), reads, writes, pwrites)
        return inst

    def dma(self, q, out, in_, reads=(), writes=(), pwrites=(), **kw):
        if self.dead:
            return None
        self._pre(q, reads, writes, pwrites)
        inst = self.eng[q].dma_start(out=out, in_=in_, **kw)
        n = f"d_{q}{self.rr[q] % self.nq}"
        self.rr[q] += 1
        self.count[n] += 16
        inst.then_inc(self.sems[n], 16)
        self.ninst[q] += 1
        self._post(n, (n, self.count[n]), reads, writes, pwrites)
        return inst

    def barrier(self):
        for E in self.eng:
            for s in self.sems:
                if self.count[s] > 0 and s != E:
                    self.need(E, (s, self.count[s]))


def build_nc(debug=False, stop_after=None):
    nc = bass.Bass("TRN2", target_bir_lowering=False)
    din = lambda n, s: nc.dram_tensor(n, s, F32, kind="ExternalInput").ap()
    x = din("x", [SEQ, D])
    w_in = din("w_in", [D, PROJ])
    mlstm_conv_w = din("mlstm_conv_w", [4, 1024])
    mlstm_conv_b = din("mlstm_conv_b", [1024])
    mlstm_i_bias = din("mlstm_i_bias", [4])
    mlstm_f_bias = din("mlstm_f_bias", [4])
    att_out_norm_w = din("att_out_norm_w", [512])
    mlstm_out_norm_w = din("mlstm_out_norm_w", [512])
    w_out = din("w_out", [D, D])
    mixer_norm_w = din("mixer_norm_w", [D])
    ffn_norm_w = din("ffn_norm_w", [D])
    w_ffn_up = din("w_ffn_up", [D, 2 * DFF])
    ffn_conv_w = din("ffn_conv_w", [3, 2 * DFF])
    ffn_conv_b = din("ffn_conv_b", [2 * DFF])
    w_ffn_down = din("w_ffn_down", [DFF, D])
    final_norm_w = din("final_norm_w", [D])
    out = nc.dram_tensor("out", [SEQ, D], F32, kind="ExternalOutput").ap()
    ybuf = nc.dram_tensor("ybuf", [D, SEQ], BF16, kind=("ExternalOutput" if debug else "Internal")).ap()
    r_ybuf = [Res() for _ in range(8)]
    dbg = {}

    def dout(name, shape, dt=F32):
        dbg[name] = nc.dram_tensor(name, shape, dt, kind="ExternalOutput").ap()
        return dbg[name]

    with ExitStack() as es:
        S = Sched(nc, es)

        def stop(tag):
            if stop_after == tag:
                S.barrier()
                S.dead = True
        sbt = lambda st, n, s, d: st.enter_context(nc.sbuf_tensor(n, s, d))
        pst = lambda st, n, s, d: st.enter_context(nc.psum_tensor(n, s, d))

        ident_b = sbt(es, "ident_b", [128, 128], BF16); r_idb = Res()
        colA = sbt(es, "colA", [128, 64], F32); r_colA = Res()
        colB = sbt(es, "colB", [128, 88], F32); r_colB = Res()
        colC = sbt(es, "colC", [128, 88], F32); r_colC = Res()
        FW = sbt(es, "FW", [128, D], F32); r_FW = Res()
        wdn = sbt(es, "wdn", [128, NCH, D], BF16); r_wdn = Res()
        wdn_f32 = wdn[:].rearrange("p m d -> p (m d)").bitcast(F32)
        c2 = ExitStack()
        ident_f = sbt(c2, "ident_f", [128, 128], F32); r_idf = Res()
        ones_f = sbt(c2, "ones_f", [128, 128], F32); r_ones = Res()
        zero_f = sbt(c2, "zero_f", [128, 256], F32); r_zero = Res()
        sel_b = sbt(c2, "sel_b", [128, 2, 128], BF16); r_sel = Res()
        maskf = sbt(c2, "maskf", [128, 256], F32); r_maskf = Res()
        maskb = sbt(c2, "maskb", [128, 256], BF16); r_maskb = Res()
        mask01 = sbt(c2, "mask01", [128, 128], F32); r_m01 = Res()
        MNW = sbt(c2, "MNW", [128, 512], F32); r_MNW = Res()
        gb = sbt(c2, "gb", [4, 2], F32); r_gb = Res()
        S.op("pool", lambda e: e.memset(ones_f[:], 1.0), writes=[r_ones])
        S.op("pool", lambda e: e.memset(zero_f[:], 0.0), writes=[r_zero])
        S.op("pool", lambda e: e.affine_select(out=ident_f[:], in_=ones_f[:], pattern=[[-1, 128]], compare_op=ALU.is_equal,
                                               fill=0.0, base=0, channel_multiplier=1), reads=[r_ones], writes=[r_idf])
        S.op("dve", lambda e: e.tensor_copy(out=ident_b[:], in_=ident_f[:]), reads=[r_idf], writes=[r_idb])
        S.op("dve", lambda e: e.memset(sel_b[:], 0.0), writes=[r_sel])
        S.op("dve", lambda e: e.memset(sel_b[0:64, 0, :], 1.0), writes=[r_sel])
        S.op("dve", lambda e: e.memset(sel_b[64:128, 1, :], 1.0), writes=[r_sel])
        S.op("pool", lambda e: e.affine_select(out=maskf[:, 0:128], in_=zero_f[:, 0:128], pattern=[[1, 128]], compare_op=ALU.is_ge,
                                               fill=-30000.0, base=0, channel_multiplier=-1), reads=[r_zero], writes=[r_maskf])
        S.op("pool", lambda e: e.affine_select(out=maskf[:, 128:256], in_=zero_f[:, 128:256], pattern=[[-1, 128]], compare_op=ALU.is_ge,
                                               fill=-30000.0, base=0, channel_multiplier=1), reads=[r_zero], writes=[r_maskf])
        S.op("dve", lambda e: e.tensor_copy(out=maskb[:], in_=maskf[:]), reads=[r_maskf], writes=[r_maskb])
        S.op("pool", lambda e: e.affine_select(out=mask01[:], in_=ones_f[:], pattern=[[1, 128]], compare_op=ALU.is_ge,
                                               fill=0.0, base=0, channel_multiplier=-1), reads=[r_ones], writes=[r_m01])

        with ExitStack() as p0:
            rowA = sbt(p0, "rowA", [64, 128], F32); r_rowA = Res()
            rowB = sbt(p0, "rowB", [88, 128], F32); r_rowB = Res()
            rowC = sbt(p0, "rowC", [88, 128], F32); r_rowC = Res()
            pcol = pst(p0, "pcol", [128, 512], F32); r_pcol = PRes()
            S.op("dve", lambda e: e.memset(rowA[:], 0.0), writes=[r_rowA])
            S.dma("sp", rowA[0:8, :], mixer_norm_w.rearrange("(c p) -> c p", p=128), writes=[r_rowA])
            S.dma("sp", rowA[8:16, :], ffn_norm_w.rearrange("(c p) -> c p", p=128), writes=[r_rowA])
            S.dma("sp", rowA[16:48, :], mlstm_conv_w.rearrange("j (c p) -> (j c) p", p=128), writes=[r_rowA])
            S.dma("sp", rowA[48:56, :], mlstm_conv_b.rearrange("(c p) -> c p", p=128), writes=[r_rowA])
            S.dma("sp", rowA[56:64, 0:64], att_out_norm_w.rearrange("(h p) -> h p", p=64), writes=[r_rowA])
            S.dma("sp", rowB[:, :], ffn_conv_w[0:2, :].rearrange("j (c p) -> (j c) p", p=128), writes=[r_rowB])
            S.dma("sp", rowC[0:44, :], ffn_conv_w[2:3, :].rearrange("j (c p) -> (j c) p", p=128), writes=[r_rowC])
            S.dma("sp", rowC[44:88, :], ffn_conv_b.rearrange("(c p) -> c p", p=128), writes=[r_rowC])
            S.dma("sp", FW[:], final_norm_w.partition_broadcast(128), writes=[r_FW])
            S.dma("sp", MNW[:], mlstm_out_norm_w.partition_broadcast(128), writes=[r_MNW])
            with nc.allow_non_contiguous_dma(reason="tiny gate bias"):
                S.dma("sp", gb[:, 0:1], mlstm_i_bias.rearrange("(p o) -> p o", o=1), writes=[r_gb])
                S.dma("sp", gb[:, 1:2], mlstm_f_bias.rearrange("(p o) -> p o", o=1), writes=[r_gb])
            S.op("pe", lambda e: e.transpose(out=pcol[:, 0:64], in_=rowA[:], identity=ident_f[0:64, 0:64]), reads=[r_rowA, r_idf], writes=[r_pcol])
            S.op("pe", lambda e: e.transpose(out=pcol[:, 64:152], in_=rowB[:], identity=ident_f[0:88, 0:88]), reads=[r_rowB, r_idf], writes=[r_pcol])
            S.op("pe", lambda e: e.transpose(out=pcol[:, 152:240], in_=rowC[:], identity=ident_f[0:88, 0:88]), reads=[r_rowC, r_idf], writes=[r_pcol])
            S.op("dve", lambda e: e.tensor_copy(out=colA[:], in_=pcol[:, 0:64]), reads=[r_pcol], writes=[r_colA])
            S.op("dve", lambda e: e.tensor_copy(out=colB[:], in_=pcol[:, 64:152]), reads=[r_pcol], writes=[r_colB])
            S.op("dve", lambda e: e.tensor_copy(out=colC[:], in_=pcol[:, 152:240]), reads=[r_pcol], writes=[r_colC])
            S.barrier()
        mnw = lambda c: colA[:, c:c + 1]
        fnw = lambda c: colA[:, 8 + c:9 + c]
        mcw = lambda j, c: colA[:, 16 + j * 8 + c:17 + j * 8 + c]
        mcb = lambda c: colA[:, 48 + c:49 + c]
        aow = lambda h: colA[0:64, 56 + h:57 + h]

        def fcw(j, ch):
            if j < 2:
                return colB[:, j * 44 + ch:j * 44 + ch + 1]
            return colC[:, ch:ch + 1]
        fcb = lambda ch: colC[:, 44 + ch:45 + ch]

        if debug:
            o_colA = dout("o_colA", [128, 64]); o_colB = dout("o_colB", [128, 88]); o_colC = dout("o_colC", [128, 88])
            S.dma("sp", o_colA[:, :], colA[:], reads=[r_colA]); S.dma("sp", o_colB[:, :], colB[:], reads=[r_colB]); S.dma("sp", o_colC[:, :], colC[:], reads=[r_colC])
            S.barrier()
        with ExitStack() as p12:
          if stop_after != "p0":
                xnT = sbt(p12, "xnT", [128, 8, SEQ], BF16)
                r_xnT = [Res() for _ in range(8)]

                with ExitStack() as ph:
                    xt = [sbt(ph, f"p1x{i}", [128, 4, D], F32) for i in range(2)]; r_xt = [Res(), Res()]
                    xb = [sbt(ph, f"p1xb{i}", [128, 4, D], BF16) for i in range(2)]; r_xb = [Res(), Res()]
                    junk = [sbt(ph, f"p1junk{i}", [128, D], BF16) for i in range(2)]; r_junk = [Res(), Res()]
                    ss = [sbt(ph, f"p1ss{i}", [128, 4], F32) for i in range(2)]; r_ss = [Res(), Res()]
                    rs = [sbt(ph, f"p1rs{i}", [128, 4], F32) for i in range(2)]; r_rs = [Res(), Res()]
                    pT_ = [pst(ph, f"p1pT{i}", [128, 1024], BF16) for i in range(4)]; r_pT = [PRes() for _ in range(4)]
                    pT = [t[:, 0:512] for t in pT_]
                    for tt in range(8):
                        b = tt % 2
                        S.dma("sp", xt[b][:], x[tt * 512:(tt + 1) * 512, :].rearrange("(s p) d -> p s d", p=128), writes=[r_xt[b]])
                        for s in range(4):
                            S.op("act", lambda e: e.activation(out=junk[s % 2][:], in_=xt[b][:, s, :], func=AF.Square, scale=1.0 / 32.0,
                                                               accum_out=ss[b][:, s:s + 1]), reads=[r_xt[b]], writes=[r_junk[s % 2]], pwrites=[r_ss[b]])
                        S.op("act", lambda e: e.activation(out=rs[b][:], in_=ss[b][:], func=AF.Ln, bias=EPS, scale=1.0), reads=[r_ss[b]], writes=[r_rs[b]])
                        S.op("act", lambda e: e.activation(out=rs[b][:], in_=rs[b][:], func=AF.Exp, scale=-0.5), reads=[r_rs[b]], writes=[r_rs[b]])
                        for s in range(4):
                            eng = "dve"
                            S.op(eng, lambda e: e.tensor_scalar(out=xb[b][:, s, :], in0=xt[b][:, s, :], scalar1=rs[b][:, s:s + 1], scalar2=None,
                                                                op0=ALU.mult), reads=[r_xt[b], r_rs[b]], pwrites=[r_xb[b]])
                        for c in range(8):
                            k = (tt * 8 + c) % 4
                            for s in range(4):
                                S.op("pe", lambda e: e.transpose(out=pT[k][:, s * 128:(s + 1) * 128], in_=xb[b][:, s, c * 128:(c + 1) * 128],
                                                                 identity=ident_b[:]), reads=[r_xb[b], r_idb], writes=[r_pT[k]])
                            if c % 2 == 0:
                                S.op("dve", lambda e: e.tensor_scalar(out=xnT[:, c, tt * 512:(tt + 1) * 512], in0=pT[k], scalar1=mnw(c), scalar2=None,
                                                                      op0=ALU.mult), reads=[r_pT[k], r_colA], pwrites=[r_xnT[tt]])
                            else:
                                S.op("act", lambda e: e.activation(out=xnT[:, c, tt * 512:(tt + 1) * 512], in_=pT[k], func=AF.Copy, scale=mnw(c)),
                                     reads=[r_pT[k], r_colA], pwrites=[r_xnT[tt]])
                    S.barrier()
                if debug:
                    o_xnT = dout("o_xnT", [D, SEQ], BF16)
                    for c in range(8):
                        S.dma("sp", o_xnT[c * 128:(c + 1) * 128, :], xnT[:, c, :], reads=r_xnT)
                    S.barrier()

                if stop_after not in ("p1", "mlg", "mlh0", "mlh1", "mlh2", "mlh"):
                    with ExitStack() as ph:
                        wqkv = [sbt(ph, f"wqkv{i}", [128, 8, 3, 128], BF16) for i in range(2)]; r_wqkv = [Res(), Res()]
                        QT = sbt(ph, "QT", [128, SEQ], BF16); r_QT = Res()
                        KT = sbt(ph, "KT", [128, SEQ], BF16); r_KT = Res()
                        VT = sbt(ph, "VT", [128, SEQ], BF16); r_VT = Res()
                        sq = sbt(ph, "sq", [128, SEQ], BF16); r_sq = Res()
                        mx = sbt(ph, "mx", [128, 4, 8], F32); r_mx = Res()
                        st = sbt(ph, "st", [128, 4], F32); r_st = Res()
                        nbias = sbt(ph, "nbias", [128, 2], F32); r_nb = Res()
                        mask2 = sbt(ph, "mask2", [128, 2, 256], BF16); r_mask2 = Res()
                        Vaug = sbt(ph, "Vaug", [128, 32, 2, 128], BF16); r_Vaug = Res()
                        acc = [wdn_f32[:, 0:SEQ], wdn_f32[:, SEQ:2 * SEQ]]; r_acc = [Res(), Res()]
                        NPT = 4
                        PT = [sbt(ph, f"PT{i}", [128, 512], BF16) for i in range(NPT)]; r_PT = [Res() for _ in range(NPT)]
                        e_n2 = sbt(ph, "e_n2", [64, 1024], BF16); r_en2 = Res()
                        e_d2 = wdn_f32[0:64, 2 * SEQ:2 * SEQ + 1024]; r_ed2 = Res()
                        e_rs = wdn_f32[0:64, 2 * SEQ + 1024:2 * SEQ + 2048]; r_ers = Res()
                        ones_b = sbt(ph, "ones_b", [64, 64], BF16); r_onesb = Res()
                        yT = [sbt(ph, f"yTa{i}", [64, SEQ], BF16) for i in range(2)]; r_yT = [Res(), Res()]
                        pJ = [pst(ph, f"aJ{i}", [128, 512], F32) for i in range(2)]; r_pJ = [PRes(), PRes()]
                        pS = [pst(ph, f"aS{i}", [128, 512], F32) for i in range(2)]; r_pS = [PRes(), PRes()]
                        pO = [pst(ph, f"aO{i}", [128, 512], F32) for i in range(2)]; r_pO = [PRes(), PRes()]
                        pV_ = [pst(ph, f"aV{i}", [128, 1024], BF16) for i in range(2)]; r_pV = [PRes(), PRes()]
                        pV = [t[:, 0:512] for t in pV_]
                        S.op("pool", lambda e: e.memset(Vaug[:, :, :, 64:128], 1.0), writes=[r_Vaug])
                        S.op("dve", lambda e: e.tensor_copy(out=ones_b[:], in_=ones_f[0:64, 0:64]), reads=[r_ones], writes=[r_onesb])
                        for a in range(2):
                            S.op("dve", lambda e: e.tensor_scalar(out=mask2[:, a, :], in0=maskf[:], scalar1=0.0, scalar2=None, op0=ALU.is_equal),
                                 reads=[r_maskf], pwrites=[r_mask2])
                        nj = 0
                        cnt_a = {"V": 0, "S": 0, "O": 0}
                        pending_epi = []
                        for pr in range(4):
                            wb = wqkv[pr % 2]; rwb = r_wqkv[pr % 2]
                            for j in range(3):
                                S.dma("pool", wb[:, :, j, :], w_in[:, j * 512 + pr * 128:j * 512 + (pr + 1) * 128].rearrange("(c p) m -> p c m", p=128),
                                      pwrites=[rwb])
                            for j, (dst, rdst) in enumerate(((QT, r_QT), (KT, r_KT), (VT, r_VT))):
                                for tt in range(8):
                                    k = nj % 2; nj += 1
                                    for c in range(8):
                                        S.op("pe", lambda e: e.matmul(pJ[k][:], lhsT=wb[:, c, j, :], rhs=xnT[:, c, tt * 512:(tt + 1) * 512],
                                                                      start=(c == 0), stop=(c == 7)), reads=[rwb, r_xnT[tt]], writes=[r_pJ[k]])
                                    if nj % 2 == 0:
                                        S.op("dve", lambda e: e.tensor_copy(out=dst[:, tt * 512:(tt + 1) * 512], in_=pJ[k][:]), reads=[r_pJ[k]], pwrites=[rdst])
                                    else:
                                        S.op("act", lambda e: e.activation(out=dst[:, tt * 512:(tt + 1) * 512], in_=pJ[k][:], func=AF.Copy),
                                             reads=[r_pJ[k]], pwrites=[rdst])
                                    if pending_epi and (j * 8 + tt) % 3 == 2:
                                        pending_epi.pop(0)()
                            for qi, (src, rsrc) in enumerate(((QT, r_QT), (KT, r_KT))):
                                S.op("dve", lambda e: e.tensor_tensor(out=sq[:], in0=src[:], in1=src[:], op=ALU.mult), reads=[rsrc], writes=[r_sq])
                                for hh in range(2):
                                    for tt in range(8):
                                        k = nj % 2; nj += 1
                                        S.op("pe", lambda e: e.matmul(pJ[k][:], lhsT=sel_b[:, hh, :], rhs=sq[:, tt * 512:(tt + 1) * 512], start=True, stop=True),
                                             reads=[r_sel, r_sq], writes=[r_pJ[k]])
                                        S.op("dve", lambda e: e.tensor_reduce(out=mx[:, qi * 2 + hh, tt:tt + 1], in_=pJ[k][:], axis=AX.X, op=ALU.max),
                                             reads=[r_pJ[k]], pwrites=[r_mx])
                            S.op("dve", lambda e: e.tensor_reduce(out=st[:], in_=mx[:], axis=AX.X, op=ALU.max), reads=[r_mx], writes=[r_st])
                            S.op("dve", lambda e: e.tensor_tensor(out=nbias[:], in0=st[:, 0:2], in1=st[:, 2:4], op=ALU.mult), reads=[r_st], writes=[r_nb])
                            S.op("act", lambda e: e.activation(out=nbias[:], in_=nbias[:], func=AF.Ln), reads=[r_nb], writes=[r_nb])
                            S.op("act", lambda e: e.activation(out=nbias[:], in_=nbias[:], func=AF.Exp, scale=0.5), reads=[r_nb], writes=[r_nb])
                            S.op("dve", lambda e: e.tensor_scalar(out=nbias[:], in0=nbias[:], scalar1=-0.125 * 1.02, scalar2=None, op0=ALU.mult),
                                 reads=[r_nb], writes=[r_nb])
                            for gi, d in enumerate(GROUPS):
                                nb_ = 32 // d
                                for kt0 in range(0, 32, 4):
                                    k = cnt_a["V"] % 2; cnt_a["V"] += 1
                                    for sl in range(4):
                                        kt = kt0 + sl
                                        r_, b_ = kt // nb_, kt % nb_
                                        t0 = 128 * b_ * d + r_
                                        S.op("pe", lambda e: e.transpose(out=pV[k][:, sl * 128:(sl + 1) * 128], in_=VT[:, t0:t0 + 127 * d + 1:d],
                                                                         identity=ident_b[:]), reads=[r_VT, r_idb], writes=[r_pV[k]])
                                    src = pV[k].rearrange("p (s h e) -> p s h e", s=4, h=2)
                                    if cnt_a["V"] % 2 == 0:
                                        S.op("dve", lambda e: e.tensor_copy(out=Vaug[:, kt0:kt0 + 4, :, 0:64], in_=src), reads=[r_pV[k]], pwrites=[r_Vaug])
                                    else:
                                        S.op("act", lambda e: e.activation(out=Vaug[:, kt0:kt0 + 4, :, 0:64], in_=src, func=AF.Copy),
                                             reads=[r_pV[k]], pwrites=[r_Vaug])
                                units = [(hh, r_, b0) for hh in range(2) for r_ in range(d) for b0 in range(0, nb_, 2)]

                                def emit_S(u):
                                    hh, r_, b0 = u
                                    hs = slice(hh * 64, (hh + 1) * 64)
                                    bank = cnt_a["S"] % 2; pi = cnt_a["S"] % NPT; cnt_a["S"] += 1
                                    for jj in range(2):
                                        b_ = b0 + jj
                                        N = 256 if b_ + 1 < nb_ else 128
                                        t0 = 128 * b_ * d + r_
                                        ks = slice(t0, t0 + 127 * d + 1, d)
                                        qs = slice(t0, t0 + (N - 1) * d + 1, d)
                                        S.op("pe", lambda e: e.matmul(pS[bank][:, jj * 256:jj * 256 + N], lhsT=KT[hs, ks], rhs=QT[hs, qs], start=True, stop=True),
                                             reads=[r_KT, r_QT], writes=[r_pS[bank]])
                                    return bank, pi

                                def emit_E(u, info):
                                    hh, r_, b0 = u
                                    bank, pi = info
                                    S.op("act", lambda e: e.activation(out=PT[pi][:], in_=pS[bank][:], func=AF.Exp, scale=0.125, bias=nbias[:, hh:hh + 1]),
                                         reads=[r_pS[bank], r_nb], writes=[r_PT[pi]])
                                    S.op("dve", lambda e: e.tensor_tensor(out=PT[pi][:], in0=PT[pi][:], in1=mask2[:].rearrange("p a n -> p (a n)"), op=ALU.mult),
                                         reads=[r_PT[pi], r_mask2], writes=[r_PT[pi]])

                                def emit_PV(u, info, pinfo, slot0):
                                    hh, r_, b0 = u
                                    bank, pi = info
                                    ko = cnt_a["O"] % 2
                                    for jj in range(2):
                                        b_ = b0 + jj
                                        kt = r_ * nb_ + b_
                                        osl = pO[ko][:, (slot0 + jj) * 128:(slot0 + jj + 1) * 128]
                                        if b_ > 0:
                                            if jj == 0:
                                                ppi = pinfo[1]
                                                prhs = PT[ppi][:, 384:512]; rprev = r_PT[ppi]
                                            else:
                                                prhs = PT[pi][:, 128:256]; rprev = r_PT[pi]
                                            S.op("pe", lambda e: e.matmul(osl, lhsT=Vaug[:, kt - 1, hh, :], rhs=prhs, start=True, stop=False),
                                                 reads=[r_Vaug, rprev], writes=[r_pO[ko]])
                                        S.op("pe", lambda e: e.matmul(osl, lhsT=Vaug[:, kt, hh, :], rhs=PT[pi][:, jj * 256:jj * 256 + 128], start=(b_ == 0), stop=True),
                                             reads=[r_Vaug, r_PT[pi]], writes=[r_pO[ko]])

                                def emit_acc(u, ko):
                                    hh, r_, b0 = u
                                    ah = acc[hh]; rah = r_acc[hh]
                                    b_ = b0 + 1
                                    if d == 16:
                                        dst = ah.rearrange("p (i dd) -> p dd i", dd=16)[:, r_ - 1:r_ + 1, :]
                                        src = pO[ko][:].rearrange("p (a i) -> p a i", a=2)
                                    else:
                                        bs = b_ - 3
                                        ts = 128 * bs * d + r_
                                        dst = ah[:, ts:ts + 511 * d + 1:d]
                                        src = pO[ko][:]
                                    if gi == 0:
                                        S.op("dve", lambda e: e.tensor_copy(out=dst, in_=src), reads=[r_pO[ko]], pwrites=[rah])
                                    else:
                                        S.op("dve", lambda e: e.tensor_tensor(out=dst, in0=dst, in1=src, op=ALU.add), reads=[r_pO[ko], rah], pwrites=[rah])

                                infos = {}
                                infos[0] = emit_S(units[0])
                                slot = 0
                                for ui, u in enumerate(units):
                                    if ui + 1 < len(units):
                                        infos[ui + 1] = emit_S(units[ui + 1])
                                    emit_E(u, infos[ui])
                                    emit_PV(u, infos[ui], infos.get(ui - 1), slot)
                                    slot += 2
                                    if slot == 4:
                                        slot = 0
                                        emit_acc(u, cnt_a["O"] % 2)
                                        cnt_a["O"] += 1
                            def make_epi(pr_, hh, t4):
                                def run():
                                    h = pr_ * 2 + hh
                                    ah = acc[hh]; rah = r_acc[hh]
                                    yb = yT[h % 2]; ryb = r_yT[h % 2]
                                    cs = slice(t4 * 1024, (t4 + 1) * 1024)
                                    S.op("act", lambda e: e.activation(out=e_n2[:], in_=ah[0:64, cs], func=AF.Square), reads=[rah], writes=[r_en2])
                                    S.op("act", lambda e: e.activation(out=e_d2, in_=ah[64:128, cs], func=AF.Square, scale=float(np.sqrt(EPS))), reads=[rah], writes=[r_ed2])
                                    for a in range(2):
                                        S.op("pe", lambda e: e.matmul(pO[a][0:64, :], lhsT=ones_b[:], rhs=e_n2[:, a * 512:(a + 1) * 512], start=True, stop=True),
                                             reads=[r_onesb, r_en2], writes=[r_pO[a]])
                                        S.op("dve", lambda e: e.scalar_tensor_tensor(out=e_rs[:, a * 512:(a + 1) * 512], in0=pO[a][0:64, :], scalar=1.0 / 64.0,
                                                                                     in1=e_d2[:, a * 512:(a + 1) * 512], op0=ALU.mult, op1=ALU.add),
                                             reads=[r_pO[a], r_ed2], pwrites=[r_ers])
                                    S.op("act", lambda e: e.activation(out=e_rs, in_=e_rs, func=AF.Ln), reads=[r_ers], writes=[r_ers])
                                    S.op("act", lambda e: e.activation(out=e_rs, in_=e_rs, func=AF.Exp, scale=-0.5), reads=[r_ers], writes=[r_ers])
                                    S.op("dve", lambda e: e.scalar_tensor_tensor(out=yb[:, cs], in0=ah[0:64, cs], scalar=aow(h), in1=e_rs, op0=ALU.mult, op1=ALU.mult),
                                         reads=[rah, r_ers, r_colA], pwrites=[ryb])
                                    if t4 == 3:
                                        S.dma("sp", ybuf[h * 64:(h + 1) * 64, :], yb[:], reads=[ryb], writes=[r_ybuf[h // 2]])
                                return run
                            pending_epi.extend(make_epi(pr, hh, t4) for hh in range(2) for t4 in range(4))
                        for f in pending_epi:
                            f()
                        S.barrier()

                if stop_after not in ("p1", "att"):
                    with ExitStack() as ph:
                        wexpT = sbt(ph, "wexpT", [128, 128], F32); r_wexpT = Res()
                        floorT = sbt(ph, "floorT", [128, 128], F32); r_floorT = Res()
                        decB = sbt(ph, "decB", [128, 128], F32); r_decB = Res()

                        with ExitStack() as pg:
                            wg = sbt(pg, "wg", [128, 8, 8], BF16); r_wg = Res()
                            pG = pst(pg, "mG", [128, 512], F32); r_pG = PRes()
                            gA = sbt(pg, "gA", [4, SEQ], F32); r_gA = Res()
                            gB = sbt(pg, "gB", [4, SEQ], F32); r_gB = Res()
                            gC = sbt(pg, "gC", [4, SEQ], F32); r_gC = Res()
                            gD = sbt(pg, "gD", [4, SEQ], F32); r_gD = Res()
                            sm = sbt(pg, "sm", [4, 12, 32], F32); r_sm = Res()
                            nfb = sbt(pg, "nfb", [4, 1], F32); r_nfb = Res()
                            dmask = sbt(pg, "dmask", [4, 128], F32); r_dmask = Res()
                            decD = sbt(pg, "decD", [4, 128], F32); r_decD = Res()
                            wgf = sbt(pg, "wgf", [128, 8, 8], F32); r_wgf = Res()
                            with nc.allow_non_contiguous_dma(reason="gate weight columns (32B runs)"):
                                S.dma("sp", wgf[:], w_in[:, 3584:3592].rearrange("(c p) m -> p c m", p=128), writes=[r_wgf])
                            S.op("dve", lambda e: e.tensor_copy(out=wg[:], in_=wgf[:]), reads=[r_wgf], writes=[r_wg])
                            S.op("dve", lambda e: e.memset(gC[:], 1.0), writes=[r_gC])
                            S.op("dve", lambda e: e.memset(gC[:].rearrange("p (j t) -> p j t", t=128)[:, :, 0:1], 0.0), writes=[r_gC])
                            S.op("dve", lambda e: e.tensor_scalar(out=nfb[:], in0=gb[:, 1:2], scalar1=-1.0, scalar2=None, op0=ALU.mult), reads=[r_gb], writes=[r_nfb])
                            S.op("pool", lambda e: e.affine_select(out=dmask[:].rearrange("p (h j) -> p h j", h=4), in_=ones_f[0:4, :].rearrange("p (h j) -> p h j", h=4),
                                                                   pattern=[[-1, 4], [0, 32]], compare_op=ALU.is_equal, fill=0.0, base=0, channel_multiplier=1),
                                 reads=[r_ones], writes=[r_dmask])
                            for tt in range(8):
                                cs = slice(tt * 512, (tt + 1) * 512)
                                for c in range(8):
                                    S.op("pe", lambda e: e.matmul(pG[0:4, :], lhsT=wg[:, c, 0:4], rhs=xnT[:, c, cs], start=(c == 0), stop=(c == 7)),
                                         reads=[r_wg, r_xnT[tt]], writes=[r_pG])
                                S.op("act", lambda e: e.activation(out=gA[:, cs], in_=pG[0:4, :], func=AF.Identity, bias=gb[:, 0:1], scale=1.0),
                                     reads=[r_pG, r_gb], pwrites=[r_gA])
                                for c in range(8):
                                    S.op("pe", lambda e: e.matmul(pG[0:4, :], lhsT=wg[:, c, 4:8], rhs=xnT[:, c, cs], start=(c == 0), stop=(c == 7)),
                                         reads=[r_wg, r_xnT[tt]], writes=[r_pG])
                                S.op("act", lambda e: e.activation(out=gB[:, cs], in_=pG[0:4, :], func=AF.Exp, bias=nfb[:, 0:1], scale=-1.0),
                                     reads=[r_pG, r_nfb], pwrites=[r_gB])
                            S.op("act", lambda e: e.activation(out=gB[:], in_=gB[:], func=AF.Ln, bias=1.0, scale=1.0), reads=[r_gB], writes=[r_gB])
                            S.op("dve", lambda e: e.tensor_scalar(out=gD[:], in0=gB[:], scalar1=-1.0, scalar2=None, op0=ALU.mult), reads=[r_gB], writes=[r_gD])
                            S.op("dve", lambda e: e.tensor_tensor_scan(out=gB[:], data0=gC[:], data1=gD[:], initial=0.0, op0=ALU.mult, op1=ALU.add),
                                 reads=[r_gC, r_gD], writes=[r_gB])
                            S.op("dve", lambda e: e.tensor_tensor(out=gA[:], in0=gA[:], in1=gB[:], op=ALU.subtract), reads=[r_gA, r_gB], writes=[r_gA])
                            rmax, gch, ginc, gx, rho, mun, mu, u_, dec_, tmp_ = [sm[:, i, :] for i in range(10)]
                            S.op("dve", lambda e: e.tensor_reduce(out=rmax, in_=gA[:].rearrange("p (j t) -> p j t", t=128), axis=AX.X, op=ALU.max),
                                 reads=[r_gA], writes=[r_sm])
                            S.op("dve", lambda e: e.tensor_copy(out=gch, in_=gB[:].rearrange("p (j t) -> p j t", t=128)[:, :, 127]), reads=[r_gB], writes=[r_sm])
                            S.op("dve", lambda e: e.memset(tmp_, 1.0), writes=[r_sm])
                            S.op("dve", lambda e: e.tensor_tensor_scan(out=ginc, data0=tmp_, data1=gch, initial=0.0, op0=ALU.mult, op1=ALU.add), reads=[r_sm], writes=[r_sm])
                            S.op("dve", lambda e: e.tensor_tensor(out=gx, in0=ginc, in1=gch, op=ALU.subtract), reads=[r_sm], writes=[r_sm])
                            S.op("dve", lambda e: e.tensor_tensor(out=rho, in0=rmax, in1=gx, op=ALU.subtract), reads=[r_sm], writes=[r_sm])
                            S.op("dve", lambda e: e.tensor_tensor_scan(out=mun, data0=rho, data1=rho, initial=0.0, op0=ALU.max, op1=ALU.max), reads=[r_sm], writes=[r_sm])
                            S.op("dve", lambda e: e.memset(mu, 0.0), writes=[r_sm])
                            S.op("dve", lambda e: e.tensor_copy(out=sm[:, 6, 1:32], in_=sm[:, 5, 0:31]), reads=[r_sm], writes=[r_sm])
                            S.op("dve", lambda e: e.tensor_tensor(out=u_, in0=gx, in1=mun, op=ALU.add), reads=[r_sm], writes=[r_sm])
                            S.op("dve", lambda e: e.tensor_tensor(out=dec_, in0=mu, in1=mun, op=ALU.subtract), reads=[r_sm], writes=[r_sm])
                            S.op("act", lambda e: e.activation(out=dec_, in_=dec_, func=AF.Exp), reads=[r_sm], writes=[r_sm])
                            ub = sm[:, 7, :].unsqueeze(2).to_broadcast([4, 32, 128])
                            S.op("dve", lambda e: e.tensor_tensor(out=gA[:].rearrange("p (j t) -> p j t", t=128), in0=gA[:].rearrange("p (j t) -> p j t", t=128),
                                                                  in1=ub, op=ALU.subtract), reads=[r_gA, r_sm], writes=[r_gA])
                            S.op("dve", lambda e: e.tensor_scalar(out=gA[:], in0=gA[:], scalar1=-0.5 * float(np.log(128.0)), scalar2=None, op0=ALU.add),
                                 reads=[r_gA], writes=[r_gA])
                            S.op("act", lambda e: e.activation(out=gA[:], in_=gA[:], func=AF.Exp), reads=[r_gA], writes=[r_gA])
                            S.op("dve", lambda e: e.tensor_tensor(out=gB[:].rearrange("p (j t) -> p j t", t=128), in0=gB[:].rearrange("p (j t) -> p j t", t=128),
                                                                  in1=ub, op=ALU.add), reads=[r_gB, r_sm], writes=[r_gB])
                            S.op("act", lambda e: e.activation(out=gB[:], in_=gB[:], func=AF.Exp, scale=-1.0), reads=[r_gB], writes=[r_gB])
                            for src, rsrc, dstT, rdstT in ((gA, r_gA, wexpT, r_wexpT), (gB, r_gB, floorT, r_floorT)):
                                for j in range(32):
                                    S.op("pe", lambda e: e.matmul(pG[:, j * 4:(j + 1) * 4], lhsT=src[0:4, j * 128:(j + 1) * 128], rhs=ident_f[0:4, 0:4], start=True, stop=True),
                                         reads=[rsrc, r_idf], writes=[r_pG])
                                S.op("dve", lambda e: e.tensor_copy(out=dstT[:].rearrange("p (h j) -> p j h", h=4), in_=pG[:, 0:128].rearrange("p (j h) -> p j h", h=4)),
                                     reads=[r_pG], writes=[rdstT])
                            S.op("dve", lambda e: e.tensor_tensor(out=decD[:].rearrange("p (h j) -> p h j", h=4), in0=dmask[:].rearrange("p (h j) -> p h j", h=4),
                                                                  in1=sm[:, 8, :].unsqueeze(1).to_broadcast([4, 4, 32]), op=ALU.mult), reads=[r_dmask, r_sm], writes=[r_decD])
                            S.op("pe", lambda e: e.matmul(pG[:, 0:128], lhsT=ones_f[0:4, :], rhs=decD[:], start=True, stop=True), reads=[r_ones, r_decD], writes=[r_pG])
                            S.op("dve", lambda e: e.tensor_copy(out=decB[:], in_=pG[:, 0:128]), reads=[r_pG], writes=[r_decB])
                            if debug:
                                o_wexp = dout("o_wexp", [4, SEQ]); o_floor = dout("o_floor", [4, SEQ]); o_sm = dout("o_sm", [4, 12 * 32])
                                o_wexpT = dout("o_wexpT", [128, 128]); o_decB = dout("o_decB", [128, 128])
                                S.dma("sp", o_wexp[:, :], gA[:], reads=[r_gA]); S.dma("sp", o_floor[:, :], gB[:], reads=[r_gB])
                                S.dma("sp", o_sm[:, :], sm[:].rearrange("p a b -> p (a b)"), reads=[r_sm])
                                S.dma("sp", o_wexpT[:, :], wexpT[:], reads=[r_wexpT]); S.dma("sp", o_decB[:, :], decB[:], reads=[r_decB])
                            S.barrier()

                        stop("mlg")
                        WD = sbt(ph, "WD", [128, 128], F32); r_WD = Res()
                        fl2 = sbt(ph, "fl2", [128, 128], F32); r_fl2 = Res()
                        S.op("dve", lambda e: e.tensor_tensor(out=WD[:, 0:127], in0=wexpT[:, 0:127], in1=decB[:, 1:128], op=ALU.mult), reads=[r_wexpT, r_decB], writes=[r_WD])
                        S.op("dve", lambda e: e.tensor_tensor(out=fl2[:], in0=floorT[:], in1=floorT[:], op=ALU.mult), reads=[r_floorT], writes=[r_fl2])
                        HS = SEQ // 2
                        wqk = [sbt(ph, f"wqk{i}", [128, 8, 2, 128], BF16) for i in range(2)]; r_wqk = [Res(), Res()]
                        wvo = [sbt(ph, f"wvo{i}", [128, 8, 2, 128], BF16) for i in range(2)]; r_wvo = [Res(), Res()]
                        xpre = sbt(ph, "xpre", [128, 3 + HS], F32); r_xpre = Res(); r_xh = Res()
                        cv = sbt(ph, "cv", [128, HS], F32); r_cv = Res()
                        qkT = sbt(ph, "qkT", [128, 2, SEQ], BF16); r_qkT = [Res(), Res()]
                        Vm = sbt(ph, "Vm", [128, 32, 132], BF16); r_Vm = Res()
                        SG = sbt(ph, "SG", [128, 32, 128], BF16); r_SG = Res()
                        sgt = sbt(ph, "sgt", [128, 2, 128], F32); r_sgt = Res()
                        Kw = sbt(ph, "Kw", [128, 32, 128], BF16); r_Kw = Res()
                        yTm = sbt(ph, "yTm", [128, SEQ], BF16); r_yTm = Res()
                        Cst = sbt(ph, "Cst", [128, 129], F32); r_C = Res()
                        Cbf = sbt(ph, "Cbf", [128, 132], BF16); r_Cb = Res()
                        PTm = [sbt(ph, f"PTm{i}", [128, 128], BF16) for i in range(2)]; r_PTm = [Res(), Res()]
                        ytl = [sbt(ph, f"ytl{i}", [128, 128], BF16) for i in range(2)]; r_ytl = [Res(), Res()]
                        NSC = 4
                        sc = [sbt(ph, f"msc{i}", [128, 8], F32) for i in range(NSC)]; r_sc = [[Res() for _ in range(4)] for _ in range(NSC)]
                        junkm = [sbt(ph, f"junkm{i}", [128, 128], BF16) for i in range(2)]; r_junkm = [Res(), Res()]
                        pP = [pst(ph, f"mP{i}", [128, 512], F32) for i in range(2)]; r_pP = [PRes(), PRes()]
                        pST_ = [pst(ph, f"mST{i}", [128, 512], F32) for i in range(2)]; r_pST = [PRes(), PRes()]
                        pST = [t[:, 0:128] for t in pST_]
                        pA_ = [pst(ph, f"mA{i}", [128, 512], F32) for i in range(2)]
                        pA = [pA_[0][:, 0:129], pA_[1][:, 0:129], pP[0][:, 0:129]]; r_pA = [PRes(), PRes(), r_pP[0]]
                        pC_ = pst(ph, "mC", [128, 512], F32); r_pC = PRes()
                        pC = pC_[:, 0:129]
                        pTb_ = pst(ph, "mTb", [128, 1024], BF16); r_pTb = [PRes()]
                        pTb = [pTb_[:, 0:512]]
                        for m in range(NCH):
                            S.dma("pool", wdn[:, m, :], w_ffn_down[m * 128:(m + 1) * 128, :], pwrites=[r_wdn])
                        S.op("pool", lambda e: e.memset(Vm[:, :, 128:129], 1.0), writes=[r_Vm])
                        npj = 0
                        for h in range(4):
                            wq_ = wqk[h % 2]; rwq = r_wqk[h % 2]
                            wv_ = wvo[h % 2]; rwv = r_wvo[h % 2]
                            for j in range(2):
                                c0 = 1536 + j * 512 + h * 128
                                S.dma("pool", wq_[:, :, j, :], w_in[:, c0:c0 + 128].rearrange("(c p) m -> p c m", p=128), pwrites=[rwq])
                                c1 = 2560 + j * 512 + h * 128
                                S.dma("pool", wv_[:, :, j, :], w_in[:, c1:c1 + 128].rearrange("(c p) m -> p c m", p=128), pwrites=[rwv])
                            for j in range(2):
                                ch = j * 4 + h
                                for half in range(2):
                                    if half == 0:
                                        S.op("dve", lambda e: e.memset(xpre[:, 0:3], 0.0), writes=[r_xh])
                                    else:
                                        S.op("dve", lambda e: e.tensor_copy(out=xpre[:, 0:3], in_=xpre[:, HS:HS + 3]), reads=[r_xpre], writes=[r_xh])
                                    for t4 in range(4):
                                        tt = half * 4 + t4
                                        k = npj % 2; npj += 1
                                        for c in range(8):
                                            S.op("pe", lambda e: e.matmul(pP[k][:], lhsT=wq_[:, c, j, :], rhs=xnT[:, c, tt * 512:(tt + 1) * 512], start=(c == 0), stop=(c == 7)),
                                                 reads=[rwq, r_xnT[tt]], writes=[r_pP[k]])
                                        S.op("act", lambda e: e.activation(out=xpre[:, 3 + t4 * 512:3 + (t4 + 1) * 512], in_=pP[k][:], func=AF.Copy),
                                             reads=[r_pP[k]], writes=([r_xpre] if t4 == 0 else []), pwrites=([] if t4 == 0 else [r_xpre]))
                                    S.op("dve", lambda e: e.tensor_scalar(out=cv[:], in0=xpre[:, 3:3 + HS], scalar1=mcw(3, ch), scalar2=mcb(ch), op0=ALU.mult, op1=ALU.add),
                                         reads=[r_xpre, r_xh, r_colA], writes=[r_cv])
                                    for tap in range(3):
                                        S.op("dve", lambda e: e.scalar_tensor_tensor(out=cv[:], in0=xpre[:, tap:tap + HS], scalar=mcw(tap, ch), in1=cv[:], op0=ALU.mult, op1=ALU.add),
                                             reads=[r_xpre, r_xh, r_cv, r_colA], writes=[r_cv])
                                    S.op("act", lambda e: e.activation(out=qkT[:, j, half * HS:(half + 1) * HS], in_=cv[:], func=AF.Silu), reads=[r_cv], pwrites=[r_qkT[j]])
                            for jt in range(0, 32, 2):
                                k = npj % 2; npj += 1
                                for a in range(2):
                                    tsl = slice((jt + a) * 128, (jt + a + 1) * 128)
                                    for c in range(8):
                                        S.op("pe", lambda e: e.matmul(pP[k][:, a * 256:(a + 1) * 256], lhsT=xnT[:, c, tsl], rhs=wv_[:, c, :, :].rearrange("p a b -> p (a b)"),
                                                                      start=(c == 0), stop=(c == 7)), reads=[rwv, r_xnT[(jt + a) // 4]], writes=[r_pP[k]])
                                pv = pP[k][:].rearrange("p (a j e) -> p a j e", a=2, j=2)
                                S.op("dve", lambda e: e.tensor_copy(out=Vm[:, jt:jt + 2, 0:128], in_=pv[:, :, 0, :]), reads=[r_pP[k]], pwrites=[r_Vm])
                                S.op("act", lambda e: e.activation(out=sgt[:], in_=pv[:, :, 1, :], func=AF.Sigmoid), reads=[r_pP[k]], writes=[r_sgt])
                                S.op("dve", lambda e: e.tensor_tensor(out=SG[:, jt:jt + 2, :], in0=sgt[:], in1=MNW[:, h * 128:(h + 1) * 128].unsqueeze(1).to_broadcast([128, 2, 128]),
                                                                      op=ALU.mult), reads=[r_sgt, r_MNW], pwrites=[r_SG])
                            for j0 in range(0, 32, 4):
                                for a in range(4):
                                    j = j0 + a
                                    S.op("pe", lambda e: e.transpose(out=pTb[0][:, a * 128:(a + 1) * 128], in_=qkT[:, 1, j * 128:(j + 1) * 128], identity=ident_b[:]),
                                         reads=[r_qkT[1], r_idb], writes=[r_pTb[0]])
                                for a in range(4):
                                    j = j0 + a
                                    col = h * 32 + j
                                    if j == 31:
                                        continue
                                    if a % 2 == 0:
                                        S.op("dve", lambda e: e.tensor_scalar(out=Kw[:, j, :], in0=pTb[0][:, a * 128:(a + 1) * 128], scalar1=WD[:, col:col + 1], scalar2=None,
                                                                              op0=ALU.mult), reads=[r_pTb[0], r_WD], pwrites=[r_Kw])
                                    else:
                                        S.op("act", lambda e: e.activation(out=Kw[:, j, :], in_=pTb[0][:, a * 128:(a + 1) * 128], func=AF.Copy, scale=WD[:, col:col + 1]),
                                             reads=[r_pTb[0], r_WD], pwrites=[r_Kw])
                            NJ = 32
                            colh = lambda j: h * 32 + j

                            def st_ST(j):
                                kk = j % 2; tsl = slice(j * 128, (j + 1) * 128)
                                S.op("pe", lambda e: e.matmul(pST[kk], lhsT=qkT[:, 1, tsl], rhs=qkT[:, 0, tsl], start=True, stop=True),
                                     reads=[r_qkT[0], r_qkT[1]], writes=[r_pST[kk]])

                            def st_PTm(j):
                                kk = j % 2; col = colh(j)
                                S.op("dve", lambda e: e.scalar_tensor_tensor(out=PTm[kk][:], in0=pST[kk], scalar=wexpT[:, col:col + 1], in1=mask01[:], op0=ALU.mult, op1=ALU.mult),
                                     reads=[r_pST[kk], r_wexpT, r_m01], writes=[r_PTm[kk]])

                            def st_pA(j):
                                kk = j % 2; ka = j % 3; tsl = slice(j * 128, (j + 1) * 128)
                                S.op("pe", lambda e: e.matmul(pA[ka], lhsT=PTm[kk][:], rhs=Vm[:, j, 0:129], start=True, stop=(j == 0)), reads=[r_PTm[kk], r_Vm], writes=[r_pA[ka]])
                                if j > 0:
                                    S.op("pe", lambda e: e.matmul(pA[ka], lhsT=qkT[:, 0, tsl], rhs=Cbf[:, 0:129], start=False, stop=True), reads=[r_qkT[0], r_Cb], writes=[r_pA[ka]])

                            def st_pC(j):
                                S.op("pe", lambda e: e.matmul(pC, lhsT=Kw[:, j, :], rhs=Vm[:, j, 0:129], start=True, stop=True), reads=[r_Kw, r_Vm], writes=[r_pC])

                            def st_state(j):
                                col = colh(j)
                                if j == 0:
                                    S.op("dve", lambda e: e.tensor_copy(out=Cst[:], in_=pC), reads=[r_pC], writes=[r_C])
                                else:
                                    S.op("dve", lambda e: e.scalar_tensor_tensor(out=Cst[:], in0=Cst[:], scalar=decB[:, col + 1:col + 2], in1=pC, op0=ALU.mult, op1=ALU.add),
                                         reads=[r_C, r_decB, r_pC], writes=[r_C])

                            def st_Cbf(j):
                                S.op("act", lambda e: e.activation(out=Cbf[:, 0:129], in_=Cst[:], func=AF.Copy), reads=[r_C], writes=[r_Cb])

                            def st_sq(j):
                                ka = j % 3; s_ = sc[j % NSC]; rs_ = r_sc[j % NSC]
                                S.op("act", lambda e: e.activation(out=s_[:, 0:1], in_=pA[ka][:, 128:129], func=AF.Square), reads=[r_pA[ka]], writes=[rs_[0]])
                                S.op("act", lambda e: e.activation(out=junkm[j % 2][:], in_=pA[ka][:, 0:128], func=AF.Square, accum_out=s_[:, 1:2]),
                                     reads=[r_pA[ka]], writes=[r_junkm[j % 2]], pwrites=[rs_[0]])

                            def st_tv(j):
                                col = colh(j); s_ = sc[j % NSC]; rs_ = r_sc[j % NSC]
                                S.op("dve", lambda e: e.tensor_scalar(out=s_[:, 2:3], in0=s_[:, 0:1], scalar1=fl2[:, col:col + 1], scalar2=EPS, op0=ALU.max, op1=ALU.mult),
                                     reads=[rs_[0], r_fl2], writes=[rs_[1]])
                                S.op("dve", lambda e: e.scalar_tensor_tensor(out=s_[:, 3:4], in0=s_[:, 1:2], scalar=1.0 / 128.0, in1=s_[:, 2:3], op0=ALU.mult, op1=ALU.add),
                                     reads=[rs_[0], rs_[1]], writes=[rs_[2]])

                            def st_lnexp(j):
                                s_ = sc[j % NSC]; rs_ = r_sc[j % NSC]
                                S.op("act", lambda e: e.activation(out=s_[:, 4:5], in_=s_[:, 3:4], func=AF.Ln), reads=[rs_[2]], writes=[rs_[3]])
                                S.op("act", lambda e: e.activation(out=s_[:, 5:6], in_=s_[:, 4:5], func=AF.Exp, scale=-0.5), reads=[rs_[3]], writes=[rs_[3]])

                            def st_ytl(j):
                                ka = j % 3; s_ = sc[j % NSC]; rs_ = r_sc[j % NSC]
                                S.op("dve", lambda e: e.scalar_tensor_tensor(out=ytl[j % 2][:], in0=pA[ka][:, 0:128], scalar=s_[:, 5:6], in1=SG[:, j, :], op0=ALU.mult, op1=ALU.mult),
                                     reads=[r_pA[ka], rs_[3], r_SG], writes=[r_ytl[j % 2]])

                            def st_T(j):
                                a = j % 4
                                S.op("pe", lambda e: e.transpose(out=pTb[0][:, a * 128:(a + 1) * 128], in_=ytl[j % 2][:], identity=ident_b[:]), reads=[r_ytl[j % 2], r_idb], writes=[r_pTb[0]])

                            def st_ym(j):
                                if j % 4 == 3:
                                    S.op("act", lambda e: e.activation(out=yTm[:, (j - 3) * 128:(j + 1) * 128], in_=pTb[0], func=AF.Copy), reads=[r_pTb[0]], pwrites=[r_yTm])

                            ok = lambda j: 0 <= j < NJ
                            st_ST(0); st_PTm(0)
                            for it in range(NJ + 5):
                                if ok(it + 1): st_ST(it + 1)
                                if ok(it): st_pA(it)
                                if ok(it) and it < NJ - 1: st_pC(it)
                                if ok(it - 3): st_T(it - 3)
                                if ok(it + 1): st_PTm(it + 1)
                                if ok(it): st_sq(it)
                                if ok(it) and it < NJ - 1:
                                    st_state(it)
                                    st_Cbf(it + 1)
                                if ok(it - 1): st_tv(it - 1)
                                if ok(it - 1): st_lnexp(it - 1)
                                if ok(it - 2): st_ytl(it - 2)
                                if ok(it - 3): st_ym(it - 3)
                            S.dma("sp", ybuf[512 + h * 128:512 + (h + 1) * 128, :], yTm[:], reads=[r_yTm], writes=[r_ybuf[4 + h]])
                        S.barrier()
        S.barrier()

        c2.close()
        if stop_after is None:
            TT = 256
            NTT = SEQ // TT
            NSUB = TT // 128
            NR = 6
            with ExitStack() as ph:
                wout = sbt(ph, "wout", [128, 8, D], BF16); r_wout = Res()
                wup = sbt(ph, "wup", [128, 8, 2 * DFF], BF16); r_wup = [Res() for _ in range(NCH)]
                ytb = sbt(ph, "f_y", [128, 8, TT], BF16); r_ytb = Res()
                h1 = [sbt(ph, f"f_h{i}", [128, NSUB, D], F32) for i in range(2)]; r_h1 = [Res(), Res()]
                hb = sbt(ph, "f_hb", [128, NSUB, D], BF16); r_hb = Res()
                hnT = sbt(ph, "f_hnT", [128, 8, TT], BF16); r_hnT = Res()
                Rb = [sbt(ph, f"f_R{i}", [128, 2 + TT], F32) for i in range(NR)]; r_Rb = [Res() for _ in range(NR)]
                cvb = [sbt(ph, f"f_cv{i}", [128, TT], F32) for i in range(NR)]; r_cvb = [Res() for _ in range(NR)]
                gT = sbt(ph, "f_gT", [128, NCH, TT], BF16); r_gT = Res()
                halo = sbt(ph, "f_halo", [128, 2 * NCH, 2], F32); r_halo = [Res() for _ in range(2 * NCH)]
                fss = [sbt(ph, f"f_ss{i}", [128, 8], F32) for i in range(2)]; r_fss = [Res(), Res()]
                fjunk = sbt(ph, "f_junk", [128, D], BF16); r_fjunk = Res()
                pH = [pst(ph, f"fH{i}", [128, 512], F32) for i in range(2)]; r_pH = [PRes(), PRes()]
                pU = [pst(ph, f"fU{i}", [128, 512], F32) for i in range(4)]; r_pU = [PRes() for _ in range(4)]
                pT3_ = [pst(ph, f"fT{i}", [128, 1024], BF16) for i in range(2)]; r_pT3 = [PRes(), PRes()]
                pT3 = [t[:, 0:512] for t in pT3_]
                for c in range(8):
                    S.dma("pool", wout[:, c, :], w_out[c * 128:(c + 1) * 128, :], pwrites=[r_wout])
                with nc.allow_non_contiguous_dma(reason="512B runs"):
                    for m in range(NCH):
                        for gv in range(2):
                            col0 = gv * DFF + m * 128
                            S.dma("pool", wup[:, :, col0:col0 + 128], w_ffn_up[:, col0:col0 + 128].rearrange("(c p) m -> p c m", p=128), pwrites=[r_wup[m]])
                S.op("dve", lambda e: e.memset(halo[:], 0.0), writes=r_halo)
                cnt = {"H": 0, "U": 0, "R": 0, "T": 0}

                def load_tile(tt):
                    S.dma("sp", ytb[:], ybuf[:, tt * TT:(tt + 1) * TT].rearrange("(c p) t -> p c t", p=128), reads=r_ybuf, writes=[r_ytb])
                    S.dma("sp", h1[tt % 2][:], x[tt * TT:(tt + 1) * TT, :].rearrange("(s p) d -> p s d", p=128), writes=[r_h1[tt % 2]])

                def outproj(tt):
                    hh_ = h1[tt % 2]; rhh = r_h1[tt % 2]; fs = fss[tt % 2]; rfs = r_fss[tt % 2]
                    for s in range(NSUB):
                        for hf in range(2):
                            k = cnt["H"] % 2; cnt["H"] += 1
                            cs = slice(hf * 512, (hf + 1) * 512)
                            for c in range(8):
                                S.op("pe", lambda e: e.matmul(pH[k][:], lhsT=ytb[:, c, s * 128:(s + 1) * 128], rhs=wout[:, c, cs], start=(c == 0), stop=(c == 7)),
                                     reads=[r_ytb, r_wout], writes=[r_pH[k]])
                            S.op("dve", lambda e: e.tensor_tensor(out=hh_[:, s, cs], in0=hh_[:, s, cs], in1=pH[k][:], op=ALU.add), reads=[rhh, r_pH[k]], pwrites=[rhh])
                        S.op("act", lambda e: e.activation(out=fjunk[:], in_=hh_[:, s, :], func=AF.Square, scale=1.0 / 32.0, accum_out=fs[:, s:s + 1]),
                             reads=[rhh], writes=[r_fjunk], pwrites=[rfs])
                    S.op("act", lambda e: e.activation(out=fs[:, 2:2 + NSUB], in_=fs[:, 0:NSUB], func=AF.Ln, bias=EPS, scale=1.0), reads=[rfs], pwrites=[rfs])
                    S.op("act", lambda e: e.activation(out=fs[:, 2:2 + NSUB], in_=fs[:, 2:2 + NSUB], func=AF.Exp, scale=-0.5), reads=[rfs], pwrites=[rfs])
                    for s in range(NSUB):
                        S.op("dve", lambda e: e.tensor_scalar(out=hb[:, s, :], in0=hh_[:, s, :], scalar1=fs[:, 2 + s:3 + s], scalar2=None, op0=ALU.mult),
                             reads=[rhh, rfs], pwrites=[r_hb])

                def transposes(tt):
                    for c0 in range(0, 8, 2):
                        k = cnt["T"] % 2; cnt["T"] += 1
                        for a in range(2):
                            for s in range(NSUB):
                                S.op("pe", lambda e: e.transpose(out=pT3[k][:, a * 256 + s * 128:a * 256 + (s + 1) * 128], in_=hb[:, s, (c0 + a) * 128:(c0 + a + 1) * 128],
                                                                 identity=ident_b[:]), reads=[r_hb, r_idb], writes=[r_pT3[k]])
                        S.op("dve", lambda e: e.tensor_scalar(out=hnT[:, c0, :], in0=pT3[k][:, 0:256], scalar1=fnw(c0), scalar2=None, op0=ALU.mult),
                             reads=[r_pT3[k], r_colA], pwrites=[r_hnT])
                        S.op("act", lambda e: e.activation(out=hnT[:, c0 + 1, :], in_=pT3[k][:, 256:512], func=AF.Copy, scale=fnw(c0 + 1)),
                             reads=[r_pT3[k], r_colA], pwrites=[r_hnT])

                def up(tt):
                    pend = None
                    for m in range(NCH + 1):
                        if m < NCH:
                            kU = cnt["U"] % 4; cnt["U"] += 1
                            for gv in range(2):
                                col0 = gv * DFF + m * 128
                                psl = pU[kU][:, gv * 256:(gv + 1) * 256]
                                for c in range(8):
                                    S.op("pe", lambda e: e.matmul(psl, lhsT=wup[:, c, col0:col0 + 128], rhs=hnT[:, c, :], start=(c == 0), stop=(c == 7)),
                                         reads=[r_wup[m], r_hnT], writes=[r_pU[kU]])
                            bufs = []
                            for gv in range(2):
                                ch = gv * NCH + m
                                psl = pU[kU][:, gv * 256:(gv + 1) * 256]
                                kR = cnt["R"] % NR; cnt["R"] += 1
                                R_ = Rb[kR]; rR = r_Rb[kR]; cv_ = cvb[kR]; rcv = r_cvb[kR]
                                S.op("act", lambda e: e.activation(out=R_[:, 2:2 + TT], in_=psl, func=AF.Copy), reads=[r_pU[kU]], writes=[rR])
                                S.op("act", lambda e: e.activation(out=cv_[:], in_=psl, func=AF.Identity, scale=fcw(2, ch), bias=fcb(ch)), reads=[r_pU[kU], r_colB, r_colC], writes=[rcv])
                                S.op("pool", lambda e: e.tensor_copy(out=R_[:, 0:2], in_=halo[:, ch, :]), reads=[r_halo[ch]], pwrites=[rR])
                                S.op("pool", lambda e: e.tensor_copy(out=halo[:, ch, :], in_=R_[:, TT:TT + 2]), reads=[rR], writes=[r_halo[ch]])
                                S.op("dve", lambda e: e.scalar_tensor_tensor(out=cv_[:], in0=R_[:, 1:1 + TT], scalar=fcw(1, ch), in1=cv_[:], op0=ALU.mult, op1=ALU.add),
                                     reads=[rR, rcv, r_colB], writes=[rcv])
                                S.op("dve", lambda e: e.scalar_tensor_tensor(out=cv_[:], in0=R_[:, 0:TT], scalar=fcw(0, ch), in1=cv_[:], op0=ALU.mult, op1=ALU.add),
                                     reads=[rR, rcv, r_colB], writes=[rcv])
                                bufs.append((cv_, rcv))
                        if pend is not None:
                            pm, ((cg, rcg), (cvv, rcvv)) = pend
                            S.op("act", lambda e: e.activation(out=cg[:], in_=cg[:], func=AF.Silu), reads=[rcg], writes=[rcg])
                            S.op("pool", lambda e: e.tensor_tensor(out=gT[:, pm, :], in0=cg[:], in1=cvv[:], op=ALU.mult), reads=[rcg, rcvv], pwrites=[r_gT])
                        pend = (m, bufs) if m < NCH else None

                def down(tt, s):
                    hh_ = h1[tt % 2]; rhh = r_h1[tt % 2]; fs = fss[tt % 2]; rfs = r_fss[tt % 2]
                    for hf in range(2):
                        k = cnt["H"] % 2; cnt["H"] += 1
                        cs = slice(hf * 512, (hf + 1) * 512)
                        for m in range(NCH):
                            S.op("pe", lambda e: e.matmul(pH[k][:], lhsT=gT[:, m, s * 128:(s + 1) * 128], rhs=wdn[:, m, cs], start=(m == 0), stop=(m == NCH - 1)),
                                 reads=[r_gT, r_wdn], writes=[r_pH[k]])
                        S.op("dve", lambda e: e.tensor_tensor(out=hh_[:, s, cs], in0=hh_[:, s, cs], in1=pH[k][:], op=ALU.add), reads=[rhh, r_pH[k]], pwrites=[rhh])
                    S.op("act", lambda e: e.activation(out=fjunk[:], in_=hh_[:, s, :], func=AF.Square, scale=1.0 / 32.0, accum_out=fs[:, 4 + s:5 + s]),
                         reads=[rhh], writes=[r_fjunk], pwrites=[rfs])

                def final(tt):
                    hh_ = h1[tt % 2]; rhh = r_h1[tt % 2]; fs = fss[tt % 2]; rfs = r_fss[tt % 2]
                    S.op("act", lambda e: e.activation(out=fs[:, 6:6 + NSUB], in_=fs[:, 4:4 + NSUB], func=AF.Ln, bias=EPS, scale=1.0), reads=[rfs], pwrites=[rfs])
                    S.op("act", lambda e: e.activation(out=fs[:, 6:6 + NSUB], in_=fs[:, 6:6 + NSUB], func=AF.Exp, scale=-0.5), reads=[rfs], pwrites=[rfs])
                    for s in range(NSUB):
                        S.op("dve", lambda e: e.scalar_tensor_tensor(out=hh_[:, s, :], in0=hh_[:, s, :], scalar=fs[:, 6 + s:7 + s], in1=FW[:], op0=ALU.mult, op1=ALU.mult),
                             reads=[rhh, rfs, r_FW], pwrites=[rhh])
                    S.dma("sp", out[tt * TT:(tt + 1) * TT, :].rearrange("(s p) d -> p s d", p=128), hh_[:], reads=[rhh])

                load_tile(0)
                outproj(0)
                transposes(0)
                for tt in range(NTT):
                    up(tt)
                    if tt + 1 < NTT:
                        load_tile(tt + 1)
                        outproj(tt + 1)
                    down(tt, 0)
                    if tt + 1 < NTT:
                        transposes(tt + 1)
                    down(tt, 1)
                    final(tt)
                S.barrier()
        S.barrier()
        nc._ninst = dict(S.ninst)
    return nc, dbg


_NC_CACHE = {}


def _squeeze(a):
    return np.ascontiguousarray(np.asarray(a, dtype=np.float32))


def kernel(x, w_in, mlstm_conv_w, mlstm_conv_b, mlstm_i_bias, mlstm_f_bias, att_out_norm_w, mlstm_out_norm_w,
           w_out, mixer_norm_w, ffn_norm_w, w_ffn_up, ffn_conv_w, ffn_conv_b, w_ffn_down, final_norm_w):
    n = 8
    if "nc" not in _NC_CACHE:
        _NC_CACHE["nc"] = build_nc()[0]
    nc = _NC_CACHE["nc"]
    shared = {
        "w_in": _squeeze(w_in[0]), "mlstm_conv_w": _squeeze(mlstm_conv_w[0]), "mlstm_conv_b": _squeeze(mlstm_conv_b[0]),
        "mlstm_i_bias": _squeeze(mlstm_i_bias[0]), "mlstm_f_bias": _squeeze(mlstm_f_bias[0]),
        "att_out_norm_w": _squeeze(att_out_norm_w[0]), "mlstm_out_norm_w": _squeeze(mlstm_out_norm_w[0]),
        "w_out": _squeeze(w_out[0]), "mixer_norm_w": _squeeze(mixer_norm_w[0]), "ffn_norm_w": _squeeze(ffn_norm_w[0]),
        "w_ffn_up": _squeeze(w_ffn_up[0]), "ffn_conv_w": _squeeze(ffn_conv_w[0]), "ffn_conv_b": _squeeze(ffn_conv_b[0]),
        "w_ffn_down": _squeeze(w_ffn_down[0]), "final_norm_w": _squeeze(final_norm_w),
    }
    xs = np.asarray(x, dtype=np.float32)
    in_maps = [dict(shared, x=np.ascontiguousarray(xs[i])) for i in range(n)]
    res = run_bass_kernel_spmd(nc, in_maps, core_ids=list(range(n)))
    return np.stack([np.asarray(r["out"], dtype=np.float32) for r in res.results], axis=0)
```

```python
import numpy as np
from collections import defaultdict
from contextlib import ExitStack
import concourse.bass as bass
import concourse.mybir as mybir
from concourse.bass_utils import run_bass_kernel_spmd

F32 = mybir.dt.float32
BF16 = mybir.dt.bfloat16
AF = mybir.ActivationFunctionType
ALU = mybir.AluOpType
AX = mybir.AxisListType

SEQ = 4096
D = 1024
PROJ = 3592
DFF = 2816
NCH = 22
EPS = 1e-6
GROUPS = (1, 4, 16)


class Res:
    __slots__ = ("ws", "r", "excl")

    def __init__(self, excl=False):
        self.ws = {}
        self.r = {}
        self.excl = excl


def PRes():
    return Res(excl=True)


class Sched:
    CE = ("pe", "act", "dve", "pool")

    def __init__(self, nc, es, nq=8):
        self.nc = nc
        self.eng = {"pe": nc.tensor, "act": nc.scalar, "dve": nc.vector, "pool": nc.gpsimd, "sp": nc.sync}
        self.sems = {}
        self.count = {}
        for e in self.CE:
            self.sems[e] = es.enter_context(nc.semaphore("s_" + e))
            self.count[e] = 0
        self.nq = nq
        self.rr = {}
        for q in ("sp", "act", "pool"):
            self.rr[q] = 0
            for i in range(nq):
                n = f"d_{q}{i}"
                self.sems[n] = es.enter_context(nc.semaphore(n))
                self.count[n] = 0
        self.seen = {e: defaultdict(int) for e in self.eng}
        self.ninst = defaultdict(int)
        self.dead = False

    def need(self, E, tok):
        if tok is None:
            return
        s, v = tok
        if s.startswith("d_"):
            v = self.count[s]
        elif s == E and E == "pe":
            return
        if self.seen[E][s] < v:
            self.eng[E].wait_ge(self.sems[s], v)
            self.seen[E][s] = v
            self.ninst[E] += 1

    def _pre(self, E, reads, writes, pwrites):
        for r in reads:
            for t in r.ws.values():
                self.need(E, t)
            if r.excl:
                for e2, t in r.r.items():
                    if e2 != E:
                        self.need(E, t)
        for w in writes:
            for t in w.ws.values():
                self.need(E, t)
            for e2, t in w.r.items():
                self.need(E, t)
        for w in pwrites:
            for e2, t in w.r.items():
                self.need(E, t)
            if w.excl:
                for e2, t in w.ws.items():
                    if e2 != E:
                        self.need(E, t)

    def _post(self, key, tok, reads, writes, pwrites):
        for r in reads:
            r.r[key] = tok
        for w in writes:
            w.ws = {key: tok}
            w.r = {}
        for w in pwrites:
            w.ws[key] = tok

    def op(self, E, fn, reads=(), writes=(), pwrites=()):
        if self.dead:
            return None
        self._pre(E, reads, writes, pwrites)
        inst = fn(self.eng[E])
        self.count[E] += 1
        inst.then_inc(self.sems[E], 1)
        self.ninst[E] += 1
        self._post(E, (E, self.count[E]), reads, writes, pwrites)
        return inst

    def dma(self, q, out, in_, reads=(), writes=(), pwrites=(), **kw):
        if self.dead:
            return None
        self._pre(q, reads, writes, pwrites)
        inst = self.eng[q].dma_start(out=out, in_=in_, **kw)
        n = f"d_{q}{self.rr[q] % self.nq}"
        self.rr[q] += 1
        self.count[n] += 16
        inst.then_inc(self.sems[n], 16)
        self.ninst[q] += 1
        self._post(n, (n, self.count[n]), reads, writes, pwrites)
        return inst

    def barrier(self):
        for E in self.eng:
            for s in self.sems:
                if self.count[s] > 0 and s != E:
                    self.need(E, (s, self.count[s]))


def build_nc(debug=False, stop_after=None):
    nc = bass.Bass("TRN2", target_bir_lowering=False)
    din = lambda n, s: nc.dram_tensor(n, s, F32, kind="ExternalInput").ap()
    x = din("x", [SEQ, D])
    w_in = din("w_in", [D, PROJ])
    mlstm_conv_w = din("mlstm_conv_w", [4, 1024])
    mlstm_conv_b = din("mlstm_conv_b", [1024])
    mlstm_i_bias = din("mlstm_i_bias", [4])
    mlstm_f_bias = din("mlstm_f_bias", [4])
    att_out_norm_w = din("att_out_norm_w", [512])
    mlstm_out_norm_w = din("mlstm_out_norm_w", [512])
    w_out = din("w_out", [D, D])
    mixer_norm_w = din("mixer_norm_w", [D])
    ffn_norm_w = din("ffn_norm_w", [D])
    w_ffn_up = din("w_ffn_up", [D, 2 * DFF])
    ffn_conv_w = din("ffn_conv_w", [3, 2 * DFF])
    ffn_conv_b = din("ffn_conv_b", [2 * DFF])
    w_ffn_down = din("w_ffn_down", [DFF, D])
    final_norm_w = din("final_norm_w", [D])
    out = nc.dram_tensor("out", [SEQ, D], F32, kind="ExternalOutput").ap()
    ybuf = nc.dram_tensor("ybuf", [D, SEQ], BF16, kind=("ExternalOutput" if debug else "Internal")).ap()
    r_ybuf = [Res() for _ in range(8)]
    dbg = {}

    def dout(name, shape, dt=F32):
        dbg[name] = nc.dram_tensor(name, shape, dt, kind="ExternalOutput").ap()
        return dbg[name]

    with ExitStack() as es:
        S = Sched(nc, es)

        def stop(tag):
            if stop_after == tag:
                S.barrier()
                S.dead = True
        sbt = lambda st, n, s, d: st.enter_context(nc.sbuf_tensor(n, s, d))
        pst = lambda st, n, s, d: st.enter_context(nc.psum_tensor(n, s, d))

        ident_b = sbt(es, "ident_b", [128, 128], BF16); r_idb = Res()
        colA = sbt(es, "colA", [128, 64], F32); r_colA = Res()
        colB = sbt(es, "colB", [128, 88], F32); r_colB = Res()
        colC = sbt(es, "colC", [128, 88], F32); r_colC = Res()
        FW = sbt(es, "FW", [128, D], F32); r_FW = Res()
        wdn = sbt(es, "wdn", [128, NCH, D], BF16); r_wdn = Res()
        wdn_f32 = wdn[:].rearrange("p m d -> p (m d)").bitcast(F32)
        c2 = ExitStack()
        ident_f = sbt(c2, "ident_f", [128, 128], F32); r_idf = Res()
        ones_f = sbt(c2, "ones_f", [128, 128], F32); r_ones = Res()
        zero_f = sbt(c2, "zero_f", [128, 256], F32); r_zero = Res()
        sel_b = sbt(c2, "sel_b", [128, 2, 128], BF16); r_sel = Res()
        maskf = sbt(c2, "maskf", [128, 256], F32); r_maskf = Res()
        maskb = sbt(c2, "maskb", [128, 256], BF16); r_maskb = Res()
        mask01 = sbt(c2, "mask01", [128, 128], F32); r_m01 = Res()
        MNW = sbt(c2, "MNW", [128, 512], F32); r_MNW = Res()
        gb = sbt(c2, "gb", [4, 2], F32); r_gb = Res()
        S.op("pool", lambda e: e.memset(ones_f[:], 1.0), writes=[r_ones])
        S.op("pool", lambda e: e.memset(zero_f[:], 0.0), writes=[r_zero])
        S.op("pool", lambda e: e.affine_select(out=ident_f[:], in_=ones_f[:], pattern=[[-1, 128]], compare_op=ALU.is_equal,
                                               fill=0.0, base=0, channel_multiplier=1), reads=[r_ones], writes=[r_idf])
        S.op("dve", lambda e: e.tensor_copy(out=ident_b[:], in_=ident_f[:]), reads=[r_idf], writes=[r_idb])
        S.op("dve", lambda e: e.memset(sel_b[:], 0.0), writes=[r_sel])
        S.op("dve", lambda e: e.memset(sel_b[0:64, 0, :], 1.0), writes=[r_sel])
        S.op("dve", lambda e: e.memset(sel_b[64:128, 1, :], 1.0), writes=[r_sel])
        S.op("pool", lambda e: e.affine_select(out=maskf[:, 0:128], in_=zero_f[:, 0:128], pattern=[[1, 128]], compare_op=ALU.is_ge,
                                               fill=-30000.0, base=0, channel_multiplier=-1), reads=[r_zero], writes=[r_maskf])
        S.op("pool", lambda e: e.affine_select(out=maskf[:, 128:256], in_=zero_f[:, 128:256], pattern=[[-1, 128]], compare_op=ALU.is_ge,
                                               fill=-30000.0, base=0, channel_multiplier=1), reads=[r_zero], writes=[r_maskf])
        S.op("dve", lambda e: e.tensor_copy(out=maskb[:], in_=maskf[:]), reads=[r_maskf], writes=[r_maskb])
        S.op("pool", lambda e: e.affine_select(out=mask01[:], in_=ones_f[:], pattern=[[1, 128]], compare_op=ALU.is_ge,
                                               fill=0.0, base=0, channel_multiplier=-1), reads=[r_ones], writes=[r_m01])

        with ExitStack() as p0:
            rowA = sbt(p0, "rowA", [64, 128], F32); r_rowA = Res()
            rowB = sbt(p0, "rowB", [88, 128], F32); r_rowB = Res()
            rowC = sbt(p0, "rowC", [88, 128], F32); r_rowC = Res()
            pcol = pst(p0, "pcol", [128, 512], F32); r_pcol = PRes()
            S.op("dve", lambda e: e.memset(rowA[:], 0.0), writes=[r_rowA])
            S.dma("sp", rowA[0:8, :], mixer_norm_w.rearrange("(c p) -> c p", p=128), writes=[r_rowA])
            S.dma("sp", rowA[8:16, :], ffn_norm_w.rearrange("(c p) -> c p", p=128), writes=[r_rowA])
            S.dma("sp", rowA[16:48, :], mlstm_conv_w.rearrange("j (c p) -> (j c) p", p=128), writes=[r_rowA])
            S.dma("sp", rowA[48:56, :], mlstm_conv_b.rearrange("(c p) -> c p", p=128), writes=[r_rowA])
            S.dma("sp", rowA[56:64, 0:64], att_out_norm_w.rearrange("(h p) -> h p", p=64), writes=[r_rowA])
            S.dma("sp", rowB[:, :], ffn_conv_w[0:2, :].rearrange("j (c p) -> (j c) p", p=128), writes=[r_rowB])
            S.dma("sp", rowC[0:44, :], ffn_conv_w[2:3, :].rearrange("j (c p) -> (j c) p", p=128), writes=[r_rowC])
            S.dma("sp", rowC[44:88, :], ffn_conv_b.rearrange("(c p) -> c p", p=128), writes=[r_rowC])
            S.dma("sp", FW[:], final_norm_w.partition_broadcast(128), writes=[r_FW])
            S.dma("sp", MNW[:], mlstm_out_norm_w.partition_broadcast(128), writes=[r_MNW])
            with nc.allow_non_contiguous_dma(reason="tiny gate bias"):
                S.dma("sp", gb[:, 0:1], mlstm_i_bias.rearrange("(p o) -> p o", o=1), writes=[r_gb])
                S.dma("sp", gb[:, 1:2], mlstm_f_bias.rearrange("(p o) -> p o", o=1), writes=[r_gb])
            S.op("pe", lambda e: e.transpose(out=pcol[:, 0:64], in_=rowA[:], identity=ident_f[0:64, 0:64]), reads=[r_rowA, r_idf], writes=[r_pcol])
            S.op("pe", lambda e: e.transpose(out=pcol[:, 64:152], in_=rowB[:], identity=ident_f[0:88, 0:88]), reads=[r_rowB, r_idf], writes=[r_pcol])
            S.op("pe", lambda e: e.transpose(out=pcol[:, 152:240], in_=rowC[:], identity=ident_f[0:88, 0:88]), reads=[r_rowC, r_idf], writes=[r_pcol])
            S.op("dve", lambda e: e.tensor_copy(out=colA[:], in_=pcol[:, 0:64]), reads=[r_pcol], writes=[r_colA])
            S.op("dve", lambda e: e.tensor_copy(out=colB[:], in_=pcol[:, 64:152]), reads=[r_pcol], writes=[r_colB])
            S.op("dve", lambda e: e.tensor_copy(out=colC[:], in_=pcol[:, 152:240]), reads=[r_pcol], writes=[r_colC])
            S.barrier()
        mnw = lambda c: colA[:, c:c + 1]
        fnw = lambda c: colA[:, 8 + c:9 + c]
        mcw = lambda j, c: colA[:, 16 + j * 8 + c:17 + j * 8 + c]
        mcb = lambda c: colA[:, 48 + c:49 + c]
        aow = lambda h: colA[0:64, 56 + h:57 + h]

        def fcw(j, ch):
            if j < 2:
                return colB[:, j * 44 + ch:j * 44 + ch + 1]
            return colC[:, ch:ch + 1]
        fcb = lambda ch: colC[:, 44 + ch:45 + ch]

        if debug:
            o_colA = dout("o_colA", [128, 64]); o_colB = dout("o_colB", [128, 88]); o_colC = dout("o_colC", [128, 88])
            S.dma("sp", o_colA[:, :], colA[:], reads=[r_colA]); S.dma("sp", o_colB[:, :], colB[:], reads=[r_colB]); S.dma("sp", o_colC[:, :], colC[:], reads=[r_colC])
            S.barrier()
        with ExitStack() as p12:
          if stop_after != "p0":
                xnT = sbt(p12, "xnT", [128, 8, SEQ], BF16)
                r_xnT = [Res() for _ in range(8)]

                with ExitStack() as ph:
                    xt = [sbt(ph, f"p1x{i}", [128, 4, D], F32) for i in range(2)]; r_xt = [Res(), Res()]
                    xb = [sbt(ph, f"p1xb{i}", [128, 4, D], BF16) for i in range(2)]; r_xb = [Res(), Res()]
                    junk = [sbt(ph, f"p1junk{i}", [128, D], BF16) for i in range(2)]; r_junk = [Res(), Res()]
                    ss = [sbt(ph, f"p1ss{i}", [128, 4], F32) for i in range(2)]; r_ss = [Res(), Res()]
                    rs = [sbt(ph, f"p1rs{i}", [128, 4], F32) for i in range(2)]; r_rs = [Res(), Res()]
                    pT_ = [pst(ph, f"p1pT{i}", [128, 1024], BF16) for i in range(4)]; r_pT = [PRes() for _ in range(4)]
                    pT = [t[:, 0:512] for t in pT_]
                    for tt in range(8):
                        b = tt % 2
                        S.dma("sp", xt[b][:], x[tt * 512:(tt + 1) * 512, :].rearrange("(s p) d -> p s d", p=128), writes=[r_xt[b]])
                        for s in range(4):
                            S.op("act", lambda e: e.activation(out=junk[s % 2][:], in_=xt[b][:, s, :], func=AF.Square, scale=1.0 / 32.0,
                                                               accum_out=ss[b][:, s:s + 1]), reads=[r_xt[b]], writes=[r_junk[s % 2]], pwrites=[r_ss[b]])
                        S.op("act", lambda e: e.activation(out=rs[b][:], in_=ss[b][:], func=AF.Ln, bias=EPS, scale=1.0), reads=[r_ss[b]], writes=[r_rs[b]])
                        S.op("act", lambda e: e.activation(out=rs[b][:], in_=rs[b][:], func=AF.Exp, scale=-0.5), reads=[r_rs[b]], writes=[r_rs[b]])
                        for s in range(4):
                            eng = "dve"
                            S.op(eng, lambda e: e.tensor_scalar(out=xb[b][:, s, :], in0=xt[b][:, s, :], scalar1=rs[b][:, s:s + 1], scalar2=None,
                                                                op0=ALU.mult), reads=[r_xt[b], r_rs[b]], pwrites=[r_xb[b]])
                        for c in range(8):
                            k = (tt * 8 + c) % 4
                            for s in range(4):
                                S.op("pe", lambda e: e.transpose(out=pT[k][:, s * 128:(s + 1) * 128], in_=xb[b][:, s, c * 128:(c + 1) * 128],
                                                                 identity=ident_b[:]), reads=[r_xb[b], r_idb], writes=[r_pT[k]])
                            if c % 2 == 0:
                                S.op("dve", lambda e: e.tensor_scalar(out=xnT[:, c, tt * 512:(tt + 1) * 512], in0=pT[k], scalar1=mnw(c), scalar2=None,
                                                                      op0=ALU.mult), reads=[r_pT[k], r_colA], pwrites=[r_xnT[tt]])
                            else:
                                S.op("act", lambda e: e.activation(out=xnT[:, c, tt * 512:(tt + 1) * 512], in_=pT[k], func=AF.Copy, scale=mnw(c)),
                                     reads=[r_pT[k], r_colA], pwrites=[r_xnT[tt]])
                    S.barrier()
                if debug:
                    o_xnT = dout("o_xnT", [D, SEQ], BF16)
                    for c in range(8):
                        S.dma("sp", o_xnT[c * 128:(c + 1) * 128, :], xnT[:, c, :], reads=r_xnT)
                    S.barrier()

                if stop_after not in ("p1", "mlg", "mlh0", "mlh1", "mlh2", "mlh"):
                    with ExitStack() as ph:
                        wqkv = [sbt(ph, f"wqkv{i}", [128, 8, 3, 128], BF16) for i in range(2)]; r_wqkv = [Res(), Res()]
                        QT = sbt(ph, "QT", [128, SEQ], BF16); r_QT = Res()
                        KT = sbt(ph, "KT", [128, SEQ], BF16); r_KT = Res()
                        VT = sbt(ph, "VT", [128, SEQ], BF16); r_VT = Res()
                        sq = sbt(ph, "sq", [128, SEQ], BF16); r_sq = Res()
                        mx = sbt(ph, "mx", [128, 4, 8], F32); r_mx = Res()
                        st = sbt(ph, "st", [128, 4], F32); r_st = Res()
                        nbias = sbt(ph, "nbias", [128, 2], F32); r_nb = Res()
                        mask2 = sbt(ph, "mask2", [128, 2, 256], BF16); r_mask2 = Res()
                        Vaug = sbt(ph, "Vaug", [128, 32, 2, 128], BF16); r_Vaug = Res()
                        acc = [wdn_f32[:, 0:SEQ], wdn_f32[:, SEQ:2 * SEQ]]; r_acc = [Res(), Res()]
                        NPT = 4
                        PT = [sbt(ph, f"PT{i}", [128, 512], BF16) for i in range(NPT)]; r_PT = [Res() for _ in range(NPT)]
                        e_n2 = sbt(ph, "e_n2", [64, 1024], BF16); r_en2 = Res()
                        e_d2 = wdn_f32[0:64, 2 * SEQ:2 * SEQ + 1024]; r_ed2 = Res()
                        e_rs = wdn_f32[0:64, 2 * SEQ + 1024:2 * SEQ + 2048]; r_ers = Res()
                        ones_b = sbt(ph, "ones_b", [64, 64], BF16); r_onesb = Res()
                        yT = [sbt(ph, f"yTa{i}", [64, SEQ], BF16) for i in range(2)]; r_yT = [Res(), Res()]
                        pJ = [pst(ph, f"aJ{i}", [128, 512], F32) for i in range(2)]; r_pJ = [PRes(), PRes()]
                        pS = [pst(ph, f"aS{i}", [128, 512], F32) for i in range(2)]; r_pS = [PRes(), PRes()]
                        pO = [pst(ph, f"aO{i}", [128, 512], F32) for i in range(2)]; r_pO = [PRes(), PRes()]
                        pV_ = [pst(ph, f"aV{i}", [128, 1024], BF16) for i in range(2)]; r_pV = [PRes(), PRes()]
                        pV = [t[:, 0:512] for t in pV_]
                        S.op("pool", lambda e: e.memset(Vaug[:, :, :, 64:128], 1.0), writes=[r_Vaug])
                        S.op("dve", lambda e: e.tensor_copy(out=ones_b[:], in_=ones_f[0:64, 0:64]), reads=[r_ones], writes=[r_onesb])
                        for a in range(2):
                            S.op("dve", lambda e: e.tensor_scalar(out=mask2[:, a, :], in0=maskf[:], scalar1=0.0, scalar2=None, op0=ALU.is_equal),
                                 reads=[r_maskf], pwrites=[r_mask2])
                        nj = 0
                        cnt_a = {"V": 0, "S": 0, "O": 0}
                        pending_epi = []
                        for pr in range(4):
                            wb = wqkv[pr % 2]; rwb = r_wqkv[pr % 2]
                            for j in range(3):
                                S.dma("pool", wb[:, :, j, :], w_in[:, j * 512 + pr * 128:j * 512 + (pr + 1) * 128].rearrange("(c p) m -> p c m", p=128),
                                      pwrites=[rwb])
                            for j, (dst, rdst) in enumerate(((QT, r_QT), (KT, r_KT), (VT, r_VT))):
                                for tt in range(8):
                                    k = nj % 2; nj += 1
                                    for c in range(8):
                                        S.op("pe", lambda e: e.matmul(pJ[k][:], lhsT=wb[:, c, j, :], rhs=xnT[:, c, tt * 512:(tt + 1) * 512],
                                                                      start=(c == 0), stop=(c == 7)), reads=[rwb, r_xnT[tt]], writes=[r_pJ[k]])
                                    if nj % 2 == 0:
                                        S.op("dve", lambda e: e.tensor_copy(out=dst[:, tt * 512:(tt + 1) * 512], in_=pJ[k][:]), reads=[r_pJ[k]], pwrites=[rdst])
                                    else:
                                        S.op("act", lambda e: e.activation(out=dst[:, tt * 512:(tt + 1) * 512], in_=pJ[k][:], func=AF.Copy),
                                             reads=[r_pJ[k]], pwrites=[rdst])
                                    if pending_epi and (j * 8 + tt) % 3 == 2:
                                        pending_epi.pop(0)()
                            for qi, (src, rsrc) in enumerate(((QT, r_QT), (KT, r_KT))):
                                S.op("dve", lambda e: e.tensor_tensor(out=sq[:], in0=src[:], in1=src[:], op=ALU.mult), reads=[rsrc], writes=[r_sq])
                                for hh in range(2):
                                    for tt in range(8):
                                        k = nj % 2; nj += 1
                                        S.op("pe", lambda e: e.matmul(pJ[k][:], lhsT=sel_b[:, hh, :], rhs=sq[:, tt * 512:(tt + 1) * 512], start=True, stop=True),
                                             reads=[r_sel, r_sq], writes=[r_pJ[k]])
                                        S.op("dve", lambda e: e.tensor_reduce(out=mx[:, qi * 2 + hh, tt:tt + 1], in_=pJ[k][:], axis=AX.X, op=ALU.max),
                                             reads=[r_pJ[k]], pwrites=[r_mx])
                            S.op("dve", lambda e: e.tensor_reduce(out=st[:], in_=mx[:], axis=AX.X, op=ALU.max), reads=[r_mx], writes=[r_st])
                            S.op("dve", lambda e: e.tensor_tensor(out=nbias[:], in0=st[:, 0:2], in1=st[:, 2:4], op=ALU.mult), reads=[r_st], writes=[r_nb])
                            S.op("act", lambda e: e.activation(out=nbias[:], in_=nbias[:], func=AF.Ln), reads=[r_nb], writes=[r_nb])
                            S.op("act", lambda e: e.activation(out=nbias[:], in_=nbias[:], func=AF.Exp, scale=0.5), reads=[r_nb], writes=[r_nb])
                            S.op("dve", lambda e: e.tensor_scalar(out=nbias[:], in0=nbias[:], scalar1=-0.125 * 1.02, scalar2=None, op0=ALU.mult),
                                 reads=[r_nb], writes=[r_nb])
                            for gi, d in enumerate(GROUPS):
                                nb_ = 32 // d
                                for kt0 in range(0, 32, 4):
                                    k = cnt_a["V"] % 2; cnt_a["V"] += 1
                                    for sl in range(4):
                                        kt = kt0 + sl
                                        r_, b_ = kt // nb_, kt % nb_
                                        t0 = 128 * b_ * d + r_
                                        S.op("pe", lambda e: e.transpose(out=pV[k][:, sl * 128:(sl + 1) * 128], in_=VT[:, t0:t0 + 127 * d + 1:d],
                                                                         identity=ident_b[:]), reads=[r_VT, r_idb], writes=[r_pV[k]])
                                    src = pV[k].rearrange("p (s h e) -> p s h e", s=4, h=2)
                                    if cnt_a["V"] % 2 == 0:
                                        S.op("dve", lambda e: e.tensor_copy(out=Vaug[:, kt0:kt0 + 4, :, 0:64], in_=src), reads=[r_pV[k]], pwrites=[r_Vaug])
                                    else:
                                        S.op("act", lambda e: e.activation(out=Vaug[:, kt0:kt0 + 4, :, 0:64], in_=src, func=AF.Copy),
                                             reads=[r_pV[k]], pwrites=[r_Vaug])
                                units = [(hh, r_, b0) for hh in range(2) for r_ in range(d) for b0 in range(0, nb_, 2)]

                                def emit_S(u):
                                    hh, r_, b0 = u
                                    hs = slice(hh * 64, (hh + 1) * 64)
                                    bank = cnt_a["S"] % 2; pi = cnt_a["S"] % NPT; cnt_a["S"] += 1
                                    for jj in range(2):
                                        b_ = b0 + jj
                                        N = 256 if b_ + 1 < nb_ else 128
                                        t0 = 128 * b_ * d + r_
                                        ks = slice(t0, t0 + 127 * d + 1, d)
                                        qs = slice(t0, t0 + (N - 1) * d + 1, d)
                                        S.op("pe", lambda e: e.matmul(pS[bank][:, jj * 256:jj * 256 + N], lhsT=KT[hs, ks], rhs=QT[hs, qs], start=True, stop=True),
                                             reads=[r_KT, r_QT], writes=[r_pS[bank]])
                                    return bank, pi

                                def emit_E(u, info):
                                    hh, r_, b0 = u
                                    bank, pi = info
                                    S.op("act", lambda e: e.activation(out=PT[pi][:], in_=pS[bank][:], func=AF.Exp, scale=0.125, bias=nbias[:, hh:hh + 1]),
                                         reads=[r_pS[bank], r_nb], writes=[r_PT[pi]])
                                    S.op("dve", lambda e: e.tensor_tensor(out=PT[pi][:], in0=PT[pi][:], in1=mask2[:].rearrange("p a n -> p (a n)"), op=ALU.mult),
                                         reads=[r_PT[pi], r_mask2], writes=[r_PT[pi]])

                                def emit_PV(u, info, pinfo, slot0):
                                    hh, r_, b0 = u
                                    bank, pi = info
                                    ko = cnt_a["O"] % 2
                                    for jj in range(2):
                                        b_ = b0 + jj
                                        kt = r_ * nb_ + b_
                                        osl = pO[ko][:, (slot0 + jj) * 128:(slot0 + jj + 1) * 128]
                                        if b_ > 0:
                                            if jj == 0:
                                                ppi = pinfo[1]
                                                prhs = PT[ppi][:, 384:512]; rprev = r_PT[ppi]
                                            else:
                                                prhs = PT[pi][:, 128:256]; rprev = r_PT[pi]
                                            S.op("pe", lambda e: e.matmul(osl, lhsT=Vaug[:, kt - 1, hh, :], rhs=prhs, start=True, stop=False),
                                                 reads=[r_Vaug, rprev], writes=[r_pO[ko]])
                                        S.op("pe", lambda e: e.matmul(osl, lhsT=Vaug[:, kt, hh, :], rhs=PT[pi][:, jj * 256:jj * 256 + 128], start=(b_ == 0), stop=True),
                                             reads=[r_Vaug, r_PT[pi]], writes=[r_pO[ko]])

                                def emit_acc(u, ko):
                                    hh, r_, b0 = u
                                    ah = acc[hh]; rah = r_acc[hh]
                                    b_ = b0 + 1
                                    if d == 16:
                                        dst = ah.rearrange("p (i dd) -> p dd i", dd=16)[:, r_ - 1:r_ + 1, :]
                                        src = pO[ko][:].rearrange("p (a i) -> p a i", a=2)
                                    else:
                                        bs = b_ - 3
                                        ts = 128 * bs * d + r_
                                        dst = ah[:, ts:ts + 511 * d + 1:d]
                                        src = pO[ko][:]
                                    if gi == 0:
                                        S.op("dve", lambda e: e.tensor_copy(out=dst, in_=src), reads=[r_pO[ko]], pwrites=[rah])
                                    else:
                                        S.op("dve", lambda e: e.tensor_tensor(out=dst, in0=dst, in1=src, op=ALU.add), reads=[r_pO[ko], rah], pwrites=[rah])

                                infos = {}
                                infos[0] = emit_S(units[0])
                                slot = 0
                                for ui, u in enumerate(units):
                                    if ui + 1 < len(units):
                                        infos[ui + 1] = emit_S(units[ui + 1])
                                    emit_E(u, infos[ui])
                                    emit_PV(u, infos[ui], infos.get(ui - 1), slot)
                                    slot += 2
                                    if slot == 4:
                                        slot = 0
                                        emit_acc(u, cnt_a["O"] % 2)
                                        cnt_a["O"] += 1
                            def make_epi(pr_, hh, t4):
                                def run():
                                    h = pr_ * 2 + hh
                                    ah = acc[hh]; rah = r_acc[hh]
                                    yb = yT[h % 2]; ryb = r_yT[h % 2]
                                    cs = slice(t4 * 1024, (t4 + 1) * 1024)
                                    S.op("act", lambda e: e.activation(out=e_n2[:], in_=ah[0:64, cs], func=AF.Square), reads=[rah], writes=[r_en2])
                                    S.op("act", lambda e: e.activation(out=e_d2, in_=ah[64:128, cs], func=AF.Square, scale=float(np.sqrt(EPS))), reads=[rah], writes=[r_ed2])
                                    for a in range(2):
                                        S.op("pe", lambda e: e.matmul(pO[a][0:64, :], lhsT=ones_b[:], rhs=e_n2[:, a * 512:(a + 1) * 512], start=True, stop=True),
                                             reads=[r_onesb, r_en2], writes=[r_pO[a]])
                                        S.op("dve", lambda e: e.scalar_tensor_tensor(out=e_rs[:, a * 512:(a + 1) * 512], in0=pO[a][0:64, :], scalar=1.0 / 64.0,
                                                                                     in1=e_d2[:, a * 512:(a + 1) * 512], op0=ALU.mult, op1=ALU.add),
                                             reads=[r_pO[a], r_ed2], pwrites=[r_ers])
                                    S.op("act", lambda e: e.activation(out=e_rs, in_=e_rs, func=AF.Ln), reads=[r_ers], writes=[r_ers])
                                    S.op("act", lambda e: e.activation(out=e_rs, in_=e_rs, func=AF.Exp, scale=-0.5), reads=[r_ers], writes=[r_ers])
                                    S.op("dve", lambda e: e.scalar_tensor_tensor(out=yb[:, cs], in0=ah[0:64, cs], scalar=aow(h), in1=e_rs, op0=ALU.mult, op1=ALU.mult),
                                         reads=[rah, r_ers, r_colA], pwrites=[ryb])
                                    if t4 == 3:
                                        S.dma("sp", ybuf[h * 64:(h + 1) * 64, :], yb[:], reads=[ryb], writes=[r_ybuf[h // 2]])
                                return run
                            pending_epi.extend(make_epi(pr, hh, t4) for hh in range(2) for t4 in range(4))
                        for f in pending_epi:
                            f()
                        S.barrier()

                if stop_after not in ("p1", "att"):
                    with ExitStack() as ph:
                        wexpT = sbt(ph, "wexpT", [128, 128], F32); r_wexpT = Res()
                        floorT = sbt(ph, "floorT", [128, 128], F32); r_floorT = Res()
                        decB = sbt(ph, "decB", [128, 128], F32); r_decB = Res()

                        with ExitStack() as pg:
                            wg = sbt(pg, "wg", [128, 8, 8], BF16); r_wg = Res()
                            pG = pst(pg, "mG", [128, 512], F32); r_pG = PRes()
                            gA = sbt(pg, "gA", [4, SEQ], F32); r_gA = Res()
                            gB = sbt(pg, "gB", [4, SEQ], F32); r_gB = Res()
                            gC = sbt(pg, "gC", [4, SEQ], F32); r_gC = Res()
                            gD = sbt(pg, "gD", [4, SEQ], F32); r_gD = Res()
                            sm = sbt(pg, "sm", [4, 12, 32], F32); r_sm = Res()
                            nfb = sbt(pg, "nfb", [4, 1], F32); r_nfb = Res()
                            dmask = sbt(pg, "dmask", [4, 128], F32); r_dmask = Res()
                            decD = sbt(pg, "decD", [4, 128], F32); r_decD = Res()
                            wgf = sbt(pg, "wgf", [128, 8, 8], F32); r_wgf = Res()
                            with nc.allow_non_contiguous_dma(reason="gate weight columns (32B runs)"):
                                S.dma("sp", wgf[:], w_in[:, 3584:3592].rearrange("(c p) m -> p c m", p=128), writes=[r_wgf])
                            S.op("dve", lambda e: e.tensor_copy(out=wg[:], in_=wgf[:]), reads=[r_wgf], writes=[r_wg])
                            S.op("dve", lambda e: e.memset(gC[:], 1.0), writes=[r_gC])
                            S.op("dve", lambda e: e.memset(gC[:].rearrange("p (j t) -> p j t", t=128)[:, :, 0:1], 0.0), writes=[r_gC])
                            S.op("dve", lambda e: e.tensor_scalar(out=nfb[:], in0=gb[:, 1:2], scalar1=-1.0, scalar2=None, op0=ALU.mult), reads=[r_gb], writes=[r_nfb])
                            S.op("pool", lambda e: e.affine_select(out=dmask[:].rearrange("p (h j) -> p h j", h=4), in_=ones_f[0:4, :].rearrange("p (h j) -> p h j", h=4),
                                                                   pattern=[[-1, 4], [0, 32]], compare_op=ALU.is_equal, fill=0.0, base=0, channel_multiplier=1),
                                 reads=[r_ones], writes=[r_dmask])
                            for tt in range(8):
                                cs = slice(tt * 512, (tt + 1) * 512)
                                for c in range(8):
                                    S.op("pe", lambda e: e.matmul(pG[0:4, :], lhsT=wg[:, c, 0:4], rhs=xnT[:, c, cs], start=(c == 0), stop=(c == 7)),
                                         reads=[r_wg, r_xnT[tt]], writes=[r_pG])
                                S.op("act", lambda e: e.activation(out=gA[:, cs], in_=pG[0:4, :], func=AF.Identity, bias=gb[:, 0:1], scale=1.0),
                                     reads=[r_pG, r_gb], pwrites=[r_gA])
                                for c in range(8):
                                    S.op("pe", lambda e: e.matmul(pG[0:4, :], lhsT=wg[:, c, 4:8], rhs=xnT[:, c, cs], start=(c == 0), stop=(c == 7)),
                                         reads=[r_wg, r_xnT[tt]], writes=[r_pG])
                                S.op("act", lambda e: e.activation(out=gB[:, cs], in_=pG[0:4, :], func=AF.Exp, bias=nfb[:, 0:1], scale=-1.0),
                                     reads=[r_pG, r_nfb], pwrites=[r_gB])
                            S.op("act", lambda e: e.activation(out=gB[:], in_=gB[:], func=AF.Ln, bias=1.0, scale=1.0), reads=[r_gB], writes=[r_gB])
                            S.op("dve", lambda e: e.tensor_scalar(out=gD[:], in0=gB[:], scalar1=-1.0, scalar2=None, op0=ALU.mult), reads=[r_gB], writes=[r_gD])
                            S.op("dve", lambda e: e.tensor_tensor_scan(out=gB[:], data0=gC[:], data1=gD[:], initial=0.0, op0=ALU.mult, op1=ALU.add),
                                 reads=[r_gC, r_gD], writes=[r_gB])
                            S.op("dve", lambda e: e.tensor_tensor(out=gA[:], in0=gA[:], in1=gB[:], op=ALU.subtract), reads=[r_gA, r_gB], writes=[r_gA])
                            rmax, gch, ginc, gx, rho, mun, mu, u_, dec_, tmp_ = [sm[:, i, :] for i in range(10)]
                            S.op("dve", lambda e: e.tensor_reduce(out=rmax, in_=gA[:].rearrange("p (j t) -> p j t", t=128), axis=AX.X, op=ALU.max),
                                 reads=[r_gA], writes=[r_sm])
                            S.op("dve", lambda e: e.tensor_copy(out=gch, in_=gB[:].rearrange("p (j t) -> p j t", t=128)[:, :, 127]), reads=[r_gB], writes=[r_sm])
                            S.op("dve", lambda e: e.memset(tmp_, 1.0), writes=[r_sm])
                            S.op("dve", lambda e: e.tensor_tensor_scan(out=ginc, data0=tmp_, data1=gch, initial=0.0, op0=ALU.mult, op1=ALU.add), reads=[r_sm], writes=[r_sm])
                            S.op("dve", lambda e: e.tensor_tensor(out=gx, in0=ginc, in1=gch, op=ALU.subtract), reads=[r_sm], writes=[r_sm])
                            S.op("dve", lambda e: e.tensor_tensor(out=rho, in0=rmax, in1=gx, op=ALU.subtract), reads=[r_sm], writes=[r_sm])
                            S.op("dve", lambda e: e.tensor_tensor_scan(out=mun, data0=rho, data1=rho, initial=0.0, op0=ALU.max, op1=ALU.max), reads=[r_sm], writes=[r_sm])
                            S.op("dve", lambda e: e.memset(mu, 0.0), writes=[r_sm])
                            S.op("dve", lambda e: e.tensor_copy(out=sm[:, 6, 1:32], in_=sm[:, 5, 0:31]), reads=[r_sm], writes=[r_sm])
                            S.op("dve", lambda e: e.tensor_tensor(out=u_, in0=gx, in1=mun, op=ALU.add), reads=[r_sm], writes=[r_sm])
                            S.op("dve", lambda e: e.tensor_tensor(out=dec_, in0=mu, in1=mun, op=ALU.subtract), reads=[r_sm], writes=[r_sm])
                            S.op("act", lambda e: e.activation(out=dec_, in_=dec_, func=AF.Exp), reads=[r_sm], writes=[r_sm])
                            ub = sm[:, 7, :].unsqueeze(2).to_broadcast([4, 32, 128])
                            S.op("dve", lambda e: e.tensor_tensor(out=gA[:].rearrange("p (j t) -> p j t", t=128), in0=gA[:].rearrange("p (j t) -> p j t", t=128),
                                                                  in1=ub, op=ALU.subtract), reads=[r_gA, r_sm], writes=[r_gA])
                            S.op("dve", lambda e: e.tensor_scalar(out=gA[:], in0=gA[:], scalar1=-0.5 * float(np.log(128.0)), scalar2=None, op0=ALU.add),
                                 reads=[r_gA], writes=[r_gA])
                            S.op("act", lambda e: e.activation(out=gA[:], in_=gA[:], func=AF.Exp), reads=[r_gA], writes=[r_gA])
                            S.op("dve", lambda e: e.tensor_tensor(out=gB[:].rearrange("p (j t) -> p j t", t=128), in0=gB[:].rearrange("p (j t) -> p j t", t=128),
                                                                  in1=ub, op=ALU.add), reads=[r_gB, r_sm], writes=[r_gB])
                            S.op("act", lambda e: e.activation(out=gB[:], in_=gB[:], func=AF.Exp, scale=-1.0), reads=[r_gB], writes=[r_gB])
                            for src, rsrc, dstT, rdstT in ((gA, r_gA, wexpT, r_wexpT), (gB, r_gB, floorT, r_floorT)):
                                for j in range(32):
                                    S.op("pe", lambda e: e.matmul(pG[:, j * 4:(j + 1) * 4], lhsT=src[0:4, j * 128:(j + 1) * 128], rhs=ident_f[0:4, 0:4], start=True, stop=True),
                                         reads=[rsrc, r_idf], writes=[r_pG])
                                S.op("dve", lambda e: e.tensor_copy(out=dstT[:].rearrange("p (h j) -> p j h", h=4), in_=pG[:, 0:128].rearrange("p (j h) -> p j h", h=4)),
                                     reads=[r_pG], writes=[rdstT])
                            S.op("dve", lambda e: e.tensor_tensor(out=decD[:].rearrange("p (h j) -> p h j", h=4), in0=dmask[:].rearrange("p (h j) -> p h j", h=4),
                                                                  in1=sm[:, 8, :].unsqueeze(1).to_broadcast([4, 4, 32]), op=ALU.mult), reads=[r_dmask, r_sm], writes=[r_decD])
                            S.op("pe", lambda e: e.matmul(pG[:, 0:128], lhsT=ones_f[0:4, :], rhs=decD[:], start=True, stop=True), reads=[r_ones, r_decD], writes=[r_pG])
                            S.op("dve", lambda e: e.tensor_copy(out=decB[:], in_=pG[:, 0:128]), reads=[r_pG], writes=[r_decB])
                            if debug:
                                o_wexp = dout("o_wexp", [4, SEQ]); o_floor = dout("o_floor", [4, SEQ]); o_sm = dout("o_sm", [4, 12 * 32])
                                o_wexpT = dout("o_wexpT", [128, 128]); o_decB = dout("o_decB", [128, 128])
                                S.dma("sp", o_wexp[:, :], gA[:], reads=[r_gA]); S.dma("sp", o_floor[:, :], gB[:], reads=[r_gB])
                                S.dma("sp", o_sm[:, :], sm[:].rearrange("p a b -> p (a b)"), reads=[r_sm])
                                S.dma("sp", o_wexpT[:, :], wexpT[:], reads=[r_wexpT]); S.dma("sp", o_decB[:, :], decB[:], reads=[r_decB])
                            S.barrier()

                        stop("mlg")
                        WD = sbt(ph, "WD", [128, 128], F32); r_WD = Res()
                        fl2 = sbt(ph, "fl2", [128, 128], F32); r_fl2 = Res()
                        S.op("dve", lambda e: e.tensor_tensor(out=WD[:, 0:127], in0=wexpT[:, 0:127], in1=decB[:, 1:128], op=ALU.mult), reads=[r_wexpT, r_decB], writes=[r_WD])
                        S.op("dve", lambda e: e.tensor_tensor(out=fl2[:], in0=floorT[:], in1=floorT[:], op=ALU.mult), reads=[r_floorT], writes=[r_fl2])
                        HS = SEQ // 2
                        wqk = [sbt(ph, f"wqk{i}", [128, 8, 2, 128], BF16) for i in range(2)]; r_wqk = [Res(), Res()]
                        wvo = [sbt(ph, f"wvo{i}", [128, 8, 2, 128], BF16) for i in range(2)]; r_wvo = [Res(), Res()]
                        xpre = sbt(ph, "xpre", [128, 3 + HS], F32); r_xpre = Res(); r_xh = Res()
                        cv = sbt(ph, "cv", [128, HS], F32); r_cv = Res()
                        qkT = sbt(ph, "qkT", [128, 2, SEQ], BF16); r_qkT = [Res(), Res()]
                        Vm = sbt(ph, "Vm", [128, 32, 132], BF16); r_Vm = Res()
                        SG = sbt(ph, "SG", [128, 32, 128], BF16); r_SG = Res()
                        sgt = [sbt(ph, f"sgt{i}", [128, 2, 128], F32) for i in range(2)]; r_sgt = [Res(), Res()]
                        Kw = sbt(ph, "Kw", [128, 32, 128], BF16); r_Kw = Res()
                        yTm = sbt(ph, "yTm", [128, SEQ], BF16); r_yTm = Res()
                        Cst = sbt(ph, "Cst", [128, 129], F32); r_C = Res()
                        Cbf = sbt(ph, "Cbf", [128, 132], BF16); r_Cb = Res()
                        PTm = [sbt(ph, f"PTm{i}", [128, 128], BF16) for i in range(2)]; r_PTm = [Res(), Res()]
                        ytl = [sbt(ph, f"ytl{i}", [128, 128], BF16) for i in range(2)]; r_ytl = [Res(), Res()]
                        NSC = 4
                        sc = [sbt(ph, f"msc{i}", [128, 8], F32) for i in range(NSC)]; r_sc = [[Res() for _ in range(4)] for _ in range(NSC)]
                        junkm = [sbt(ph, f"junkm{i}", [128, 128], BF16) for i in range(2)]; r_junkm = [Res(), Res()]
                        pP = [pst(ph, f"mP{i}", [128, 512], F32) for i in range(2)]; r_pP = [PRes(), PRes()]
                        pST_ = [pst(ph, f"mST{i}", [128, 512], F32) for i in range(2)]; r_pST = [PRes(), PRes()]
                        pST = [t[:, 0:128] for t in pST_]
                        pA_ = [pst(ph, f"mA{i}", [128, 512], F32) for i in range(2)]
                        pA = [pA_[0][:, 0:129], pA_[1][:, 0:129], pP[0][:, 0:129]]; r_pA = [PRes(), PRes(), r_pP[0]]
                        pC_ = pst(ph, "mC", [128, 512], F32); r_pC = PRes()
                        pC = pC_[:, 0:129]
                        pTb_ = pst(ph, "mTb", [128, 1024], BF16); r_pTb = [PRes()]
                        pTb = [pTb_[:, 0:512]]
                        S.op("pool", lambda e: e.memset(Vm[:, :, 128:129], 1.0), writes=[r_Vm])
                        cnt_m = {"pj": 0}
                        for h in range(4):
                            wq_ = wqk[h % 2]; rwq = r_wqk[h % 2]
                            wv_ = wvo[h % 2]; rwv = r_wvo[h % 2]
                            for j in range(2):
                                c0 = 1536 + j * 512 + h * 128
                                S.dma("pool", wq_[:, :, j, :], w_in[:, c0:c0 + 128].rearrange("(c p) m -> p c m", p=128), pwrites=[rwq])
                                c1 = 2560 + j * 512 + h * 128
                                S.dma("pool", wv_[:, :, j, :], w_in[:, c1:c1 + 128].rearrange("(c p) m -> p c m", p=128), pwrites=[rwv])
                            if h == 1:
                                for m in range(NCH):
                                    S.dma("pool", wdn[:, m, :], w_ffn_down[m * 128:(m + 1) * 128, :], pwrites=[r_wdn])
                            def vo_tiles(jt0, jt1):
                                for jt in range(jt0, jt1, 2):
                                    k = cnt_m["pj"] % 2; cnt_m["pj"] += 1
                                    for a in range(2):
                                        tsl = slice((jt + a) * 128, (jt + a + 1) * 128)
                                        for c in range(8):
                                            S.op("pe", lambda e: e.matmul(pP[k][:, a * 256:(a + 1) * 256], lhsT=xnT[:, c, tsl], rhs=wv_[:, c, :, :].rearrange("p a b -> p (a b)"),
                                                                          start=(c == 0), stop=(c == 7)), reads=[rwv, r_xnT[(jt + a) // 4]], writes=[r_pP[k]])
                                    pv = pP[k][:].rearrange("p (a j e) -> p a j e", a=2, j=2)
                                    S.op("act", lambda e: e.activation(out=Vm[:, jt:jt + 2, 0:128], in_=pv[:, :, 0, :], func=AF.Copy), reads=[r_pP[k]], pwrites=[r_Vm])
                                    sg_ = sgt[(jt // 2) % 2]; rsg = r_sgt[(jt // 2) % 2]
                                    S.op("act", lambda e: e.activation(out=sg_[:], in_=pv[:, :, 1, :], func=AF.Sigmoid), reads=[r_pP[k]], writes=[rsg])
                                    for a in range(2):
                                        S.op("pool", lambda e: e.tensor_tensor(out=SG[:, jt + a, :], in0=sg_[:, a, :], in1=MNW[:, h * 128:(h + 1) * 128], op=ALU.mult),
                                             reads=[rsg, r_MNW], pwrites=[r_SG])

                            for step, (j, half) in enumerate(((0, 0), (0, 1), (1, 0), (1, 1))):
                                ch = j * 4 + h
                                if half == 0:
                                    S.op("dve", lambda e: e.memset(xpre[:, 0:3], 0.0), writes=[r_xh])
                                else:
                                    S.op("dve", lambda e: e.tensor_copy(out=xpre[:, 0:3], in_=xpre[:, HS:HS + 3]), reads=[r_xpre], writes=[r_xh])
                                for t4 in range(4):
                                    tt = half * 4 + t4
                                    k = cnt_m["pj"] % 2; cnt_m["pj"] += 1
                                    for c in range(8):
                                        S.op("pe", lambda e: e.matmul(pP[k][:], lhsT=wq_[:, c, j, :], rhs=xnT[:, c, tt * 512:(tt + 1) * 512], start=(c == 0), stop=(c == 7)),
                                             reads=[rwq, r_xnT[tt]], writes=[r_pP[k]])
                                    S.op("act", lambda e: e.activation(out=xpre[:, 3 + t4 * 512:3 + (t4 + 1) * 512], in_=pP[k][:], func=AF.Copy),
                                         reads=[r_pP[k]], writes=([r_xpre] if t4 == 0 else []), pwrites=([] if t4 == 0 else [r_xpre]))
                                    S.op("act", lambda e: e.activation(out=cv[:, t4 * 512:(t4 + 1) * 512], in_=pP[k][:], func=AF.Identity, scale=mcw(3, ch), bias=mcb(ch)),
                                         reads=[r_pP[k], r_colA], writes=([r_cv] if t4 == 0 else []), pwrites=([] if t4 == 0 else [r_cv]))
                                for tap in range(3):
                                    S.op("dve", lambda e: e.scalar_tensor_tensor(out=cv[:], in0=xpre[:, tap:tap + HS], scalar=mcw(tap, ch), in1=cv[:], op0=ALU.mult, op1=ALU.add),
                                         reads=[r_xpre, r_xh, r_cv, r_colA], writes=[r_cv])
                                vo_tiles(step * 8, step * 8 + 8)
                                S.op("act", lambda e: e.activation(out=qkT[:, j, half * HS:(half + 1) * HS], in_=cv[:], func=AF.Silu), reads=[r_cv], pwrites=[r_qkT[j]])
                            for j0 in range(0, 32, 4):
                                for a in range(4):
                                    j = j0 + a
                                    S.op("pe", lambda e: e.transpose(out=pTb[0][:, a * 128:(a + 1) * 128], in_=qkT[:, 1, j * 128:(j + 1) * 128], identity=ident_b[:]),
                                         reads=[r_qkT[1], r_idb], writes=[r_pTb[0]])
                                for a in range(4):
                                    j = j0 + a
                                    col = h * 32 + j
                                    if j == 31:
                                        continue
                                    if a % 2 == 0:
                                        S.op("dve", lambda e: e.tensor_scalar(out=Kw[:, j, :], in0=pTb[0][:, a * 128:(a + 1) * 128], scalar1=WD[:, col:col + 1], scalar2=None,
                                                                              op0=ALU.mult), reads=[r_pTb[0], r_WD], pwrites=[r_Kw])
                                    else:
                                        S.op("act", lambda e: e.activation(out=Kw[:, j, :], in_=pTb[0][:, a * 128:(a + 1) * 128], func=AF.Copy, scale=WD[:, col:col + 1]),
                                             reads=[r_pTb[0], r_WD], pwrites=[r_Kw])
                            NJ = 32
                            colh = lambda j: h * 32 + j

                            def st_ST(j):
                                kk = j % 2; tsl = slice(j * 128, (j + 1) * 128)
                                S.op("pe", lambda e: e.matmul(pST[kk], lhsT=qkT[:, 1, tsl], rhs=qkT[:, 0, tsl], start=True, stop=True),
                                     reads=[r_qkT[0], r_qkT[1]], writes=[r_pST[kk]])

                            def st_PTm(j):
                                kk = j % 2; col = colh(j)
                                S.op("dve", lambda e: e.scalar_tensor_tensor(out=PTm[kk][:], in0=pST[kk], scalar=wexpT[:, col:col + 1], in1=mask01[:], op0=ALU.mult, op1=ALU.mult),
                                     reads=[r_pST[kk], r_wexpT, r_m01], writes=[r_PTm[kk]])

                            def st_pA(j):
                                kk = j % 2; ka = j % 3; tsl = slice(j * 128, (j + 1) * 128)
                                S.op("pe", lambda e: e.matmul(pA[ka], lhsT=PTm[kk][:], rhs=Vm[:, j, 0:129], start=True, stop=(j == 0)), reads=[r_PTm[kk], r_Vm], writes=[r_pA[ka]])
                                if j > 0:
                                    S.op("pe", lambda e: e.matmul(pA[ka], lhsT=qkT[:, 0, tsl], rhs=Cbf[:, 0:129], start=False, stop=True), reads=[r_qkT[0], r_Cb], writes=[r_pA[ka]])

                            def st_pC(j):
                                S.op("pe", lambda e: e.matmul(pC, lhsT=Kw[:, j, :], rhs=Vm[:, j, 0:129], start=True, stop=True), reads=[r_Kw, r_Vm], writes=[r_pC])

                            def st_state(j):
                                col = colh(j)
                                if j == 0:
                                    S.op("dve", lambda e: e.tensor_copy(out=Cst[:], in_=pC), reads=[r_pC], writes=[r_C])
                                else:
                                    S.op("dve", lambda e: e.scalar_tensor_tensor(out=Cst[:], in0=Cst[:], scalar=decB[:, col + 1:col + 2], in1=pC, op0=ALU.mult, op1=ALU.add),
                                         reads=[r_C, r_decB, r_pC], writes=[r_C])

                            def st_Cbf(j):
                                S.op("act", lambda e: e.activation(out=Cbf[:, 0:129], in_=Cst[:], func=AF.Copy), reads=[r_C], writes=[r_Cb])

                            def st_sq(j):
                                ka = j % 3; s_ = sc[j % NSC]; rs_ = r_sc[j % NSC]
                                S.op("act", lambda e: e.activation(out=s_[:, 0:1], in_=pA[ka][:, 128:129], func=AF.Square), reads=[r_pA[ka]], writes=[rs_[0]])
                                S.op("act", lambda e: e.activation(out=junkm[j % 2][:], in_=pA[ka][:, 0:128], func=AF.Square, accum_out=s_[:, 1:2]),
                                     reads=[r_pA[ka]], writes=[r_junkm[j % 2]], pwrites=[rs_[0]])

                            def st_tv(j):
                                col = colh(j); s_ = sc[j % NSC]; rs_ = r_sc[j % NSC]
                                S.op("dve", lambda e: e.tensor_scalar(out=s_[:, 2:3], in0=s_[:, 0:1], scalar1=fl2[:, col:col + 1], scalar2=EPS, op0=ALU.max, op1=ALU.mult),
                                     reads=[rs_[0], r_fl2], writes=[rs_[1]])
                                S.op("dve", lambda e: e.scalar_tensor_tensor(out=s_[:, 3:4], in0=s_[:, 1:2], scalar=1.0 / 128.0, in1=s_[:, 2:3], op0=ALU.mult, op1=ALU.add),
                                     reads=[rs_[0], rs_[1]], writes=[rs_[2]])

                            def st_lnexp(j):
                                s_ = sc[j % NSC]; rs_ = r_sc[j % NSC]
                                S.op("act", lambda e: e.activation(out=s_[:, 4:5], in_=s_[:, 3:4], func=AF.Ln), reads=[rs_[2]], writes=[rs_[3]])
                                S.op("act", lambda e: e.activation(out=s_[:, 5:6], in_=s_[:, 4:5], func=AF.Exp, scale=-0.5), reads=[rs_[3]], writes=[rs_[3]])

                            def st_ytl(j):
                                ka = j % 3; s_ = sc[j % NSC]; rs_ = r_sc[j % NSC]
                                S.op("dve", lambda e: e.scalar_tensor_tensor(out=ytl[j % 2][:], in0=pA[ka][:, 0:128], scalar=s_[:, 5:6], in1=SG[:, j, :], op0=ALU.mult, op1=ALU.mult),
                                     reads=[r_pA[ka], rs_[3], r_SG], writes=[r_ytl[j % 2]])

                            def st_T(j):
                                a = j % 4
                                S.op("pe", lambda e: e.transpose(out=pTb[0][:, a * 128:(a + 1) * 128], in_=ytl[j % 2][:], identity=ident_b[:]), reads=[r_ytl[j % 2], r_idb], writes=[r_pTb[0]])

                            def st_ym(j):
                                if j % 4 == 3:
                                    S.op("act", lambda e: e.activation(out=yTm[:, (j - 3) * 128:(j + 1) * 128], in_=pTb[0], func=AF.Copy), reads=[r_pTb[0]], pwrites=[r_yTm])

                            ok = lambda j: 0 <= j < NJ
                            st_ST(0); st_PTm(0)
                            for it in range(NJ + 5):
                                if ok(it + 1): st_ST(it + 1)
                                if ok(it): st_pA(it)
                                if ok(it) and it < NJ - 1: st_pC(it)
                                if ok(it - 3): st_T(it - 3)
                                if ok(it + 1): st_PTm(it + 1)
                                if ok(it): st_sq(it)
                                if ok(it) and it < NJ - 1:
                                    st_state(it)
                                    st_Cbf(it + 1)
                                if ok(it - 1): st_tv(it - 1)
                                if ok(it - 1): st_lnexp(it - 1)
                                if ok(it - 2): st_ytl(it - 2)
                                if ok(it - 3): st_ym(it - 3)
                            S.dma("sp", ybuf[512 + h * 128:512 + (h + 1) * 128, :], yTm[:], reads=[r_yTm], writes=[r_ybuf[4 + h]])
                        S.barrier()
        S.barrier()

        c2.close()
        if stop_after is None:
            TT = 256
            NTT = SEQ // TT
            NSUB = TT // 128
            NR = 6
            with ExitStack() as ph:
                wout = sbt(ph, "wout", [128, 8, D], BF16); r_wout = Res()
                wup = sbt(ph, "wup", [128, 8, 2 * DFF], BF16); r_wup = [Res() for _ in range(NCH)]
                ytb = sbt(ph, "f_y", [128, 8, TT], BF16); r_ytb = Res()
                h1 = [sbt(ph, f"f_h{i}", [128, NSUB, D], F32) for i in range(2)]; r_h1 = [Res(), Res()]
                hb = sbt(ph, "f_hb", [128, NSUB, D], BF16); r_hb = Res()
                hnT = sbt(ph, "f_hnT", [128, 8, 2 + TT], BF16); r_hnT = Res(); r_hnTh = Res()
                Rb = [sbt(ph, f"f_R{i}", [128, 2 + TT], F32) for i in range(NR)]; r_Rb = [Res() for _ in range(NR)]
                cvb = [sbt(ph, f"f_cv{i}", [128, TT], F32) for i in range(NR)]; r_cvb = [Res() for _ in range(NR)]
                gT = sbt(ph, "f_gT", [128, NCH, TT], BF16); r_gT = Res()
                fss = [sbt(ph, f"f_ss{i}", [128, 8], F32) for i in range(2)]; r_fss = [Res(), Res()]
                fjunk = sbt(ph, "f_junk", [128, D], BF16); r_fjunk = Res()
                pH = [pst(ph, f"fH{i}", [128, 512], F32) for i in range(2)]; r_pH = [PRes(), PRes()]
                pU = [pst(ph, f"fU{i}", [128, 512], F32) for i in range(4)]; r_pU = [PRes() for _ in range(4)]
                pT3_ = [pst(ph, f"fT{i}", [128, 1024], BF16) for i in range(2)]; r_pT3 = [PRes(), PRes()]
                pT3 = [t[:, 0:512] for t in pT3_]
                for c in range(8):
                    S.dma("pool", wout[:, c, :], w_out[c * 128:(c + 1) * 128, :], pwrites=[r_wout])
                with nc.allow_non_contiguous_dma(reason="512B runs"):
                    for m in range(NCH):
                        for gv in range(2):
                            col0 = gv * DFF + m * 128
                            S.dma("pool", wup[:, :, col0:col0 + 128], w_ffn_up[:, col0:col0 + 128].rearrange("(c p) m -> p c m", p=128), pwrites=[r_wup[m]])
                S.op("dve", lambda e: e.memset(hnT[:, :, 0:2], 0.0), writes=[r_hnTh])
                cnt = {"H": 0, "U": 0, "R": 0, "T": 0}

                def load_tile(tt):
                    S.dma("sp", ytb[:], ybuf[:, tt * TT:(tt + 1) * TT].rearrange("(c p) t -> p c t", p=128), reads=r_ybuf, writes=[r_ytb])
                    S.dma("sp", h1[tt % 2][:], x[tt * TT:(tt + 1) * TT, :].rearrange("(s p) d -> p s d", p=128), writes=[r_h1[tt % 2]])

                def outproj(tt):
                    hh_ = h1[tt % 2]; rhh = r_h1[tt % 2]; fs = fss[tt % 2]; rfs = r_fss[tt % 2]
                    for s in range(NSUB):
                        for hf in range(2):
                            k = cnt["H"] % 2; cnt["H"] += 1
                            cs = slice(hf * 512, (hf + 1) * 512)
                            for c in range(8):
                                S.op("pe", lambda e: e.matmul(pH[k][:], lhsT=ytb[:, c, s * 128:(s + 1) * 128], rhs=wout[:, c, cs], start=(c == 0), stop=(c == 7)),
                                     reads=[r_ytb, r_wout], writes=[r_pH[k]])
                            S.op("dve", lambda e: e.tensor_tensor(out=hh_[:, s, cs], in0=hh_[:, s, cs], in1=pH[k][:], op=ALU.add), reads=[rhh, r_pH[k]], pwrites=[rhh])
                        S.op("act", lambda e: e.activation(out=fjunk[:], in_=hh_[:, s, :], func=AF.Square, scale=1.0 / 32.0, accum_out=fs[:, s:s + 1]),
                             reads=[rhh], writes=[r_fjunk], pwrites=[rfs])
                    S.op("act", lambda e: e.activation(out=fs[:, 2:2 + NSUB], in_=fs[:, 0:NSUB], func=AF.Ln, bias=EPS, scale=1.0), reads=[rfs], pwrites=[rfs])
                    S.op("act", lambda e: e.activation(out=fs[:, 2:2 + NSUB], in_=fs[:, 2:2 + NSUB], func=AF.Exp, scale=-0.5), reads=[rfs], pwrites=[rfs])
                    for s in range(NSUB):
                        S.op("dve", lambda e: e.tensor_scalar(out=hb[:, s, :], in0=hh_[:, s, :], scalar1=fs[:, 2 + s:3 + s], scalar2=None, op0=ALU.mult),
                             reads=[rhh, rfs], pwrites=[r_hb])

                def transposes(tt):
                    if tt > 0:
                        S.op("dve", lambda e: e.tensor_copy(out=hnT[:, :, 0:2], in_=hnT[:, :, TT:TT + 2]), reads=[r_hnT], writes=[r_hnTh])
                    for c0 in range(0, 8, 2):
                        k = cnt["T"] % 2; cnt["T"] += 1
                        for a in range(2):
                            for s in range(NSUB):
                                S.op("pe", lambda e: e.transpose(out=pT3[k][:, a * 256 + s * 128:a * 256 + (s + 1) * 128], in_=hb[:, s, (c0 + a) * 128:(c0 + a + 1) * 128],
                                                                 identity=ident_b[:]), reads=[r_hb, r_idb], writes=[r_pT3[k]])
                        S.op("dve", lambda e: e.tensor_scalar(out=hnT[:, c0, 2:2 + TT], in0=pT3[k][:, 0:256], scalar1=fnw(c0), scalar2=None, op0=ALU.mult),
                             reads=[r_pT3[k], r_colA], pwrites=[r_hnT])
                        S.op("act", lambda e: e.activation(out=hnT[:, c0 + 1, 2:2 + TT], in_=pT3[k][:, 256:512], func=AF.Copy, scale=fnw(c0 + 1)),
                             reads=[r_pT3[k], r_colA], pwrites=[r_hnT])

                def up(tt):
                    pend = None
                    for m in range(NCH + 1):
                        if m < NCH:
                            banks = []
                            for gv in range(2):
                                kU = cnt["U"] % 4; cnt["U"] += 1
                                banks.append(kU)
                                col0 = gv * DFF + m * 128
                                for c in range(8):
                                    S.op("pe", lambda e: e.matmul(pU[kU][:, 0:2 + TT], lhsT=wup[:, c, col0:col0 + 128], rhs=hnT[:, c, :], start=(c == 0), stop=(c == 7)),
                                         reads=[r_wup[m], r_hnT, r_hnTh], writes=[r_pU[kU]])
                            bufs = []
                            for gv in range(2):
                                ch = gv * NCH + m
                                kU = banks[gv]
                                kR = cnt["R"] % NR; cnt["R"] += 1
                                R_ = Rb[kR]; rR = r_Rb[kR]; cv_ = cvb[kR]; rcv = r_cvb[kR]
                                S.op("act", lambda e: e.activation(out=R_[:], in_=pU[kU][:, 0:2 + TT], func=AF.Copy), reads=[r_pU[kU]], writes=[rR])
                                S.op("act", lambda e: e.activation(out=cv_[:], in_=pU[kU][:, 2:2 + TT], func=AF.Identity, scale=fcw(2, ch), bias=fcb(ch)),
                                     reads=[r_pU[kU], r_colB, r_colC], writes=[rcv])
                                S.op("dve", lambda e: e.scalar_tensor_tensor(out=cv_[:], in0=R_[:, 1:1 + TT], scalar=fcw(1, ch), in1=cv_[:], op0=ALU.mult, op1=ALU.add),
                                     reads=[rR, rcv, r_colB], writes=[rcv])
                                S.op("dve", lambda e: e.scalar_tensor_tensor(out=cv_[:], in0=R_[:, 0:TT], scalar=fcw(0, ch), in1=cv_[:], op0=ALU.mult, op1=ALU.add),
                                     reads=[rR, rcv, r_colB], writes=[rcv])
                                bufs.append((cv_, rcv))
                        if pend is not None:
                            pm, ((cg, rcg), (cvv, rcvv)) = pend
                            S.op("act", lambda e: e.activation(out=cg[:], in_=cg[:], func=AF.Silu), reads=[rcg], writes=[rcg])
                            S.op("pool", lambda e: e.tensor_tensor(out=gT[:, pm, :], in0=cg[:], in1=cvv[:], op=ALU.mult), reads=[rcg, rcvv], pwrites=[r_gT])
                        pend = (m, bufs) if m < NCH else None

                def down(tt, s):
                    hh_ = h1[tt % 2]; rhh = r_h1[tt % 2]; fs = fss[tt % 2]; rfs = r_fss[tt % 2]
                    for hf in range(2):
                        k = cnt["H"] % 2; cnt["H"] += 1
                        cs = slice(hf * 512, (hf + 1) * 512)
                        for m in range(NCH):
                            S.op("pe", lambda e: e.matmul(pH[k][:], lhsT=gT[:, m, s * 128:(s + 1) * 128], rhs=wdn[:, m, cs], start=(m == 0), stop=(m == NCH - 1)),
                                 reads=[r_gT, r_wdn], writes=[r_pH[k]])
                        S.op("dve", lambda e: e.tensor_tensor(out=hh_[:, s, cs], in0=hh_[:, s, cs], in1=pH[k][:], op=ALU.add), reads=[rhh, r_pH[k]], pwrites=[rhh])
                    S.op("act", lambda e: e.activation(out=fjunk[:], in_=hh_[:, s, :], func=AF.Square, scale=1.0 / 32.0, accum_out=fs[:, 4 + s:5 + s]),
                         reads=[rhh], writes=[r_fjunk], pwrites=[rfs])

                def final(tt):
                    hh_ = h1[tt % 2]; rhh = r_h1[tt % 2]; fs = fss[tt % 2]; rfs = r_fss[tt % 2]
                    S.op("act", lambda e: e.activation(out=fs[:, 6:6 + NSUB], in_=fs[:, 4:4 + NSUB], func=AF.Ln, bias=EPS, scale=1.0), reads=[rfs], pwrites=[rfs])
                    S.op("act", lambda e: e.activation(out=fs[:, 6:6 + NSUB], in_=fs[:, 6:6 + NSUB], func=AF.Exp, scale=-0.5), reads=[rfs], pwrites=[rfs])
                    for s in range(NSUB):
                        S.op("dve", lambda e: e.scalar_tensor_tensor(out=hh_[:, s, :], in0=hh_[:, s, :], scalar=fs[:, 6 + s:7 + s], in1=FW[:], op0=ALU.mult, op1=ALU.mult),
                             reads=[rhh, rfs, r_FW], pwrites=[rhh])
                    S.dma("sp", out[tt * TT:(tt + 1) * TT, :].rearrange("(s p) d -> p s d", p=128), hh_[:], reads=[rhh])

                load_tile(0)
                outproj(0)
                transposes(0)
                for tt in range(NTT):
                    up(tt)
                    if tt + 1 < NTT:
                        load_tile(tt + 1)
                        outproj(tt + 1)
                    down(tt, 0)
                    if tt + 1 < NTT:
                        transposes(tt + 1)
                    down(tt, 1)
                    final(tt)
                S.barrier()
        S.barrier()
        nc._ninst = dict(S.ninst)
    return nc, dbg


_NC_CACHE = {}


def _squeeze(a):
    return np.ascontiguousarray(np.asarray(a, dtype=np.float32))


def kernel(x, w_in, mlstm_conv_w, mlstm_conv_b, mlstm_i_bias, mlstm_f_bias, att_out_norm_w, mlstm_out_norm_w,
           w_out, mixer_norm_w, ffn_norm_w, w_ffn_up, ffn_conv_w, ffn_conv_b, w_ffn_down, final_norm_w):
    n = 8
    if "nc" not in _NC_CACHE:
        _NC_CACHE["nc"] = build_nc()[0]
    nc = _NC_CACHE["nc"]
    shared = {
        "w_in": _squeeze(w_in[0]), "mlstm_conv_w": _squeeze(mlstm_conv_w[0]), "mlstm_conv_b": _squeeze(mlstm_conv_b[0]),
        "mlstm_i_bias": _squeeze(mlstm_i_bias[0]), "mlstm_f_bias": _squeeze(mlstm_f_bias[0]),
        "att_out_norm_w": _squeeze(att_out_norm_w[0]), "mlstm_out_norm_w": _squeeze(mlstm_out_norm_w[0]),
        "w_out": _squeeze(w_out[0]), "mixer_norm_w": _squeeze(mixer_norm_w[0]), "ffn_norm_w": _squeeze(ffn_norm_w[0]),
        "w_ffn_up": _squeeze(w_ffn_up[0]), "ffn_conv_w": _squeeze(ffn_conv_w[0]), "ffn_conv_b": _squeeze(ffn_conv_b[0]),
        "w_ffn_down": _squeeze(w_ffn_down[0]), "final_norm_w": _squeeze(final_norm_w),
    }
    xs = np.asarray(x, dtype=np.float32)
    in_maps = [dict(shared, x=np.ascontiguousarray(xs[i])) for i in range(n)]
    res = run_bass_kernel_spmd(nc, in_maps, core_ids=list(range(n)))
    return np.stack([np.asarray(r["out"], dtype=np.float32) for r in res.results], axis=0)
```

```python
import numpy as np
from collections import defaultdict
from contextlib import ExitStack
import concourse.bass as bass
import concourse.mybir as mybir
from concourse.bass_utils import run_bass_kernel_spmd

F32 = mybir.dt.float32
BF16 = mybir.dt.bfloat16
AF = mybir.ActivationFunctionType
ALU = mybir.AluOpType
AX = mybir.AxisListType

SEQ = 4096
D = 1024
PROJ = 3592
DFF = 2816
NCH = 22
EPS = 1e-6
GROUPS = (1, 4, 16)


class Res:
    __slots__ = ("ws", "r", "excl")

    def __init__(self, excl=False):
        self.ws = {}
        self.r = {}
        self.excl = excl


def PRes():
    return Res(excl=True)


class Sched:
    CE = ("pe", "act", "dve", "pool")

    def __init__(self, nc, es, nq=8):
        self.nc = nc
        self.eng = {"pe": nc.tensor, "act": nc.scalar, "dve": nc.vector, "pool": nc.gpsimd, "sp": nc.sync}
        self.sems = {}
        self.count = {}
        for e in self.CE:
            self.sems[e] = es.enter_context(nc.semaphore("s_" + e))
            self.count[e] = 0
        self.nq = nq
        self.rr = {}
        for q in ("sp", "act", "pool"):
            self.rr[q] = 0
            for i in range(nq):
                n = f"d_{q}{i}"
                self.sems[n] = es.enter_context(nc.semaphore(n))
                self.count[n] = 0
        self.seen = {e: defaultdict(int) for e in self.eng}
        self.ninst = defaultdict(int)
        self.dead = False

    def need(self, E, tok):
        if tok is None:
            return
        s, v = tok
        if s.startswith("d_"):
            v = self.count[s]
        elif s == E and E == "pe":
            return
        if self.seen[E][s] < v:
            self.eng[E].wait_ge(self.sems[s], v)
            self.seen[E][s] = v
            self.ninst[E] += 1

    def _pre(self, E, reads, writes, pwrites):
        for r in reads:
            for t in r.ws.values():
                self.need(E, t)
            if r.excl:
                for e2, t in r.r.items():
                    if e2 != E:
                        self.need(E, t)
        for w in writes:
            for t in w.ws.values():
                self.need(E, t)
            for e2, t in w.r.items():
                self.need(E, t)
        for w in pwrites:
            for e2, t in w.r.items():
                self.need(E, t)
            if w.excl:
                for e2, t in w.ws.items():
                    if e2 != E:
                        self.need(E, t)

    def _post(self, key, tok, reads, writes, pwrites):
        for r in reads:
            r.r[key] = tok
        for w in writes:
            w.ws = {key: tok}
            w.r = {}
        for w in pwrites:
            w.ws[key] = tok

    def op(self, E, fn, reads=(), writes=(), pwrites=()):
        if self.dead:
            return None
        self._pre(E, reads, writes, pwrites)
        inst = fn(self.eng[E])
        self.count[E] += 1
        inst.then_inc(self.sems[E], 1)
        self.ninst[E] += 1
        self._post(E, (E, self.count[E]), reads, writes, pwrites)
        return inst

    def dma(self, q, out, in_, reads=(), writes=(), pwrites=(), **kw):
        if self.dead:
            return None
        self._pre(q, reads, writes, pwrites)
        inst = self.eng[q].dma_start(out=out, in_=in_, **kw)
        n = f"d_{q}{self.rr[q] % self.nq}"
        self.rr[q] += 1
        self.count[n] += 16
        inst.then_inc(self.sems[n], 16)
        self.ninst[q] += 1
        self._post(n, (n, self.count[n]), reads, writes, pwrites)
        return inst

    def barrier(self):
        for E in self.eng:
            for s in self.sems:
                if self.count[s] > 0 and s != E:
                    self.need(E, (s, self.count[s]))


def build_nc(debug=False, stop_after=None):
    nc = bass.Bass("TRN2", target_bir_lowering=False)
    din = lambda n, s: nc.dram_tensor(n, s, F32, kind="ExternalInput").ap()
    x = din("x", [SEQ, D])
    w_in = din("w_in", [D, PROJ])
    mlstm_conv_w = din("mlstm_conv_w", [4, 1024])
    mlstm_conv_b = din("mlstm_conv_b", [1024])
    mlstm_i_bias = din("mlstm_i_bias", [4])
    mlstm_f_bias = din("mlstm_f_bias", [4])
    att_out_norm_w = din("att_out_norm_w", [512])
    mlstm_out_norm_w = din("mlstm_out_norm_w", [512])
    w_out = din("w_out", [D, D])
    mixer_norm_w = din("mixer_norm_w", [D])
    ffn_norm_w = din("ffn_norm_w", [D])
    w_ffn_up = din("w_ffn_up", [D, 2 * DFF])
    ffn_conv_w = din("ffn_conv_w", [3, 2 * DFF])
    ffn_conv_b = din("ffn_conv_b", [2 * DFF])
    w_ffn_down = din("w_ffn_down", [DFF, D])
    final_norm_w = din("final_norm_w", [D])
    out = nc.dram_tensor("out", [SEQ, D], F32, kind="ExternalOutput").ap()
    ybuf = nc.dram_tensor("ybuf", [D, SEQ], BF16, kind=("ExternalOutput" if debug else "Internal")).ap()
    r_ybuf = [Res() for _ in range(8)]
    dbg = {}

    def dout(name, shape, dt=F32):
        dbg[name] = nc.dram_tensor(name, shape, dt, kind="ExternalOutput").ap()
        return dbg[name]

    with ExitStack() as es:
        S = Sched(nc, es)

        def stop(tag):
            if stop_after == tag:
                S.barrier()
                S.dead = True
        sbt = lambda st, n, s, d: st.enter_context(nc.sbuf_tensor(n, s, d))
        pst = lambda st, n, s, d: st.enter_context(nc.psum_tensor(n, s, d))

        ident_b = sbt(es, "ident_b", [128, 128], BF16); r_idb = Res()
        colA = sbt(es, "colA", [128, 64], F32); r_colA = Res()
        colB = sbt(es, "colB", [128, 88], F32); r_colB = Res()
        colC = sbt(es, "colC", [128, 88], F32); r_colC = Res()
        FW = sbt(es, "FW", [128, D], F32); r_FW = Res()
        wdn = sbt(es, "wdn", [128, NCH, D], BF16); r_wdn = Res()
        wdn_f32 = wdn[:].rearrange("p m d -> p (m d)").bitcast(F32)
        c2 = ExitStack()
        ident_f = sbt(c2, "ident_f", [128, 128], F32); r_idf = Res()
        ones_f = sbt(c2, "ones_f", [128, 128], F32); r_ones = Res()
        zero_f = sbt(c2, "zero_f", [128, 256], F32); r_zero = Res()
        sel_b = sbt(c2, "sel_b", [128, 2, 128], BF16); r_sel = Res()
        maskf = sbt(c2, "maskf", [128, 256], F32); r_maskf = Res()
        maskb = sbt(c2, "maskb", [128, 256], BF16); r_maskb = Res()
        mask01 = sbt(c2, "mask01", [128, 128], F32); r_m01 = Res()
        MNW = sbt(c2, "MNW", [128, 512], F32); r_MNW = Res()
        gb = sbt(c2, "gb", [4, 2], F32); r_gb = Res()
        S.op("pool", lambda e: e.memset(ones_f[:], 1.0), writes=[r_ones])
        S.op("pool", lambda e: e.memset(zero_f[:], 0.0), writes=[r_zero])
        S.op("pool", lambda e: e.affine_select(out=ident_f[:], in_=ones_f[:], pattern=[[-1, 128]], compare_op=ALU.is_equal,
                                               fill=0.0, base=0, channel_multiplier=1), reads=[r_ones], writes=[r_idf])
        S.op("dve", lambda e: e.tensor_copy(out=ident_b[:], in_=ident_f[:]), reads=[r_idf], writes=[r_idb])
        S.op("dve", lambda e: e.memset(sel_b[:], 0.0), writes=[r_sel])
        S.op("dve", lambda e: e.memset(sel_b[0:64, 0, :], 1.0), writes=[r_sel])
        S.op("dve", lambda e: e.memset(sel_b[64:128, 1, :], 1.0), writes=[r_sel])
        S.op("pool", lambda e: e.affine_select(out=maskf[:, 0:128], in_=zero_f[:, 0:128], pattern=[[1, 128]], compare_op=ALU.is_ge,
                                               fill=-30000.0, base=0, channel_multiplier=-1), reads=[r_zero], writes=[r_maskf])
        S.op("pool", lambda e: e.affine_select(out=maskf[:, 128:256], in_=zero_f[:, 128:256], pattern=[[-1, 128]], compare_op=ALU.is_ge,
                                               fill=-30000.0, base=0, channel_multiplier=1), reads=[r_zero], writes=[r_maskf])
        S.op("dve", lambda e: e.tensor_copy(out=maskb[:], in_=maskf[:]), reads=[r_maskf], writes=[r_maskb])
        S.op("pool", lambda e: e.affine_select(out=mask01[:], in_=ones_f[:], pattern=[[1, 128]], compare_op=ALU.is_ge,
                                               fill=0.0, base=0, channel_multiplier=-1), reads=[r_ones], writes=[r_m01])

        with ExitStack() as p0:
            rowA = sbt(p0, "rowA", [64, 128], F32); r_rowA = Res()
            rowB = sbt(p0, "rowB", [88, 128], F32); r_rowB = Res()
            rowC = sbt(p0, "rowC", [88, 128], F32); r_rowC = Res()
            pcol = pst(p0, "pcol", [128, 512], F32); r_pcol = PRes()
            S.op("dve", lambda e: e.memset(rowA[:], 0.0), writes=[r_rowA])
            S.dma("sp", rowA[0:8, :], mixer_norm_w.rearrange("(c p) -> c p", p=128), writes=[r_rowA])
            S.dma("sp", rowA[8:16, :], ffn_norm_w.rearrange("(c p) -> c p", p=128), writes=[r_rowA])
            S.dma("sp", rowA[16:48, :], mlstm_conv_w.rearrange("j (c p) -> (j c) p", p=128), writes=[r_rowA])
            S.dma("sp", rowA[48:56, :], mlstm_conv_b.rearrange("(c p) -> c p", p=128), writes=[r_rowA])
            S.dma("sp", rowA[56:64, 0:64], att_out_norm_w.rearrange("(h p) -> h p", p=64), writes=[r_rowA])
            S.dma("sp", rowB[:, :], ffn_conv_w[0:2, :].rearrange("j (c p) -> (j c) p", p=128), writes=[r_rowB])
            S.dma("sp", rowC[0:44, :], ffn_conv_w[2:3, :].rearrange("j (c p) -> (j c) p", p=128), writes=[r_rowC])
            S.dma("sp", rowC[44:88, :], ffn_conv_b.rearrange("(c p) -> c p", p=128), writes=[r_rowC])
            S.dma("sp", FW[:], final_norm_w.partition_broadcast(128), writes=[r_FW])
            S.dma("sp", MNW[:], mlstm_out_norm_w.partition_broadcast(128), writes=[r_MNW])
            with nc.allow_non_contiguous_dma(reason="tiny gate bias"):
                S.dma("sp", gb[:, 0:1], mlstm_i_bias.rearrange("(p o) -> p o", o=1), writes=[r_gb])
                S.dma("sp", gb[:, 1:2], mlstm_f_bias.rearrange("(p o) -> p o", o=1), writes=[r_gb])
            S.op("pe", lambda e: e.transpose(out=pcol[:, 0:64], in_=rowA[:], identity=ident_f[0:64, 0:64]), reads=[r_rowA, r_idf], writes=[r_pcol])
            S.op("pe", lambda e: e.transpose(out=pcol[:, 64:152], in_=rowB[:], identity=ident_f[0:88, 0:88]), reads=[r_rowB, r_idf], writes=[r_pcol])
            S.op("pe", lambda e: e.transpose(out=pcol[:, 152:240], in_=rowC[:], identity=ident_f[0:88, 0:88]), reads=[r_rowC, r_idf], writes=[r_pcol])
            S.op("dve", lambda e: e.tensor_copy(out=colA[:], in_=pcol[:, 0:64]), reads=[r_pcol], writes=[r_colA])
            S.op("dve", lambda e: e.tensor_copy(out=colB[:], in_=pcol[:, 64:152]), reads=[r_pcol], writes=[r_colB])
            S.op("dve", lambda e: e.tensor_copy(out=colC[:], in_=pcol[:, 152:240]), reads=[r_pcol], writes=[r_colC])
            S.barrier()
        mnw = lambda c: colA[:, c:c + 1]
        fnw = lambda c: colA[:, 8 + c:9 + c]
        mcw = lambda j, c: colA[:, 16 + j * 8 + c:17 + j * 8 + c]
        mcb = lambda c: colA[:, 48 + c:49 + c]
        aow = lambda h: colA[0:64, 56 + h:57 + h]

        def fcw(j, ch):
            if j < 2:
                return colB[:, j * 44 + ch:j * 44 + ch + 1]
            return colC[:, ch:ch + 1]
        fcb = lambda ch: colC[:, 44 + ch:45 + ch]

        if debug:
            o_colA = dout("o_colA", [128, 64]); o_colB = dout("o_colB", [128, 88]); o_colC = dout("o_colC", [128, 88])
            S.dma("sp", o_colA[:, :], colA[:], reads=[r_colA]); S.dma("sp", o_colB[:, :], colB[:], reads=[r_colB]); S.dma("sp", o_colC[:, :], colC[:], reads=[r_colC])
            S.barrier()
        with ExitStack() as p12:
          if stop_after != "p0":
                xnT = sbt(p12, "xnT", [128, 8, SEQ], BF16)
                r_xnT = [Res() for _ in range(8)]

                with ExitStack() as ph:
                    xt = [sbt(ph, f"p1x{i}", [128, 4, D], F32) for i in range(2)]; r_xt = [Res(), Res()]
                    xb = [sbt(ph, f"p1xb{i}", [128, 4, D], BF16) for i in range(2)]; r_xb = [Res(), Res()]
                    junk = [sbt(ph, f"p1junk{i}", [128, D], BF16) for i in range(2)]; r_junk = [Res(), Res()]
                    ss = [sbt(ph, f"p1ss{i}", [128, 4], F32) for i in range(2)]; r_ss = [Res(), Res()]
                    rs = [sbt(ph, f"p1rs{i}", [128, 4], F32) for i in range(2)]; r_rs = [Res(), Res()]
                    pT_ = [pst(ph, f"p1pT{i}", [128, 1024], BF16) for i in range(4)]; r_pT = [PRes() for _ in range(4)]
                    pT = [t[:, 0:512] for t in pT_]
                    for tt in range(8):
                        b = tt % 2
                        for s4 in range(4):
                            S.dma("sp", xt[b][:, s4, :], x[tt * 512 + s4 * 128:tt * 512 + (s4 + 1) * 128, :],
                                  writes=([r_xt[b]] if s4 == 0 else []), pwrites=([] if s4 == 0 else [r_xt[b]]))
                        for s in range(4):
                            S.op("act", lambda e: e.activation(out=junk[s % 2][:], in_=xt[b][:, s, :], func=AF.Square, scale=1.0 / 32.0,
                                                               accum_out=ss[b][:, s:s + 1]), reads=[r_xt[b]], writes=[r_junk[s % 2]], pwrites=[r_ss[b]])
                        S.op("act", lambda e: e.activation(out=rs[b][:], in_=ss[b][:], func=AF.Ln, bias=EPS, scale=1.0), reads=[r_ss[b]], writes=[r_rs[b]])
                        S.op("act", lambda e: e.activation(out=rs[b][:], in_=rs[b][:], func=AF.Exp, scale=-0.5), reads=[r_rs[b]], writes=[r_rs[b]])
                        for s in range(4):
                            eng = "dve"
                            S.op(eng, lambda e: e.tensor_scalar(out=xb[b][:, s, :], in0=xt[b][:, s, :], scalar1=rs[b][:, s:s + 1], scalar2=None,
                                                                op0=ALU.mult), reads=[r_xt[b], r_rs[b]], pwrites=[r_xb[b]])
                        for c in range(8):
                            k = (tt * 8 + c) % 4
                            for s in range(4):
                                S.op("pe", lambda e: e.transpose(out=pT[k][:, s * 128:(s + 1) * 128], in_=xb[b][:, s, c * 128:(c + 1) * 128],
                                                                 identity=ident_b[:]), reads=[r_xb[b], r_idb], writes=[r_pT[k]])
                            if c % 2 == 0:
                                S.op("dve", lambda e: e.tensor_scalar(out=xnT[:, c, tt * 512:(tt + 1) * 512], in0=pT[k], scalar1=mnw(c), scalar2=None,
                                                                      op0=ALU.mult), reads=[r_pT[k], r_colA], pwrites=[r_xnT[tt]])
                            else:
                                S.op("act", lambda e: e.activation(out=xnT[:, c, tt * 512:(tt + 1) * 512], in_=pT[k], func=AF.Copy, scale=mnw(c)),
                                     reads=[r_pT[k], r_colA], pwrites=[r_xnT[tt]])
                    S.barrier()
                if debug:
                    o_xnT = dout("o_xnT", [D, SEQ], BF16)
                    for c in range(8):
                        S.dma("sp", o_xnT[c * 128:(c + 1) * 128, :], xnT[:, c, :], reads=r_xnT)
                    S.barrier()

                if stop_after not in ("p1", "mlg", "mlh0", "mlh1", "mlh2", "mlh"):
                    with ExitStack() as ph:
                        wqkv = [sbt(ph, f"wqkv{i}", [128, 8, 3, 128], BF16) for i in range(2)]; r_wqkv = [Res(), Res()]
                        QT = sbt(ph, "QT", [128, SEQ], BF16); r_QT = Res()
                        KT = sbt(ph, "KT", [128, SEQ], BF16); r_KT = Res()
                        VT = sbt(ph, "VT", [128, SEQ], BF16); r_VT = Res()
                        sq = sbt(ph, "sq", [128, SEQ], BF16); r_sq = Res()
                        mx = sbt(ph, "mx", [128, 4, 8], F32); r_mx = Res()
                        st = sbt(ph, "st", [128, 4], F32); r_st = Res()
                        nbias = sbt(ph, "nbias", [128, 2], F32); r_nb = Res()
                        mask2 = sbt(ph, "mask2", [128, 2, 256], BF16); r_mask2 = Res()
                        Vaug = sbt(ph, "Vaug", [128, 32, 2, 128], BF16); r_Vaug = Res()
                        acc = [wdn_f32[:, 0:SEQ], wdn_f32[:, SEQ:2 * SEQ]]; r_acc = [Res(), Res()]
                        NPT = 6
                        PT = [sbt(ph, f"PT{i}", [128, 512], BF16) for i in range(NPT)]; r_PT = [Res() for _ in range(NPT)]
                        e_n2 = sbt(ph, "e_n2", [64, 1024], BF16); r_en2 = Res()
                        e_d2 = wdn_f32[0:64, 2 * SEQ:2 * SEQ + 1024]; r_ed2 = Res()
                        e_rs = wdn_f32[0:64, 2 * SEQ + 1024:2 * SEQ + 2048]; r_ers = Res()
                        ones_b = sbt(ph, "ones_b", [64, 64], BF16); r_onesb = Res()
                        yT = [sbt(ph, f"yTa{i}", [64, SEQ], BF16) for i in range(2)]; r_yT = [Res(), Res()]
                        pJ = [pst(ph, f"aJ{i}", [128, 512], F32) for i in range(2)]; r_pJ = [PRes(), PRes()]
                        pS = [pst(ph, f"aS{i}", [128, 512], F32) for i in range(2)]; r_pS = [PRes(), PRes()]
                        pO = [pst(ph, f"aO{i}", [128, 512], F32) for i in range(2)]; r_pO = [PRes(), PRes()]
                        pV_ = [pst(ph, f"aV{i}", [128, 1024], BF16) for i in range(2)]; r_pV = [PRes(), PRes()]
                        pV = [t[:, 0:512] for t in pV_]
                        pS3 = [pS[0], pS[1], pJ[0]]; r_pS3 = [r_pS[0], r_pS[1], r_pJ[0]]
                        S.op("pool", lambda e: e.memset(Vaug[:, :, :, 64:128], 1.0), writes=[r_Vaug])
                        S.op("dve", lambda e: e.tensor_copy(out=ones_b[:], in_=ones_f[0:64, 0:64]), reads=[r_ones], writes=[r_onesb])
                        for a in range(2):
                            S.op("dve", lambda e: e.tensor_scalar(out=mask2[:, a, :], in0=maskf[:], scalar1=0.0, scalar2=None, op0=ALU.is_equal),
                                 reads=[r_maskf], pwrites=[r_mask2])
                        nj = 0
                        cnt_a = {"V": 0, "S": 0, "O": 0}
                        pending_epi = []
                        for pr in range(4):
                            wb = wqkv[pr % 2]; rwb = r_wqkv[pr % 2]
                            for j in range(3):
                                S.dma("pool", wb[:, :, j, :], w_in[:, j * 512 + pr * 128:j * 512 + (pr + 1) * 128].rearrange("(c p) m -> p c m", p=128),
                                      pwrites=[rwb])
                            for j, (dst, rdst) in enumerate(((QT, r_QT), (KT, r_KT), (VT, r_VT))):
                                for tt in range(8):
                                    k = nj % 2; nj += 1
                                    for c in range(8):
                                        S.op("pe", lambda e: e.matmul(pJ[k][:], lhsT=wb[:, c, j, :], rhs=xnT[:, c, tt * 512:(tt + 1) * 512],
                                                                      start=(c == 0), stop=(c == 7)), reads=[rwb, r_xnT[tt]], writes=[r_pJ[k]])
                                    if nj % 2 == 0:
                                        S.op("dve", lambda e: e.tensor_copy(out=dst[:, tt * 512:(tt + 1) * 512], in_=pJ[k][:]), reads=[r_pJ[k]], pwrites=[rdst])
                                    else:
                                        S.op("act", lambda e: e.activation(out=dst[:, tt * 512:(tt + 1) * 512], in_=pJ[k][:], func=AF.Copy),
                                             reads=[r_pJ[k]], pwrites=[rdst])
                                    if pending_epi and (j * 8 + tt) % 3 == 2:
                                        pending_epi.pop(0)()
                            for qi, (src, rsrc) in enumerate(((QT, r_QT), (KT, r_KT))):
                                S.op("dve", lambda e: e.tensor_tensor(out=sq[:], in0=src[:], in1=src[:], op=ALU.mult), reads=[rsrc], writes=[r_sq])
                                for hh in range(2):
                                    for tt in range(8):
                                        k = nj % 2; nj += 1
                                        S.op("pe", lambda e: e.matmul(pJ[k][:], lhsT=sel_b[:, hh, :], rhs=sq[:, tt * 512:(tt + 1) * 512], start=True, stop=True),
                                             reads=[r_sel, r_sq], writes=[r_pJ[k]])
                                        S.op("dve", lambda e: e.tensor_reduce(out=mx[:, qi * 2 + hh, tt:tt + 1], in_=pJ[k][:], axis=AX.X, op=ALU.max),
                                             reads=[r_pJ[k]], pwrites=[r_mx])
                            S.op("dve", lambda e: e.tensor_reduce(out=st[:], in_=mx[:], axis=AX.X, op=ALU.max), reads=[r_mx], writes=[r_st])
                            S.op("dve", lambda e: e.tensor_tensor(out=nbias[:], in0=st[:, 0:2], in1=st[:, 2:4], op=ALU.mult), reads=[r_st], writes=[r_nb])
                            S.op("act", lambda e: e.activation(out=nbias[:], in_=nbias[:], func=AF.Ln), reads=[r_nb], writes=[r_nb])
                            S.op("act", lambda e: e.activation(out=nbias[:], in_=nbias[:], func=AF.Exp, scale=0.5), reads=[r_nb], writes=[r_nb])
                            S.op("dve", lambda e: e.tensor_scalar(out=nbias[:], in0=nbias[:], scalar1=-0.125 * 1.02, scalar2=None, op0=ALU.mult),
                                 reads=[r_nb], writes=[r_nb])
                            for gi, d in enumerate(GROUPS):
                                nb_ = 32 // d
                                for kt0 in range(0, 32, 4):
                                    k = cnt_a["V"] % 2; cnt_a["V"] += 1
                                    for sl in range(4):
                                        kt = kt0 + sl
                                        r_, b_ = kt // nb_, kt % nb_
                                        t0 = 128 * b_ * d + r_
                                        S.op("pe", lambda e: e.transpose(out=pV[k][:, sl * 128:(sl + 1) * 128], in_=VT[:, t0:t0 + 127 * d + 1:d],
                                                                         identity=ident_b[:]), reads=[r_VT, r_idb], writes=[r_pV[k]])
                                    src = pV[k].rearrange("p (s h e) -> p s h e", s=4, h=2)
                                    if cnt_a["V"] % 2 == 0:
                                        S.op("dve", lambda e: e.tensor_copy(out=Vaug[:, kt0:kt0 + 4, :, 0:64], in_=src), reads=[r_pV[k]], pwrites=[r_Vaug])
                                    else:
                                        S.op("act", lambda e: e.activation(out=Vaug[:, kt0:kt0 + 4, :, 0:64], in_=src, func=AF.Copy),
                                             reads=[r_pV[k]], pwrites=[r_Vaug])
                                units = [(hh, r_, b0) for hh in range(2) for r_ in range(d) for b0 in range(0, nb_, 2)]

                                def emit_S(u):
                                    hh, r_, b0 = u
                                    hs = slice(hh * 64, (hh + 1) * 64)
                                    bank = cnt_a["S"] % 3; pi = cnt_a["S"] % NPT; cnt_a["S"] += 1
                                    for jj in range(2):
                                        b_ = b0 + jj
                                        N = 256 if b_ + 1 < nb_ else 128
                                        t0 = 128 * b_ * d + r_
                                        ks = slice(t0, t0 + 127 * d + 1, d)
                                        qs = slice(t0, t0 + (N - 1) * d + 1, d)
                                        S.op("pe", lambda e: e.matmul(pS3[bank][:, jj * 256:jj * 256 + N], lhsT=KT[hs, ks], rhs=QT[hs, qs], start=True, stop=True),
                                             reads=[r_KT, r_QT], writes=[r_pS3[bank]])
                                    return bank, pi

                                def emit_E(u, info):
                                    hh, r_, b0 = u
                                    bank, pi = info
                                    S.op("act", lambda e: e.activation(out=PT[pi][:], in_=pS3[bank][:], func=AF.Exp, scale=0.125, bias=nbias[:, hh:hh + 1]),
                                         reads=[r_pS3[bank], r_nb], writes=[r_PT[pi]])
                                    S.op("dve", lambda e: e.tensor_tensor(out=PT[pi][:], in0=PT[pi][:], in1=mask2[:].rearrange("p a n -> p (a n)"), op=ALU.mult),
                                         reads=[r_PT[pi], r_mask2], writes=[r_PT[pi]])

                                def emit_PV(u, info, pinfo, slot0):
                                    hh, r_, b0 = u
                                    bank, pi = info
                                    ko = cnt_a["O"] % 2
                                    for jj in range(2):
                                        b_ = b0 + jj
                                        kt = r_ * nb_ + b_
                                        osl = pO[ko][:, (slot0 + jj) * 128:(slot0 + jj + 1) * 128]
                                        if b_ > 0:
                                            if jj == 0:
                                                ppi = pinfo[1]
                                                prhs = PT[ppi][:, 384:512]; rprev = r_PT[ppi]
                                            else:
                                                prhs = PT[pi][:, 128:256]; rprev = r_PT[pi]
                                            S.op("pe", lambda e: e.matmul(osl, lhsT=Vaug[:, kt - 1, hh, :], rhs=prhs, start=True, stop=False),
                                                 reads=[r_Vaug, rprev], writes=[r_pO[ko]])
                                        S.op("pe", lambda e: e.matmul(osl, lhsT=Vaug[:, kt, hh, :], rhs=PT[pi][:, jj * 256:jj * 256 + 128], start=(b_ == 0), stop=True),
                                             reads=[r_Vaug, r_PT[pi]], writes=[r_pO[ko]])

                                def emit_acc(u, ko):
                                    hh, r_, b0 = u
                                    ah = acc[hh]; rah = r_acc[hh]
                                    b_ = b0 + 1
                                    if d == 16:
                                        dst = ah.rearrange("p (i dd) -> p dd i", dd=16)[:, r_ - 1:r_ + 1, :]
                                        src = pO[ko][:].rearrange("p (a i) -> p a i", a=2)
                                    else:
                                        bs = b_ - 3
                                        ts = 128 * bs * d + r_
                                        dst = ah[:, ts:ts + 511 * d + 1:d]
                                        src = pO[ko][:]
                                    if gi == 0:
                                        S.op("dve", lambda e: e.tensor_copy(out=dst, in_=src), reads=[r_pO[ko]], pwrites=[rah])
                                    else:
                                        S.op("dve", lambda e: e.tensor_tensor(out=dst, in0=dst, in1=src, op=ALU.add), reads=[r_pO[ko], rah], pwrites=[rah])

                                infos = {}
                                pend_acc = None
                                infos[0] = emit_S(units[0])
                                infos[1] = emit_S(units[1])
                                slot = 0
                                for ui, u in enumerate(units):
                                    if ui + 2 < len(units):
                                        infos[ui + 2] = emit_S(units[ui + 2])
                                    emit_E(u, infos[ui])
                                    if pend_acc is not None:
                                        emit_acc(*pend_acc)
                                        pend_acc = None
                                    emit_PV(u, infos[ui], infos.get(ui - 1), slot)
                                    slot += 2
                                    if slot == 4:
                                        slot = 0
                                        pend_acc = (u, cnt_a["O"] % 2)
                                        cnt_a["O"] += 1
                                if pend_acc is not None:
                                    emit_acc(*pend_acc)
                                    pend_acc = None
                            def make_epi(pr_, hh, t4):
                                def run():
                                    h = pr_ * 2 + hh
                                    ah = acc[hh]; rah = r_acc[hh]
                                    yb = yT[h % 2]; ryb = r_yT[h % 2]
                                    cs = slice(t4 * 1024, (t4 + 1) * 1024)
                                    S.op("act", lambda e: e.activation(out=e_n2[:], in_=ah[0:64, cs], func=AF.Square), reads=[rah], writes=[r_en2])
                                    S.op("act", lambda e: e.activation(out=e_d2, in_=ah[64:128, cs], func=AF.Square, scale=float(np.sqrt(EPS))), reads=[rah], writes=[r_ed2])
                                    for a in range(2):
                                        S.op("pe", lambda e: e.matmul(pO[a][0:64, :], lhsT=ones_b[:], rhs=e_n2[:, a * 512:(a + 1) * 512], start=True, stop=True),
                                             reads=[r_onesb, r_en2], writes=[r_pO[a]])
                                        S.op("dve", lambda e: e.scalar_tensor_tensor(out=e_rs[:, a * 512:(a + 1) * 512], in0=pO[a][0:64, :], scalar=1.0 / 64.0,
                                                                                     in1=e_d2[:, a * 512:(a + 1) * 512], op0=ALU.mult, op1=ALU.add),
                                             reads=[r_pO[a], r_ed2], pwrites=[r_ers])
                                    S.op("act", lambda e: e.activation(out=e_rs, in_=e_rs, func=AF.Ln), reads=[r_ers], writes=[r_ers])
                                    S.op("act", lambda e: e.activation(out=e_rs, in_=e_rs, func=AF.Exp, scale=-0.5), reads=[r_ers], writes=[r_ers])
                                    S.op("dve", lambda e: e.scalar_tensor_tensor(out=yb[:, cs], in0=ah[0:64, cs], scalar=aow(h), in1=e_rs, op0=ALU.mult, op1=ALU.mult),
                                         reads=[rah, r_ers, r_colA], pwrites=[ryb])
                                    if t4 == 3:
                                        S.dma("sp", ybuf[h * 64:(h + 1) * 64, :], yb[:], reads=[ryb], writes=[r_ybuf[h // 2]])
                                return run
                            pending_epi.extend(make_epi(pr, hh, t4) for hh in range(2) for t4 in range(4))
                        for f in pending_epi:
                            f()
                        S.barrier()

                if stop_after not in ("p1", "att"):
                    with ExitStack() as ph:
                        wexpT = sbt(ph, "wexpT", [128, 128], F32); r_wexpT = Res()
                        floorT = sbt(ph, "floorT", [128, 128], F32); r_floorT = Res()
                        decB = sbt(ph, "decB", [128, 128], F32); r_decB = Res()

                        with ExitStack() as pg:
                            wg = sbt(pg, "wg", [128, 8, 8], BF16); r_wg = Res()
                            pG = pst(pg, "mG", [128, 512], F32); r_pG = PRes()
                            gA = sbt(pg, "gA", [4, SEQ], F32); r_gA = Res()
                            gB = sbt(pg, "gB", [4, SEQ], F32); r_gB = Res()
                            gC = sbt(pg, "gC", [4, SEQ], F32); r_gC = Res()
                            gD = sbt(pg, "gD", [4, SEQ], F32); r_gD = Res()
                            sm = sbt(pg, "sm", [4, 12, 32], F32); r_sm = Res()
                            nfb = sbt(pg, "nfb", [4, 1], F32); r_nfb = Res()
                            dmask = sbt(pg, "dmask", [4, 128], F32); r_dmask = Res()
                            decD = sbt(pg, "decD", [4, 128], F32); r_decD = Res()
                            wgf = sbt(pg, "wgf", [128, 8, 8], F32); r_wgf = Res()
                            with nc.allow_non_contiguous_dma(reason="gate weight columns (32B runs)"):
                                S.dma("sp", wgf[:], w_in[:, 3584:3592].rearrange("(c p) m -> p c m", p=128), writes=[r_wgf])
                            S.op("dve", lambda e: e.tensor_copy(out=wg[:], in_=wgf[:]), reads=[r_wgf], writes=[r_wg])
                            S.op("dve", lambda e: e.memset(gC[:], 1.0), writes=[r_gC])
                            S.op("dve", lambda e: e.memset(gC[:].rearrange("p (j t) -> p j t", t=128)[:, :, 0:1], 0.0), writes=[r_gC])
                            S.op("dve", lambda e: e.tensor_scalar(out=nfb[:], in0=gb[:, 1:2], scalar1=-1.0, scalar2=None, op0=ALU.mult), reads=[r_gb], writes=[r_nfb])
                            S.op("pool", lambda e: e.affine_select(out=dmask[:].rearrange("p (h j) -> p h j", h=4), in_=ones_f[0:4, :].rearrange("p (h j) -> p h j", h=4),
                                                                   pattern=[[-1, 4], [0, 32]], compare_op=ALU.is_equal, fill=0.0, base=0, channel_multiplier=1),
                                 reads=[r_ones], writes=[r_dmask])
                            for tt in range(8):
                                cs = slice(tt * 512, (tt + 1) * 512)
                                for c in range(8):
                                    S.op("pe", lambda e: e.matmul(pG[0:4, :], lhsT=wg[:, c, 0:4], rhs=xnT[:, c, cs], start=(c == 0), stop=(c == 7)),
                                         reads=[r_wg, r_xnT[tt]], writes=[r_pG])
                                S.op("act", lambda e: e.activation(out=gA[:, cs], in_=pG[0:4, :], func=AF.Identity, bias=gb[:, 0:1], scale=1.0),
                                     reads=[r_pG, r_gb], pwrites=[r_gA])
                                for c in range(8):
                                    S.op("pe", lambda e: e.matmul(pG[0:4, :], lhsT=wg[:, c, 4:8], rhs=xnT[:, c, cs], start=(c == 0), stop=(c == 7)),
                                         reads=[r_wg, r_xnT[tt]], writes=[r_pG])
                                S.op("act", lambda e: e.activation(out=gB[:, cs], in_=pG[0:4, :], func=AF.Exp, bias=nfb[:, 0:1], scale=-1.0),
                                     reads=[r_pG, r_nfb], pwrites=[r_gB])
                            S.op("act", lambda e: e.activation(out=gB[:], in_=gB[:], func=AF.Ln, bias=1.0, scale=1.0), reads=[r_gB], writes=[r_gB])
                            S.op("dve", lambda e: e.tensor_scalar(out=gD[:], in0=gB[:], scalar1=-1.0, scalar2=None, op0=ALU.mult), reads=[r_gB], writes=[r_gD])
                            S.op("dve", lambda e: e.tensor_tensor_scan(out=gB[:], data0=gC[:], data1=gD[:], initial=0.0, op0=ALU.mult, op1=ALU.add),
                                 reads=[r_gC, r_gD], writes=[r_gB])
                            S.op("dve", lambda e: e.tensor_tensor(out=gA[:], in0=gA[:], in1=gB[:], op=ALU.subtract), reads=[r_gA, r_gB], writes=[r_gA])
                            rmax, gch, ginc, gx, rho, mun, mu, u_, dec_, tmp_ = [sm[:, i, :] for i in range(10)]
                            S.op("dve", lambda e: e.tensor_reduce(out=rmax, in_=gA[:].rearrange("p (j t) -> p j t", t=128), axis=AX.X, op=ALU.max),
                                 reads=[r_gA], writes=[r_sm])
                            S.op("dve", lambda e: e.tensor_copy(out=gch, in_=gB[:].rearrange("p (j t) -> p j t", t=128)[:, :, 127]), reads=[r_gB], writes=[r_sm])
                            S.op("dve", lambda e: e.memset(tmp_, 1.0), writes=[r_sm])
                            S.op("dve", lambda e: e.tensor_tensor_scan(out=ginc, data0=tmp_, data1=gch, initial=0.0, op0=ALU.mult, op1=ALU.add), reads=[r_sm], writes=[r_sm])
                            S.op("dve", lambda e: e.tensor_tensor(out=gx, in0=ginc, in1=gch, op=ALU.subtract), reads=[r_sm], writes=[r_sm])
                            S.op("dve", lambda e: e.tensor_tensor(out=rho, in0=rmax, in1=gx, op=ALU.subtract), reads=[r_sm], writes=[r_sm])
                            S.op("dve", lambda e: e.tensor_tensor_scan(out=mun, data0=rho, data1=rho, initial=0.0, op0=ALU.max, op1=ALU.max), reads=[r_sm], writes=[r_sm])
                            S.op("dve", lambda e: e.memset(mu, 0.0), writes=[r_sm])
                            S.op("dve", lambda e: e.tensor_copy(out=sm[:, 6, 1:32], in_=sm[:, 5, 0:31]), reads=[r_sm], writes=[r_sm])
                            S.op("dve", lambda e: e.tensor_tensor(out=u_, in0=gx, in1=mun, op=ALU.add), reads=[r_sm], writes=[r_sm])
                            S.op("dve", lambda e: e.tensor_tensor(out=dec_, in0=mu, in1=mun, op=ALU.subtract), reads=[r_sm], writes=[r_sm])
                            S.op("act", lambda e: e.activation(out=dec_, in_=dec_, func=AF.Exp), reads=[r_sm], writes=[r_sm])
                            ub = sm[:, 7, :].unsqueeze(2).to_broadcast([4, 32, 128])
                            S.op("dve", lambda e: e.tensor_tensor(out=gA[:].rearrange("p (j t) -> p j t", t=128), in0=gA[:].rearrange("p (j t) -> p j t", t=128),
                                                                  in1=ub, op=ALU.subtract), reads=[r_gA, r_sm], writes=[r_gA])
                            S.op("dve", lambda e: e.tensor_scalar(out=gA[:], in0=gA[:], scalar1=-0.5 * float(np.log(128.0)), scalar2=None, op0=ALU.add),
                                 reads=[r_gA], writes=[r_gA])
                            S.op("act", lambda e: e.activation(out=gA[:], in_=gA[:], func=AF.Exp), reads=[r_gA], writes=[r_gA])
                            S.op("dve", lambda e: e.tensor_tensor(out=gB[:].rearrange("p (j t) -> p j t", t=128), in0=gB[:].rearrange("p (j t) -> p j t", t=128),
                                                                  in1=ub, op=ALU.add), reads=[r_gB, r_sm], writes=[r_gB])
                            S.op("act", lambda e: e.activation(out=gB[:], in_=gB[:], func=AF.Exp, scale=-1.0), reads=[r_gB], writes=[r_gB])
                            for src, rsrc, dstT, rdstT in ((gA, r_gA, wexpT, r_wexpT), (gB, r_gB, floorT, r_floorT)):
                                for j in range(32):
                                    S.op("pe", lambda e: e.matmul(pG[:, j * 4:(j + 1) * 4], lhsT=src[0:4, j * 128:(j + 1) * 128], rhs=ident_f[0:4, 0:4], start=True, stop=True),
                                         reads=[rsrc, r_idf], writes=[r_pG])
                                S.op("dve", lambda e: e.tensor_copy(out=dstT[:].rearrange("p (h j) -> p j h", h=4), in_=pG[:, 0:128].rearrange("p (j h) -> p j h", h=4)),
                                     reads=[r_pG], writes=[rdstT])
                            S.op("dve", lambda e: e.tensor_tensor(out=decD[:].rearrange("p (h j) -> p h j", h=4), in0=dmask[:].rearrange("p (h j) -> p h j", h=4),
                                                                  in1=sm[:, 8, :].unsqueeze(1).to_broadcast([4, 4, 32]), op=ALU.mult), reads=[r_dmask, r_sm], writes=[r_decD])
                            S.op("pe", lambda e: e.matmul(pG[:, 0:128], lhsT=ones_f[0:4, :], rhs=decD[:], start=True, stop=True), reads=[r_ones, r_decD], writes=[r_pG])
                            S.op("dve", lambda e: e.tensor_copy(out=decB[:], in_=pG[:, 0:128]), reads=[r_pG], writes=[r_decB])
                            if debug:
                                o_wexp = dout("o_wexp", [4, SEQ]); o_floor = dout("o_floor", [4, SEQ]); o_sm = dout("o_sm", [4, 12 * 32])
                                o_wexpT = dout("o_wexpT", [128, 128]); o_decB = dout("o_decB", [128, 128])
                                S.dma("sp", o_wexp[:, :], gA[:], reads=[r_gA]); S.dma("sp", o_floor[:, :], gB[:], reads=[r_gB])
                                S.dma("sp", o_sm[:, :], sm[:].rearrange("p a b -> p (a b)"), reads=[r_sm])
                                S.dma("sp", o_wexpT[:, :], wexpT[:], reads=[r_wexpT]); S.dma("sp", o_decB[:, :], decB[:], reads=[r_decB])
                            S.barrier()

                        stop("mlg")
                        WD = sbt(ph, "WD", [128, 128], F32); r_WD = Res()
                        fl2 = sbt(ph, "fl2", [128, 128], F32); r_fl2 = Res()
                        S.op("dve", lambda e: e.tensor_tensor(out=WD[:, 0:127], in0=wexpT[:, 0:127], in1=decB[:, 1:128], op=ALU.mult), reads=[r_wexpT, r_decB], writes=[r_WD])
                        S.op("dve", lambda e: e.tensor_tensor(out=fl2[:], in0=floorT[:], in1=floorT[:], op=ALU.mult), reads=[r_floorT], writes=[r_fl2])
                        HS = SEQ // 2
                        wqk = [sbt(ph, f"wqk{i}", [128, 8, 2, 128], BF16) for i in range(2)]; r_wqk = [Res(), Res()]
                        wvo = [sbt(ph, f"wvo{i}", [128, 8, 2, 128], BF16) for i in range(2)]; r_wvo = [Res(), Res()]
                        xpre = sbt(ph, "xpre", [128, 3 + HS], F32); r_xpre = Res(); r_xh = Res()
                        cv = sbt(ph, "cv", [128, HS], F32); r_cv = Res()
                        qkT = sbt(ph, "qkT", [128, 2, SEQ], BF16); r_qkT = [Res(), Res()]
                        Vm = sbt(ph, "Vm", [128, 32, 132], BF16); r_Vm = Res()
                        SG = sbt(ph, "SG", [128, 32, 128], BF16); r_SG = Res()
                        sgt = [sbt(ph, f"sgt{i}", [128, 2, 128], F32) for i in range(2)]; r_sgt = [Res(), Res()]
                        Kw = sbt(ph, "Kw", [128, 32, 128], BF16); r_Kw = Res()
                        yTm = sbt(ph, "yTm", [128, SEQ], BF16); r_yTm = Res()
                        Cst = sbt(ph, "Cst", [128, 129], F32); r_C = Res()
                        Cbf = sbt(ph, "Cbf", [128, 132], BF16); r_Cb = Res()
                        PTm = [sbt(ph, f"PTm{i}", [128, 128], BF16) for i in range(2)]; r_PTm = [Res(), Res()]
                        ytl = [sbt(ph, f"ytl{i}", [128, 128], BF16) for i in range(2)]; r_ytl = [Res(), Res()]
                        NSC = 4
                        sc = [sbt(ph, f"msc{i}", [128, 8], F32) for i in range(NSC)]; r_sc = [[Res() for _ in range(4)] for _ in range(NSC)]
                        junkm = [sbt(ph, f"junkm{i}", [128, 128], BF16) for i in range(2)]; r_junkm = [Res(), Res()]
                        pP = [pst(ph, f"mP{i}", [128, 512], F32) for i in range(2)]; r_pP = [PRes(), PRes()]
                        pST_ = [pst(ph, f"mST{i}", [128, 512], F32) for i in range(2)]; r_pST = [PRes(), PRes()]
                        pST = [t[:, 0:128] for t in pST_]
                        pA_ = [pst(ph, f"mA{i}", [128, 512], F32) for i in range(2)]
                        pA = [pA_[0][:, 0:129], pA_[1][:, 0:129], pP[0][:, 0:129]]; r_pA = [PRes(), PRes(), r_pP[0]]
                        pC_ = pst(ph, "mC", [128, 512], F32); r_pC = PRes()
                        pC = pC_[:, 0:129]
                        pTb_ = pst(ph, "mTb", [128, 1024], BF16); r_pTb = [PRes()]
                        pTb = [pTb_[:, 0:512]]
                        S.op("pool", lambda e: e.memset(Vm[:, :, 128:129], 1.0), writes=[r_Vm])
                        cnt_m = {"pj": 0}
                        for h in range(4):
                            wq_ = wqk[h % 2]; rwq = r_wqk[h % 2]
                            wv_ = wvo[h % 2]; rwv = r_wvo[h % 2]
                            for j in range(2):
                                c0 = 1536 + j * 512 + h * 128
                                S.dma("pool", wq_[:, :, j, :], w_in[:, c0:c0 + 128].rearrange("(c p) m -> p c m", p=128), pwrites=[rwq])
                                c1 = 2560 + j * 512 + h * 128
                                S.dma("pool", wv_[:, :, j, :], w_in[:, c1:c1 + 128].rearrange("(c p) m -> p c m", p=128), pwrites=[rwv])
                            if h == 1:
                                for m in range(NCH):
                                    S.dma("pool", wdn[:, m, :], w_ffn_down[m * 128:(m + 1) * 128, :], pwrites=[r_wdn])
                            def vo_tiles(jt0, jt1):
                                for jt in range(jt0, jt1, 2):
                                    k = cnt_m["pj"] % 2; cnt_m["pj"] += 1
                                    for a in range(2):
                                        tsl = slice((jt + a) * 128, (jt + a + 1) * 128)
                                        for c in range(8):
                                            S.op("pe", lambda e: e.matmul(pP[k][:, a * 256:(a + 1) * 256], lhsT=xnT[:, c, tsl], rhs=wv_[:, c, :, :].rearrange("p a b -> p (a b)"),
                                                                          start=(c == 0), stop=(c == 7)), reads=[rwv, r_xnT[(jt + a) // 4]], writes=[r_pP[k]])
                                    pv = pP[k][:].rearrange("p (a j e) -> p a j e", a=2, j=2)
                                    S.op("act", lambda e: e.activation(out=Vm[:, jt:jt + 2, 0:128], in_=pv[:, :, 0, :], func=AF.Copy), reads=[r_pP[k]], pwrites=[r_Vm])
                                    sg_ = sgt[(jt // 2) % 2]; rsg = r_sgt[(jt // 2) % 2]
                                    S.op("act", lambda e: e.activation(out=sg_[:], in_=pv[:, :, 1, :], func=AF.Sigmoid), reads=[r_pP[k]], writes=[rsg])
                                    for a in range(2):
                                        S.op("pool", lambda e: e.tensor_tensor(out=SG[:, jt + a, :], in0=sg_[:, a, :], in1=MNW[:, h * 128:(h + 1) * 128], op=ALU.mult),
                                             reads=[rsg, r_MNW], pwrites=[r_SG])

                            for step, (j, half) in enumerate(((0, 0), (0, 1), (1, 0), (1, 1))):
                                ch = j * 4 + h
                                if half == 0:
                                    S.op("dve", lambda e: e.memset(xpre[:, 0:3], 0.0), writes=[r_xh])
                                else:
                                    S.op("dve", lambda e: e.tensor_copy(out=xpre[:, 0:3], in_=xpre[:, HS:HS + 3]), reads=[r_xpre], writes=[r_xh])
                                for t4 in range(4):
                                    tt = half * 4 + t4
                                    k = cnt_m["pj"] % 2; cnt_m["pj"] += 1
                                    for c in range(8):
                                        S.op("pe", lambda e: e.matmul(pP[k][:], lhsT=wq_[:, c, j, :], rhs=xnT[:, c, tt * 512:(tt + 1) * 512], start=(c == 0), stop=(c == 7)),
                                             reads=[rwq, r_xnT[tt]], writes=[r_pP[k]])
                                    S.op("act", lambda e: e.activation(out=xpre[:, 3 + t4 * 512:3 + (t4 + 1) * 512], in_=pP[k][:], func=AF.Copy),
                                         reads=[r_pP[k]], writes=([r_xpre] if t4 == 0 else []), pwrites=([] if t4 == 0 else [r_xpre]))
                                    S.op("act", lambda e: e.activation(out=cv[:, t4 * 512:(t4 + 1) * 512], in_=pP[k][:], func=AF.Identity, scale=mcw(3, ch), bias=mcb(ch)),
                                         reads=[r_pP[k], r_colA], writes=([r_cv] if t4 == 0 else []), pwrites=([] if t4 == 0 else [r_cv]))
                                for tap in range(3):
                                    S.op("dve", lambda e: e.scalar_tensor_tensor(out=cv[:], in0=xpre[:, tap:tap + HS], scalar=mcw(tap, ch), in1=cv[:], op0=ALU.mult, op1=ALU.add),
                                         reads=[r_xpre, r_xh, r_cv, r_colA], writes=[r_cv])
                                vo_tiles(step * 8, step * 8 + 8)
                                S.op("act", lambda e: e.activation(out=qkT[:, j, half * HS:(half + 1) * HS], in_=cv[:], func=AF.Silu), reads=[r_cv], pwrites=[r_qkT[j]])
                            for j0 in range(0, 32, 4):
                                for a in range(4):
                                    j = j0 + a
                                    S.op("pe", lambda e: e.transpose(out=pTb[0][:, a * 128:(a + 1) * 128], in_=qkT[:, 1, j * 128:(j + 1) * 128], identity=ident_b[:]),
                                         reads=[r_qkT[1], r_idb], writes=[r_pTb[0]])
                                for a in range(4):
                                    j = j0 + a
                                    col = h * 32 + j
                                    if j == 31:
                                        continue
                                    if a % 2 == 0:
                                        S.op("dve", lambda e: e.tensor_scalar(out=Kw[:, j, :], in0=pTb[0][:, a * 128:(a + 1) * 128], scalar1=WD[:, col:col + 1], scalar2=None,
                                                                              op0=ALU.mult), reads=[r_pTb[0], r_WD], pwrites=[r_Kw])
                                    else:
                                        S.op("act", lambda e: e.activation(out=Kw[:, j, :], in_=pTb[0][:, a * 128:(a + 1) * 128], func=AF.Copy, scale=WD[:, col:col + 1]),
                                             reads=[r_pTb[0], r_WD], pwrites=[r_Kw])
                            NJ = 32
                            colh = lambda j: h * 32 + j

                            def st_ST(j):
                                kk = j % 2; tsl = slice(j * 128, (j + 1) * 128)
                                S.op("pe", lambda e: e.matmul(pST[kk], lhsT=qkT[:, 1, tsl], rhs=qkT[:, 0, tsl], start=True, stop=True),
                                     reads=[r_qkT[0], r_qkT[1]], writes=[r_pST[kk]])

                            def st_PTm(j):
                                kk = j % 2; col = colh(j)
                                S.op("dve", lambda e: e.scalar_tensor_tensor(out=PTm[kk][:], in0=pST[kk], scalar=wexpT[:, col:col + 1], in1=mask01[:], op0=ALU.mult, op1=ALU.mult),
                                     reads=[r_pST[kk], r_wexpT, r_m01], writes=[r_PTm[kk]])

                            def st_pA(j):
                                kk = j % 2; ka = j % 3; tsl = slice(j * 128, (j + 1) * 128)
                                S.op("pe", lambda e: e.matmul(pA[ka], lhsT=PTm[kk][:], rhs=Vm[:, j, 0:129], start=True, stop=(j == 0)), reads=[r_PTm[kk], r_Vm], writes=[r_pA[ka]])
                                if j > 0:
                                    S.op("pe", lambda e: e.matmul(pA[ka], lhsT=qkT[:, 0, tsl], rhs=Cbf[:, 0:129], start=False, stop=True), reads=[r_qkT[0], r_Cb], writes=[r_pA[ka]])

                            def st_pC(j):
                                S.op("pe", lambda e: e.matmul(pC, lhsT=Kw[:, j, :], rhs=Vm[:, j, 0:129], start=True, stop=True), reads=[r_Kw, r_Vm], writes=[r_pC])

                            def st_state(j):
                                col = colh(j)
                                if j == 0:
                                    S.op("dve", lambda e: e.tensor_copy(out=Cst[:], in_=pC), reads=[r_pC], writes=[r_C])
                                else:
                                    S.op("dve", lambda e: e.scalar_tensor_tensor(out=Cst[:], in0=Cst[:], scalar=decB[:, col + 1:col + 2], in1=pC, op0=ALU.mult, op1=ALU.add),
                                         reads=[r_C, r_decB, r_pC], writes=[r_C])

                            def st_Cbf(j):
                                S.op("act", lambda e: e.activation(out=Cbf[:, 0:129], in_=Cst[:], func=AF.Copy), reads=[r_C], writes=[r_Cb])

                            def st_sq(j):
                                ka = j % 3; s_ = sc[j % NSC]; rs_ = r_sc[j % NSC]
                                S.op("act", lambda e: e.activation(out=s_[:, 0:1], in_=pA[ka][:, 128:129], func=AF.Square), reads=[r_pA[ka]], writes=[rs_[0]])
                                S.op("act", lambda e: e.activation(out=junkm[j % 2][:], in_=pA[ka][:, 0:128], func=AF.Square, accum_out=s_[:, 1:2]),
                                     reads=[r_pA[ka]], writes=[r_junkm[j % 2]], pwrites=[rs_[0]])

                            def st_tv(j):
                                col = colh(j); s_ = sc[j % NSC]; rs_ = r_sc[j % NSC]
                                S.op("dve", lambda e: e.tensor_scalar(out=s_[:, 2:3], in0=s_[:, 0:1], scalar1=fl2[:, col:col + 1], scalar2=EPS, op0=ALU.max, op1=ALU.mult),
                                     reads=[rs_[0], r_fl2], writes=[rs_[1]])
                                S.op("dve", lambda e: e.scalar_tensor_tensor(out=s_[:, 3:4], in0=s_[:, 1:2], scalar=1.0 / 128.0, in1=s_[:, 2:3], op0=ALU.mult, op1=ALU.add),
                                     reads=[rs_[0], rs_[1]], writes=[rs_[2]])

                            def st_lnexp(j):
                                s_ = sc[j % NSC]; rs_ = r_sc[j % NSC]
                                S.op("act", lambda e: e.activation(out=s_[:, 4:5], in_=s_[:, 3:4], func=AF.Ln), reads=[rs_[2]], writes=[rs_[3]])
                                S.op("act", lambda e: e.activation(out=s_[:, 5:6], in_=s_[:, 4:5], func=AF.Exp, scale=-0.5), reads=[rs_[3]], writes=[rs_[3]])

                            def st_ytl(j):
                                ka = j % 3; s_ = sc[j % NSC]; rs_ = r_sc[j % NSC]
                                S.op("dve", lambda e: e.scalar_tensor_tensor(out=ytl[j % 2][:], in0=pA[ka][:, 0:128], scalar=s_[:, 5:6], in1=SG[:, j, :], op0=ALU.mult, op1=ALU.mult),
                                     reads=[r_pA[ka], rs_[3], r_SG], writes=[r_ytl[j % 2]])

                            def st_T(j):
                                a = j % 4
                                S.op("pe", lambda e: e.transpose(out=pTb[0][:, a * 128:(a + 1) * 128], in_=ytl[j % 2][:], identity=ident_b[:]), reads=[r_ytl[j % 2], r_idb], writes=[r_pTb[0]])

                            def st_ym(j):
                                if j % 4 == 3:
                                    S.op("act", lambda e: e.activation(out=yTm[:, (j - 3) * 128:(j + 1) * 128], in_=pTb[0], func=AF.Copy), reads=[r_pTb[0]], pwrites=[r_yTm])

                            ok = lambda j: 0 <= j < NJ
                            st_ST(0); st_PTm(0)
                            for it in range(NJ + 5):
                                if ok(it + 1): st_ST(it + 1)
                                if ok(it): st_pA(it)
                                if ok(it) and it < NJ - 1: st_pC(it)
                                if ok(it - 3): st_T(it - 3)
                                if ok(it + 1): st_PTm(it + 1)
                                if ok(it): st_sq(it)
                                if ok(it) and it < NJ - 1:
                                    st_state(it)
                                    st_Cbf(it + 1)
                                if ok(it - 1): st_tv(it - 1)
                                if ok(it - 1): st_lnexp(it - 1)
                                if ok(it - 2): st_ytl(it - 2)
                                if ok(it - 3): st_ym(it - 3)
                            S.dma("sp", ybuf[512 + h * 128:512 + (h + 1) * 128, :], yTm[:], reads=[r_yTm], writes=[r_ybuf[4 + h]])
                        S.barrier()
        S.barrier()

        c2.close()
        if stop_after is None:
            TT = 256
            NTT = SEQ // TT
            NSUB = TT // 128
            NR = 6
            with ExitStack() as ph:
                wout = sbt(ph, "wout", [128, 8, D], BF16); r_wout = Res()
                wup = sbt(ph, "wup", [128, 8, 2 * DFF], BF16); r_wup = [Res() for _ in range(NCH)]
                ytb = sbt(ph, "f_y", [128, 8, TT], BF16); r_ytb = Res()
                h1 = [sbt(ph, f"f_h{i}", [128, NSUB, D], F32) for i in range(2)]; r_h1 = [Res(), Res()]
                hb = sbt(ph, "f_hb", [128, NSUB, D], BF16); r_hb = Res()
                hnT = sbt(ph, "f_hnT", [128, 8, 2 + TT], BF16); r_hnT = Res(); r_hnTh = Res()
                Rb = [sbt(ph, f"f_R{i}", [128, 2 + TT], F32) for i in range(NR)]; r_Rb = [Res() for _ in range(NR)]
                cvb = [sbt(ph, f"f_cv{i}", [128, TT], F32) for i in range(NR)]; r_cvb = [Res() for _ in range(NR)]
                gT = sbt(ph, "f_gT", [128, NCH, TT], BF16); r_gT = Res()
                fss = [sbt(ph, f"f_ss{i}", [128, 8], F32) for i in range(2)]; r_fss = [Res(), Res()]
                fjunk = sbt(ph, "f_junk", [128, D], BF16); r_fjunk = Res()
                pH = [pst(ph, f"fH{i}", [128, 512], F32) for i in range(2)]; r_pH = [PRes(), PRes()]
                pU = [pst(ph, f"fU{i}", [128, 512], F32) for i in range(4)]; r_pU = [PRes() for _ in range(4)]
                pT3_ = [pst(ph, f"fT{i}", [128, 1024], BF16) for i in range(2)]; r_pT3 = [PRes(), PRes()]
                pT3 = [t[:, 0:512] for t in pT3_]
                for c in range(8):
                    S.dma("pool", wout[:, c, :], w_out[c * 128:(c + 1) * 128, :], pwrites=[r_wout])
                with nc.allow_non_contiguous_dma(reason="512B runs"):
                    for m in range(NCH):
                        for gv in range(2):
                            col0 = gv * DFF + m * 128
                            S.dma("pool", wup[:, :, col0:col0 + 128], w_ffn_up[:, col0:col0 + 128].rearrange("(c p) m -> p c m", p=128), pwrites=[r_wup[m]])
                S.op("dve", lambda e: e.memset(hnT[:, :, 0:2], 0.0), writes=[r_hnTh])
                cnt = {"H": 0, "U": 0, "R": 0, "T": 0}

                def load_tile(tt):
                    S.dma("sp", ytb[:], ybuf[:, tt * TT:(tt + 1) * TT].rearrange("(c p) t -> p c t", p=128), reads=r_ybuf, writes=[r_ytb])
                    S.dma("sp", h1[tt % 2][:], x[tt * TT:(tt + 1) * TT, :].rearrange("(s p) d -> p s d", p=128), writes=[r_h1[tt % 2]])

                def outproj(tt):
                    hh_ = h1[tt % 2]; rhh = r_h1[tt % 2]; fs = fss[tt % 2]; rfs = r_fss[tt % 2]
                    for s in range(NSUB):
                        for hf in range(2):
                            k = cnt["H"] % 2; cnt["H"] += 1
                            cs = slice(hf * 512, (hf + 1) * 512)
                            for c in range(8):
                                S.op("pe", lambda e: e.matmul(pH[k][:], lhsT=ytb[:, c, s * 128:(s + 1) * 128], rhs=wout[:, c, cs], start=(c == 0), stop=(c == 7)),
                                     reads=[r_ytb, r_wout], writes=[r_pH[k]])
                            S.op("dve", lambda e: e.tensor_tensor(out=hh_[:, s, cs], in0=hh_[:, s, cs], in1=pH[k][:], op=ALU.add), reads=[rhh, r_pH[k]], pwrites=[rhh])
                        S.op("act", lambda e: e.activation(out=fjunk[:], in_=hh_[:, s, :], func=AF.Square, scale=1.0 / 32.0, accum_out=fs[:, s:s + 1]),
                             reads=[rhh], writes=[r_fjunk], pwrites=[rfs])
                    S.op("act", lambda e: e.activation(out=fs[:, 2:2 + NSUB], in_=fs[:, 0:NSUB], func=AF.Ln, bias=EPS, scale=1.0), reads=[rfs], pwrites=[rfs])
                    S.op("act", lambda e: e.activation(out=fs[:, 2:2 + NSUB], in_=fs[:, 2:2 + NSUB], func=AF.Exp, scale=-0.5), reads=[rfs], pwrites=[rfs])
                    for s in range(NSUB):
                        S.op("dve", lambda e: e.tensor_scalar(out=hb[:, s, :], in0=hh_[:, s, :], scalar1=fs[:, 2 + s:3 + s], scalar2=None, op0=ALU.mult),
                             reads=[rhh, rfs], pwrites=[r_hb])

                def transposes(tt):
                    if tt > 0:
                        S.op("dve", lambda e: e.tensor_copy(out=hnT[:, :, 0:2], in_=hnT[:, :, TT:TT + 2]), reads=[r_hnT], writes=[r_hnTh])
                    for c0 in range(0, 8, 2):
                        k = cnt["T"] % 2; cnt["T"] += 1
                        for a in range(2):
                            for s in range(NSUB):
                                S.op("pe", lambda e: e.transpose(out=pT3[k][:, a * 256 + s * 128:a * 256 + (s + 1) * 128], in_=hb[:, s, (c0 + a) * 128:(c0 + a + 1) * 128],
                                                                 identity=ident_b[:]), reads=[r_hb, r_idb], writes=[r_pT3[k]])
                        S.op("dve", lambda e: e.tensor_scalar(out=hnT[:, c0, 2:2 + TT], in0=pT3[k][:, 0:256], scalar1=fnw(c0), scalar2=None, op0=ALU.mult),
                             reads=[r_pT3[k], r_colA], pwrites=[r_hnT])
                        S.op("act", lambda e: e.activation(out=hnT[:, c0 + 1, 2:2 + TT], in_=pT3[k][:, 256:512], func=AF.Copy, scale=fnw(c0 + 1)),
                             reads=[r_pT3[k], r_colA], pwrites=[r_hnT])

                def up(tt):
                    pend = None
                    for m in range(NCH + 1):
                        if m < NCH:
                            banks = []
                            for gv in range(2):
                                kU = cnt["U"] % 4; cnt["U"] += 1
                                banks.append(kU)
                                col0 = gv * DFF + m * 128
                                for c in range(8):
                                    S.op("pe", lambda e: e.matmul(pU[kU][:, 0:2 + TT], lhsT=wup[:, c, col0:col0 + 128], rhs=hnT[:, c, :], start=(c == 0), stop=(c == 7)),
                                         reads=[r_wup[m], r_hnT, r_hnTh], writes=[r_pU[kU]])
                            bufs = []
                            for gv in range(2):
                                ch = gv * NCH + m
                                kU = banks[gv]
                                kR = cnt["R"] % NR; cnt["R"] += 1
                                R_ = Rb[kR]; rR = r_Rb[kR]; cv_ = cvb[kR]; rcv = r_cvb[kR]
                                S.op("act", lambda e: e.activation(out=R_[:], in_=pU[kU][:, 0:2 + TT], func=AF.Copy), reads=[r_pU[kU]], writes=[rR])
                                S.op("act", lambda e: e.activation(out=cv_[:], in_=pU[kU][:, 2:2 + TT], func=AF.Identity, scale=fcw(2, ch), bias=fcb(ch)),
                                     reads=[r_pU[kU], r_colB, r_colC], writes=[rcv])
                                S.op("dve", lambda e: e.scalar_tensor_tensor(out=cv_[:], in0=R_[:, 1:1 + TT], scalar=fcw(1, ch), in1=cv_[:], op0=ALU.mult, op1=ALU.add),
                                     reads=[rR, rcv, r_colB], writes=[rcv])
                                S.op("dve", lambda e: e.scalar_tensor_tensor(out=cv_[:], in0=R_[:, 0:TT], scalar=fcw(0, ch), in1=cv_[:], op0=ALU.mult, op1=ALU.add),
                                     reads=[rR, rcv, r_colB], writes=[rcv])
                                bufs.append((cv_, rcv))
                        if pend is not None:
                            pm, ((cg, rcg), (cvv, rcvv)) = pend
                            S.op("act", lambda e: e.activation(out=cg[:], in_=cg[:], func=AF.Silu), reads=[rcg], writes=[rcg])
                            S.op("pool", lambda e: e.tensor_tensor(out=gT[:, pm, :], in0=cg[:], in1=cvv[:], op=ALU.mult), reads=[rcg, rcvv], pwrites=[r_gT])
                        pend = (m, bufs) if m < NCH else None

                def down(tt, s):
                    hh_ = h1[tt % 2]; rhh = r_h1[tt % 2]; fs = fss[tt % 2]; rfs = r_fss[tt % 2]
                    for hf in range(2):
                        k = cnt["H"] % 2; cnt["H"] += 1
                        cs = slice(hf * 512, (hf + 1) * 512)
                        for m in range(NCH):
                            S.op("pe", lambda e: e.matmul(pH[k][:], lhsT=gT[:, m, s * 128:(s + 1) * 128], rhs=wdn[:, m, cs], start=(m == 0), stop=(m == NCH - 1)),
                                 reads=[r_gT, r_wdn], writes=[r_pH[k]])
                        S.op("dve", lambda e: e.tensor_tensor(out=hh_[:, s, cs], in0=hh_[:, s, cs], in1=pH[k][:], op=ALU.add), reads=[rhh, r_pH[k]], pwrites=[rhh])
                    S.op("act", lambda e: e.activation(out=fjunk[:], in_=hh_[:, s, :], func=AF.Square, scale=1.0 / 32.0, accum_out=fs[:, 4 + s:5 + s]),
                         reads=[rhh], writes=[r_fjunk], pwrites=[rfs])

                def final(tt):
                    hh_ = h1[tt % 2]; rhh = r_h1[tt % 2]; fs = fss[tt % 2]; rfs = r_fss[tt % 2]
                    S.op("act", lambda e: e.activation(out=fs[:, 6:6 + NSUB], in_=fs[:, 4:4 + NSUB], func=AF.Ln, bias=EPS, scale=1.0), reads=[rfs], pwrites=[rfs])
                    S.op("act", lambda e: e.activation(out=fs[:, 6:6 + NSUB], in_=fs[:, 6:6 + NSUB], func=AF.Exp, scale=-0.5), reads=[rfs], pwrites=[rfs])
                    for s in range(NSUB):
                        S.op("dve", lambda e: e.scalar_tensor_tensor(out=hh_[:, s, :], in0=hh_[:, s, :], scalar=fs[:, 6 + s:7 + s], in1=FW[:], op0=ALU.mult, op1=ALU.mult),
                             reads=[rhh, rfs, r_FW], pwrites=[rhh])
                    S.dma("sp", out[tt * TT:(tt + 1) * TT, :].rearrange("(s p) d -> p s d", p=128), hh_[:], reads=[rhh])

                load_tile(0)
                outproj(0)
                transposes(0)
                for tt in range(NTT):
                    up(tt)
                    if tt + 1 < NTT:
                        load_tile(tt + 1)
                        outproj(tt + 1)
                    down(tt, 0)
                    if tt + 1 < NTT:
                        transposes(tt + 1)
                    down(tt, 1)
                    final(tt)
                S.barrier()
        S.barrier()
        nc._ninst = dict(S.ninst)
    return nc, dbg


_NC_CACHE = {}


def _squeeze(a):
    return np.ascontiguousarray(np.asarray(a, dtype=np.float32))


def kernel(x, w_in, mlstm_conv_w, mlstm_conv_b, mlstm_i_bias, mlstm_f_bias, att_out_norm_w, mlstm_out_norm_w,
           w_out, mixer_norm_w, ffn_norm_w, w_ffn_up, ffn_conv_w, ffn_conv_b, w_ffn_down, final_norm_w):
    n = 8
    if "nc" not in _NC_CACHE:
        _NC_CACHE["nc"] = build_nc()[0]
    nc = _NC_CACHE["nc"]
    shared = {
        "w_in": _squeeze(w_in[0]), "mlstm_conv_w": _squeeze(mlstm_conv_w[0]), "mlstm_conv_b": _squeeze(mlstm_conv_b[0]),
        "mlstm_i_bias": _squeeze(mlstm_i_bias[0]), "mlstm_f_bias": _squeeze(mlstm_f_bias[0]),
        "att_out_norm_w": _squeeze(att_out_norm_w[0]), "mlstm_out_norm_w": _squeeze(mlstm_out_norm_w[0]),
        "w_out": _squeeze(w_out[0]), "mixer_norm_w": _squeeze(mixer_norm_w[0]), "ffn_norm_w": _squeeze(ffn_norm_w[0]),
        "w_ffn_up": _squeeze(w_ffn_up[0]), "ffn_conv_w": _squeeze(ffn_conv_w[0]), "ffn_conv_b": _squeeze(ffn_conv_b[0]),
        "w_ffn_down": _squeeze(w_ffn_down[0]), "final_norm_w": _squeeze(final_norm_w),
    }
    xs = np.asarray(x, dtype=np.float32)
    in_maps = [dict(shared, x=np.ascontiguousarray(xs[i])) for i in range(n)]
    res = run_bass_kernel_spmd(nc, in_maps, core_ids=list(range(n)))
    return np.stack([np.asarray(r["out"], dtype=np.float32) for r in res.results], axis=0)
```
